# Optimizing a Trainium2 kernel written in Bass

```python
import jax
import jax.numpy as jnp
from jax import lax
import numpy as np

D_MODEL = 1024
BATCH = 8
SEQ = 2048
DEPTH = 2
DEC_BATCH = 128
DEC_SEQ = 8
PAST_LEN = 16384
PAGE_SIZE = 128

N_META = 16
CONV_W = 4
CHUNK = 64
LRU_W = 256
LRU_BLOCKS = 4
LRU_BD = LRU_W // LRU_BLOCKS
LRU_C = 8.0
ML_H = 4
ML_D = 96
ML_W = ML_H * ML_D
GLA_H = 4
GLA_DK = 48
GLA_DV = 96
GLA_KW = GLA_H * GLA_DK
GLA_VW = GLA_H * GLA_DV
GLA_RANK = 16
GLA_TAU = 16.0
MIX_W = LRU_W + ML_W + GLA_VW
IN_SIZES = (LRU_W, LRU_W, ML_W, ML_W, GLA_KW, GLA_KW, GLA_VW, GLA_VW, GLA_RANK)
D_IN = 2 * LRU_W + 2 * ML_W + 2 * GLA_KW + 2 * GLA_VW + GLA_RANK
D_FF = 2816
EPS = 1e-6

kernel_name = 'hymba_style_rglru_mlstm_gla_macaron_step'


def _split_points():
    pts, acc = [], 0
    for s in IN_SIZES[:-1]:
        acc += s
        pts.append(acc)
    return pts


def rmsnorm(x, g):
    x32 = x.astype(jnp.float32)
    y = x32 * lax.rsqrt(jnp.mean(x32 * x32, axis=-1, keepdims=True) + EPS)
    return (y * g.astype(jnp.float32)).astype(x.dtype)


def head_rmsnorm(h, g):
    Bsz, L, H, Dh = h.shape
    y = h * lax.rsqrt(jnp.mean(h * h, axis=-1, keepdims=True) + EPS)
    return y.reshape(Bsz, L, H * Dh) * g.astype(jnp.float32)


def swiglu(x, w1, w3, w2):
    return (jax.nn.silu(x @ w1) * (x @ w3)) @ w2


def causal_conv(u, buf, w, b):
    L = u.shape[1]
    ext = jnp.concatenate([buf.astype(u.dtype), u], axis=1)
    out = ext[:, 0:L] * w[0]
    for j in range(1, CONV_W):
        out = out + ext[:, j:j + L] * w[j]
    return out + b, ext[:, ext.shape[1] - (CONV_W - 1):]


def rglru(x, h0, wa, ba, wx, bx, lam):
    Bsz, L, _ = x.shape
    xb = x.reshape(Bsz, L, LRU_BLOCKS, LRU_BD)
    r = jax.nn.sigmoid(jnp.einsum('blni,nij->blnj', xb, wa).reshape(Bsz, L, LRU_W) + ba)
    ig = jax.nn.sigmoid(jnp.einsum('blni,nij->blnj', xb, wx).reshape(Bsz, L, LRU_W) + bx)
    log_a = -LRU_C * r * jax.nn.softplus(-lam)
    a = jnp.exp(log_a)
    bterm = jnp.sqrt(-jnp.expm1(2.0 * log_a)) * (ig * x)
    bterm = bterm.at[:, 0].add(a[:, 0] * h0)

    def combine(left, right):
        a1, b1 = left
        a2, b2 = right
        return a1 * a2, a2 * b1 + b2

    _, hs = lax.associative_scan(combine, (a, bterm), axis=1)
    return hs, hs[:, -1]


def mlstm_chunk(state, inp):
    C = state[0].astype(jnp.float32)
    n = state[1].astype(jnp.float32)
    m = state[2].astype(jnp.float32)
    q, k, v, li, lf = inp
    T = q.shape[1]
    causal = jnp.tril(jnp.ones((T, T), dtype=bool))
    qs = q * (ML_D ** -0.5)
    b = jnp.cumsum(lf, axis=1).transpose(0, 2, 1)
    lih = li.transpose(0, 2, 1)
    dmat = jnp.where(causal, b[..., :, None] - b[..., None, :] + lih[..., None, :], -jnp.inf)
    m_inter = b + m[..., None]
    m_t = jnp.maximum(m_inter, jnp.max(dmat, axis=-1))
    w_intra = jnp.exp(dmat - m_t[..., None])
    w_inter = jnp.exp(m_inter - m_t)
    s = jnp.einsum('bthd,bshd->bhts', qs, k) * w_intra
    num = (jnp.einsum('bhts,bshe->bthe', s, v)
           + jnp.einsum('bthd,bhde->bthe', qs, C) * w_inter.transpose(0, 2, 1)[..., None])
    den = jnp.sum(s, axis=-1) + w_inter * jnp.einsum('bthd,bhd->bht', qs, n)
    denom = jnp.maximum(jnp.abs(den), jnp.exp(-m_t)).transpose(0, 2, 1)[..., None]
    h = num / denom
    m_new = m_t[..., -1]
    wk = jnp.exp(b[..., -1:] - b + lih - m_new[..., None])
    decay = jnp.exp(b[..., -1] + m - m_new)
    C_new = decay[..., None, None] * C + jnp.einsum('bhs,bshd,bshe->bhde', wk, k, v)
    n_new = decay[..., None] * n + jnp.einsum('bhs,bshd->bhd', wk, k)
    return (C_new, n_new, m_new), h


def gla_chunk(S, inp):
    S = S.astype(jnp.float32)
    q, k, v, lg = inp
    T = q.shape[1]
    causal = jnp.tril(jnp.ones((T, T), dtype=bool))
    bc = jnp.cumsum(lg, axis=1)
    diff = jnp.where(causal[None, :, :, None, None], bc[:, :, None] - bc[:, None, :], -jnp.inf)
    A = jnp.einsum('bthk,bshk,btshk->bhts', q, k, jnp.exp(diff))
    o = jnp.einsum('bhts,bshv->bthv', A, v) + jnp.einsum('bthk,bhkv->bthv', q * jnp.exp(bc), S)
    last = bc[:, -1]
    S_new = jnp.exp(last)[..., None] * S + jnp.einsum('bshk,bshv->bhkv', k * jnp.exp(last[:, None] - bc), v)
    return S_new, o


def run_chunked(chunk_fn, state, inputs, prompt):
    if not prompt:
        return chunk_fn(state, inputs)
    state, out_meta = chunk_fn(state, tuple(a[:, :N_META] for a in inputs))
    rest = tuple(a[:, N_META:] for a in inputs)
    Bsz, L = rest[0].shape[0], rest[0].shape[1]
    nc = L // CHUNK
    chunks = tuple(a.reshape((Bsz, nc, CHUNK) + a.shape[2:]).swapaxes(0, 1) for a in rest)
    state, outs = lax.scan(chunk_fn, state, chunks)
    out_rest = outs.swapaxes(0, 1).reshape((Bsz, L) + outs.shape[3:])
    return state, jnp.concatenate([out_meta, out_rest], axis=1)


def mix(h, st, lp, prompt):
    f32 = jnp.float32
    lru_h, lru_conv, m_C, m_n, m_m, m_conv, g_S = st
    Bsz, L, _ = h.shape
    z = h @ lp['w_in']
    u_r, g_r, u_m, z_m, q_g, k_g, v_g, g_g, a_lr = jnp.split(z, _split_points(), axis=-1)
    xr, lru_conv_new = causal_conv(u_r, lru_conv, lp['lru_conv_w'], lp['lru_conv_b'])
    hr, lru_h_new = rglru(xr.astype(f32), lru_h.astype(f32), lp['lru_wa'], lp['lru_ba'],
                          lp['lru_wx'], lp['lru_bx'], lp['lru_lambda'].astype(f32))
    y_r = jax.nn.gelu(g_r.astype(f32)) * hr
    cm, m_conv_new = causal_conv(u_m, m_conv, lp['ml_conv_w'], lp['ml_conv_b'])
    cm = jax.nn.silu(cm.astype(f32))
    cmh = cm.reshape(Bsz, L, ML_H, ML_D)
    umh = u_m.astype(f32).reshape(Bsz, L, ML_H, ML_D)
    mq = jnp.einsum('blhd,hde->blhe', cmh, lp['ml_wq'])
    mk = jnp.einsum('blhd,hde->blhe', cmh, lp['ml_wk'])
    mv = jnp.einsum('blhd,hde->blhe', umh, lp['ml_wv'])
    gin = jnp.concatenate([mq.reshape(Bsz, L, ML_W), mk.reshape(Bsz, L, ML_W), mv.reshape(Bsz, L, ML_W)], axis=-1)
    gates = gin @ lp['ml_w_if'] + lp['ml_b_if']
    li = gates[..., :ML_H]
    lf = jax.nn.log_sigmoid(gates[..., ML_H:])
    (m_C_new, m_n_new, m_m_new), hm = run_chunked(
        mlstm_chunk, (m_C.astype(f32), m_n.astype(f32), m_m.astype(f32)), (mq, mk, mv, li, lf), prompt)
    y_m = jax.nn.sigmoid(z_m.astype(f32)) * (head_rmsnorm(hm, lp['ml_norm_g']) + lp['ml_skip'] * cm)
    gq = q_g.astype(f32).reshape(Bsz, L, GLA_H, GLA_DK) * (GLA_DK ** -0.5)
    gk = k_g.astype(f32).reshape(Bsz, L, GLA_H, GLA_DK)
    gv = v_g.astype(f32).reshape(Bsz, L, GLA_H, GLA_DV)
    lg = (jax.nn.log_sigmoid(a_lr.astype(f32) @ lp['gla_w_up'] + lp['gla_b_up']) / GLA_TAU).reshape(Bsz, L, GLA_H, GLA_DK)
    g_S_new, og = run_chunked(gla_chunk, g_S.astype(f32), (gq, gk, gv, lg), prompt)
    y_g = head_rmsnorm(og, lp['gla_norm_g']) * jax.nn.silu(g_g.astype(f32))
    y = jnp.concatenate([y_r, y_m, y_g], axis=-1).astype(h.dtype) @ lp['w_out']
    return y, (lru_h_new, lru_conv_new, m_C_new, m_n_new, m_m_new, m_conv_new, g_S_new)


def trunk(x, states, lps, prompt):
    new_states = []
    for l in range(DEPTH):
        lp = lps[l]
        x = x + 0.5 * swiglu(rmsnorm(x, lp['ffn1_norm_g']), lp['ffn1_w1'], lp['ffn1_w3'], lp['ffn1_w2'])
        y, st = mix(rmsnorm(x, lp['mix_norm_g']), states[l], lp, prompt)
        x = x + y
        x = x + 0.5 * swiglu(rmsnorm(x, lp['ffn2_norm_g']), lp['ffn2_w1'], lp['ffn2_w3'], lp['ffn2_w2'])
        new_states.append(st)
    stacked = [jnp.stack([s[i] for s in new_states]) for i in range(7)]
    return x, stacked


def setup_inputs(seed: int = 0) -> dict:
    key = jax.random.key(seed)
    ks = iter(jax.random.split(key, 64))
    f32 = jnp.float32

    def nrm(shape, scale):
        return jax.random.normal(next(ks), shape, f32) * scale

    def gain(shape):
        return 1.0 + nrm(shape, 0.02)

    u = jax.random.uniform(next(ks), (DEPTH, LRU_W), f32, minval=0.9, maxval=0.999)
    s = u ** (1.0 / LRU_C)
    lru_lambda = jnp.log(s) - jnp.log1p(-s)
    ml_b_if = jnp.concatenate([nrm((DEPTH, ML_H), 0.1),
                               jax.random.uniform(next(ks), (DEPTH, ML_H), f32, minval=3.0, maxval=6.0)], axis=-1)
    return {
        'x_prompt': nrm((BATCH, SEQ, D_MODEL), 1.0),
        'x_sample': nrm((DEC_BATCH, DEC_SEQ, D_MODEL), 1.0),
        'state_lru_h': nrm((DEPTH, DEC_BATCH, LRU_W), 0.5),
        'state_lru_conv': nrm((DEPTH, DEC_BATCH, CONV_W - 1, LRU_W), 1.0),
        'state_mlstm_C': nrm((DEPTH, DEC_BATCH, ML_H, ML_D, ML_D), 1.0),
        'state_mlstm_n': jnp.abs(nrm((DEPTH, DEC_BATCH, ML_H, ML_D), 1.0)),
        'state_mlstm_m': nrm((DEPTH, DEC_BATCH, ML_H), 1.0),
        'state_mlstm_conv': nrm((DEPTH, DEC_BATCH, CONV_W - 1, ML_W), 1.0),
        'state_gla_S': nrm((DEPTH, DEC_BATCH, GLA_H, GLA_DK, GLA_DV), 0.3),
        'meta_tokens': nrm((N_META, D_MODEL), 1.0),
        'ffn1_norm_g': gain((DEPTH, D_MODEL)),
        'ffn1_w1': nrm((DEPTH, D_MODEL, D_FF), D_MODEL ** -0.5),
        'ffn1_w3': nrm((DEPTH, D_MODEL, D_FF), D_MODEL ** -0.5),
        'ffn1_w2': nrm((DEPTH, D_FF, D_MODEL), D_FF ** -0.5),
        'mix_norm_g': gain((DEPTH, D_MODEL)),
        'w_in': nrm((DEPTH, D_MODEL, D_IN), D_MODEL ** -0.5),
        'lru_conv_w': nrm((DEPTH, CONV_W, LRU_W), CONV_W ** -0.5),
        'lru_conv_b': nrm((DEPTH, LRU_W), 0.02),
        'lru_wa': nrm((DEPTH, LRU_BLOCKS, LRU_BD, LRU_BD), LRU_BD ** -0.5),
        'lru_ba': nrm((DEPTH, LRU_W), 0.02),
        'lru_wx': nrm((DEPTH, LRU_BLOCKS, LRU_BD, LRU_BD), LRU_BD ** -0.5),
        'lru_bx': nrm((DEPTH, LRU_W), 0.02),
        'lru_lambda': lru_lambda,
        'ml_conv_w': nrm((DEPTH, CONV_W, ML_W), CONV_W ** -0.5),
        'ml_conv_b': nrm((DEPTH, ML_W), 0.02),
        'ml_wq': nrm((DEPTH, ML_H, ML_D, ML_D), ML_D ** -0.5),
        'ml_wk': nrm((DEPTH, ML_H, ML_D, ML_D), ML_D ** -0.5),
        'ml_wv': nrm((DEPTH, ML_H, ML_D, ML_D), ML_D ** -0.5),
        'ml_w_if': nrm((DEPTH, 3 * ML_W, 2 * ML_H), (3 * ML_W) ** -0.5),
        'ml_b_if': ml_b_if,
        'ml_norm_g': gain((DEPTH, ML_W)),
        'ml_skip': gain((DEPTH, ML_W)),
        'gla_w_up': nrm((DEPTH, GLA_RANK, GLA_KW), GLA_RANK ** -0.5),
        'gla_b_up': nrm((DEPTH, GLA_KW), 0.1),
        'gla_norm_g': gain((DEPTH, GLA_VW)),
        'w_out': nrm((DEPTH, MIX_W, D_MODEL), MIX_W ** -0.5),
        'ffn2_norm_g': gain((DEPTH, D_MODEL)),
        'ffn2_w1': nrm((DEPTH, D_MODEL, D_FF), D_MODEL ** -0.5),
        'ffn2_w3': nrm((DEPTH, D_MODEL, D_FF), D_MODEL ** -0.5),
        'ffn2_w2': nrm((DEPTH, D_FF, D_MODEL), D_FF ** -0.5),
        'final_norm_g': gain((D_MODEL,)),
    }


def reference(x_prompt, x_sample, state_lru_h, state_lru_conv, state_mlstm_C, state_mlstm_n,
              state_mlstm_m, state_mlstm_conv, state_gla_S, meta_tokens,
              ffn1_norm_g, ffn1_w1, ffn1_w3, ffn1_w2, mix_norm_g, w_in,
              lru_conv_w, lru_conv_b, lru_wa, lru_ba, lru_wx, lru_bx, lru_lambda,
              ml_conv_w, ml_conv_b, ml_wq, ml_wk, ml_wv, ml_w_if, ml_b_if, ml_norm_g, ml_skip,
              gla_w_up, gla_b_up, gla_norm_g, w_out,
              ffn2_norm_g, ffn2_w1, ffn2_w3, ffn2_w2, final_norm_g):
    f32 = jnp.float32
    lps = [dict(ffn1_norm_g=ffn1_norm_g[l], ffn1_w1=ffn1_w1[l], ffn1_w3=ffn1_w3[l], ffn1_w2=ffn1_w2[l],
                mix_norm_g=mix_norm_g[l], w_in=w_in[l],
                lru_conv_w=lru_conv_w[l], lru_conv_b=lru_conv_b[l], lru_wa=lru_wa[l], lru_ba=lru_ba[l],
                lru_wx=lru_wx[l], lru_bx=lru_bx[l], lru_lambda=lru_lambda[l],
                ml_conv_w=ml_conv_w[l], ml_conv_b=ml_conv_b[l], ml_wq=ml_wq[l], ml_wk=ml_wk[l],
                ml_wv=ml_wv[l], ml_w_if=ml_w_if[l], ml_b_if=ml_b_if[l], ml_norm_g=ml_norm_g[l],
                ml_skip=ml_skip[l], gla_w_up=gla_w_up[l], gla_b_up=gla_b_up[l], gla_norm_g=gla_norm_g[l],
                w_out=w_out[l], ffn2_norm_g=ffn2_norm_g[l], ffn2_w1=ffn2_w1[l], ffn2_w3=ffn2_w3[l],
                ffn2_w2=ffn2_w2[l]) for l in range(DEPTH)]

    Bp = x_prompt.shape[0]
    meta = jnp.broadcast_to(meta_tokens.astype(x_prompt.dtype)[None], (Bp, N_META, D_MODEL))
    xp = jnp.concatenate([meta, x_prompt], axis=1)
    zero_state = (jnp.zeros((Bp, LRU_W), f32), jnp.zeros((Bp, CONV_W - 1, LRU_W), f32),
                  jnp.zeros((Bp, ML_H, ML_D, ML_D), f32), jnp.zeros((Bp, ML_H, ML_D), f32),
                  jnp.zeros((Bp, ML_H), f32), jnp.zeros((Bp, CONV_W - 1, ML_W), f32),
                  jnp.zeros((Bp, GLA_H, GLA_DK, GLA_DV), f32))
    xp, p_new = trunk(xp, [zero_state for _ in range(DEPTH)], lps, True)
    y_prompt = rmsnorm(xp, final_norm_g)[:, N_META:]

    s_states = [(state_lru_h[l], state_lru_conv[l], state_mlstm_C[l], state_mlstm_n[l],
                 state_mlstm_m[l], state_mlstm_conv[l], state_gla_S[l]) for l in range(DEPTH)]
    xs, s_new = trunk(x_sample, s_states, lps, False)
    y_sample = rmsnorm(xs, final_norm_g)

    return (y_prompt, y_sample,
            p_new[0], p_new[1], p_new[2], p_new[3], p_new[4], p_new[5], p_new[6],
            s_new[0], s_new[1], s_new[2], s_new[3], s_new[4], s_new[5], s_new[6])
```

```python
import numpy as np
from contextlib import ExitStack
import concourse.bass as bass
import concourse.mybir as mybir
from concourse.bass_utils import run_bass_kernel_spmd

F32 = mybir.dt.float32
BF16 = mybir.dt.bfloat16
AF = mybir.ActivationFunctionType
ALU = mybir.AluOpType
AX = mybir.AxisListType

ENGS = ("pe", "act", "dve", "pool", "sp")

D = 1024
DEPTH = 2
NPR = 2064
NSM = 128
NT = NPR + NSM
NSEQ = 16
TS = 8
DFF = 2816
NF = DFF // 128
EPS = 1e-6
TILES = [(0, 512), (512, 512), (1024, 512), (1536, 512), (2048, 144)]
FGROUPS = [(0, 4), (4, 4), (8, 4), (12, 4), (16, 4), (20, 2)]
DIN = 2448


class Sched:
    def __init__(self, nc, n_dma_sems=10):
        self.nc = nc
        self.prog = {e: [] for e in ENGS}
        self.cnt = {}
        self.sems = {}
        self.seen = {e: {} for e in ENGS}
        self.last_w = {}
        self.readers = {}
        self.n_dma_sems = n_dma_sems
        self.dma_rr = {e: 0 for e in ENGS}
        self.n_ops = 0

    def open(self, stack):
        for e in ENGS:
            self.sems[e] = stack.enter_context(self.nc.semaphore(f"s_{e}"))
            self.cnt[e] = 0
        for q in ("sp", "pool", "act"):
            for i in range(self.n_dma_sems):
                k = f"d_{q}{i}"
                self.sems[k] = stack.enter_context(self.nc.semaphore(k))
                self.cnt[k] = 0

    CHILD = {"P0": ("P0k", "P0s"), "P1": ("P1k", "P1s"), "P6": ("P6v", "P6t"), "P7": ("P7v", "P7t")}

    def _expand(self, names):
        out = []
        for n in names:
            out.append(n)
            out.extend(self.CHILD.get(n, ()))
        return out

    def _deps(self, eng, reads, writes, skip_same_pe=False):
        deps = {}

        def add(ev):
            if ev is None:
                return
            k, v = ev
            if skip_same_pe and k == "pe":
                return
            if deps.get(k, 0) < v:
                deps[k] = v
        for r in reads:
            add(self.last_w.get(r))
            if r[0] == "P" and r[1:2].isdigit():
                for ev in self.readers.get(r, ()):
                    if ev[0] != eng:
                        add(ev)
        for w in writes:
            add(self.last_w.get(w))
            for ev in self.readers.get(w, ()):
                add(ev)
        out = []
        for k, v in deps.items():
            if self.seen[eng].get(k, 0) < v:
                self.seen[eng][k] = v
                out.append((k, v))
        return out

    def _commit(self, ev, reads, writes):
        for w in writes:
            self.last_w[w] = ev
            self.readers[w] = []
        for r in reads:
            if r in writes:
                continue
            self.readers.setdefault(r, []).append(ev)

    def op(self, eng, fn, reads=(), writes=()):
        reads, writes = self._expand(reads), self._expand(writes)
        waits = self._deps(eng, reads, writes, skip_same_pe=(eng == "pe"))
        self.cnt[eng] += 1
        ev = (eng, self.cnt[eng])
        sems = self.sems

        def emit(e, waits=waits, fn=fn, sem=sems[eng]):
            for k, v in waits:
                e.wait_ge(sems[k], v)
            fn(e).then_inc(sem, 1)
        self.prog[eng].append(emit)
        self._commit(ev, reads, writes)
        self.n_ops += 1
        return ev

    def dma(self, q, out, in_, reads=(), writes=(), **kw):
        i = self.dma_rr[q]
        self.dma_rr[q] = (i + 1) % self.n_dma_sems
        k = f"d_{q}{i}"
        reads, writes = self._expand(reads), self._expand(writes)
        waits = self._deps(q, reads, writes)
        prev = self.cnt[k]
        if prev and self.seen[q].get(k, 0) < prev:
            self.seen[q][k] = prev
            waits.append((k, prev))
        self.cnt[k] = prev + 16
        ev = (k, prev + 16)
        sems = self.sems

        def emit(e, waits=waits, sem=sems[k], out=out, in_=in_, kw=kw):
            for kk, v in waits:
                e.wait_ge(sems[kk], v)
            e.dma_start(out=out, in_=in_, **kw).then_inc(sem, 16)
        self.prog[q].append(emit)
        self._commit(ev, reads, writes)
        self.n_ops += 1
        return ev

    def barrier(self):
        final = {k: v for k, v in self.cnt.items() if v > 0}
        sems = self.sems
        for eng in ENGS:
            waits = [(k, v) for k, v in final.items() if self.seen[eng].get(k, 0) < v]
            for k, v in waits:
                self.seen[eng][k] = v

            def emit(e, waits=waits):
                for k, v in waits:
                    e.wait_ge(sems[k], v)
            self.prog[eng].append(emit)
        self.last_w = {}
        self.readers = {}

    def run(self, block):
        prog = self.prog

        @block.sync
        def _(e):
            for f in prog["sp"]:
                f(e)

        @block.tensor
        def _(e):
            for f in prog["pe"]:
                f(e)

        @block.scalar
        def _(e):
            for f in prog["act"]:
                f(e)

        @block.vector
        def _(e):
            for f in prog["dve"]:
                f(e)

        @block.gpsimd
        def _(e):
            for f in prog["pool"]:
                f(e)


class Arena:
    def __init__(self, ap, nwords):
        self.ap = ap
        self.n = nwords
        self.top = 0
        self.marks = []

    def f32(self, nwords, parts=128):
        req = nwords
        nwords = (nwords + 7) // 8 * 8
        assert self.top + nwords <= self.n, f"arena overflow {self.top}+{nwords}>{self.n}"
        v = self.ap[0:parts, self.top:self.top + req]
        self.top += nwords
        return v

    def bf16(self, nelem, parts=128):
        nw = (nelem + 1) // 2
        nw = (nw + 7) // 8 * 8
        assert self.top + nw <= self.n, f"arena overflow {self.top}+{nw}>{self.n}"
        v = self.ap[0:parts, self.top:self.top + nw].bitcast(BF16)
        self.top += nw
        return v[:, 0:nelem]

    def mark(self):
        self.marks.append(self.top)

    def release(self):
        self.top = self.marks.pop()


class Ctx:
    pass


def build(stage=99):
    nc = bass.Bass("TRN2", target_bir_lowering=False)
    c = Ctx()
    c.nc = nc
    import os
    c.debug = bool(os.environ.get("KDEBUG"))
    di = lambda name, shape: nc.dram_tensor(name, list(shape), F32, kind="ExternalInput").ap()
    do = lambda name, shape: nc.dram_tensor(name, list(shape), F32, kind="ExternalOutput").ap()
    I = {}
    I["xp"] = di("xp", [2048, D])
    I["xs"] = di("xs", [NSM, D])
    I["meta"] = di("meta", [16, D])
    for nm in ("ffn1_norm_g", "mix_norm_g", "ffn2_norm_g"):
        I[nm] = di(nm, [DEPTH, D])
    I["final_norm_g"] = di("final_norm_g", [D])
    for nm in ("ffn1_w1", "ffn1_w3", "ffn2_w1", "ffn2_w3"):
        I[nm] = di(nm, [DEPTH, D, DFF])
    for nm in ("ffn1_w2", "ffn2_w2"):
        I[nm] = di(nm, [DEPTH, DFF, D])
    I["w_in"] = di("w_in", [DEPTH, D, DIN])
    I["w_out"] = di("w_out", [DEPTH, D, D])
    for nm, shp in (("lru_conv_w", [4, 256]), ("lru_conv_b", [256]), ("lru_wa", [4, 64, 64]), ("lru_ba", [256]),
                    ("lru_wx", [4, 64, 64]), ("lru_bx", [256]), ("lru_lambda", [256]),
                    ("ml_conv_w", [4, 384]), ("ml_conv_b", [384]), ("ml_wq", [4, 96, 96]), ("ml_wk", [4, 96, 96]),
                    ("ml_wv", [4, 96, 96]), ("ml_w_if", [1152, 8]), ("ml_b_if", [8]), ("ml_norm_g", [384]),
                    ("ml_skip", [384]), ("gla_w_up", [16, 192]), ("gla_b_up", [192]), ("gla_norm_g", [384])):
        I[nm] = di(nm, [DEPTH] + shp)
    for nm, shp in (("state_lru_h", [256]), ("state_lru_conv", [3, 256]), ("state_mlstm_C", [4, 96, 96]),
                    ("state_mlstm_n", [4, 96]), ("state_mlstm_m", [4]), ("state_mlstm_conv", [3, 384]),
                    ("state_gla_S", [4, 48, 96])):
        I[nm] = di(nm, [DEPTH, NSEQ] + shp)
    O = {}
    for nm, shp in (("lru_h", [256]), ("lru_conv", [3, 256]), ("mlstm_C", [4, 96, 96]), ("mlstm_n", [4, 96]),
                    ("mlstm_m", [4]), ("mlstm_conv", [3, 384]), ("gla_S", [4, 48, 96])):
        O["p_" + nm] = do("p_" + nm, [DEPTH] + shp)
        O["s_" + nm] = do("s_" + nm, [DEPTH, NSEQ] + shp)
    O["yp"] = do("yp", [2048, D])
    O["ys"] = do("ys", [NSM, D])
    if stage < 0:
        O["dbg"] = do("dbg", [128, 6, 512])
    c.I, c.O = I, O

    with ExitStack() as st:
        S = Sched(nc)
        S.open(st)
        c.S = S
        NW = 212000 // 4
        arena_t = st.enter_context(nc.sbuf_tensor("arena", [128, NW], F32))
        A = Arena(arena_t[:], NW)
        c.A = A
        c.P = [st.enter_context(nc.psum_tensor(f"P{i}", [128, 512], F32))[:] for i in range(8)]
        block = st.enter_context(nc.Block())

        c.X = A.f32(8 * NT).rearrange("p (k t) -> p k t", k=8)
        c.XN = A.bf16(8 * NT).rearrange("p (k t) -> p k t", k=8)
        c.ident = A.f32(128)
        c.ones_bf = A.bf16(128)
        c.gains = A.f32(7 * 8).rearrange("p (n k) -> p n k", n=7)
        c.MASKC = A.f32(64)
        c.MASKB = A.f32(64)
        c.PM = A.f32(8)
        c.SEL = A.f32(4 * 96)
        c.SMASK = A.f32(128)
        c.SNEG = A.f32(128)
        c.sq = [A.bf16(512), A.bf16(512)]
        c.rstd = [A.f32(512), A.f32(512)]
        setup_consts(c)
        load_x(c)
        if stage < 0:
            c.dbg = A.f32(6 * 512).rearrange("p (n t) -> p n t", n=6)
            S.op("pool", lambda e: e.memset(c.dbg, 0.0), writes=["dbg"])
            ffn(c, 0, "ffn1", 0, dbg=True)
            S.barrier()
            S.dma("sp", O["dbg"], c.dbg, reads=["dbg"])
        for l in range(DEPTH):
            if stage >= 1 and stage not in (31, 32):
                ffn(c, l, "ffn1", 3 * l + 0)
            if stage >= 2:
                mixer(c, l, {31: 11, 32: 12}.get(stage, stage))
            if stage >= 3 and (stage < 10 or stage >= 40):
                ffn(c, l, "ffn2", 3 * l + 2)
            if 10 <= stage < 40:
                break
        store_y(c)
        S.barrier()
        S.run(block)
    return nc


def setup_consts(c):
    S, nc = c.S, c.nc
    S.op("pool", lambda e: e.memset(c.ident, 1.0), writes=["ident"])
    S.op("pool", lambda e: e.affine_select(out=c.ident, in_=c.ident, pattern=[[-1, 128]],
                                           compare_op=ALU.is_equal, fill=0.0, base=0,
                                           channel_multiplier=1), reads=["ident"], writes=["ident"])
    S.op("pool", lambda e: e.memset(c.ones_bf, 1.0), writes=["ones_bf"])
    for t_ in (c.MASKC, c.MASKB, c.PM, c.SEL, c.SMASK):
        S.op("pool", lambda e, t_=t_: e.memset(t_, 1.0), writes=["consts"])
    S.op("pool", lambda e: e.memset(c.SNEG, 0.0), writes=["consts"])
    mc, mb_ = c.MASKC[0:64, :], c.MASKB[0:64, :].rearrange("p (i j) -> p i j", j=8)
    S.op("pool", lambda e: e.affine_select(out=mc, in_=mc, pattern=[[1, 64]], compare_op=ALU.is_ge, fill=0.0,
                                           base=0, channel_multiplier=-1), reads=["consts"], writes=["consts"])
    S.op("pool", lambda e: e.affine_select(out=mb_, in_=mb_, pattern=[[8, 8], [1, 8]], compare_op=ALU.is_ge,
                                           fill=0.0, base=0, channel_multiplier=-1), reads=["consts"], writes=["consts"])
    S.op("pool", lambda e: e.affine_select(out=mb_, in_=mb_, pattern=[[-8, 8], [0, 8]], compare_op=ALU.is_ge,
                                           fill=0.0, base=0, channel_multiplier=1), reads=["consts"], writes=["consts"])
    pm = c.PM[0:64, :]
    S.op("pool", lambda e: e.affine_select(out=pm, in_=pm, pattern=[[-8, 8]], compare_op=ALU.is_ge, fill=0.0,
                                           base=0, channel_multiplier=1), reads=["consts"], writes=["consts"])
    S.op("pool", lambda e: e.affine_select(out=pm, in_=pm, pattern=[[8, 8]], compare_op=ALU.is_ge, fill=0.0,
                                           base=7, channel_multiplier=-1), reads=["consts"], writes=["consts"])
    sel = c.SEL[0:4, :].rearrange("p (h m) -> p h m", h=4)
    S.op("pool", lambda e: e.affine_select(out=sel, in_=sel, pattern=[[1, 4], [0, 96]], compare_op=ALU.is_equal,
                                           fill=0.0, base=0, channel_multiplier=-1), reads=["consts"], writes=["consts"])
    S.op("pool", lambda e: e.memset(c.SMASK.rearrange("p (i j) -> p i j", j=8)[:, :, 0:1], 0.0),
         reads=["consts"], writes=["consts"])
    S.op("pool", lambda e: e.memset(c.SNEG.rearrange("p (i j) -> p i j", j=8)[:, :, 0:1], -1e30),
         reads=["consts"], writes=["consts"])
    names = ["ffn1_norm_g", "mix_norm_g", "ffn2_norm_g"]
    for l in range(DEPTH):
        for j, nm in enumerate(names):
            S.dma("sp", c.gains[:, 3 * l + j, :], c.I[nm][l].rearrange("(k p) -> p k", p=128),
                  writes=["gains"], allow_slow_non_contiguous=True)
    S.dma("sp", c.gains[:, 6, :], c.I["final_norm_g"].rearrange("(k p) -> p k", p=128),
          writes=["gains"], allow_slow_non_contiguous=True)


def load_x(c):
    S, A = c.S, c.A
    A.mark()
    stg = [A.f32(D), A.f32(D)]
    blocks = [("meta", 0, 16, 0)] + [("xp", i * 128, 128, 16 + i * 128) for i in range(16)] + [("xs", 0, 128, NPR)]
    for bi, (src, r0, n, t0) in enumerate(blocks):
        sg = stg[bi % 2]
        S.dma("sp", sg[0:n, :], c.I[src][r0:r0 + n, :], writes=[f"stg{bi % 2}"])
        for half in range(2):
            pb = c.P[6 + half]
            for kk in range(4):
                k = half * 4 + kk
                S.op("pe", lambda e, pb=pb, kk=kk, k=k, sg=sg, n=n: e.transpose(
                    out=pb[:, kk * 128:kk * 128 + n], in_=sg[0:n, k * 128:(k + 1) * 128], identity=c.ident[0:n, 0:n]),
                    reads=[f"stg{bi % 2}", "ident"], writes=[f"P{6 + half}"])
            eng = "dve" if half == 0 else "act"
            src_ap = pb.rearrange("p (k t) -> p k t", k=4)[:, :, 0:n]
            dst_ap = c.X[:, half * 4:half * 4 + 4, t0:t0 + n]
            if eng == "dve":
                S.op("dve", lambda e, s=src_ap, d=dst_ap: e.tensor_copy(out=d, in_=s),
                     reads=[f"P{6 + half}"], writes=[f"Xb{bi}"])
            else:
                S.op("act", lambda e, s=src_ap, d=dst_ap: e.copy(out=d, in_=s),
                     reads=[f"P{6 + half}"], writes=[f"Xb{bi}"])
    S.barrier()
    A.release()


def xres(m, ti):
    return f"X_{m}_{ti}"


def rmsnorm_to_xn(c, gi, tiles=None):
    S, A = c.S, c.A
    sq, rstd = c.sq, c.rstd
    n = 0
    for ti, (t0, tn) in enumerate(TILES):
        pb = c.P[6 + ti % 2]
        pbn = f"P{6 + ti % 2}"
        for k in range(8):
            b = n % 2
            n += 1
            S.op("act", lambda e, b=b, k=k, t0=t0, tn=tn: e.activation(
                out=sq[b][:, 0:tn], in_=c.X[:, k, t0:t0 + tn], func=AF.Square),
                reads=[xres(k, ti)], writes=[f"sq{b}"])
            S.op("pe", lambda e, b=b, k=k, tn=tn, pb=pb: e.matmul(
                pb[:, 0:tn], lhsT=c.ones_bf, rhs=sq[b][:, 0:tn], start=(k == 0), stop=(k == 7)),
                reads=[f"sq{b}", "ones_bf"], writes=[pbn])
        rb = rstd[ti % 2]
        rbn = f"rstd{ti % 2}"
        S.op("act", lambda e, rb=rb, pb=pb, tn=tn: e.activation(
            out=rb[:, 0:tn], in_=pb[:, 0:tn], func=AF.Sqrt, bias=EPS, scale=1.0 / D),
            reads=[pbn], writes=[rbn])
        S.op("dve", lambda e, rb=rb, tn=tn: e.reciprocal(out=rb[:, 0:tn], in_=rb[:, 0:tn]),
             reads=[rbn], writes=[rbn])
        for k in range(8):
            eng = "dve"
            S.op(eng, lambda e, rb=rb, k=k, t0=t0, tn=tn: e.scalar_tensor_tensor(
                out=c.XN[:, k, t0:t0 + tn], in0=c.X[:, k, t0:t0 + tn], scalar=c.gains[:, gi, k:k + 1],
                in1=rb[:, 0:tn], op0=ALU.mult, op1=ALU.mult),
                reads=[xres(k, ti), rbn, "gains"], writes=[f"XN_{ti}"])


def ffn(c, l, which, gi, dbg=False):
    S, A = c.S, c.A
    rmsnorm_to_xn(c, gi)
    A.mark()
    w1d = c.I[f"{which}_w1"][l].rearrange("(k p) f -> p k f", p=128)
    w3d = c.I[f"{which}_w3"][l].rearrange("(k p) f -> p k f", p=128)
    w2d = c.I[f"{which}_w2"][l].rearrange("(f p) m -> p f m", p=128)
    W1 = [A.bf16(8 * 512).rearrange("p (k f) -> p k f", k=8) for _ in range(2)]
    W3 = [A.bf16(8 * 512).rearrange("p (k f) -> p k f", k=8) for _ in range(2)]
    W2 = [A.bf16(4 * 1024).rearrange("p (f m) -> p f m", f=4) for _ in range(2)]
    G = [A.bf16(4 * 512).rearrange("p (f t) -> p f t", f=4) for _ in range(2)]
    SL = [A.f32(512) for _ in range(2)]

    def load(gidx):
        f0, F = FGROUPS[gidx]
        b = gidx % 2
        S.dma("pool", W1[b][:, :, 0:F * 128], w1d[:, :, f0 * 128:(f0 + F) * 128], writes=[f"W1_{b}"])
        S.dma("pool", W3[b][:, :, 0:F * 128], w3d[:, :, f0 * 128:(f0 + F) * 128], writes=[f"W3_{b}"])
        S.dma("pool", W2[b][:, 0:F, :], w2d[:, f0:f0 + F, :], writes=[f"W2_{b}"])

    load(0)
    if dbg:
        S.op("dve", lambda e: e.tensor_copy(out=c.dbg[:, 0, :], in_=c.XN[:, 0, 0:512]), reads=["XN_0"], writes=["dbg"])
        S.op("dve", lambda e: e.tensor_copy(out=c.dbg[:, 1, :], in_=W1[0][:, 0, :]), reads=["W1_0"], writes=["dbg"])
        S.op("dve", lambda e: e.tensor_copy(out=c.dbg[:, 2, :], in_=W2[0][:, 0, 0:512]), reads=["W2_0"], writes=["dbg"])
    it = 0
    pendingB = None
    nsl = 0
    for gidx, (f0, F) in enumerate(FGROUPS):
        b = gidx % 2
        for ti, (t0, tn) in enumerate(TILES):
            gb = it % 2
            for fi in range(F):
                hb = (it * 4 + fi) % 2
                p1, p3 = c.P[hb], c.P[2 + hb]
                for k in range(8):
                    S.op("pe", lambda e, p1=p1, k=k, fi=fi, b=b, t0=t0, tn=tn: e.matmul(
                        p1[:, 0:tn], lhsT=W1[b][:, k, fi * 128:(fi + 1) * 128], rhs=c.XN[:, k, t0:t0 + tn],
                        start=(k == 0), stop=(k == 7)),
                        reads=[f"W1_{b}", f"XN_{ti}"], writes=[f"P{hb}"])
                for k in range(8):
                    S.op("pe", lambda e, p3=p3, k=k, fi=fi, b=b, t0=t0, tn=tn: e.matmul(
                        p3[:, 0:tn], lhsT=W3[b][:, k, fi * 128:(fi + 1) * 128], rhs=c.XN[:, k, t0:t0 + tn],
                        start=(k == 0), stop=(k == 7)),
                        reads=[f"W3_{b}", f"XN_{ti}"], writes=[f"P{2 + hb}"])
                sb_ = nsl % 2
                nsl += 1
                S.op("act", lambda e, p1=p1, sb_=sb_, tn=tn: e.activation(
                    out=SL[sb_][:, 0:tn], in_=p1[:, 0:tn], func=AF.Silu),
                    reads=[f"P{hb}"], writes=[f"SL{sb_}"])
                S.op("dve", lambda e, p3=p3, sb_=sb_, gb=gb, fi=fi, tn=tn: e.tensor_tensor(
                    out=G[gb][:, fi, 0:tn], in0=p3[:, 0:tn], in1=SL[sb_][:, 0:tn], op=ALU.mult),
                    reads=[f"P{2 + hb}", f"SL{sb_}"], writes=[f"G{gb}_{fi}"])
                if dbg and it == 0 and fi == 0:
                    S.op("dve", lambda e, sb_=sb_: e.tensor_copy(out=c.dbg[:, 3, :], in_=SL[sb_]), reads=[f"SL{sb_}"], writes=["dbg"])
                    S.op("dve", lambda e, gb=gb: e.tensor_copy(out=c.dbg[:, 4, :], in_=G[gb][:, 0, :]), reads=[f"G{gb}_0"], writes=["dbg"])
                    S.op("dve", lambda e, p3=p3: e.tensor_copy(out=c.dbg[:, 5, :], in_=p3), reads=[f"P{2 + hb}"], writes=["dbg"])
            if pendingB is not None:
                pendingB()
            if ti == 0 and gidx + 1 < len(FGROUPS):
                load(gidx + 1)

            def phaseB(gb=gb, b=b, F=F, ti=ti, t0=t0, tn=tn, it=it):
                for m in range(8):
                    yb = m % 2
                    py = c.P[4 + yb]
                    for fi in range(F):
                        S.op("pe", lambda e, py=py, fi=fi, m=m: e.matmul(
                            py[:, 0:tn], lhsT=W2[b][:, fi, m * 128:(m + 1) * 128], rhs=G[gb][:, fi, 0:tn],
                            start=(fi == 0), stop=(fi == F - 1)),
                            reads=[f"W2_{b}", f"G{gb}_{fi}"], writes=[f"P{4 + yb}"])
                    S.op("dve", lambda e, py=py, m=m: e.scalar_tensor_tensor(
                        out=c.X[:, m, t0:t0 + tn], in0=py[:, 0:tn], scalar=0.5, in1=c.X[:, m, t0:t0 + tn],
                        op0=ALU.mult, op1=ALU.add),
                        reads=[f"P{4 + yb}", xres(m, ti)], writes=[xres(m, ti)])
            pendingB = phaseB
            it += 1
    pendingB()
    S.barrier()
    A.release()


def store_y(c):
    S, A = c.S, c.A
    A.mark()
    sq, rstd = c.sq, c.rstd
    YF = A.f32(8 * 512).rearrange("p (k t) -> p k t", k=8)
    ostg = [A.f32(D), A.f32(D)]
    nsq = 0
    nob = 0
    for ti, (t0, tn) in enumerate(TILES):
        pb = c.P[6 + ti % 2]
        pbn = f"P{6 + ti % 2}"
        for k in range(8):
            b = nsq % 2
            nsq += 1
            S.op("act", lambda e, b=b, k=k, t0=t0, tn=tn: e.activation(
                out=sq[b][:, 0:tn], in_=c.X[:, k, t0:t0 + tn], func=AF.Square),
                reads=[xres(k, ti)], writes=[f"sq{b}"])
            S.op("pe", lambda e, b=b, k=k, tn=tn, pb=pb: e.matmul(
                pb[:, 0:tn], lhsT=c.ones_bf, rhs=sq[b][:, 0:tn], start=(k == 0), stop=(k == 7)),
                reads=[f"sq{b}", "ones_bf"], writes=[pbn])
        rb = rstd[ti % 2]
        rbn = f"rstd{ti % 2}"
        S.op("act", lambda e, rb=rb, pb=pb, tn=tn: e.activation(
            out=rb[:, 0:tn], in_=pb[:, 0:tn], func=AF.Sqrt, bias=EPS, scale=1.0 / D),
            reads=[pbn], writes=[rbn])
        S.op("dve", lambda e, rb=rb, tn=tn: e.reciprocal(out=rb[:, 0:tn], in_=rb[:, 0:tn]),
             reads=[rbn], writes=[rbn])
        for k in range(8):
            eng = "dve"
            S.op(eng, lambda e, rb=rb, k=k, t0=t0, tn=tn: e.scalar_tensor_tensor(
                out=YF[:, k, 0:tn], in0=c.X[:, k, t0:t0 + tn], scalar=c.gains[:, 6, k:k + 1],
                in1=rb[:, 0:tn], op0=ALU.mult, op1=ALU.mult),
                reads=[xres(k, ti), rbn, "gains"], writes=["YF"])
        blks = []
        if ti < 4:
            for j in range(4):
                tok = t0 + j * 128
                lo = max(tok, 16)
                blks.append((lo - t0, tok + 128 - lo, c.O["yp"], lo - 16))
        else:
            blks.append((0, 16, c.O["yp"], 2032))
            blks.append((16, 128, c.O["ys"], 0))
        for (o0, n, dst, r0) in blks:
            ob = nob % 2
            nob += 1
            for half in range(2):
                pb2 = c.P[half]
                for kk in range(4):
                    k = half * 4 + kk
                    S.op("pe", lambda e, pb2=pb2, kk=kk, k=k, o0=o0, n=n: e.transpose(
                        out=pb2[0:n, kk * 128:(kk + 1) * 128], in_=YF[:, k, o0:o0 + n], identity=c.ident),
                        reads=["YF", "ident"], writes=[f"P{half}"])
                if half == 0:
                    S.op("dve", lambda e, pb2=pb2, ob=ob, n=n: e.tensor_copy(
                        out=ostg[ob][0:n, 0:512], in_=pb2[0:n, :]), reads=["P0"], writes=[f"ostg{ob}"])
                else:
                    S.op("act", lambda e, pb2=pb2, ob=ob, n=n: e.copy(
                        out=ostg[ob][0:n, 512:1024], in_=pb2[0:n, :]), reads=["P1"], writes=[f"ostg{ob}"])
            S.dma("sp", dst[r0:r0 + n, :], ostg[ob][0:n, :], reads=[f"ostg{ob}"])
    A.release()


NEX = 2246
NE = 2243
SMP0 = 2067
ETILES = [(0, 512), (512, 512), (1024, 512), (1536, 512), (2048, 195)]


def ext_in_views(U):
    return U[:, 3:3 + NPR], U[:, SMP0 + 3:SMP0 + 3 + 176].rearrange("p (i e) -> p i e", e=11)[:, :, 0:8]


def ext_out_views(V):
    return V[:, 0:NPR], V[:, SMP0:SMP0 + 176].rearrange("p (i e) -> p i e", e=11)[:, :, 0:8]


def dst_views(arr, layout, t0, tn):
    if layout == "norm":
        return [(0, tn, arr[:, t0:t0 + tn], False)]
    pr, sm = ext_in_views(arr) if layout == "ext_in" else ext_out_views(arr)
    if t0 + tn <= NPR:
        return [(0, tn, pr[:, t0:t0 + tn], False)]
    return [(0, NPR - t0, pr[:, t0:NPR], False), (NPR - t0, tn, sm, True)]


def pview(ps, c0, c1, strided):
    v = ps[:, c0:c1]
    return v.rearrange("p (i j) -> p i j", j=8) if strided else v


def proj_fm(c, W, wres, col0, M, evac):
    S = c.S
    for ti, (t0, tn) in enumerate(TILES):
        pi = c.pp % 2
        c.pp += 1
        ps = c.P[pi]
        for k in range(8):
            S.op("pe", lambda e, ps=ps, k=k, t0=t0, tn=tn: e.matmul(
                ps[0:M, 0:tn], lhsT=W[:, k, col0:col0 + M], rhs=c.XN[:, k, t0:t0 + tn],
                start=(k == 0), stop=(k == 7)), reads=[wres, f"XN_{ti}"], writes=[f"P{pi}"])
        evac(ti, t0, tn, ps[0:M, 0:tn], f"P{pi}")
        drain(c, 1)


def load_T(c, dram_ap, n, F, blocks):
    S = c.S
    S.dma("sp", c.stgT[0:n, 0:F], dram_ap, writes=["stgT"])
    for (col0, nb, dst, dres, j) in blocks:
        S.op("pe", lambda e, col0=col0, nb=nb: e.transpose(
            out=c.P[7][0:nb, 0:n], in_=c.stgT[0:n, col0:col0 + nb], identity=c.ident[0:n, 0:n]),
            reads=["stgT", "ident"], writes=["P7"])
        src = c.P[7][0:nb, 0:n]
        if j:
            src = src.rearrange("p (i j) -> p i j", j=j)
        S.op("act", lambda e, dst=dst, src=src: e.copy(out=dst, in_=src), reads=["P7"], writes=[dres])


def store_T(c, src_ap, sres, P_, n, dram_ap):
    S = c.S
    S.op("pe", lambda e: e.transpose(out=c.P[7][0:n, 0:P_], in_=src_ap, identity=c.ident[0:P_, 0:P_]),
         reads=[sres, "ident"], writes=["P7"])
    S.op("act", lambda e: e.copy(out=c.stgO[0:n, 0:P_], in_=c.P[7][0:n, 0:P_]), reads=["P7"], writes=["stgO"])
    S.dma("sp", dram_ap, c.stgO[0:n, 0:P_], reads=["stgO"])


def colvec(c, dst, dram_1d, pattern, res, **kw):
    c.S.dma("sp", dst, dram_1d.rearrange(pattern, **kw), writes=[res], allow_slow_non_contiguous=True)


def dbg(c, name, ap, reads):
    if not getattr(c, "debug", False):
        return
    d = c.nc.dram_tensor("dbg_" + name, list(ap.shape), F32, kind="ExternalOutput").ap()
    c.S.dma("sp", d, ap, reads=reads)


class CutHere(Exception):
    pass


def cut(c, n):
    import os
    if int(os.environ.get("KCUT", "0")) == n:
        raise CutHere()


def mixer(c, l, stage=99):
    S, A = c.S, c.A
    rmsnorm_to_xn(c, 3 * l + 1)
    A.mark()
    saved = (A.top, list(A.marks))
    c.pp = 0
    c.pending = []
    c.stgT = A.f32(512)
    c.stgO = A.f32(128)
    c.WO = A.bf16(2 * 1024).rearrange("p (k m) -> p k m", k=2)
    try:
        lru_group(c, l)
        if stage >= 11:
            mlstm_group(c, l)
        if stage >= 12:
            gla_group(c, l)
    except CutHere:
        A.top, A.marks = saved[0], saved[1]
    drain(c, 99)
    S.barrier()
    A.release()


def drain(c, n=1):
    while n > 0 and c.pending:
        c.pending.pop(0)()
        n -= 1


def apply_wout(c, l, Y, yres, row0, kp, nk, defer=False):
    S = c.S
    wod = c.I["w_out"][l]
    drain(c, 99)
    if nk == 1:
        c.wo_slot = (getattr(c, "wo_slot", 0) + 1) % 2
        slots = [c.wo_slot]
    else:
        slots = [0, 1]
    for j in range(nk):
        S.dma("pool", c.WO[0:kp, slots[j], :], wod[row0 + kp * j:row0 + kp * (j + 1), :], writes=[f"WO{slots[j]}"])

    def piece(ti, t0, tn):
        for m in range(8):
            yb = m % 2
            py = c.P[4 + yb]
            pn = f"P{4 + yb}"
            for j in range(nk):
                S.op("pe", lambda e, py=py, j=j, m=m: e.matmul(
                    py[:, 0:tn], lhsT=c.WO[0:kp, slots[j], m * 128:(m + 1) * 128], rhs=Y[:, j, t0:t0 + tn],
                    start=(j == 0), stop=(j == nk - 1)), reads=[f"WO{slots[j]}", yres], writes=[pn])
            S.op("dve", lambda e, py=py, m=m: e.tensor_tensor(
                out=c.X[:, m, t0:t0 + tn], in0=py[:, 0:tn], in1=c.X[:, m, t0:t0 + tn], op=ALU.add),
                reads=[pn, xres(m, ti)], writes=[xres(m, ti)])
    for ti, (t0, tn) in enumerate(TILES):
        if defer:
            c.pending.append(lambda ti=ti, t0=t0, tn=tn: piece(ti, t0, tn))
        else:
            piece(ti, t0, tn)


def lru_group(c, l):
    S, A, I, O = c.S, c.A, c.I, c.O
    A.mark()
    WIN = A.bf16(8 * 512).rearrange("p (k f) -> p k f", k=8)
    S.dma("pool", WIN, I["w_in"][l].rearrange("(k p) f -> p k f", p=128)[:, :, 0:512], writes=["WINlru"])
    YR = A.bf16(2 * NT).rearrange("p (k t) -> p k t", k=2)
    CW = A.f32(8).rearrange("p (c j) -> p c j", c=2)
    CB, BA, BX, LAM, CNEG = A.f32(8), A.f32(8), A.f32(8), A.f32(8), A.f32(8)
    WA = A.f32(256).rearrange("p (c m) -> p c m", c=2)
    WX = A.f32(256).rearrange("p (c m) -> p c m", c=2)
    H0 = A.f32(16)
    HL = A.f32(16)
    CS = A.f32(48)
    UE, GR, XR, AC, IG, T1 = A.f32(NEX), A.f32(NT), A.f32(NE), A.f32(NE), A.f32(NE), A.f32(NE)
    for cc in range(2):
        colvec(c, CW[:, cc, :], I["lru_conv_w"][l][:, cc * 128:(cc + 1) * 128], "j p -> p j", "lruc")
    colvec(c, CB[:, 0:2], I["lru_conv_b"][l], "(c p) -> p c", "lruc", p=128)
    colvec(c, BA[:, 0:2], I["lru_ba"][l], "(c p) -> p c", "lruc", p=128)
    colvec(c, BX[:, 0:2], I["lru_bx"][l], "(c p) -> p c", "lruc", p=128)
    colvec(c, LAM[:, 0:2], I["lru_lambda"][l], "(c p) -> p c", "lruc", p=128)
    S.op("pool", lambda e: e.memset(WA, 0.0), writes=["lruW"])
    S.op("pool", lambda e: e.memset(WX, 0.0), writes=["lruW"])
    for cc in range(2):
        for bb in range(2):
            n = 2 * cc + bb
            S.dma("sp", WA[64 * bb:64 * bb + 64, cc, 64 * bb:64 * bb + 64], I["lru_wa"][l, n], writes=["lruW"])
            S.dma("sp", WX[64 * bb:64 * bb + 64, cc, 64 * bb:64 * bb + 64], I["lru_wx"][l, n], writes=["lruW"])
    S.op("act", lambda e: e.activation(out=CNEG[:, 0:2], in_=LAM[:, 0:2], func=AF.Exp, scale=-1.0),
         reads=["lruc"], writes=["cneg"])
    S.op("act", lambda e: e.activation(out=CNEG[:, 0:2], in_=CNEG[:, 0:2], func=AF.Ln, bias=1.0),
         reads=["cneg"], writes=["cneg"])
    S.op("dve", lambda e: e.tensor_scalar_mul(out=CNEG[:, 0:2], in0=CNEG[:, 0:2], scalar1=-8.0),
         reads=["cneg"], writes=["cneg"])
    hist = UE[:, SMP0:SMP0 + 176].rearrange("p (i e) -> p i e", e=11)[:, :, 0:3]
    for cc in range(2):
        S.op("pool", lambda e: e.memset(UE, 0.0), writes=["UE"])
        load_T(c, I["state_lru_conv"][l].rearrange("i j c -> (i j) c"), 48, 256,
               [(cc * 128, 128, hist, "UE", 3)])
        load_T(c, I["state_lru_h"][l], 16, 256, [(cc * 128, 128, H0[:, 0:16], "H0", 0)])

        def ev_u(ti, t0, tn, ps, pres):
            for (c0, c1, dst, st_) in dst_views(UE, "ext_in", t0, tn):
                S.op("act", lambda e, s=pview(ps, c0, c1, st_), d=dst: e.copy(out=d, in_=s),
                     reads=[pres], writes=["UE"])
        proj_fm(c, WIN, "WINlru", cc * 128, 128, ev_u)

        def ev_g(ti, t0, tn, ps, pres):
            S.op("act", lambda e, ps=ps, t0=t0, tn=tn: e.activation(out=GR[:, t0:t0 + tn], in_=ps,
                                                                     func=AF.Gelu_apprx_tanh),
                 reads=[pres], writes=["GR"])
        proj_fm(c, WIN, "WINlru", 256 + cc * 128, 128, ev_g)
        S.op("dve", lambda e, cc=cc: e.tensor_scalar(out=XR, in0=UE[:, 0:NE], scalar1=CW[:, cc, 0:1],
                                                     scalar2=CB[:, cc:cc + 1], op0=ALU.mult, op1=ALU.add),
             reads=["UE", "lruc"], writes=["XR"])
        for j in range(1, 4):
            S.op("dve", lambda e, cc=cc, j=j: e.scalar_tensor_tensor(
                out=XR, in0=UE[:, j:NE + j], scalar=CW[:, cc, j:j + 1], in1=XR, op0=ALU.mult, op1=ALU.add),
                reads=["UE", "XR", "lruc"], writes=["XR"])
        for (t0, tn) in ETILES:
            for (Wm, bias, dst, dres) in ((WA, BA, AC, "AC"), (WX, BX, IG, "IG")):
                pi = c.pp % 2
                c.pp += 1
                ps = c.P[pi]
                S.op("pe", lambda e, ps=ps, Wm=Wm, cc=cc, t0=t0, tn=tn: e.matmul(
                    ps[:, 0:tn], lhsT=Wm[:, cc, :], rhs=XR[:, t0:t0 + tn], start=True, stop=True),
                    reads=["lruW", "XR"], writes=[f"P{pi}"])
                S.op("act", lambda e, ps=ps, bias=bias, dst=dst, cc=cc, t0=t0, tn=tn: e.activation(
                    out=dst[:, t0:t0 + tn], in_=ps[:, 0:tn], func=AF.Sigmoid, bias=bias[:, cc:cc + 1]),
                    reads=[f"P{pi}", "lruc"], writes=[dres])
        S.op("act", lambda e, cc=cc: e.activation(out=AC, in_=AC, func=AF.Exp, scale=CNEG[:, cc:cc + 1]),
             reads=["AC", "cneg"], writes=["AC"])
        S.op("pool", lambda e: e.tensor_tensor(out=T1, in0=IG, in1=XR, op=ALU.mult),
             reads=["IG", "XR"], writes=["T1"])
        S.op("dve", lambda e: e.tensor_tensor(out=IG, in0=AC, in1=AC, op=ALU.mult),
             reads=["AC", "T1"], writes=["IG"])
        S.op("dve", lambda e: e.tensor_scalar(out=IG, in0=IG, scalar1=-1.0, scalar2=1.0, op0=ALU.mult, op1=ALU.add),
             reads=["IG"], writes=["IG"])
        S.op("act", lambda e: e.activation(out=IG, in_=IG, func=AF.Sqrt), reads=["IG"], writes=["IG"])
        S.op("dve", lambda e: e.tensor_tensor(out=IG, in0=IG, in1=T1, op=ALU.mult),
             reads=["IG", "T1"], writes=["IG"])
        fix_a = AC[:, SMP0 - 1:SMP0 - 1 + 176].rearrange("p (i e) -> p i e", e=11)[:, :, 0]
        fix_b = IG[:, SMP0 - 1:SMP0 - 1 + 176].rearrange("p (i e) -> p i e", e=11)[:, :, 0]
        S.op("dve", lambda e, fa=fix_a: e.memset(fa, 0.0), reads=["AC"], writes=["AC"])
        S.op("dve", lambda e, fb=fix_b: e.tensor_copy(out=fb, in_=H0[:, 0:16]), reads=["IG", "H0"], writes=["IG"])
        S.op("dve", lambda e: e.tensor_tensor_scan(out=T1, data0=AC, data1=IG, initial=0.0,
                                                   op0=ALU.mult, op1=ALU.add),
             reads=["AC", "IG"], writes=["T1"])
        hp, hs = ext_out_views(T1)
        S.op("dve", lambda e, cc=cc, hp=hp: e.tensor_tensor(out=YR[:, cc, 0:NPR], in0=GR[:, 0:NPR], in1=hp, op=ALU.mult),
             reads=["GR", "T1"], writes=["YR"])
        S.op("dve", lambda e, cc=cc, hs=hs: e.tensor_tensor(
            out=YR[:, cc, NPR:NT].rearrange("p (i j) -> p i j", j=8),
            in0=GR[:, NPR:NT].rearrange("p (i j) -> p i j", j=8), in1=hs, op=ALU.mult),
            reads=["GR", "T1"], writes=["YR"])
        S.dma("sp", O["p_lru_h"][l, cc * 128:(cc + 1) * 128].rearrange("(p o) -> p o", o=1), T1[:, NPR - 1:NPR],
              reads=["T1"])
        S.op("pool", lambda e, hs=hs: e.tensor_copy(out=HL[:, 0:16], in_=hs[:, :, 7]), reads=["T1"], writes=["HL"])
        store_T(c, HL[:, 0:16], "HL", 128, 16, O["s_lru_h"][l][:, cc * 128:(cc + 1) * 128])
        up, us = ext_in_views(UE)
        S.dma("sp", O["p_lru_conv"][l][:, cc * 128:(cc + 1) * 128].rearrange("j p -> p j"), up[:, NPR - 3:NPR],
              reads=["UE"], allow_slow_non_contiguous=True)
        S.op("pool", lambda e, us=us: e.tensor_copy(out=CS[:, 0:48].rearrange("p (i j) -> p i j", j=3),
                                                    in_=us[:, :, 5:8]), reads=["UE"], writes=["CS"])
        store_T(c, CS[:, 0:48], "CS", 128, 48,
                O["s_lru_conv"][l].rearrange("i j c -> (i j) c")[:, cc * 128:(cc + 1) * 128])
    apply_wout(c, l, YR, "YR", 0, 128, 2)
    S.barrier()
    A.release()


CHUNKS = [(0, 16, 1, 0)] + [(16 + 64 * j, 64, 1, 0) for j in range(32)] + [(NPR, 64, 8, 0), (NPR + 64, 64, 8, 8)]
NCH = len(CHUNKS)
NGAM = 33 + NSEQ
QSCALE_M = 96.0 ** -0.5


def diag_view(ap, nblk, blk, rowlen):
    pstep, pn = ap.ap[0]
    return bass.AP(ap.tensor, ap.offset, [[pstep, pn], [rowlen + blk, nblk], [1, blk]])


def small_mm_tiles(c, Wm, wres, src, sres, M, evac):
    S = c.S
    for ti, (t0, tn) in enumerate(TILES):
        pi = c.pp % 2
        c.pp += 1
        ps = c.P[pi]
        S.op("pe", lambda e, ps=ps, t0=t0, tn=tn: e.matmul(ps[0:M, 0:tn], lhsT=Wm, rhs=src[:, t0:t0 + tn],
                                                          start=True, stop=True),
             reads=[wres, sres], writes=[f"P{pi}"])
        evac(ti, t0, tn, ps[0:M, 0:tn], f"P{pi}")


def mlstm_group(c, l):
    S, A, I, O = c.S, c.A, c.I, c.O
    A.mark()
    P = c.P
    w_in_l = I["w_in"][l].rearrange("(k p) f -> p k f", p=128)
    WQ = A.bf16(384).rearrange("p (h e) -> p h e", h=4)[0:96]
    WK = A.bf16(384).rearrange("p (h e) -> p h e", h=4)[0:96]
    WV = A.bf16(384).rearrange("p (h e) -> p h e", h=4)[0:96]
    WIF = A.bf16(96).rearrange("p (x g) -> p x g", g=8)[0:96]
    MCW = A.f32(16).rearrange("p (h j) -> p h j", h=4)[0:96]
    MCB, NG, SK = A.f32(8)[0:96], A.f32(8)[0:96], A.f32(8)[0:96]
    BI, NBF = A.f32(8)[0:4], A.f32(8)[0:4]
    M0T = A.f32(16)[0:4]
    AE = A.f32(56)[0:4]
    MNEW = A.f32(24)[0:4]
    GT = A.f32(NCH * 16).rearrange("p (c g) -> p c g", g=16)[0:64]
    GAM = A.f32(4 * NGAM).rearrange("p (h g) -> p h g", h=4)[0:96]
    UMbA = [A.bf16(NT)[0:96] for _ in range(4)]
    CMbA = [A.bf16(NT)[0:96] for _ in range(4)]
    CSs = A.f32(48)[0:96]
    A.mark()
    G8 = A.f32(NT)[0:8]
    for W_, nm in ((WQ, "ml_wq"), (WK, "ml_wk"), (WV, "ml_wv")):
        S.dma("pool", W_, I[nm][l].rearrange("h d e -> d h e"), writes=["mlW"])
    S.dma("pool", WIF, I["ml_w_if"][l].rearrange("(x d) g -> d x g", d=96), writes=["mlW"])
    for h in range(4):
        colvec(c, MCW[:, h, :], I["ml_conv_w"][l][:, h * 96:(h + 1) * 96], "j p -> p j", "mlc")
    colvec(c, MCB[:, 0:4], I["ml_conv_b"][l], "(h p) -> p h", "mlc", p=96)
    colvec(c, NG[:, 0:4], I["ml_norm_g"][l], "(h p) -> p h", "mlc", p=96)
    colvec(c, SK[:, 0:4], I["ml_skip"][l], "(h p) -> p h", "mlc", p=96)
    colvec(c, BI[:, 0:1], I["ml_b_if"][l][0:4], "(g o) -> g o", "mlc", o=1)
    colvec(c, NBF[:, 0:1], I["ml_b_if"][l][4:8], "(g o) -> g o", "mlc", o=1)
    colvec(c, M0T[:, 0:16], I["state_mlstm_m"][l], "i h -> h i", "mlc")
    S.op("dve", lambda e: e.tensor_scalar_mul(out=NBF[:, 0:1], in0=NBF[:, 0:1], scalar1=-1.0),
         reads=["mlc"], writes=["mlc"])

    cut(c, 10)

    def head_feats(h, B, need_v):
        UMb_h, CMb_h = B.UMb, B.CMb
        if not need_v:
            for (Wm, dst, dres) in ((WQ[:, h, :], B.MQ, "MQ"), (WK[:, h, :], B.MK, "MK")):
                def ev2(ti, t0, tn, ps, pres, dst=dst, dres=dres):
                    if ti % 2 == 0:
                        S.op("act", lambda e: e.copy(out=dst[:, t0:t0 + tn], in_=ps), reads=[pres], writes=[dres])
                    else:
                        S.op("dve", lambda e: e.tensor_copy(out=dst[:, t0:t0 + tn], in_=ps), reads=[pres], writes=[dres])
                small_mm_tiles(c, Wm, "mlW", CMb_h, "CMb", 96, ev2)
            return
        S.dma("pool", B.WINh, w_in_l[:, :, 512 + 96 * h:512 + 96 * (h + 1)], writes=["WINh"])
        S.op("pool", lambda e: e.memset(B.UMx, 0.0), writes=["UMx"])
        hist = B.UMx[:, SMP0:SMP0 + 176].rearrange("p (i e) -> p i e", e=11)[:, :, 0:3]
        load_T(c, I["state_mlstm_conv"][l].rearrange("i j c -> (i j) c"), 48, 384, [(h * 96, 96, hist, "UMx", 3)])

        cut(c, 14)

        def ev_u(ti, t0, tn, ps, pres):
            for (c0, c1, dst, st_) in dst_views(B.UMx, "ext_in", t0, tn):
                S.op("act", lambda e, s_=pview(ps, c0, c1, st_), d=dst: e.copy(out=d, in_=s_),
                     reads=[pres], writes=["UMx"])
            S.op("act", lambda e, ps=ps, t0=t0, tn=tn: e.copy(out=UMb_h[:, t0:t0 + tn], in_=ps),
                 reads=[pres], writes=["UMb"])
        proj_fm(c, B.WINh, "WINh", 0, 96, ev_u)
        cut(c, 11)
        cmp_, cms = B.CM[:, 0:NPR], B.CM[:, NPR:NT].rearrange("p (i j) -> p i j", j=8)
        for j in range(4):
            up = B.UMx[:, j:j + NPR]
            us = B.UMx[:, SMP0 + j:SMP0 + j + 176].rearrange("p (i e) -> p i e", e=11)[:, :, 0:8]
            for eng, src, dst in (("dve", up, cmp_), ("dve", us, cms)):
                if j == 0:
                    S.op(eng, lambda e, src=src, dst=dst: e.tensor_scalar(
                        out=dst, in0=src, scalar1=MCW[:, h, 0:1], scalar2=MCB[:, h:h + 1], op0=ALU.mult, op1=ALU.add),
                        reads=["UMx", "mlc"], writes=["CM"])
                else:
                    S.op("dve", lambda e, src=src, dst=dst, j=j: e.scalar_tensor_tensor(
                        out=dst, in0=src, scalar=MCW[:, h, j:j + 1], in1=dst, op0=ALU.mult, op1=ALU.add),
                        reads=["UMx", "CM", "mlc"], writes=["CM"])
        S.op("act", lambda e: e.activation(out=B.CM, in_=B.CM, func=AF.Silu), reads=["CM"], writes=["CM"])
        S.op("act", lambda e: e.copy(out=CMb_h, in_=B.CM), reads=["CM"], writes=["CMb"])
        cut(c, 12)
        todo = [(WQ[:, h, :], CMb_h, "CMb", B.MQ, "MQ"), (WK[:, h, :], CMb_h, "CMb", B.MK, "MK")]
        if need_v:
            todo.append((WV[:, h, :], UMb_h, "UMb", B.MV, "MV"))
        for (Wm, src, sres, dst, dres) in todo:
            def ev(ti, t0, tn, ps, pres, dst=dst, dres=dres):
                eng = "act" if ti % 2 == 0 else "dve"
                if eng == "act":
                    S.op("act", lambda e: e.copy(out=dst[:, t0:t0 + tn], in_=ps), reads=[pres], writes=[dres])
                else:
                    S.op("dve", lambda e: e.tensor_copy(out=dst[:, t0:t0 + tn], in_=ps), reads=[pres], writes=[dres])
            small_mm_tiles(c, Wm, "mlW", src, sres, 96, ev)
        cut(c, 13)

    class Bufs:
        pass

    A.mark()
    B = Bufs()
    B.WINh = A.bf16(8 * 96).rearrange("p (k f) -> p k f", k=8)
    B.UMx, B.CM = A.f32(NEX)[0:96], A.f32(NT)[0:96]
    B.MQ, B.MK, B.MV = A.bf16(NT)[0:96], A.bf16(NT)[0:96], A.bf16(NT)[0:96]
    def p1_head(h):
        B.UMb, B.CMb = UMbA[h], CMbA[h]
        head_feats(h, B, True)
        up, us = ext_in_views(B.UMx)
        S.dma("sp", O["p_mlstm_conv"][l][:, h * 96:(h + 1) * 96].rearrange("j p -> p j"), up[:, NPR - 3:NPR],
              reads=["UMx"], allow_slow_non_contiguous=True)
        S.op("dve", lambda e, us=us: e.tensor_copy(out=CSs[:, 0:48].rearrange("p (i j) -> p i j", j=3), in_=us[:, :, 5:8]),
             reads=["UMx"], writes=["CSs"])
        store_T(c, CSs[:, 0:48], "CSs", 96, 48, O["s_mlstm_conv"][l].rearrange("i j c -> (i j) c")[:, h * 96:(h + 1) * 96])
        for ti, (t0, tn) in enumerate(TILES):
            ps = P[2 + ti % 2]
            for xi, (src, sres) in enumerate(((B.MQ, "MQ"), (B.MK, "MK"), (B.MV, "MV"))):
                S.op("pe", lambda e, ps=ps, xi=xi, src=src, t0=t0, tn=tn: e.matmul(
                    ps[0:8, 0:tn], lhsT=WIF[:, xi * 4 + h, :], rhs=src[:, t0:t0 + tn], start=(xi == 0), stop=(xi == 2)),
                    reads=["mlW", sres], writes=[f"P{2 + ti % 2}"])
            if h == 0:
                S.op("dve", lambda e, ps=ps, t0=t0, tn=tn: e.tensor_copy(out=G8[:, t0:t0 + tn], in_=ps[0:8, 0:tn]),
                     reads=[f"P{2 + ti % 2}"], writes=["G8"])
            else:
                S.op("dve", lambda e, ps=ps, t0=t0, tn=tn: e.tensor_tensor(
                    out=G8[:, t0:t0 + tn], in0=ps[0:8, 0:tn], in1=G8[:, t0:t0 + tn], op=ALU.add),
                    reads=[f"P{2 + ti % 2}", "G8"], writes=["G8"])
    for h in range(4):
        p1_head(h)
    import os
    KCUT = int(os.environ.get("KCUT", "0"))
    if KCUT == 1:
        S.barrier(); A.release(); A.release(); A.release()
        return
    if l == 0:
        dbg(c, "G8", G8, ["G8"])
        dbg(c, "CM3", B.CM, ["CM"])
        dbg(c, "MQ3", B.MQ, ["MQ"])
        dbg(c, "MV3", B.MV, ["MV"])
        dbg(c, "UMn3", UMbA[3], ["UMb"])
    S.barrier()
    A.release()

    A.mark()
    RF, RB, RA, RG = [A.f32(NT)[0:4] for _ in range(4)]
    RR = G8[0:4, :]
    S.dma("sp", RF, G8[4:8, :], reads=["G8"], writes=["RF"])
    LI = G8[0:4, :]
    S.op("act", lambda e: e.activation(out=LI, in_=LI, func=AF.Identity, bias=BI[:, 0:1]),
         reads=["G8", "mlc", "RF"], writes=["G8"])
    S.op("act", lambda e: e.activation(out=RF, in_=RF, func=AF.Exp, scale=-1.0, bias=NBF[:, 0:1]),
         reads=["RF", "mlc"], writes=["RF"])
    S.op("act", lambda e: e.activation(out=RF, in_=RF, func=AF.Ln, bias=1.0), reads=["RF"], writes=["RF"])
    sm = lambda X_: X_[:, NPR:NT]
    pr = lambda X_: X_[:, 0:NPR]
    S.op("dve", lambda e: e.tensor_tensor_scan(out=pr(RB), data0=pr(RF), data1=pr(RF), initial=0.0,
                                               op0=ALU.add, op1=ALU.bypass), reads=["RF"], writes=["RB"])
    S.op("dve", lambda e: e.tensor_tensor_scan(out=sm(RB), data0=c.SMASK[0:4, :], data1=sm(RF), initial=0.0,
                                               op0=ALU.mult, op1=ALU.add), reads=["RF", "consts"], writes=["RB"])
    S.op("dve", lambda e: e.tensor_tensor(out=RA, in0=LI, in1=RB, op=ALU.add), reads=["G8", "RB"], writes=["RA"])
    S.op("dve", lambda e: e.tensor_copy(out=RG, in_=RA), reads=["RA"], writes=["RG"])
    S.op("dve", lambda e: e.tensor_scalar_max(out=RG[:, 0:1], in0=RG[:, 0:1], scalar1=0.0), reads=["RG"], writes=["RG"])
    st_v = lambda X_: X_[:, NPR:NT].rearrange("p (i j) -> p i j", j=8)
    S.op("dve", lambda e: e.tensor_tensor(out=st_v(RG)[:, :, 0], in0=st_v(RG)[:, :, 0], in1=M0T[:, 0:16], op=ALU.max),
         reads=["RG", "mlc"], writes=["RG"])
    S.op("dve", lambda e: e.tensor_tensor_scan(out=pr(RF), data0=pr(RG), data1=pr(RG), initial=-1e30,
                                               op0=ALU.max, op1=ALU.max), reads=["RG", "RB", "RA"], writes=["RF"])
    S.op("dve", lambda e: e.tensor_tensor_scan(out=sm(RF), data0=c.SNEG[0:4, :], data1=sm(RG), initial=-1e30,
                                               op0=ALU.add, op1=ALU.max), reads=["RG", "consts"], writes=["RF"])
    S.op("dve", lambda e: e.memset(RG[:, 0:16], 0.0), reads=["RF"], writes=["RG"])
    S.op("dve", lambda e: e.tensor_copy(
        out=RG[:, 16:NPR].rearrange("p (c t) -> p c t", t=64),
        in_=RF[:, 15:15 + 2048].rearrange("p (c t) -> p c t", t=64)[:, :, 0:1].to_broadcast([4, 32, 64])),
        reads=["RF"], writes=["RG"])
    S.op("dve", lambda e: e.tensor_copy(out=st_v(RG), in_=M0T[:, 0:16].unsqueeze(2).to_broadcast([4, 16, 8])),
         reads=["mlc"], writes=["RG"])
    S.op("dve", lambda e: e.tensor_tensor(out=RR, in0=RB, in1=RF, op=ALU.subtract), reads=["RB", "RF"], writes=["G8"])
    S.op("dve", lambda e: e.tensor_scalar_mul(out=MNEW[:, 0:1], in0=RR[:, NPR - 1:NPR], scalar1=-1.0),
         reads=["G8"], writes=["MNEW"])
    S.op("dve", lambda e: e.tensor_scalar_mul(out=MNEW[:, 1:17], in0=st_v(RR)[:, :, 7], scalar1=-1.0),
         reads=["G8"], writes=["MNEW"])
    S.dma("sp", O["p_mlstm_m"][l].rearrange("(h o) -> h o", o=1), MNEW[:, 0:1], reads=["MNEW"])
    S.dma("sp", O["s_mlstm_m"][l].rearrange("i h -> h i"), MNEW[:, 1:17], reads=["MNEW"], allow_slow_non_contiguous=True)
    S.op("act", lambda e: e.activation(out=RR, in_=RR, func=AF.Exp), reads=["G8", "MNEW"], writes=["G8"])
    S.op("dve", lambda e: e.tensor_copy(out=RB[:, 0:16], in_=RF[:, 15:16].to_broadcast([4, 16])), reads=["RF", "G8"], writes=["RB"])
    S.op("dve", lambda e: e.tensor_copy(
        out=RB[:, 16:NPR].rearrange("p (c t) -> p c t", t=64),
        in_=RF[:, 16:NPR].rearrange("p (c t) -> p c t", t=64)[:, :, 63:64].to_broadcast([4, 32, 64])),
        reads=["RF"], writes=["RB"])
    S.op("dve", lambda e: e.tensor_copy(out=st_v(RB), in_=st_v(RF)[:, :, 7:8].to_broadcast([4, 16, 8])), reads=["RF"], writes=["RB"])
    S.op("dve", lambda e: e.tensor_tensor(out=RB, in0=RA, in1=RB, op=ALU.subtract), reads=["RA", "RB"], writes=["RB"])
    S.op("act", lambda e: e.activation(out=RB, in_=RB, func=AF.Exp), reads=["RB"], writes=["RB"])
    S.op("dve", lambda e: e.tensor_tensor(out=RA, in0=RA, in1=RG, op=ALU.subtract), reads=["RA", "RG"], writes=["RA"])
    S.op("act", lambda e: e.activation(out=RA, in_=RA, func=AF.Exp), reads=["RA"], writes=["RA"])
    S.op("dve", lambda e: e.tensor_tensor(out=RG, in0=RG, in1=RF, op=ALU.subtract), reads=["RG", "RF", "RA"], writes=["RG"])
    S.op("act", lambda e: e.activation(out=RG, in_=RG, func=AF.Exp), reads=["RG"], writes=["RG"])
    S.op("dve", lambda e: e.tensor_copy(out=AE[:, 0:1], in_=RG[:, 15:16]), reads=["RG"], writes=["AE"])
    S.op("dve", lambda e: e.tensor_copy(out=AE[:, 1:33], in_=RG[:, 16:NPR].rearrange("p (c t) -> p c t", t=64)[:, :, 63]),
         reads=["RG"], writes=["AE"])
    S.op("dve", lambda e: e.tensor_copy(out=AE[:, 33:49], in_=st_v(RG)[:, :, 7]), reads=["RG"], writes=["AE"])
    S.op("dve", lambda e: e.tensor_scalar_mul(out=RG, in0=RG, scalar1=QSCALE_M), reads=["RG", "AE"], writes=["RG"])
    S.op("dve", lambda e: e.reciprocal(out=RF, in_=RG), reads=["RG", "RF"], writes=["RF"])
    S.op("dve", lambda e: e.tensor_tensor(out=RR, in0=RR, in1=RF, op=ALU.mult), reads=["G8", "RF"], writes=["G8"])
    for ci, (tok0, T, nseq, seq0) in enumerate(CHUNKS):
        for gi_, (Rw, rn) in enumerate(((RG, "RG"), (RA, "RA"), (RR, "G8"), (RB, "RB"))):
            S.op("pe", lambda e, Rw=Rw, gi_=gi_, tok0=tok0, T=T: e.transpose(
                out=P[6][0:T, gi_ * 4:gi_ * 4 + 4], in_=Rw[:, tok0:tok0 + T], identity=c.ident[0:4, 0:4]),
                reads=[rn, "ident"], writes=["P6"])
        S.op("act", lambda e, ci=ci, T=T: e.copy(out=GT[0:T, ci, :], in_=P[6][0:T, 0:16]), reads=["P6"], writes=["GT"])
    if l == 0:
        dbg(c, "alpha", RG, ["RG"])
        dbg(c, "beta", RA, ["RA"])
        dbg(c, "eps", RR, ["G8"])
        dbg(c, "G", RF, ["RF"])
        dbg(c, "negB", RB, ["RB"])
        dbg(c, "GT", GT, ["GT"])
    sel = c.SEL[0:4, :].rearrange("p (h m) -> p h m", h=4)
    for h in range(4):
        S.op("pe", lambda e, h=h: e.matmul(P[7][0:96, 0:NGAM], lhsT=sel[:, h, :], rhs=AE[:, 0:NGAM], start=True, stop=True),
             reads=["consts", "AE"], writes=["P7"])
        S.op("act", lambda e, h=h: e.copy(out=GAM[:, h, :], in_=P[7][0:96, 0:NGAM]), reads=["P7"], writes=["GAM"])
    S.barrier()
    A.release()
    A.release()
    if KCUT == 2:
        A.release()
        return

    A.mark()
    B = Bufs()
    WINz = A.bf16(8 * 96).rearrange("p (k f) -> p k f", k=8)
    B.MQ, B.MK = A.bf16(NT)[0:96], A.bf16(NT)[0:96]
    YMh = A.bf16(NT).rearrange("p (o t) -> p o t", o=1)[0:96]
    ZS = A.bf16(NT)[0:96]
    HNT = A.f32(NT)[0:96]
    CEp = A.f32(104)[0:96]
    CEb = [A.bf16(104)[0:96] for _ in range(2)]
    CE = A.f32(NSEQ * 97).rearrange("p (i e) -> p i e", e=97)[0:96]
    CEsb = A.bf16(NSEQ * 97).rearrange("p (i e) -> p i e", e=97)[0:96]
    QZ = A.bf16(8 * 64).rearrange("p (i t) -> p i t", i=8)[0:96]
    KTz = A.bf16(8 * 96).rearrange("p (i d) -> p i d", i=8)[0:64]
    VT = [A.bf16(104)[0:64] for _ in range(4)]
    KTb = [A.bf16(96)[0:64] for _ in range(4)]
    STm = [A.bf16(64)[0:64] for _ in range(4)]
    Hh = [A.f32(96)[0:64] for _ in range(4)]
    HN = [A.f32(96)[0:64] for _ in range(2)]
    JK = A.f32(96)[0:64]
    SMALL = [A.f32(8)[0:64] for _ in range(6)]
    S.op("pool", lambda e: e.memset(QZ, 0.0), writes=["QZ"])
    for par in range(4):
        S.op("pool", lambda e, par=par: e.memset(VT[par][:, 96:97], 1.0), writes=[f"VT{par}"])

    def p2_head(h):
        B.UMb, B.CMb = UMbA[h], CMbA[h]
        UMb_h, CMb_h = UMbA[h], CMbA[h]
        head_feats(h, B, False)
        S.dma("pool", WINz, w_in_l[:, :, 896 + 96 * h:896 + 96 * (h + 1)], writes=["WINz"])

        def ev_z(ti, t0, tn, ps, pres):
            S.op("act", lambda e: e.activation(out=ZS[:, t0:t0 + tn], in_=ps, func=AF.Sigmoid), reads=[pres], writes=["ZS"])
        proj_fm(c, WINz, "WINz", 0, 96, ev_z)
        S.op("pool", lambda e: e.memset(CEp[:, 0:97], 0.0), writes=["CEp"])
        S.op("pool", lambda e: e.memset(CEb[0][:, 0:97], 0.0), writes=["CEb0"])
        S.dma("sp", CE[:, :, 0:96], I["state_mlstm_C"][l][:, h].rearrange("i d e -> d i e"), writes=["CE"])
        S.dma("sp", CE[:, :, 96], I["state_mlstm_n"][l][:, h, :].rearrange("i d -> d i"), writes=["CE"],
              allow_slow_non_contiguous=True)
        S.op("dve", lambda e: e.tensor_copy(out=CEsb, in_=CE), reads=["CE"], writes=["CEsb"])
        def ctx(ci):
            tok0, T, nseq, seq0 = CHUNKS[ci]
            return tok0, T, nseq, seq0, slice(tok0, tok0 + T)

        def s1(ci):
            tok0, T, nseq, seq0, tk = ctx(ci)
            Pk, nk = P[ci % 2], f"P{ci % 2}"
            S.op("pe", lambda e: e.matmul(Pk[0:T, 0:96], lhsT=UMb_h[:, tk], rhs=WV[:, h, :], start=True, stop=True),
                 reads=["UMb", "mlW"], writes=[nk])
            S.op("pe", lambda e: e.matmul(Pk[0:T, 96:192], lhsT=CMb_h[:, tk], rhs=WK[:, h, :], start=True, stop=True),
                 reads=["CMb", "mlW"], writes=[nk])
            S.op("pe", lambda e: e.matmul(Pk[0:T, 192:192 + T], lhsT=B.MK[:, tk], rhs=B.MQ[:, tk], start=True, stop=True),
                 reads=["MK", "MQ"], writes=[nk])

        def s2(ci):
            tok0, T, nseq, seq0, tk = ctx(ci)
            b4 = ci % 4
            be, bg = GT[0:T, ci, 4 + h:5 + h], GT[0:T, ci, 12 + h:13 + h]
            mask = (c.MASKC if nseq == 1 else c.MASKB)[0:T, 0:T]
            Pk, nk = P[ci % 2], f"P{ci % 2}"
            S.op("dve", lambda e: e.tensor_copy(out=VT[b4][0:T, 0:96], in_=Pk[0:T, 0:96]), reads=[nk], writes=[f"VT{b4}"])
            S.op("dve", lambda e: e.tensor_scalar_mul(out=KTb[b4][0:T, :], in0=Pk[0:T, 96:192], scalar1=bg),
                 reads=[nk, "GT"], writes=[f"KTb{b4}"])
            S.op("dve", lambda e: e.scalar_tensor_tensor(out=STm[b4][0:T, 0:T], in0=Pk[0:T, 192:192 + T], scalar=be, in1=mask,
                                                         op0=ALU.mult, op1=ALU.mult),
                 reads=[nk, "GT", "consts"], writes=[f"STm{b4}"])

        def s3(ci):
            tok0, T, nseq, seq0, tk = ctx(ci)
            b4 = ci % 4
            if nseq == 1:
                Pd, nd = P[5 + ci % 2], f"P{5 + ci % 2}"
                S.op("pe", lambda e: e.matmul(Pd[0:96, 0:97], lhsT=KTb[b4][0:T, :], rhs=VT[b4][0:T, 0:97], start=True, stop=True),
                     reads=[f"KTb{b4}", f"VT{b4}"], writes=[nd])
            else:
                S.op("dve", lambda e: e.tensor_tensor(
                    out=KTz, in0=KTb[b4][0:64, :].unsqueeze(1).to_broadcast([64, 8, 96]),
                    in1=c.PM[0:64, :].unsqueeze(2).to_broadcast([64, 8, 96]), op=ALU.mult),
                    reads=[f"KTb{b4}", "consts"], writes=["KTz"])
                for half in range(2):
                    pb = P[5 + half]
                    for j in range(4):
                        S.op("pe", lambda e, pb=pb, j=j, half=half: e.matmul(
                            pb[0:96, j * 97:(j + 1) * 97], lhsT=KTz[:, 4 * half + j, :], rhs=VT[b4][0:64, 0:97],
                            start=True, stop=True), reads=["KTz", f"VT{b4}"], writes=[f"P{5 + half}"])

        def s4(ci):
            tok0, T, nseq, seq0, tk = ctx(ci)
            if nseq == 1:
                Pd, nd = P[5 + ci % 2], f"P{5 + ci % 2}"
                S.op("dve", lambda e: e.scalar_tensor_tensor(
                    out=CEp[:, 0:97], in0=CEp[:, 0:97], scalar=GAM[:, h, ci:ci + 1], in1=Pd[0:96, 0:97], op0=ALU.mult, op1=ALU.add),
                    reads=[nd, "CEp", "GAM"], writes=["CEp"])
            else:
                for half in range(2):
                    pb = P[5 + half]
                    s0 = seq0 + 4 * half
                    cev = CE[:, s0:s0 + 4, :]
                    S.op("dve", lambda e, cev=cev, s0=s0: e.tensor_tensor(
                        out=cev, in0=cev, in1=GAM[:, h, 33 + s0:33 + s0 + 4].unsqueeze(2).to_broadcast([96, 4, 97]), op=ALU.mult),
                        reads=["CE", "GAM"], writes=["CE"])
                    S.op("dve", lambda e, pb=pb, cev=cev: e.tensor_tensor(
                        out=cev, in0=pb[0:96, 0:388].rearrange("p (i e) -> p i e", e=97), in1=cev, op=ALU.add),
                        reads=[f"P{5 + half}", "CE"], writes=["CE"])

        def s5(ci):
            tok0, T, nseq, seq0, tk = ctx(ci)
            b4 = ci % 4
            Pn, nn = P[2 + ci % 3], f"P{2 + ci % 3}"
            S.op("pe", lambda e: e.matmul(Pn[0:T, 0:97], lhsT=STm[b4][0:T, 0:T], rhs=VT[b4][0:T, 0:97], start=True, stop=False),
                 reads=[f"STm{b4}", f"VT{b4}"], writes=[nn])
            if nseq == 1:
                S.op("pe", lambda e: e.matmul(Pn[0:T, 0:97], lhsT=B.MQ[:, tk], rhs=CEb[ci % 2][:, 0:97], start=False, stop=True),
                     reads=["MQ", f"CEb{ci % 2}"], writes=[nn])
                S.op("act", lambda e: e.copy(out=CEb[(ci + 1) % 2][:, 0:97], in_=CEp[:, 0:97]), reads=["CEp"], writes=[f"CEb{(ci + 1) % 2}"])
            else:
                S.op("act", lambda e: e.copy(out=diag_view(QZ, 8, 8, 64), in_=B.MQ[:, tk].rearrange("p (i j) -> p i j", j=8)),
                     reads=["MQ"], writes=["QZ"])
                for i in range(8):
                    S.op("pe", lambda e, i=i: e.matmul(Pn[0:64, 0:97], lhsT=QZ[:, i, :], rhs=CEsb[:, seq0 + i, :],
                                                       start=False, stop=(i == 7)),
                         reads=["QZ", "CEsb"], writes=[nn])

        def s6(ci):
            tok0, T, nseq, seq0, tk = ctx(ci)
            Pn, nn = P[2 + ci % 3], f"P{2 + ci % 3}"
            sm_ = SMALL[ci % 6]
            S.op("act", lambda e: e.activation(out=sm_[0:T, 0:1], in_=Pn[0:T, 96:97], func=AF.Abs), reads=[nn], writes=[f"SM{ci % 6}"])

        def s7(ci):
            tok0, T, nseq, seq0, tk = ctx(ci)
            epp = GT[0:T, ci, 8 + h:9 + h]
            Pn, nn = P[2 + ci % 3], f"P{2 + ci % 3}"
            sm_, sn, b4 = SMALL[ci % 6], f"SM{ci % 6}", ci % 4
            S.op("dve", lambda e: e.tensor_tensor(out=sm_[0:T, 0:1], in0=sm_[0:T, 0:1], in1=epp, op=ALU.max), reads=[sn, "GT"], writes=[sn])
            S.op("dve", lambda e: e.reciprocal(out=sm_[0:T, 0:1], in_=sm_[0:T, 0:1]), reads=[sn], writes=[sn])
            S.op("dve", lambda e: e.tensor_scalar_mul(out=Hh[b4][0:T, :], in0=Pn[0:T, 0:96], scalar1=sm_[0:T, 0:1]),
                 reads=[nn, sn], writes=[f"H{b4}"])

        def s8(ci):
            tok0, T, nseq, seq0, tk = ctx(ci)
            sm_, sn, b4 = SMALL[ci % 6], f"SM{ci % 6}", ci % 4
            S.op("act", lambda e: e.activation(out=JK[0:T, :], in_=Hh[b4][0:T, :], func=AF.Square, accum_out=sm_[0:T, 2:3]),
                 reads=[f"H{b4}", sn], writes=["JK", sn])
            S.op("act", lambda e: e.activation(out=sm_[0:T, 3:4], in_=sm_[0:T, 2:3], func=AF.Sqrt, bias=EPS, scale=1.0 / 96),
                 reads=[sn], writes=[sn])

        def s9(ci):
            tok0, T, nseq, seq0, tk = ctx(ci)
            sm_, sn = SMALL[ci % 6], f"SM{ci % 6}"
            S.op("dve", lambda e: e.reciprocal(out=sm_[0:T, 3:4], in_=sm_[0:T, 3:4]), reads=[sn], writes=[sn])

        def s10(ci):
            tok0, T, nseq, seq0, tk = ctx(ci)
            sm_, sn, b4 = SMALL[ci % 6], f"SM{ci % 6}", ci % 4
            S.op("act", lambda e: e.activation(out=HN[ci % 2][0:T, :], in_=Hh[b4][0:T, :], func=AF.Copy, scale=sm_[0:T, 3:4]),
                 reads=[f"H{b4}", sn], writes=[f"HN{ci % 2}"])

        def s11(ci):
            tok0, T, nseq, seq0, tk = ctx(ci)
            S.op("pe", lambda e: e.transpose(out=P[7][0:96, 0:T], in_=HN[ci % 2][0:T, :], identity=c.ident[0:T, 0:T]),
                 reads=[f"HN{ci % 2}", "ident"], writes=["P7"])

        def s12(ci):
            tok0, T, nseq, seq0, tk = ctx(ci)
            S.op("act", lambda e: e.copy(out=HNT[:, tk], in_=P[7][0:96, 0:T]), reads=["P7"], writes=["HNT"])

        stages = [s1, s2, s3, s4, s5, s6, s7, s8, s9, s10, s11, s12]
        order = [11, 10, 9, 8, 7, 6, 5, 4, 3, 2, 1, 0]
        for it in range(NCH + len(stages) - 1):
            for k in order:
                ci = it - k
                if 0 <= ci < NCH:
                    stages[k](ci)
        drain(c, 99)
        S.op("act", lambda e: e.activation(out=HNT, in_=HNT, func=AF.Copy, scale=NG[:, h:h + 1]), reads=["HNT", "mlc"], writes=["HNT"])
        S.op("dve", lambda e: e.scalar_tensor_tensor(out=HNT, in0=CMb_h, scalar=SK[:, h:h + 1], in1=HNT, op0=ALU.mult, op1=ALU.add),
             reads=["HNT", "CMb", "mlc"], writes=["HNT"])
        S.op("dve", lambda e: e.tensor_tensor(out=YMh[:, 0, :], in0=HNT, in1=ZS, op=ALU.mult), reads=["HNT", "ZS"], writes=["YMh"])
        apply_wout(c, l, YMh, "YMh", 256 + 96 * h, 96, 1, defer=True)
        S.dma("sp", O["p_mlstm_C"][l, h], CEp[:, 0:96], reads=["CEp"])
        S.dma("sp", O["p_mlstm_n"][l, h].rearrange("(d o) -> d o", o=1), CEp[:, 96:97], reads=["CEp"])
        S.dma("sp", O["s_mlstm_C"][l][:, h].rearrange("i d e -> d i e"), CE[:, :, 0:96], reads=["CE"])
        S.dma("sp", O["s_mlstm_n"][l][:, h, :].rearrange("i d -> d i"), CE[:, :, 96], reads=["CE"], allow_slow_non_contiguous=True)
    for h in range(4):
        p2_head(h)
    drain(c, 99)
    S.barrier()
    A.release()
    A.release()


QSCALE_G = 48.0 ** -0.5


def gla_group(c, l):
    S, A, I, O = c.S, c.A, c.I, c.O
    P = c.P
    A.mark()
    w_in_l = I["w_in"][l].rearrange("(k p) f -> p k f", p=128)
    WA_ = A.bf16(8 * 16).rearrange("p (k f) -> p k f", k=8)
    ALR = A.f32(NT)[0:16]
    WUP = A.f32(192)[0:16]
    NBUP, GNG = A.f32(8)[0:48], A.f32(8)[0:96]
    MCH = A.f32(NT)[0:48]
    GAMg = A.f32(NGAM + 7)[0:48]
    S.dma("pool", WA_, w_in_l[:, :, 2432:2448], writes=["WA_"])
    S.dma("sp", WUP, I["gla_w_up"][l], writes=["glac"])
    colvec(c, NBUP[:, 0:4], I["gla_b_up"][l], "(h k) -> k h", "glac", k=48)
    colvec(c, GNG[:, 0:4], I["gla_norm_g"][l], "(h p) -> p h", "glac", p=96)
    S.op("dve", lambda e: e.tensor_scalar_mul(out=NBUP[:, 0:4], in0=NBUP[:, 0:4], scalar1=-1.0), reads=["glac"], writes=["glac"])
    S.op("pool", lambda e: e.memset(MCH, 1.0), writes=["MCH"])
    S.op("pool", lambda e: e.memset(MCH[:, 0:1], 0.0), reads=["MCH"], writes=["MCH"])
    S.op("pool", lambda e: e.memset(MCH[:, 16:NPR].rearrange("p (c t) -> p c t", t=64)[:, :, 0:1], 0.0), reads=["MCH"], writes=["MCH"])
    S.op("pool", lambda e: e.memset(MCH[:, NPR:NT].rearrange("p (i j) -> p i j", j=8)[:, :, 0:1], 0.0), reads=["MCH"], writes=["MCH"])

    def ev_a(ti, t0, tn, ps, pres):
        S.op("act", lambda e: e.copy(out=ALR[:, t0:t0 + tn], in_=ps), reads=[pres], writes=["ALR"])
    proj_fm(c, WA_, "WA_", 0, 16, ev_a)

    WQg = A.bf16(8 * 48).rearrange("p (k f) -> p k f", k=8)
    WKg = A.bf16(8 * 48).rearrange("p (k f) -> p k f", k=8)
    WVg = A.bf16(8 * 96).rearrange("p (k f) -> p k f", k=8)
    WGg = A.bf16(8 * 96).rearrange("p (k f) -> p k f", k=8)
    BC, EO, QG, KG = A.f32(NT)[0:48], A.f32(NT)[0:96], A.f32(NT)[0:48], A.f32(NT)[0:48]
    VF = A.f32(NT)[0:96]
    GGs = A.bf16(NT)[0:96]
    YGh = A.bf16(NT).rearrange("p (o t) -> p o t", o=1)[0:96]
    Sp = [A.f32(96)[0:48] for _ in range(2)]
    Ss = A.f32(NSEQ * 96).rearrange("p (i e) -> p i e", e=96)[0:48]
    QZ = A.f32(8 * 64).rearrange("p (i t) -> p i t", i=8)[0:48]
    KTz = A.bf16(8 * 48).rearrange("p (i d) -> p i d", i=8)[0:64]
    VT = [A.bf16(96)[0:64] for _ in range(4)]
    KT = [A.bf16(48)[0:64] for _ in range(4)]
    STm = [A.bf16(64)[0:64] for _ in range(4)]
    ON = [A.f32(96)[0:64] for _ in range(2)]
    JK = A.f32(96)[0:64]
    SMALL = [A.f32(8)[0:64] for _ in range(3)]
    S.op("pool", lambda e: e.memset(QZ, 0.0), writes=["QZg"])
    st_v = lambda X_: X_[:, NPR:NT].rearrange("p (i j) -> p i j", j=8)

    def g_head(h):
        S.dma("pool", WQg, w_in_l[:, :, 1280 + 48 * h:1280 + 48 * (h + 1)], writes=["WQg"])
        S.dma("pool", WKg, w_in_l[:, :, 1472 + 48 * h:1472 + 48 * (h + 1)], writes=["WKg"])
        S.dma("pool", WVg, w_in_l[:, :, 1664 + 96 * h:1664 + 96 * (h + 1)], writes=["WVg"])
        S.dma("pool", WGg, w_in_l[:, :, 2048 + 96 * h:2048 + 96 * (h + 1)], writes=["WGg"])

        def ev_q(ti, t0, tn, ps, pres):
            S.op("act", lambda e: e.activation(out=QG[:, t0:t0 + tn], in_=ps, func=AF.Copy, scale=QSCALE_G), reads=[pres], writes=["QG"])
        proj_fm(c, WQg, "WQg", 0, 48, ev_q)

        def ev_k(ti, t0, tn, ps, pres):
            S.op("act", lambda e: e.copy(out=KG[:, t0:t0 + tn], in_=ps), reads=[pres], writes=["KG"])
        proj_fm(c, WKg, "WKg", 0, 48, ev_k)

        def ev_g(ti, t0, tn, ps, pres):
            S.op("act", lambda e: e.activation(out=GGs[:, t0:t0 + tn], in_=ps, func=AF.Silu), reads=[pres], writes=["GGs"])
        proj_fm(c, WGg, "WGg", 0, 96, ev_g)

        def ev_v(ti, t0, tn, ps, pres):
            if ti % 2 == 0:
                S.op("dve", lambda e: e.tensor_copy(out=VF[:, t0:t0 + tn], in_=ps), reads=[pres], writes=["VF"])
            else:
                S.op("act", lambda e: e.copy(out=VF[:, t0:t0 + tn], in_=ps), reads=[pres], writes=["VF"])
        proj_fm(c, WVg, "WVg", 0, 96, ev_v)

        def ev_l(ti, t0, tn, ps, pres):
            S.op("act", lambda e: e.activation(out=BC[:, t0:t0 + tn], in_=ps, func=AF.Exp, scale=-1.0, bias=NBUP[:, h:h + 1]),
                 reads=[pres, "glac"], writes=["BC"])
        small_mm_tiles(c, WUP[:, 48 * h:48 * (h + 1)], "glac", ALR, "ALR", 48, ev_l)
        S.op("act", lambda e: e.activation(out=BC, in_=BC, func=AF.Ln, bias=1.0), reads=["BC"], writes=["BC"])
        S.op("dve", lambda e: e.tensor_scalar_mul(out=EO[0:48, :], in0=BC, scalar1=-1.0 / 16.0), reads=["BC"], writes=["EO"])
        S.op("dve", lambda e: e.tensor_tensor_scan(out=BC, data0=MCH, data1=EO[0:48, :], initial=0.0, op0=ALU.mult, op1=ALU.add),
             reads=["EO", "MCH"], writes=["BC"])
        S.op("act", lambda e: e.activation(out=EO[0:48, :], in_=BC, func=AF.Exp), reads=["BC"], writes=["EO"])
        S.op("dve", lambda e: e.tensor_tensor(out=QG, in0=QG, in1=EO[0:48, :], op=ALU.mult), reads=["QG", "EO"], writes=["QG"])
        S.op("dve", lambda e: e.tensor_copy(out=GAMg[:, 0:1], in_=EO[0:48, 15:16]), reads=["EO"], writes=["GAMg"])
        S.op("dve", lambda e: e.tensor_copy(out=GAMg[:, 1:33], in_=EO[0:48, 16:NPR].rearrange("p (c t) -> p c t", t=64)[:, :, 63]),
             reads=["EO"], writes=["GAMg"])
        S.op("dve", lambda e: e.tensor_copy(out=GAMg[:, 33:49], in_=st_v(EO[0:48, :])[:, :, 7]), reads=["EO"], writes=["GAMg"])
        S.op("act", lambda e: e.activation(out=BC, in_=BC, func=AF.Exp, scale=-1.0), reads=["BC"], writes=["BC"])
        S.op("dve", lambda e: e.tensor_tensor(out=KG, in0=KG, in1=BC, op=ALU.mult), reads=["KG", "BC"], writes=["KG"])
        S.op("pool", lambda e: e.memset(Sp[1], 0.0), writes=["Sp1"])
        S.dma("sp", Ss, I["state_gla_S"][l][:, h].rearrange("i k v -> k i v"), writes=["Ss"])
        def ctx(ci):
            tok0, T, nseq, seq0 = CHUNKS[ci]
            return tok0, T, nseq, seq0, slice(tok0, tok0 + T)

        def g1(ci):
            tok0, T, nseq, seq0, tk = ctx(ci)
            Pa, na = P[ci % 2], f"P{ci % 2}"
            S.op("pe", lambda e: e.transpose(out=Pa[0:T, 0:48], in_=KG[:, tk], identity=c.ident[0:48, 0:48]),
                 reads=["KG", "ident"], writes=[na])
            S.op("pe", lambda e: e.matmul(Pa[0:T, 64:64 + T], lhsT=KG[:, tk], rhs=QG[:, tk], start=True, stop=True),
                 reads=["KG", "QG"], writes=[na])
            S.op("pe", lambda e: e.transpose(out=Pa[0:T, 128:224], in_=VF[:, tk], identity=c.ident[0:96, 0:96]),
                 reads=["VF", "ident"], writes=[na])

        def g2(ci):
            tok0, T, nseq, seq0, tk = ctx(ci)
            b4 = ci % 4
            mask = (c.MASKC if nseq == 1 else c.MASKB)[0:T, 0:T]
            Pa, na = P[ci % 2], f"P{ci % 2}"
            S.op("dve", lambda e: e.tensor_copy(out=VT[b4][0:T, :], in_=Pa[0:T, 128:224]), reads=[na], writes=[f"gVT{b4}"])
            S.op("dve", lambda e: e.tensor_copy(out=KT[b4][0:T, :], in_=Pa[0:T, 0:48]), reads=[na], writes=[f"gKT{b4}"])
            S.op("dve", lambda e: e.tensor_tensor(out=STm[b4][0:T, 0:T], in0=Pa[0:T, 64:64 + T], in1=mask, op=ALU.mult),
                 reads=[na, "consts"], writes=[f"gST{b4}"])

        def g3(ci):
            tok0, T, nseq, seq0, tk = ctx(ci)
            b4 = ci % 4
            if nseq == 1:
                Pd, nd = P[5 + ci % 2], f"P{5 + ci % 2}"
                S.op("pe", lambda e: e.matmul(Pd[0:48, 0:96], lhsT=KT[b4][0:T, :], rhs=VT[b4][0:T, :], start=True, stop=True),
                     reads=[f"gKT{b4}", f"gVT{b4}"], writes=[nd])

        def g4(ci):
            tok0, T, nseq, seq0, tk = ctx(ci)
            if nseq == 1:
                Pd, nd = P[5 + ci % 2], f"P{5 + ci % 2}"
                S.op("dve", lambda e: e.tensor_tensor(out=Sp[ci % 2], in0=Pd[0:48, 0:96], in1=Sp[(ci + 1) % 2], op=ALU.add),
                     reads=[nd, f"Sp{(ci + 1) % 2}"], writes=[f"Sp{ci % 2}"])

        def g5(ci):
            tok0, T, nseq, seq0, tk = ctx(ci)
            b4 = ci % 4
            Pn, nn = P[2 + ci % 3], f"P{2 + ci % 3}"
            S.op("pe", lambda e: e.matmul(Pn[0:T, 0:96], lhsT=STm[b4][0:T, 0:T], rhs=VT[b4][0:T, :], start=True, stop=False),
                 reads=[f"gST{b4}", f"gVT{b4}"], writes=[nn])
            if nseq == 1:
                S.op("pe", lambda e: e.matmul(Pn[0:T, 0:96], lhsT=QG[:, tk], rhs=Sp[(ci + 1) % 2], start=False, stop=True),
                     reads=["QG", f"Sp{(ci + 1) % 2}"], writes=[nn])
                S.op("act", lambda e: e.activation(out=Sp[ci % 2], in_=Sp[ci % 2], func=AF.Copy, scale=GAMg[:, ci:ci + 1]),
                     reads=[f"Sp{ci % 2}", "GAMg"], writes=[f"Sp{ci % 2}"])
            else:
                S.op("act", lambda e: e.copy(out=diag_view(QZ, 8, 8, 64), in_=QG[:, tk].rearrange("p (i j) -> p i j", j=8)),
                     reads=["QG"], writes=["QZg"])
                for i in range(8):
                    S.op("pe", lambda e, i=i: e.matmul(Pn[0:64, 0:96], lhsT=QZ[:, i, :], rhs=Ss[:, seq0 + i, :], start=False, stop=(i == 7)),
                         reads=["QZg", "Ss"], writes=[nn])
                S.op("dve", lambda e: e.tensor_tensor(
                    out=KTz, in0=KT[b4][0:64, :].unsqueeze(1).to_broadcast([64, 8, 48]),
                    in1=c.PM[0:64, :].unsqueeze(2).to_broadcast([64, 8, 48]), op=ALU.mult),
                    reads=[f"gKT{b4}", "consts"], writes=["gKTz"])
                for half in range(2):
                    pb = P[5 + half]
                    for j in range(4):
                        S.op("pe", lambda e, pb=pb, j=j, half=half: e.matmul(
                            pb[0:48, j * 96:(j + 1) * 96], lhsT=KTz[:, 4 * half + j, :], rhs=VT[b4][0:64, :], start=True, stop=True),
                            reads=["gKTz", f"gVT{b4}"], writes=[f"P{5 + half}"])
                for half in range(2):
                    pb = P[5 + half]
                    s0 = seq0 + 4 * half
                    sv = Ss[:, s0:s0 + 4, :]
                    S.op("dve", lambda e, pb=pb, sv=sv: e.tensor_tensor(
                        out=sv, in0=pb[0:48, 0:384].rearrange("p (i e) -> p i e", e=96), in1=sv, op=ALU.add),
                        reads=[f"P{5 + half}", "Ss"], writes=["Ss"])
                    S.op("dve", lambda e, sv=sv, s0=s0: e.tensor_tensor(
                        out=sv, in0=sv, in1=GAMg[:, 33 + s0:33 + s0 + 4].unsqueeze(2).to_broadcast([48, 4, 96]), op=ALU.mult),
                        reads=["Ss", "GAMg"], writes=["Ss"])

        def g6(ci):
            tok0, T, nseq, seq0, tk = ctx(ci)
            Pn, nn = P[2 + ci % 3], f"P{2 + ci % 3}"
            sm_, sn = SMALL[ci % 3], f"gSM{ci % 3}"
            S.op("act", lambda e: e.activation(out=JK[0:T, :], in_=Pn[0:T, 0:96], func=AF.Square, accum_out=sm_[0:T, 0:1]),
                 reads=[nn], writes=["gJK", sn])
            S.op("act", lambda e: e.activation(out=sm_[0:T, 1:2], in_=sm_[0:T, 0:1], func=AF.Sqrt, bias=EPS, scale=1.0 / 96),
                 reads=[sn], writes=[sn])

        def g7(ci):
            tok0, T, nseq, seq0, tk = ctx(ci)
            Pn, nn = P[2 + ci % 3], f"P{2 + ci % 3}"
            sm_, sn = SMALL[ci % 3], f"gSM{ci % 3}"
            S.op("dve", lambda e: e.reciprocal(out=sm_[0:T, 1:2], in_=sm_[0:T, 1:2]), reads=[sn], writes=[sn])
            S.op("dve", lambda e: e.tensor_scalar_mul(out=ON[ci % 2][0:T, :], in0=Pn[0:T, 0:96], scalar1=sm_[0:T, 1:2]),
                 reads=[nn, sn], writes=[f"gON{ci % 2}"])

        def g8(ci):
            tok0, T, nseq, seq0, tk = ctx(ci)
            S.op("pe", lambda e: e.transpose(out=P[7][0:96, 0:T], in_=ON[ci % 2][0:T, :], identity=c.ident[0:T, 0:T]),
                 reads=[f"gON{ci % 2}", "ident"], writes=["P7"])

        def g9(ci):
            tok0, T, nseq, seq0, tk = ctx(ci)
            S.op("act", lambda e: e.copy(out=EO[:, tk], in_=P[7][0:96, 0:T]), reads=["P7"], writes=["EO"])

        plan = [(g9, 8), (g8, 7), (g7, 6), (g6, 5), (g5, 4), (g4, 3), (g3, 2), (g2, 1), (g1, 0)]
        for it in range(NCH + 8):
            for fn, k in plan:
                ci = it - k
                if 0 <= ci < NCH:
                    fn(ci)
        drain(c, 99)
        S.op("dve", lambda e: e.scalar_tensor_tensor(out=YGh[:, 0, :], in0=EO, scalar=GNG[:, h:h + 1], in1=GGs, op0=ALU.mult, op1=ALU.mult),
             reads=["EO", "GGs", "glac"], writes=["YGh"])
        apply_wout(c, l, YGh, "YGh", 640 + 96 * h, 96, 1, defer=True)
        S.dma("sp", O["p_gla_S"][l, h], Sp[32 % 2], reads=[f"Sp{32 % 2}"])
        S.dma("sp", O["s_gla_S"][l][:, h].rearrange("i k v -> k i v"), Ss, reads=["Ss"])

    for h in range(4):
        g_head(h)
    drain(c, 99)
    S.barrier()
    A.release()


_NC_CACHE = {}


def kernel(**inputs):
    import os
    stage = inputs.pop("_stage", int(os.environ.get("KSTAGE", "99")))
    raw = inputs.pop("_raw", False)
    if stage not in _NC_CACHE:
        _NC_CACHE[stage] = build(stage)
    nc = _NC_CACHE[stage]
    f = lambda a: np.ascontiguousarray(np.asarray(a, dtype=np.float32))
    shared = {}
    for nm in ("ffn1_norm_g", "mix_norm_g", "ffn2_norm_g", "final_norm_g", "ffn1_w1", "ffn1_w3", "ffn1_w2",
               "ffn2_w1", "ffn2_w3", "ffn2_w2", "w_in", "w_out", "lru_conv_w", "lru_conv_b", "lru_wa", "lru_ba",
               "lru_wx", "lru_bx", "lru_lambda", "ml_conv_w", "ml_conv_b", "ml_wq", "ml_wk", "ml_wv", "ml_w_if",
               "ml_b_if", "ml_norm_g", "ml_skip", "gla_w_up", "gla_b_up", "gla_norm_g"):
        shared[nm] = f(inputs[nm])
    shared["meta"] = f(inputs["meta_tokens"])
    xp = f(inputs["x_prompt"])
    xs = f(inputs["x_sample"])
    st_names = ("state_lru_h", "state_lru_conv", "state_mlstm_C", "state_mlstm_n", "state_mlstm_m",
                "state_mlstm_conv", "state_gla_S")
    states = {nm: f(inputs[nm]) for nm in st_names}
    in_maps = []
    for ci in range(8):
        m = dict(shared)
        m["xp"] = xp[ci]
        m["xs"] = xs[ci * NSEQ:(ci + 1) * NSEQ].reshape(NSM, D)
        for nm in st_names:
            m[nm] = np.ascontiguousarray(states[nm][:, ci * NSEQ:(ci + 1) * NSEQ])
        in_maps.append(m)
    res = run_bass_kernel_spmd(nc, in_maps, core_ids=list(range(8)))
    R = res.results
    if raw:
        return R
    yp = np.stack([R[ci]["yp"] for ci in range(8)], 0)
    ys = np.concatenate([R[ci]["ys"].reshape(NSEQ, TS, D) for ci in range(8)], 0)
    outs = [yp, ys]
    onames = ("lru_h", "lru_conv", "mlstm_C", "mlstm_n", "mlstm_m", "mlstm_conv", "gla_S")
    for nm in onames:
        outs.append(np.stack([R[ci]["p_" + nm] for ci in range(8)], 1))
    for nm in onames:
        outs.append(np.concatenate([R[ci]["s_" + nm] for ci in range(8)], 1))
    return tuple(outs)
```

```python
import numpy as np
from contextlib import ExitStack
import concourse.bass as bass
import concourse.mybir as mybir
from concourse.bass_utils import run_bass_kernel_spmd

F32 = mybir.dt.float32
BF16 = mybir.dt.bfloat16
AF = mybir.ActivationFunctionType
ALU = mybir.AluOpType
AX = mybir.AxisListType

ENGS = ("pe", "act", "dve", "pool", "sp")

D = 1024
DEPTH = 2
NPR = 2064
NSM = 128
NT = NPR + NSM
NSEQ = 16
TS = 8
DFF = 2816
NF = DFF // 128
EPS = 1e-6
TILES = [(0, 512), (512, 512), (1024, 512), (1536, 512), (2048, 144)]
FGROUPS = [(0, 4), (4, 4), (8, 4), (12, 4), (16, 4), (20, 2)]
DIN = 2448


class Sched:
    def __init__(self, nc, n_dma_sems=10):
        self.nc = nc
        self.prog = {e: [] for e in ENGS}
        self.cnt = {}
        self.sems = {}
        self.seen = {e: {} for e in ENGS}
        self.last_w = {}
        self.readers = {}
        self.n_dma_sems = n_dma_sems
        self.dma_rr = {e: 0 for e in ENGS}
        self.n_ops = 0

    def open(self, stack):
        for e in ENGS:
            self.sems[e] = stack.enter_context(self.nc.semaphore(f"s_{e}"))
            self.cnt[e] = 0
        for q in ("sp", "pool", "act"):
            for i in range(self.n_dma_sems):
                k = f"d_{q}{i}"
                self.sems[k] = stack.enter_context(self.nc.semaphore(k))
                self.cnt[k] = 0

    CHILD = {"P0": ("P0k", "P0s"), "P1": ("P1k", "P1s"), "P6": ("P6v", "P6t"), "P7": ("P7v", "P7t")}

    def _expand(self, names):
        out = []
        for n in names:
            out.append(n)
            out.extend(self.CHILD.get(n, ()))
        return out

    def _deps(self, eng, reads, writes, skip_same_pe=False):
        deps = {}

        def add(ev):
            if ev is None:
                return
            k, v = ev
            if skip_same_pe and k == "pe":
                return
            if deps.get(k, 0) < v:
                deps[k] = v
        for r in reads:
            add(self.last_w.get(r))
            if r[0] == "P" and r[1:2].isdigit():
                for ev in self.readers.get(r, ()):
                    if ev[0] != eng:
                        add(ev)
        for w in writes:
            add(self.last_w.get(w))
            for ev in self.readers.get(w, ()):
                add(ev)
        out = []
        for k, v in deps.items():
            if self.seen[eng].get(k, 0) < v:
                self.seen[eng][k] = v
                out.append((k, v))
        return out

    def _commit(self, ev, reads, writes):
        for w in writes:
            self.last_w[w] = ev
            self.readers[w] = []
        for r in reads:
            if r in writes:
                continue
            self.readers.setdefault(r, []).append(ev)

    def op(self, eng, fn, reads=(), writes=()):
        reads, writes = self._expand(reads), self._expand(writes)
        waits = self._deps(eng, reads, writes, skip_same_pe=(eng == "pe"))
        self.cnt[eng] += 1
        ev = (eng, self.cnt[eng])
        sems = self.sems

        def emit(e, waits=waits, fn=fn, sem=sems[eng]):
            for k, v in waits:
                e.wait_ge(sems[k], v)
            fn(e).then_inc(sem, 1)
        self.prog[eng].append(emit)
        self._commit(ev, reads, writes)
        self.n_ops += 1
        return ev

    def dma(self, q, out, in_, reads=(), writes=(), **kw):
        i = self.dma_rr[q]
        self.dma_rr[q] = (i + 1) % self.n_dma_sems
        k = f"d_{q}{i}"
        reads, writes = self._expand(reads), self._expand(writes)
        waits = self._deps(q, reads, writes)
        prev = self.cnt[k]
        if prev and self.seen[q].get(k, 0) < prev:
            self.seen[q][k] = prev
            waits.append((k, prev))
        self.cnt[k] = prev + 16
        ev = (k, prev + 16)
        sems = self.sems

        def emit(e, waits=waits, sem=sems[k], out=out, in_=in_, kw=kw):
            for kk, v in waits:
                e.wait_ge(sems[kk], v)
            e.dma_start(out=out, in_=in_, **kw).then_inc(sem, 16)
        self.prog[q].append(emit)
        self._commit(ev, reads, writes)
        self.n_ops += 1
        return ev

    def barrier(self):
        final = {k: v for k, v in self.cnt.items() if v > 0}
        sems = self.sems
        for eng in ENGS:
            waits = [(k, v) for k, v in final.items() if self.seen[eng].get(k, 0) < v]
            for k, v in waits:
                self.seen[eng][k] = v

            def emit(e, waits=waits):
                for k, v in waits:
                    e.wait_ge(sems[k], v)
            self.prog[eng].append(emit)
        self.last_w = {}
        self.readers = {}

    def run(self, block):
        prog = self.prog

        @block.sync
        def _(e):
            for f in prog["sp"]:
                f(e)

        @block.tensor
        def _(e):
            for f in prog["pe"]:
                f(e)

        @block.scalar
        def _(e):
            for f in prog["act"]:
                f(e)

        @block.vector
        def _(e):
            for f in prog["dve"]:
                f(e)

        @block.gpsimd
        def _(e):
            for f in prog["pool"]:
                f(e)


class Arena:
    def __init__(self, ap, nwords):
        self.ap = ap
        self.n = nwords
        self.top = 0
        self.marks = []

    def f32(self, nwords, parts=128):
        req = nwords
        nwords = (nwords + 7) // 8 * 8
        assert self.top + nwords <= self.n, f"arena overflow {self.top}+{nwords}>{self.n}"
        v = self.ap[0:parts, self.top:self.top + req]
        self.top += nwords
        return v

    def bf16(self, nelem, parts=128):
        nw = (nelem + 1) // 2
        nw = (nw + 7) // 8 * 8
        assert self.top + nw <= self.n, f"arena overflow {self.top}+{nw}>{self.n}"
        v = self.ap[0:parts, self.top:self.top + nw].bitcast(BF16)
        self.top += nw
        return v[:, 0:nelem]

    def mark(self):
        self.marks.append(self.top)

    def release(self):
        self.top = self.marks.pop()


class Ctx:
    pass


def build(stage=99):
    nc = bass.Bass("TRN2", target_bir_lowering=False)
    c = Ctx()
    c.nc = nc
    import os
    c.debug = bool(os.environ.get("KDEBUG"))
    di = lambda name, shape: nc.dram_tensor(name, list(shape), F32, kind="ExternalInput").ap()
    do = lambda name, shape: nc.dram_tensor(name, list(shape), F32, kind="ExternalOutput").ap()
    I = {}
    I["xp"] = di("xp", [2048, D])
    I["xs"] = di("xs", [NSM, D])
    I["meta"] = di("meta", [16, D])
    for nm in ("ffn1_norm_g", "mix_norm_g", "ffn2_norm_g"):
        I[nm] = di(nm, [DEPTH, D])
    I["final_norm_g"] = di("final_norm_g", [D])
    for nm in ("ffn1_w1", "ffn1_w3", "ffn2_w1", "ffn2_w3"):
        I[nm] = di(nm, [DEPTH, D, DFF])
    for nm in ("ffn1_w2", "ffn2_w2"):
        I[nm] = di(nm, [DEPTH, DFF, D])
    I["w_in"] = di("w_in", [DEPTH, D, DIN])
    I["w_out"] = di("w_out", [DEPTH, D, D])
    for nm, shp in (("lru_conv_w", [4, 256]), ("lru_conv_b", [256]), ("lru_wa", [4, 64, 64]), ("lru_ba", [256]),
                    ("lru_wx", [4, 64, 64]), ("lru_bx", [256]), ("lru_lambda", [256]),
                    ("ml_conv_w", [4, 384]), ("ml_conv_b", [384]), ("ml_wq", [4, 96, 96]), ("ml_wk", [4, 96, 96]),
                    ("ml_wv", [4, 96, 96]), ("ml_w_if", [1152, 8]), ("ml_b_if", [8]), ("ml_norm_g", [384]),
                    ("ml_skip", [384]), ("gla_w_up", [16, 192]), ("gla_b_up", [192]), ("gla_norm_g", [384])):
        I[nm] = di(nm, [DEPTH] + shp)
    for nm, shp in (("state_lru_h", [256]), ("state_lru_conv", [3, 256]), ("state_mlstm_C", [4, 96, 96]),
                    ("state_mlstm_n", [4, 96]), ("state_mlstm_m", [4]), ("state_mlstm_conv", [3, 384]),
                    ("state_gla_S", [4, 48, 96])):
        I[nm] = di(nm, [DEPTH, NSEQ] + shp)
    O = {}
    for nm, shp in (("lru_h", [256]), ("lru_conv", [3, 256]), ("mlstm_C", [4, 96, 96]), ("mlstm_n", [4, 96]),
                    ("mlstm_m", [4]), ("mlstm_conv", [3, 384]), ("gla_S", [4, 48, 96])):
        O["p_" + nm] = do("p_" + nm, [DEPTH] + shp)
        O["s_" + nm] = do("s_" + nm, [DEPTH, NSEQ] + shp)
    O["yp"] = do("yp", [2048, D])
    O["ys"] = do("ys", [NSM, D])
    if stage < 0:
        O["dbg"] = do("dbg", [128, 6, 512])
    c.I, c.O = I, O

    with ExitStack() as st:
        S = Sched(nc)
        S.open(st)
        c.S = S
        NW = 212000 // 4
        arena_t = st.enter_context(nc.sbuf_tensor("arena", [128, NW], F32))
        A = Arena(arena_t[:], NW)
        c.A = A
        c.P = [st.enter_context(nc.psum_tensor(f"P{i}", [128, 512], F32))[:] for i in range(8)]
        block = st.enter_context(nc.Block())

        c.X = A.f32(8 * NT).rearrange("p (k t) -> p k t", k=8)
        c.XN = A.bf16(8 * NT).rearrange("p (k t) -> p k t", k=8)
        c.ident = A.f32(128)
        c.ones_bf = A.bf16(128)
        c.gains = A.f32(7 * 8).rearrange("p (n k) -> p n k", n=7)
        c.MASKC = A.f32(64)
        c.MASKB = A.f32(64)
        c.PM = A.f32(8)
        c.SEL = A.f32(4 * 96)
        c.SMASK = A.f32(128)
        c.SNEG = A.f32(128)
        c.sq = [A.bf16(512), A.bf16(512)]
        c.rstd = [A.f32(512), A.f32(512)]
        setup_consts(c)
        load_x(c)
        if stage < 0:
            c.dbg = A.f32(6 * 512).rearrange("p (n t) -> p n t", n=6)
            S.op("pool", lambda e: e.memset(c.dbg, 0.0), writes=["dbg"])
            ffn(c, 0, "ffn1", 0, dbg=True)
            S.barrier()
            S.dma("sp", O["dbg"], c.dbg, reads=["dbg"])
        for l in range(DEPTH):
            if stage >= 1 and stage not in (31, 32):
                ffn(c, l, "ffn1", 3 * l + 0)
            if stage >= 2:
                mixer(c, l, {31: 11, 32: 12}.get(stage, stage))
            if stage >= 3 and (stage < 10 or stage >= 40):
                ffn(c, l, "ffn2", 3 * l + 2)
            if 10 <= stage < 40:
                break
        store_y(c)
        S.barrier()
        S.run(block)
    return nc


def setup_consts(c):
    S, nc = c.S, c.nc
    S.op("pool", lambda e: e.memset(c.ident, 1.0), writes=["ident"])
    S.op("pool", lambda e: e.affine_select(out=c.ident, in_=c.ident, pattern=[[-1, 128]],
                                           compare_op=ALU.is_equal, fill=0.0, base=0,
                                           channel_multiplier=1), reads=["ident"], writes=["ident"])
    S.op("pool", lambda e: e.memset(c.ones_bf, 1.0), writes=["ones_bf"])
    for t_ in (c.MASKC, c.MASKB, c.PM, c.SEL, c.SMASK):
        S.op("pool", lambda e, t_=t_: e.memset(t_, 1.0), writes=["consts"])
    S.op("pool", lambda e: e.memset(c.SNEG, 0.0), writes=["consts"])
    mc, mb_ = c.MASKC[0:64, :], c.MASKB[0:64, :].rearrange("p (i j) -> p i j", j=8)
    S.op("pool", lambda e: e.affine_select(out=mc, in_=mc, pattern=[[1, 64]], compare_op=ALU.is_ge, fill=0.0,
                                           base=0, channel_multiplier=-1), reads=["consts"], writes=["consts"])
    S.op("pool", lambda e: e.affine_select(out=mb_, in_=mb_, pattern=[[8, 8], [1, 8]], compare_op=ALU.is_ge,
                                           fill=0.0, base=0, channel_multiplier=-1), reads=["consts"], writes=["consts"])
    S.op("pool", lambda e: e.affine_select(out=mb_, in_=mb_, pattern=[[-8, 8], [0, 8]], compare_op=ALU.is_ge,
                                           fill=0.0, base=0, channel_multiplier=1), reads=["consts"], writes=["consts"])
    pm = c.PM[0:64, :]
    S.op("pool", lambda e: e.affine_select(out=pm, in_=pm, pattern=[[-8, 8]], compare_op=ALU.is_ge, fill=0.0,
                                           base=0, channel_multiplier=1), reads=["consts"], writes=["consts"])
    S.op("pool", lambda e: e.affine_select(out=pm, in_=pm, pattern=[[8, 8]], compare_op=ALU.is_ge, fill=0.0,
                                           base=7, channel_multiplier=-1), reads=["consts"], writes=["consts"])
    sel = c.SEL[0:4, :].rearrange("p (h m) -> p h m", h=4)
    S.op("pool", lambda e: e.affine_select(out=sel, in_=sel, pattern=[[1, 4], [0, 96]], compare_op=ALU.is_equal,
                                           fill=0.0, base=0, channel_multiplier=-1), reads=["consts"], writes=["consts"])
    S.op("pool", lambda e: e.memset(c.SMASK.rearrange("p (i j) -> p i j", j=8)[:, :, 0:1], 0.0),
         reads=["consts"], writes=["consts"])
    S.op("pool", lambda e: e.memset(c.SNEG.rearrange("p (i j) -> p i j", j=8)[:, :, 0:1], -1e30),
         reads=["consts"], writes=["consts"])
    names = ["ffn1_norm_g", "mix_norm_g", "ffn2_norm_g"]
    for l in range(DEPTH):
        for j, nm in enumerate(names):
            S.dma("sp", c.gains[:, 3 * l + j, :], c.I[nm][l].rearrange("(k p) -> p k", p=128),
                  writes=["gains"], allow_slow_non_contiguous=True)
    S.dma("sp", c.gains[:, 6, :], c.I["final_norm_g"].rearrange("(k p) -> p k", p=128),
          writes=["gains"], allow_slow_non_contiguous=True)


def load_x(c):
    S, A = c.S, c.A
    A.mark()
    stg = [A.f32(D), A.f32(D)]
    blocks = [("meta", 0, 16, 0)] + [("xp", i * 128, 128, 16 + i * 128) for i in range(16)] + [("xs", 0, 128, NPR)]
    for bi, (src, r0, n, t0) in enumerate(blocks):
        sg = stg[bi % 2]
        S.dma("sp", sg[0:n, :], c.I[src][r0:r0 + n, :], writes=[f"stg{bi % 2}"])
        for half in range(2):
            pb = c.P[6 + half]
            for kk in range(4):
                k = half * 4 + kk
                S.op("pe", lambda e, pb=pb, kk=kk, k=k, sg=sg, n=n: e.transpose(
                    out=pb[:, kk * 128:kk * 128 + n], in_=sg[0:n, k * 128:(k + 1) * 128], identity=c.ident[0:n, 0:n]),
                    reads=[f"stg{bi % 2}", "ident"], writes=[f"P{6 + half}"])
            eng = "dve" if half == 0 else "act"
            src_ap = pb.rearrange("p (k t) -> p k t", k=4)[:, :, 0:n]
            dst_ap = c.X[:, half * 4:half * 4 + 4, t0:t0 + n]
            if eng == "dve":
                S.op("dve", lambda e, s=src_ap, d=dst_ap: e.tensor_copy(out=d, in_=s),
                     reads=[f"P{6 + half}"], writes=[f"Xb{bi}"])
            else:
                S.op("act", lambda e, s=src_ap, d=dst_ap: e.copy(out=d, in_=s),
                     reads=[f"P{6 + half}"], writes=[f"Xb{bi}"])
    S.barrier()
    A.release()


def xres(m, ti):
    return f"X_{m}_{ti}"


def rmsnorm_to_xn(c, gi, tiles=None):
    S, A = c.S, c.A
    sq, rstd = c.sq, c.rstd
    n = 0
    for ti, (t0, tn) in enumerate(TILES):
        pb = c.P[6 + ti % 2]
        pbn = f"P{6 + ti % 2}"
        for k in range(8):
            b = n % 2
            n += 1
            S.op("act", lambda e, b=b, k=k, t0=t0, tn=tn: e.activation(
                out=sq[b][:, 0:tn], in_=c.X[:, k, t0:t0 + tn], func=AF.Square),
                reads=[xres(k, ti)], writes=[f"sq{b}"])
            S.op("pe", lambda e, b=b, k=k, tn=tn, pb=pb: e.matmul(
                pb[:, 0:tn], lhsT=c.ones_bf, rhs=sq[b][:, 0:tn], start=(k == 0), stop=(k == 7)),
                reads=[f"sq{b}", "ones_bf"], writes=[pbn])
        rb = rstd[ti % 2]
        rbn = f"rstd{ti % 2}"
        S.op("act", lambda e, rb=rb, pb=pb, tn=tn: e.activation(
            out=rb[:, 0:tn], in_=pb[:, 0:tn], func=AF.Sqrt, bias=EPS, scale=1.0 / D),
            reads=[pbn], writes=[rbn])
        S.op("dve", lambda e, rb=rb, tn=tn: e.reciprocal(out=rb[:, 0:tn], in_=rb[:, 0:tn]),
             reads=[rbn], writes=[rbn])
        for k in range(8):
            eng = "dve"
            S.op(eng, lambda e, rb=rb, k=k, t0=t0, tn=tn: e.scalar_tensor_tensor(
                out=c.XN[:, k, t0:t0 + tn], in0=c.X[:, k, t0:t0 + tn], scalar=c.gains[:, gi, k:k + 1],
                in1=rb[:, 0:tn], op0=ALU.mult, op1=ALU.mult),
                reads=[xres(k, ti), rbn, "gains"], writes=[f"XN_{ti}"])


def ffn(c, l, which, gi, dbg=False):
    S, A = c.S, c.A
    rmsnorm_to_xn(c, gi)
    A.mark()
    w1d = c.I[f"{which}_w1"][l].rearrange("(k p) f -> p k f", p=128)
    w3d = c.I[f"{which}_w3"][l].rearrange("(k p) f -> p k f", p=128)
    w2d = c.I[f"{which}_w2"][l].rearrange("(f p) m -> p f m", p=128)
    W1 = [A.bf16(8 * 512).rearrange("p (k f) -> p k f", k=8) for _ in range(2)]
    W3 = [A.bf16(8 * 512).rearrange("p (k f) -> p k f", k=8) for _ in range(2)]
    W2 = [A.bf16(4 * 1024).rearrange("p (f m) -> p f m", f=4) for _ in range(2)]
    G = [A.bf16(4 * 512).rearrange("p (f t) -> p f t", f=4) for _ in range(2)]
    SL = [A.f32(512) for _ in range(2)]

    def load(gidx):
        f0, F = FGROUPS[gidx]
        b = gidx % 2
        S.dma("pool", W1[b][:, :, 0:F * 128], w1d[:, :, f0 * 128:(f0 + F) * 128], writes=[f"W1_{b}"])
        S.dma("pool", W3[b][:, :, 0:F * 128], w3d[:, :, f0 * 128:(f0 + F) * 128], writes=[f"W3_{b}"])
        S.dma("pool", W2[b][:, 0:F, :], w2d[:, f0:f0 + F, :], writes=[f"W2_{b}"])

    load(0)
    if dbg:
        S.op("dve", lambda e: e.tensor_copy(out=c.dbg[:, 0, :], in_=c.XN[:, 0, 0:512]), reads=["XN_0"], writes=["dbg"])
        S.op("dve", lambda e: e.tensor_copy(out=c.dbg[:, 1, :], in_=W1[0][:, 0, :]), reads=["W1_0"], writes=["dbg"])
        S.op("dve", lambda e: e.tensor_copy(out=c.dbg[:, 2, :], in_=W2[0][:, 0, 0:512]), reads=["W2_0"], writes=["dbg"])
    it = 0
    pendingB = None
    nsl = 0
    for gidx, (f0, F) in enumerate(FGROUPS):
        b = gidx % 2
        for ti, (t0, tn) in enumerate(TILES):
            gb = it % 2
            for fi in range(F):
                hb = (it * 4 + fi) % 2
                p1, p3 = c.P[hb], c.P[2 + hb]
                for k in range(8):
                    S.op("pe", lambda e, p1=p1, k=k, fi=fi, b=b, t0=t0, tn=tn: e.matmul(
                        p1[:, 0:tn], lhsT=W1[b][:, k, fi * 128:(fi + 1) * 128], rhs=c.XN[:, k, t0:t0 + tn],
                        start=(k == 0), stop=(k == 7)),
                        reads=[f"W1_{b}", f"XN_{ti}"], writes=[f"P{hb}"])
                for k in range(8):
                    S.op("pe", lambda e, p3=p3, k=k, fi=fi, b=b, t0=t0, tn=tn: e.matmul(
                        p3[:, 0:tn], lhsT=W3[b][:, k, fi * 128:(fi + 1) * 128], rhs=c.XN[:, k, t0:t0 + tn],
                        start=(k == 0), stop=(k == 7)),
                        reads=[f"W3_{b}", f"XN_{ti}"], writes=[f"P{2 + hb}"])
                sb_ = nsl % 2
                nsl += 1
                S.op("act", lambda e, p1=p1, sb_=sb_, tn=tn: e.activation(
                    out=SL[sb_][:, 0:tn], in_=p1[:, 0:tn], func=AF.Silu),
                    reads=[f"P{hb}"], writes=[f"SL{sb_}"])
                S.op("dve", lambda e, p3=p3, sb_=sb_, gb=gb, fi=fi, tn=tn: e.tensor_tensor(
                    out=G[gb][:, fi, 0:tn], in0=p3[:, 0:tn], in1=SL[sb_][:, 0:tn], op=ALU.mult),
                    reads=[f"P{2 + hb}", f"SL{sb_}"], writes=[f"G{gb}_{fi}"])
                if dbg and it == 0 and fi == 0:
                    S.op("dve", lambda e, sb_=sb_: e.tensor_copy(out=c.dbg[:, 3, :], in_=SL[sb_]), reads=[f"SL{sb_}"], writes=["dbg"])
                    S.op("dve", lambda e, gb=gb: e.tensor_copy(out=c.dbg[:, 4, :], in_=G[gb][:, 0, :]), reads=[f"G{gb}_0"], writes=["dbg"])
                    S.op("dve", lambda e, p3=p3: e.tensor_copy(out=c.dbg[:, 5, :], in_=p3), reads=[f"P{2 + hb}"], writes=["dbg"])
            if pendingB is not None:
                pendingB()
            if ti == 0 and gidx + 1 < len(FGROUPS):
                load(gidx + 1)

            def phaseB(gb=gb, b=b, F=F, ti=ti, t0=t0, tn=tn, it=it):
                for m in range(8):
                    yb = m % 2
                    py = c.P[4 + yb]
                    for fi in range(F):
                        S.op("pe", lambda e, py=py, fi=fi, m=m: e.matmul(
                            py[:, 0:tn], lhsT=W2[b][:, fi, m * 128:(m + 1) * 128], rhs=G[gb][:, fi, 0:tn],
                            start=(fi == 0), stop=(fi == F - 1)),
                            reads=[f"W2_{b}", f"G{gb}_{fi}"], writes=[f"P{4 + yb}"])
                    S.op("dve", lambda e, py=py, m=m: e.scalar_tensor_tensor(
                        out=c.X[:, m, t0:t0 + tn], in0=py[:, 0:tn], scalar=0.5, in1=c.X[:, m, t0:t0 + tn],
                        op0=ALU.mult, op1=ALU.add),
                        reads=[f"P{4 + yb}", xres(m, ti)], writes=[xres(m, ti)])
            pendingB = phaseB
            it += 1
    pendingB()
    S.barrier()
    A.release()


def store_y(c):
    S, A = c.S, c.A
    A.mark()
    sq, rstd = c.sq, c.rstd
    YF = A.f32(8 * 512).rearrange("p (k t) -> p k t", k=8)
    ostg = [A.f32(D), A.f32(D)]
    nsq = 0
    nob = 0
    for ti, (t0, tn) in enumerate(TILES):
        pb = c.P[6 + ti % 2]
        pbn = f"P{6 + ti % 2}"
        for k in range(8):
            b = nsq % 2
            nsq += 1
            S.op("act", lambda e, b=b, k=k, t0=t0, tn=tn: e.activation(
                out=sq[b][:, 0:tn], in_=c.X[:, k, t0:t0 + tn], func=AF.Square),
                reads=[xres(k, ti)], writes=[f"sq{b}"])
            S.op("pe", lambda e, b=b, k=k, tn=tn, pb=pb: e.matmul(
                pb[:, 0:tn], lhsT=c.ones_bf, rhs=sq[b][:, 0:tn], start=(k == 0), stop=(k == 7)),
                reads=[f"sq{b}", "ones_bf"], writes=[pbn])
        rb = rstd[ti % 2]
        rbn = f"rstd{ti % 2}"
        S.op("act", lambda e, rb=rb, pb=pb, tn=tn: e.activation(
            out=rb[:, 0:tn], in_=pb[:, 0:tn], func=AF.Sqrt, bias=EPS, scale=1.0 / D),
            reads=[pbn], writes=[rbn])
        S.op("dve", lambda e, rb=rb, tn=tn: e.reciprocal(out=rb[:, 0:tn], in_=rb[:, 0:tn]),
             reads=[rbn], writes=[rbn])
        for k in range(8):
            eng = "dve"
            S.op(eng, lambda e, rb=rb, k=k, t0=t0, tn=tn: e.scalar_tensor_tensor(
                out=YF[:, k, 0:tn], in0=c.X[:, k, t0:t0 + tn], scalar=c.gains[:, 6, k:k + 1],
                in1=rb[:, 0:tn], op0=ALU.mult, op1=ALU.mult),
                reads=[xres(k, ti), rbn, "gains"], writes=["YF"])
        blks = []
        if ti < 4:
            for j in range(4):
                tok = t0 + j * 128
                lo = max(tok, 16)
                blks.append((lo - t0, tok + 128 - lo, c.O["yp"], lo - 16))
        else:
            blks.append((0, 16, c.O["yp"], 2032))
            blks.append((16, 128, c.O["ys"], 0))
        for (o0, n, dst, r0) in blks:
            ob = nob % 2
            nob += 1
            for half in range(2):
                pb2 = c.P[half]
                for kk in range(4):
                    k = half * 4 + kk
                    S.op("pe", lambda e, pb2=pb2, kk=kk, k=k, o0=o0, n=n: e.transpose(
                        out=pb2[0:n, kk * 128:(kk + 1) * 128], in_=YF[:, k, o0:o0 + n], identity=c.ident),
                        reads=["YF", "ident"], writes=[f"P{half}"])
                if half == 0:
                    S.op("dve", lambda e, pb2=pb2, ob=ob, n=n: e.tensor_copy(
                        out=ostg[ob][0:n, 0:512], in_=pb2[0:n, :]), reads=["P0"], writes=[f"ostg{ob}"])
                else:
                    S.op("act", lambda e, pb2=pb2, ob=ob, n=n: e.copy(
                        out=ostg[ob][0:n, 512:1024], in_=pb2[0:n, :]), reads=["P1"], writes=[f"ostg{ob}"])
            S.dma("sp", dst[r0:r0 + n, :], ostg[ob][0:n, :], reads=[f"ostg{ob}"])
    A.release()


NEX = 2246
NE = 2243
SMP0 = 2067
ETILES = [(0, 512), (512, 512), (1024, 512), (1536, 512), (2048, 195)]


def ext_in_views(U):
    return U[:, 3:3 + NPR], U[:, SMP0 + 3:SMP0 + 3 + 176].rearrange("p (i e) -> p i e", e=11)[:, :, 0:8]


def ext_out_views(V):
    return V[:, 0:NPR], V[:, SMP0:SMP0 + 176].rearrange("p (i e) -> p i e", e=11)[:, :, 0:8]


def dst_views(arr, layout, t0, tn):
    if layout == "norm":
        return [(0, tn, arr[:, t0:t0 + tn], False)]
    pr, sm = ext_in_views(arr) if layout == "ext_in" else ext_out_views(arr)
    if t0 + tn <= NPR:
        return [(0, tn, pr[:, t0:t0 + tn], False)]
    return [(0, NPR - t0, pr[:, t0:NPR], False), (NPR - t0, tn, sm, True)]


def pview(ps, c0, c1, strided):
    v = ps[:, c0:c1]
    return v.rearrange("p (i j) -> p i j", j=8) if strided else v


def proj_fm(c, W, wres, col0, M, evac):
    S = c.S
    for ti, (t0, tn) in enumerate(TILES):
        pi = c.pp % 2
        c.pp += 1
        ps = c.P[pi]
        for k in range(8):
            S.op("pe", lambda e, ps=ps, k=k, t0=t0, tn=tn: e.matmul(
                ps[0:M, 0:tn], lhsT=W[:, k, col0:col0 + M], rhs=c.XN[:, k, t0:t0 + tn],
                start=(k == 0), stop=(k == 7)), reads=[wres, f"XN_{ti}"], writes=[f"P{pi}"])
        evac(ti, t0, tn, ps[0:M, 0:tn], f"P{pi}")
        drain(c, 1)


def load_T(c, dram_ap, n, F, blocks):
    S = c.S
    S.dma("sp", c.stgT[0:n, 0:F], dram_ap, writes=["stgT"])
    for (col0, nb, dst, dres, j) in blocks:
        S.op("pe", lambda e, col0=col0, nb=nb: e.transpose(
            out=c.P[7][0:nb, 0:n], in_=c.stgT[0:n, col0:col0 + nb], identity=c.ident[0:n, 0:n]),
            reads=["stgT", "ident"], writes=["P7"])
        src = c.P[7][0:nb, 0:n]
        if j:
            src = src.rearrange("p (i j) -> p i j", j=j)
        S.op("act", lambda e, dst=dst, src=src: e.copy(out=dst, in_=src), reads=["P7"], writes=[dres])


def store_T(c, src_ap, sres, P_, n, dram_ap):
    S = c.S
    S.op("pe", lambda e: e.transpose(out=c.P[7][0:n, 0:P_], in_=src_ap, identity=c.ident[0:P_, 0:P_]),
         reads=[sres, "ident"], writes=["P7"])
    S.op("act", lambda e: e.copy(out=c.stgO[0:n, 0:P_], in_=c.P[7][0:n, 0:P_]), reads=["P7"], writes=["stgO"])
    S.dma("sp", dram_ap, c.stgO[0:n, 0:P_], reads=["stgO"])


def colvec(c, dst, dram_1d, pattern, res, **kw):
    c.S.dma("sp", dst, dram_1d.rearrange(pattern, **kw), writes=[res], allow_slow_non_contiguous=True)


def dbg(c, name, ap, reads):
    if not getattr(c, "debug", False):
        return
    d = c.nc.dram_tensor("dbg_" + name, list(ap.shape), F32, kind="ExternalOutput").ap()
    c.S.dma("sp", d, ap, reads=reads)


class CutHere(Exception):
    pass


def cut(c, n):
    import os
    if int(os.environ.get("KCUT", "0")) == n:
        raise CutHere()


def mixer(c, l, stage=99):
    S, A = c.S, c.A
    rmsnorm_to_xn(c, 3 * l + 1)
    A.mark()
    saved = (A.top, list(A.marks))
    c.pp = 0
    c.pending = []
    c.stgT = A.f32(512)
    c.stgO = A.f32(128)
    c.WO = A.bf16(2 * 1024).rearrange("p (k m) -> p k m", k=2)
    try:
        lru_group(c, l)
        if stage >= 11:
            mlstm_group(c, l)
        if stage >= 12:
            gla_group(c, l)
    except CutHere:
        A.top, A.marks = saved[0], saved[1]
    drain(c, 99)
    S.barrier()
    A.release()


def drain(c, n=1):
    while n > 0 and c.pending:
        c.pending.pop(0)()
        n -= 1


def apply_wout(c, l, Y, yres, row0, kp, nk, defer=False):
    S = c.S
    wod = c.I["w_out"][l]
    drain(c, 99)
    if nk == 1:
        c.wo_slot = (getattr(c, "wo_slot", 0) + 1) % 2
        slots = [c.wo_slot]
    else:
        slots = [0, 1]
    for j in range(nk):
        S.dma("pool", c.WO[0:kp, slots[j], :], wod[row0 + kp * j:row0 + kp * (j + 1), :], writes=[f"WO{slots[j]}"])

    def piece(ti, t0, tn):
        for m in range(8):
            yb = m % 2
            py = c.P[4 + yb]
            pn = f"P{4 + yb}"
            for j in range(nk):
                S.op("pe", lambda e, py=py, j=j, m=m: e.matmul(
                    py[:, 0:tn], lhsT=c.WO[0:kp, slots[j], m * 128:(m + 1) * 128], rhs=Y[:, j, t0:t0 + tn],
                    start=(j == 0), stop=(j == nk - 1)), reads=[f"WO{slots[j]}", yres], writes=[pn])
            S.op("dve", lambda e, py=py, m=m: e.tensor_tensor(
                out=c.X[:, m, t0:t0 + tn], in0=py[:, 0:tn], in1=c.X[:, m, t0:t0 + tn], op=ALU.add),
                reads=[pn, xres(m, ti)], writes=[xres(m, ti)])
    for ti, (t0, tn) in enumerate(TILES):
        if defer:
            c.pending.append(lambda ti=ti, t0=t0, tn=tn: piece(ti, t0, tn))
        else:
            piece(ti, t0, tn)


def lru_group(c, l):
    S, A, I, O = c.S, c.A, c.I, c.O
    A.mark()
    WIN = A.bf16(8 * 512).rearrange("p (k f) -> p k f", k=8)
    S.dma("pool", WIN, I["w_in"][l].rearrange("(k p) f -> p k f", p=128)[:, :, 0:512], writes=["WINlru"])
    YR = A.bf16(2 * NT).rearrange("p (k t) -> p k t", k=2)
    CW = A.f32(8).rearrange("p (c j) -> p c j", c=2)
    CB, BA, BX, LAM, CNEG = A.f32(8), A.f32(8), A.f32(8), A.f32(8), A.f32(8)
    WA = A.f32(256).rearrange("p (c m) -> p c m", c=2)
    WX = A.f32(256).rearrange("p (c m) -> p c m", c=2)
    H0 = A.f32(16)
    HL = A.f32(16)
    CS = A.f32(48)
    UE, GR, XR, AC, IG, T1 = A.f32(NEX), A.f32(NT), A.f32(NE), A.f32(NE), A.f32(NE), A.f32(NE)
    for cc in range(2):
        colvec(c, CW[:, cc, :], I["lru_conv_w"][l][:, cc * 128:(cc + 1) * 128], "j p -> p j", "lruc")
    colvec(c, CB[:, 0:2], I["lru_conv_b"][l], "(c p) -> p c", "lruc", p=128)
    colvec(c, BA[:, 0:2], I["lru_ba"][l], "(c p) -> p c", "lruc", p=128)
    colvec(c, BX[:, 0:2], I["lru_bx"][l], "(c p) -> p c", "lruc", p=128)
    colvec(c, LAM[:, 0:2], I["lru_lambda"][l], "(c p) -> p c", "lruc", p=128)
    S.op("pool", lambda e: e.memset(WA, 0.0), writes=["lruW"])
    S.op("pool", lambda e: e.memset(WX, 0.0), writes=["lruW"])
    for cc in range(2):
        for bb in range(2):
            n = 2 * cc + bb
            S.dma("sp", WA[64 * bb:64 * bb + 64, cc, 64 * bb:64 * bb + 64], I["lru_wa"][l, n], writes=["lruW"])
            S.dma("sp", WX[64 * bb:64 * bb + 64, cc, 64 * bb:64 * bb + 64], I["lru_wx"][l, n], writes=["lruW"])
    S.op("act", lambda e: e.activation(out=CNEG[:, 0:2], in_=LAM[:, 0:2], func=AF.Exp, scale=-1.0),
         reads=["lruc"], writes=["cneg"])
    S.op("act", lambda e: e.activation(out=CNEG[:, 0:2], in_=CNEG[:, 0:2], func=AF.Ln, bias=1.0),
         reads=["cneg"], writes=["cneg"])
    S.op("dve", lambda e: e.tensor_scalar_mul(out=CNEG[:, 0:2], in0=CNEG[:, 0:2], scalar1=-8.0),
         reads=["cneg"], writes=["cneg"])
    hist = UE[:, SMP0:SMP0 + 176].rearrange("p (i e) -> p i e", e=11)[:, :, 0:3]
    for cc in range(2):
        S.op("pool", lambda e: e.memset(UE, 0.0), writes=["UE"])
        load_T(c, I["state_lru_conv"][l].rearrange("i j c -> (i j) c"), 48, 256,
               [(cc * 128, 128, hist, "UE", 3)])
        load_T(c, I["state_lru_h"][l], 16, 256, [(cc * 128, 128, H0[:, 0:16], "H0", 0)])

        def ev_u(ti, t0, tn, ps, pres):
            for (c0, c1, dst, st_) in dst_views(UE, "ext_in", t0, tn):
                S.op("act", lambda e, s=pview(ps, c0, c1, st_), d=dst: e.copy(out=d, in_=s),
                     reads=[pres], writes=["UE"])
        proj_fm(c, WIN, "WINlru", cc * 128, 128, ev_u)

        def ev_g(ti, t0, tn, ps, pres):
            S.op("act", lambda e, ps=ps, t0=t0, tn=tn: e.activation(out=GR[:, t0:t0 + tn], in_=ps,
                                                                     func=AF.Gelu_apprx_tanh),
                 reads=[pres], writes=["GR"])
        proj_fm(c, WIN, "WINlru", 256 + cc * 128, 128, ev_g)
        S.op("dve", lambda e, cc=cc: e.tensor_scalar(out=XR, in0=UE[:, 0:NE], scalar1=CW[:, cc, 0:1],
                                                     scalar2=CB[:, cc:cc + 1], op0=ALU.mult, op1=ALU.add),
             reads=["UE", "lruc"], writes=["XR"])
        for j in range(1, 4):
            S.op("dve", lambda e, cc=cc, j=j: e.scalar_tensor_tensor(
                out=XR, in0=UE[:, j:NE + j], scalar=CW[:, cc, j:j + 1], in1=XR, op0=ALU.mult, op1=ALU.add),
                reads=["UE", "XR", "lruc"], writes=["XR"])
        for (t0, tn) in ETILES:
            for (Wm, bias, dst, dres) in ((WA, BA, AC, "AC"), (WX, BX, IG, "IG")):
                pi = c.pp % 2
                c.pp += 1
                ps = c.P[pi]
                S.op("pe", lambda e, ps=ps, Wm=Wm, cc=cc, t0=t0, tn=tn: e.matmul(
                    ps[:, 0:tn], lhsT=Wm[:, cc, :], rhs=XR[:, t0:t0 + tn], start=True, stop=True),
                    reads=["lruW", "XR"], writes=[f"P{pi}"])
                S.op("act", lambda e, ps=ps, bias=bias, dst=dst, cc=cc, t0=t0, tn=tn: e.activation(
                    out=dst[:, t0:t0 + tn], in_=ps[:, 0:tn], func=AF.Sigmoid, bias=bias[:, cc:cc + 1]),
                    reads=[f"P{pi}", "lruc"], writes=[dres])
        S.op("act", lambda e, cc=cc: e.activation(out=AC, in_=AC, func=AF.Exp, scale=CNEG[:, cc:cc + 1]),
             reads=["AC", "cneg"], writes=["AC"])
        S.op("pool", lambda e: e.tensor_tensor(out=T1, in0=IG, in1=XR, op=ALU.mult),
             reads=["IG", "XR"], writes=["T1"])
        S.op("dve", lambda e: e.tensor_tensor(out=IG, in0=AC, in1=AC, op=ALU.mult),
             reads=["AC", "T1"], writes=["IG"])
        S.op("dve", lambda e: e.tensor_scalar(out=IG, in0=IG, scalar1=-1.0, scalar2=1.0, op0=ALU.mult, op1=ALU.add),
             reads=["IG"], writes=["IG"])
        S.op("act", lambda e: e.activation(out=IG, in_=IG, func=AF.Sqrt), reads=["IG"], writes=["IG"])
        S.op("dve", lambda e: e.tensor_tensor(out=IG, in0=IG, in1=T1, op=ALU.mult),
             reads=["IG", "T1"], writes=["IG"])
        fix_a = AC[:, SMP0 - 1:SMP0 - 1 + 176].rearrange("p (i e) -> p i e", e=11)[:, :, 0]
        fix_b = IG[:, SMP0 - 1:SMP0 - 1 + 176].rearrange("p (i e) -> p i e", e=11)[:, :, 0]
        S.op("dve", lambda e, fa=fix_a: e.memset(fa, 0.0), reads=["AC"], writes=["AC"])
        S.op("dve", lambda e, fb=fix_b: e.tensor_copy(out=fb, in_=H0[:, 0:16]), reads=["IG", "H0"], writes=["IG"])
        S.op("dve", lambda e: e.tensor_tensor_scan(out=T1, data0=AC, data1=IG, initial=0.0,
                                                   op0=ALU.mult, op1=ALU.add),
             reads=["AC", "IG"], writes=["T1"])
        hp, hs = ext_out_views(T1)
        S.op("dve", lambda e, cc=cc, hp=hp: e.tensor_tensor(out=YR[:, cc, 0:NPR], in0=GR[:, 0:NPR], in1=hp, op=ALU.mult),
             reads=["GR", "T1"], writes=["YR"])
        S.op("dve", lambda e, cc=cc, hs=hs: e.tensor_tensor(
            out=YR[:, cc, NPR:NT].rearrange("p (i j) -> p i j", j=8),
            in0=GR[:, NPR:NT].rearrange("p (i j) -> p i j", j=8), in1=hs, op=ALU.mult),
            reads=["GR", "T1"], writes=["YR"])
        S.dma("sp", O["p_lru_h"][l, cc * 128:(cc + 1) * 128].rearrange("(p o) -> p o", o=1), T1[:, NPR - 1:NPR],
              reads=["T1"])
        S.op("pool", lambda e, hs=hs: e.tensor_copy(out=HL[:, 0:16], in_=hs[:, :, 7]), reads=["T1"], writes=["HL"])
        store_T(c, HL[:, 0:16], "HL", 128, 16, O["s_lru_h"][l][:, cc * 128:(cc + 1) * 128])
        up, us = ext_in_views(UE)
        S.dma("sp", O["p_lru_conv"][l][:, cc * 128:(cc + 1) * 128].rearrange("j p -> p j"), up[:, NPR - 3:NPR],
              reads=["UE"], allow_slow_non_contiguous=True)
        S.op("pool", lambda e, us=us: e.tensor_copy(out=CS[:, 0:48].rearrange("p (i j) -> p i j", j=3),
                                                    in_=us[:, :, 5:8]), reads=["UE"], writes=["CS"])
        store_T(c, CS[:, 0:48], "CS", 128, 48,
                O["s_lru_conv"][l].rearrange("i j c -> (i j) c")[:, cc * 128:(cc + 1) * 128])
    apply_wout(c, l, YR, "YR", 0, 128, 2)
    S.barrier()
    A.release()


CHUNKS = [(0, 16, 1, 0)] + [(16 + 64 * j, 64, 1, 0) for j in range(32)] + [(NPR, 64, 8, 0), (NPR + 64, 64, 8, 8)]
NCH = len(CHUNKS)
NGAM = 33 + NSEQ
QSCALE_M = 96.0 ** -0.5


def diag_view(ap, nblk, blk, rowlen):
    pstep, pn = ap.ap[0]
    return bass.AP(ap.tensor, ap.offset, [[pstep, pn], [rowlen + blk, nblk], [1, blk]])


def small_mm_tiles(c, Wm, wres, src, sres, M, evac):
    S = c.S
    for ti, (t0, tn) in enumerate(TILES):
        pi = c.pp % 2
        c.pp += 1
        ps = c.P[pi]
        S.op("pe", lambda e, ps=ps, t0=t0, tn=tn: e.matmul(ps[0:M, 0:tn], lhsT=Wm, rhs=src[:, t0:t0 + tn],
                                                          start=True, stop=True),
             reads=[wres, sres], writes=[f"P{pi}"])
        evac(ti, t0, tn, ps[0:M, 0:tn], f"P{pi}")


def mlstm_group(c, l):
    S, A, I, O = c.S, c.A, c.I, c.O
    A.mark()
    P = c.P
    w_in_l = I["w_in"][l].rearrange("(k p) f -> p k f", p=128)
    WQ = A.bf16(384).rearrange("p (h e) -> p h e", h=4)[0:96]
    WK = A.bf16(384).rearrange("p (h e) -> p h e", h=4)[0:96]
    WV = A.bf16(384).rearrange("p (h e) -> p h e", h=4)[0:96]
    WIF = A.bf16(96).rearrange("p (x g) -> p x g", g=8)[0:96]
    MCW = A.f32(16).rearrange("p (h j) -> p h j", h=4)[0:96]
    MCB, NG, SK = A.f32(8)[0:96], A.f32(8)[0:96], A.f32(8)[0:96]
    BI, NBF = A.f32(8)[0:4], A.f32(8)[0:4]
    M0T = A.f32(16)[0:4]
    AE = A.f32(56)[0:4]
    MNEW = A.f32(24)[0:4]
    GT = A.f32(NCH * 16).rearrange("p (c g) -> p c g", g=16)[0:64]
    GAM = A.f32(4 * NGAM).rearrange("p (h g) -> p h g", h=4)[0:96]
    UMbA = [A.bf16(NT)[0:96] for _ in range(4)]
    CMbA = [A.bf16(NT)[0:96] for _ in range(4)]
    CSs = A.f32(48)[0:96]
    A.mark()
    G8 = A.f32(NT)[0:8]
    for W_, nm in ((WQ, "ml_wq"), (WK, "ml_wk"), (WV, "ml_wv")):
        S.dma("pool", W_, I[nm][l].rearrange("h d e -> d h e"), writes=["mlW"])
    S.dma("pool", WIF, I["ml_w_if"][l].rearrange("(x d) g -> d x g", d=96), writes=["mlW"])
    for h in range(4):
        colvec(c, MCW[:, h, :], I["ml_conv_w"][l][:, h * 96:(h + 1) * 96], "j p -> p j", "mlc")
    colvec(c, MCB[:, 0:4], I["ml_conv_b"][l], "(h p) -> p h", "mlc", p=96)
    colvec(c, NG[:, 0:4], I["ml_norm_g"][l], "(h p) -> p h", "mlc", p=96)
    colvec(c, SK[:, 0:4], I["ml_skip"][l], "(h p) -> p h", "mlc", p=96)
    colvec(c, BI[:, 0:1], I["ml_b_if"][l][0:4], "(g o) -> g o", "mlc", o=1)
    colvec(c, NBF[:, 0:1], I["ml_b_if"][l][4:8], "(g o) -> g o", "mlc", o=1)
    colvec(c, M0T[:, 0:16], I["state_mlstm_m"][l], "i h -> h i", "mlc")
    S.op("dve", lambda e: e.tensor_scalar_mul(out=NBF[:, 0:1], in0=NBF[:, 0:1], scalar1=-1.0),
         reads=["mlc"], writes=["mlc"])

    cut(c, 10)

    def head_feats(h, B, need_v, part="both"):
        UMb_h, CMb_h = B.UMb, B.CMb
        if not need_v:
            for (Wm, dst, dres) in ((WQ[:, h, :], B.MQ, "MQ"), (WK[:, h, :], B.MK, "MK")):
                def ev2(ti, t0, tn, ps, pres, dst=dst, dres=dres):
                    if ti % 2 == 0:
                        S.op("act", lambda e: e.copy(out=dst[:, t0:t0 + tn], in_=ps), reads=[pres], writes=[dres])
                    else:
                        S.op("dve", lambda e: e.tensor_copy(out=dst[:, t0:t0 + tn], in_=ps), reads=[pres], writes=[dres])
                small_mm_tiles(c, Wm, "mlW", CMb_h, f"CMb{h}", 96, ev2)
            return
        if part == "ii":
            return head_feats_ii(h, B, UMb_h, CMb_h)
        S.dma("pool", B.WINh, w_in_l[:, :, 512 + 96 * h:512 + 96 * (h + 1)], writes=["WINh"])
        S.op("pool", lambda e: e.memset(B.UMx, 0.0), writes=["UMx"])
        hist = B.UMx[:, SMP0:SMP0 + 176].rearrange("p (i e) -> p i e", e=11)[:, :, 0:3]
        load_T(c, I["state_mlstm_conv"][l].rearrange("i j c -> (i j) c"), 48, 384, [(h * 96, 96, hist, "UMx", 3)])

        cut(c, 14)

        def ev_u(ti, t0, tn, ps, pres):
            for (c0, c1, dst, st_) in dst_views(B.UMx, "ext_in", t0, tn):
                S.op("act", lambda e, s_=pview(ps, c0, c1, st_), d=dst: e.copy(out=d, in_=s_),
                     reads=[pres], writes=["UMx"])
            S.op("act", lambda e, ps=ps, t0=t0, tn=tn: e.copy(out=UMb_h[:, t0:t0 + tn], in_=ps),
                 reads=[pres], writes=[f"UMb{h}"])
        proj_fm(c, B.WINh, "WINh", 0, 96, ev_u)
        cut(c, 11)
        cmp_, cms = B.CM[:, 0:NPR], B.CM[:, NPR:NT].rearrange("p (i j) -> p i j", j=8)
        for j in range(4):
            up = B.UMx[:, j:j + NPR]
            us = B.UMx[:, SMP0 + j:SMP0 + j + 176].rearrange("p (i e) -> p i e", e=11)[:, :, 0:8]
            for eng, src, dst in (("dve", up, cmp_), ("dve", us, cms)):
                if j == 0:
                    S.op(eng, lambda e, src=src, dst=dst: e.tensor_scalar(
                        out=dst, in0=src, scalar1=MCW[:, h, 0:1], scalar2=MCB[:, h:h + 1], op0=ALU.mult, op1=ALU.add),
                        reads=["UMx", "mlc"], writes=["CM"])
                else:
                    S.op("dve", lambda e, src=src, dst=dst, j=j: e.scalar_tensor_tensor(
                        out=dst, in0=src, scalar=MCW[:, h, j:j + 1], in1=dst, op0=ALU.mult, op1=ALU.add),
                        reads=["UMx", "CM", "mlc"], writes=["CM"])
        S.op("act", lambda e: e.activation(out=B.CM, in_=B.CM, func=AF.Silu), reads=["CM"], writes=["CM"])
        S.op("act", lambda e: e.copy(out=CMb_h, in_=B.CM), reads=["CM"], writes=[f"CMb{h}"])
        cut(c, 12)
        if part == "i":
            return
        head_feats_ii(h, B, UMb_h, CMb_h)

    def head_feats_ii(h, B, UMb_h, CMb_h):
        todo = [(WQ[:, h, :], CMb_h, f"CMb{h}", B.MQ, "MQ"), (WK[:, h, :], CMb_h, f"CMb{h}", B.MK, "MK"),
                (WV[:, h, :], UMb_h, f"UMb{h}", B.MV, "MV")]
        for (Wm, src, sres, dst, dres) in todo:
            def ev(ti, t0, tn, ps, pres, dst=dst, dres=dres):
                eng = "act" if ti % 2 == 0 else "dve"
                if eng == "act":
                    S.op("act", lambda e: e.copy(out=dst[:, t0:t0 + tn], in_=ps), reads=[pres], writes=[dres])
                else:
                    S.op("dve", lambda e: e.tensor_copy(out=dst[:, t0:t0 + tn], in_=ps), reads=[pres], writes=[dres])
            small_mm_tiles(c, Wm, "mlW", src, sres, 96, ev)
        cut(c, 13)

    class Bufs:
        pass

    A.mark()
    B = Bufs()
    B.WINh = A.bf16(8 * 96).rearrange("p (k f) -> p k f", k=8)
    B.UMx, B.CM = A.f32(NEX)[0:96], A.f32(NT)[0:96]
    B.MQ, B.MK, B.MV = A.bf16(NT)[0:96], A.bf16(NT)[0:96], A.bf16(NT)[0:96]
    def p1_i(h):
        B.UMb, B.CMb = UMbA[h], CMbA[h]
        head_feats(h, B, True, part="i")
        up, us = ext_in_views(B.UMx)
        S.dma("sp", O["p_mlstm_conv"][l][:, h * 96:(h + 1) * 96].rearrange("j p -> p j"), up[:, NPR - 3:NPR],
              reads=["UMx"], allow_slow_non_contiguous=True)
        S.op("dve", lambda e, us=us: e.tensor_copy(out=CSs[:, 0:48].rearrange("p (i j) -> p i j", j=3), in_=us[:, :, 5:8]),
             reads=["UMx"], writes=["CSs"])
        store_T(c, CSs[:, 0:48], "CSs", 96, 48, O["s_mlstm_conv"][l].rearrange("i j c -> (i j) c")[:, h * 96:(h + 1) * 96])

    def p1_ii(h):
        B.UMb, B.CMb = UMbA[h], CMbA[h]
        head_feats(h, B, True, part="ii")
        for ti, (t0, tn) in enumerate(TILES):
            ps = P[2 + ti % 2]
            for xi, (src, sres) in enumerate(((B.MQ, "MQ"), (B.MK, "MK"), (B.MV, "MV"))):
                S.op("pe", lambda e, ps=ps, xi=xi, src=src, t0=t0, tn=tn: e.matmul(
                    ps[0:8, 0:tn], lhsT=WIF[:, xi * 4 + h, :], rhs=src[:, t0:t0 + tn], start=(xi == 0), stop=(xi == 2)),
                    reads=["mlW", sres], writes=[f"P{2 + ti % 2}"])
            if h == 0:
                S.op("dve", lambda e, ps=ps, t0=t0, tn=tn: e.tensor_copy(out=G8[:, t0:t0 + tn], in_=ps[0:8, 0:tn]),
                     reads=[f"P{2 + ti % 2}"], writes=["G8"])
            else:
                S.op("dve", lambda e, ps=ps, t0=t0, tn=tn: e.tensor_tensor(
                    out=G8[:, t0:t0 + tn], in0=ps[0:8, 0:tn], in1=G8[:, t0:t0 + tn], op=ALU.add),
                    reads=[f"P{2 + ti % 2}", "G8"], writes=["G8"])
    p1_i(0)
    for h in range(4):
        if h + 1 < 4:
            p1_i(h + 1)
        p1_ii(h)
    import os
    KCUT = int(os.environ.get("KCUT", "0"))
    if KCUT == 1:
        S.barrier(); A.release(); A.release(); A.release()
        return
    if l == 0:
        dbg(c, "G8", G8, ["G8"])
        dbg(c, "CM3", B.CM, ["CM"])
        dbg(c, "MQ3", B.MQ, ["MQ"])
        dbg(c, "MV3", B.MV, ["MV"])
        dbg(c, "UMn3", UMbA[3], ["UMb3"])
    S.barrier()
    A.release()

    A.mark()
    RF, RB, RA, RG = [A.f32(NT)[0:4] for _ in range(4)]
    RR = G8[0:4, :]
    S.dma("sp", RF, G8[4:8, :], reads=["G8"], writes=["RF"])
    LI = G8[0:4, :]
    S.op("act", lambda e: e.activation(out=LI, in_=LI, func=AF.Identity, bias=BI[:, 0:1]),
         reads=["G8", "mlc", "RF"], writes=["G8"])
    S.op("act", lambda e: e.activation(out=RF, in_=RF, func=AF.Exp, scale=-1.0, bias=NBF[:, 0:1]),
         reads=["RF", "mlc"], writes=["RF"])
    S.op("act", lambda e: e.activation(out=RF, in_=RF, func=AF.Ln, bias=1.0), reads=["RF"], writes=["RF"])
    sm = lambda X_: X_[:, NPR:NT]
    pr = lambda X_: X_[:, 0:NPR]
    S.op("dve", lambda e: e.tensor_tensor_scan(out=pr(RB), data0=pr(RF), data1=pr(RF), initial=0.0,
                                               op0=ALU.add, op1=ALU.bypass), reads=["RF"], writes=["RB"])
    S.op("dve", lambda e: e.tensor_tensor_scan(out=sm(RB), data0=c.SMASK[0:4, :], data1=sm(RF), initial=0.0,
                                               op0=ALU.mult, op1=ALU.add), reads=["RF", "consts"], writes=["RB"])
    S.op("dve", lambda e: e.tensor_tensor(out=RA, in0=LI, in1=RB, op=ALU.add), reads=["G8", "RB"], writes=["RA"])
    S.op("dve", lambda e: e.tensor_copy(out=RG, in_=RA), reads=["RA"], writes=["RG"])
    S.op("dve", lambda e: e.tensor_scalar_max(out=RG[:, 0:1], in0=RG[:, 0:1], scalar1=0.0), reads=["RG"], writes=["RG"])
    st_v = lambda X_: X_[:, NPR:NT].rearrange("p (i j) -> p i j", j=8)
    S.op("dve", lambda e: e.tensor_tensor(out=st_v(RG)[:, :, 0], in0=st_v(RG)[:, :, 0], in1=M0T[:, 0:16], op=ALU.max),
         reads=["RG", "mlc"], writes=["RG"])
    S.op("dve", lambda e: e.tensor_tensor_scan(out=pr(RF), data0=pr(RG), data1=pr(RG), initial=-1e30,
                                               op0=ALU.max, op1=ALU.max), reads=["RG", "RB", "RA"], writes=["RF"])
    S.op("dve", lambda e: e.tensor_tensor_scan(out=sm(RF), data0=c.SNEG[0:4, :], data1=sm(RG), initial=-1e30,
                                               op0=ALU.add, op1=ALU.max), reads=["RG", "consts"], writes=["RF"])
    S.op("dve", lambda e: e.memset(RG[:, 0:16], 0.0), reads=["RF"], writes=["RG"])
    S.op("dve", lambda e: e.tensor_copy(
        out=RG[:, 16:NPR].rearrange("p (c t) -> p c t", t=64),
        in_=RF[:, 15:15 + 2048].rearrange("p (c t) -> p c t", t=64)[:, :, 0:1].to_broadcast([4, 32, 64])),
        reads=["RF"], writes=["RG"])
    S.op("dve", lambda e: e.tensor_copy(out=st_v(RG), in_=M0T[:, 0:16].unsqueeze(2).to_broadcast([4, 16, 8])),
         reads=["mlc"], writes=["RG"])
    S.op("dve", lambda e: e.tensor_tensor(out=RR, in0=RB, in1=RF, op=ALU.subtract), reads=["RB", "RF"], writes=["G8"])
    S.op("dve", lambda e: e.tensor_scalar_mul(out=MNEW[:, 0:1], in0=RR[:, NPR - 1:NPR], scalar1=-1.0),
         reads=["G8"], writes=["MNEW"])
    S.op("dve", lambda e: e.tensor_scalar_mul(out=MNEW[:, 1:17], in0=st_v(RR)[:, :, 7], scalar1=-1.0),
         reads=["G8"], writes=["MNEW"])
    S.dma("sp", O["p_mlstm_m"][l].rearrange("(h o) -> h o", o=1), MNEW[:, 0:1], reads=["MNEW"])
    S.dma("sp", O["s_mlstm_m"][l].rearrange("i h -> h i"), MNEW[:, 1:17], reads=["MNEW"], allow_slow_non_contiguous=True)
    S.op("act", lambda e: e.activation(out=RR, in_=RR, func=AF.Exp), reads=["G8", "MNEW"], writes=["G8"])
    S.op("dve", lambda e: e.tensor_copy(out=RB[:, 0:16], in_=RF[:, 15:16].to_broadcast([4, 16])), reads=["RF", "G8"], writes=["RB"])
    S.op("dve", lambda e: e.tensor_copy(
        out=RB[:, 16:NPR].rearrange("p (c t) -> p c t", t=64),
        in_=RF[:, 16:NPR].rearrange("p (c t) -> p c t", t=64)[:, :, 63:64].to_broadcast([4, 32, 64])),
        reads=["RF"], writes=["RB"])
    S.op("dve", lambda e: e.tensor_copy(out=st_v(RB), in_=st_v(RF)[:, :, 7:8].to_broadcast([4, 16, 8])), reads=["RF"], writes=["RB"])
    S.op("dve", lambda e: e.tensor_tensor(out=RB, in0=RA, in1=RB, op=ALU.subtract), reads=["RA", "RB"], writes=["RB"])
    S.op("act", lambda e: e.activation(out=RB, in_=RB, func=AF.Exp), reads=["RB"], writes=["RB"])
    S.op("dve", lambda e: e.tensor_tensor(out=RA, in0=RA, in1=RG, op=ALU.subtract), reads=["RA", "RG"], writes=["RA"])
    S.op("act", lambda e: e.activation(out=RA, in_=RA, func=AF.Exp), reads=["RA"], writes=["RA"])
    S.op("dve", lambda e: e.tensor_tensor(out=RG, in0=RG, in1=RF, op=ALU.subtract), reads=["RG", "RF", "RA"], writes=["RG"])
    S.op("act", lambda e: e.activation(out=RG, in_=RG, func=AF.Exp), reads=["RG"], writes=["RG"])
    S.op("dve", lambda e: e.tensor_copy(out=AE[:, 0:1], in_=RG[:, 15:16]), reads=["RG"], writes=["AE"])
    S.op("dve", lambda e: e.tensor_copy(out=AE[:, 1:33], in_=RG[:, 16:NPR].rearrange("p (c t) -> p c t", t=64)[:, :, 63]),
         reads=["RG"], writes=["AE"])
    S.op("dve", lambda e: e.tensor_copy(out=AE[:, 33:49], in_=st_v(RG)[:, :, 7]), reads=["RG"], writes=["AE"])
    S.op("dve", lambda e: e.tensor_scalar_mul(out=RG, in0=RG, scalar1=QSCALE_M), reads=["RG", "AE"], writes=["RG"])
    S.op("dve", lambda e: e.reciprocal(out=RF, in_=RG), reads=["RG", "RF"], writes=["RF"])
    S.op("dve", lambda e: e.tensor_tensor(out=RR, in0=RR, in1=RF, op=ALU.mult), reads=["G8", "RF"], writes=["G8"])
    for ci, (tok0, T, nseq, seq0) in enumerate(CHUNKS):
        for gi_, (Rw, rn) in enumerate(((RG, "RG"), (RA, "RA"), (RR, "G8"), (RB, "RB"))):
            S.op("pe", lambda e, Rw=Rw, gi_=gi_, tok0=tok0, T=T: e.transpose(
                out=P[6][0:T, gi_ * 4:gi_ * 4 + 4], in_=Rw[:, tok0:tok0 + T], identity=c.ident[0:4, 0:4]),
                reads=[rn, "ident"], writes=["P6"])
        S.op("act", lambda e, ci=ci, T=T: e.copy(out=GT[0:T, ci, :], in_=P[6][0:T, 0:16]), reads=["P6"], writes=["GT"])
    if l == 0:
        dbg(c, "alpha", RG, ["RG"])
        dbg(c, "beta", RA, ["RA"])
        dbg(c, "eps", RR, ["G8"])
        dbg(c, "G", RF, ["RF"])
        dbg(c, "negB", RB, ["RB"])
        dbg(c, "GT", GT, ["GT"])
    sel = c.SEL[0:4, :].rearrange("p (h m) -> p h m", h=4)
    for h in range(4):
        S.op("pe", lambda e, h=h: e.matmul(P[7][0:96, 0:NGAM], lhsT=sel[:, h, :], rhs=AE[:, 0:NGAM], start=True, stop=True),
             reads=["consts", "AE"], writes=["P7"])
        S.op("act", lambda e, h=h: e.copy(out=GAM[:, h, :], in_=P[7][0:96, 0:NGAM]), reads=["P7"], writes=["GAM"])
    S.barrier()
    A.release()
    A.release()
    if KCUT == 2:
        A.release()
        return

    A.mark()
    B = Bufs()
    WINz = A.bf16(8 * 96).rearrange("p (k f) -> p k f", k=8)
    B.MQ, B.MK = A.bf16(NT)[0:96], A.bf16(NT)[0:96]
    YMh = A.bf16(NT).rearrange("p (o t) -> p o t", o=1)[0:96]
    ZS = A.bf16(NT)[0:96]
    HNT = A.f32(NT)[0:96]
    CEp = A.f32(104)[0:96]
    CEb = [A.bf16(104)[0:96] for _ in range(2)]
    CE = A.f32(NSEQ * 97).rearrange("p (i e) -> p i e", e=97)[0:96]
    CEsb = A.bf16(NSEQ * 97).rearrange("p (i e) -> p i e", e=97)[0:96]
    QZ = A.bf16(8 * 64).rearrange("p (i t) -> p i t", i=8)[0:96]
    KTz = A.bf16(8 * 96).rearrange("p (i d) -> p i d", i=8)[0:64]
    VT = [A.bf16(104)[0:64] for _ in range(4)]
    KTb = [A.bf16(96)[0:64] for _ in range(4)]
    STm = [A.bf16(64)[0:64] for _ in range(4)]
    Hh = [A.f32(96)[0:64] for _ in range(4)]
    HN = [A.f32(96)[0:64] for _ in range(2)]
    JK = A.f32(96)[0:64]
    SMALL = [A.f32(8)[0:64] for _ in range(6)]
    S.op("pool", lambda e: e.memset(QZ, 0.0), writes=["QZ"])
    for par in range(4):
        S.op("pool", lambda e, par=par: e.memset(VT[par][:, 96:97], 1.0), writes=[f"VT{par}"])

    def p2_head(h):
        B.UMb, B.CMb = UMbA[h], CMbA[h]
        UMb_h, CMb_h = UMbA[h], CMbA[h]
        head_feats(h, B, False)
        S.dma("pool", WINz, w_in_l[:, :, 896 + 96 * h:896 + 96 * (h + 1)], writes=["WINz"])

        def ev_z(ti, t0, tn, ps, pres):
            S.op("act", lambda e: e.activation(out=ZS[:, t0:t0 + tn], in_=ps, func=AF.Sigmoid), reads=[pres], writes=["ZS"])
        proj_fm(c, WINz, "WINz", 0, 96, ev_z)
        S.op("pool", lambda e: e.memset(CEp[:, 0:97], 0.0), writes=["CEp"])
        S.op("pool", lambda e: e.memset(CEb[0][:, 0:97], 0.0), writes=["CEb0"])
        S.dma("sp", CE[:, :, 0:96], I["state_mlstm_C"][l][:, h].rearrange("i d e -> d i e"), writes=["CE"])
        S.dma("sp", CE[:, :, 96], I["state_mlstm_n"][l][:, h, :].rearrange("i d -> d i"), writes=["CE"],
              allow_slow_non_contiguous=True)
        S.op("dve", lambda e: e.tensor_copy(out=CEsb, in_=CE), reads=["CE"], writes=["CEsb"])
        def ctx(ci):
            tok0, T, nseq, seq0 = CHUNKS[ci]
            return tok0, T, nseq, seq0, slice(tok0, tok0 + T)

        def s1(ci):
            tok0, T, nseq, seq0, tk = ctx(ci)
            Pk, nk = P[ci % 2], f"P{ci % 2}"
            S.op("pe", lambda e: e.matmul(Pk[0:T, 0:96], lhsT=UMb_h[:, tk], rhs=WV[:, h, :], start=True, stop=True),
                 reads=[f"UMb{h}", "mlW"], writes=[nk])
            S.op("pe", lambda e: e.matmul(Pk[0:T, 96:192], lhsT=CMb_h[:, tk], rhs=WK[:, h, :], start=True, stop=True),
                 reads=[f"CMb{h}", "mlW"], writes=[nk])
            S.op("pe", lambda e: e.matmul(Pk[0:T, 192:192 + T], lhsT=B.MK[:, tk], rhs=B.MQ[:, tk], start=True, stop=True),
                 reads=["MK", "MQ"], writes=[nk])

        def s2(ci):
            tok0, T, nseq, seq0, tk = ctx(ci)
            b4 = ci % 4
            be, bg = GT[0:T, ci, 4 + h:5 + h], GT[0:T, ci, 12 + h:13 + h]
            mask = (c.MASKC if nseq == 1 else c.MASKB)[0:T, 0:T]
            Pk, nk = P[ci % 2], f"P{ci % 2}"
            S.op("dve", lambda e: e.tensor_copy(out=VT[b4][0:T, 0:96], in_=Pk[0:T, 0:96]), reads=[nk], writes=[f"VT{b4}"])
            S.op("dve", lambda e: e.tensor_scalar_mul(out=KTb[b4][0:T, :], in0=Pk[0:T, 96:192], scalar1=bg),
                 reads=[nk, "GT"], writes=[f"KTb{b4}"])
            S.op("dve", lambda e: e.scalar_tensor_tensor(out=STm[b4][0:T, 0:T], in0=Pk[0:T, 192:192 + T], scalar=be, in1=mask,
                                                         op0=ALU.mult, op1=ALU.mult),
                 reads=[nk, "GT", "consts"], writes=[f"STm{b4}"])

        def s3(ci):
            tok0, T, nseq, seq0, tk = ctx(ci)
            b4 = ci % 4
            if nseq == 1:
                Pd, nd = P[5 + ci % 2], f"P{5 + ci % 2}"
                S.op("pe", lambda e: e.matmul(Pd[0:96, 0:97], lhsT=KTb[b4][0:T, :], rhs=VT[b4][0:T, 0:97], start=True, stop=True),
                     reads=[f"KTb{b4}", f"VT{b4}"], writes=[nd])
            else:
                S.op("dve", lambda e: e.tensor_tensor(
                    out=KTz, in0=KTb[b4][0:64, :].unsqueeze(1).to_broadcast([64, 8, 96]),
                    in1=c.PM[0:64, :].unsqueeze(2).to_broadcast([64, 8, 96]), op=ALU.mult),
                    reads=[f"KTb{b4}", "consts"], writes=["KTz"])
                for half in range(2):
                    pb = P[5 + half]
                    for j in range(4):
                        S.op("pe", lambda e, pb=pb, j=j, half=half: e.matmul(
                            pb[0:96, j * 97:(j + 1) * 97], lhsT=KTz[:, 4 * half + j, :], rhs=VT[b4][0:64, 0:97],
                            start=True, stop=True), reads=["KTz", f"VT{b4}"], writes=[f"P{5 + half}"])

        def s4(ci):
            tok0, T, nseq, seq0, tk = ctx(ci)
            if nseq == 1:
                Pd, nd = P[5 + ci % 2], f"P{5 + ci % 2}"
                S.op("dve", lambda e: e.scalar_tensor_tensor(
                    out=CEp[:, 0:97], in0=CEp[:, 0:97], scalar=GAM[:, h, ci:ci + 1], in1=Pd[0:96, 0:97], op0=ALU.mult, op1=ALU.add),
                    reads=[nd, "CEp", "GAM"], writes=["CEp"])
            else:
                for half in range(2):
                    pb = P[5 + half]
                    s0 = seq0 + 4 * half
                    cev = CE[:, s0:s0 + 4, :]
                    S.op("dve", lambda e, cev=cev, s0=s0: e.tensor_tensor(
                        out=cev, in0=cev, in1=GAM[:, h, 33 + s0:33 + s0 + 4].unsqueeze(2).to_broadcast([96, 4, 97]), op=ALU.mult),
                        reads=["CE", "GAM"], writes=["CE"])
                    S.op("dve", lambda e, pb=pb, cev=cev: e.tensor_tensor(
                        out=cev, in0=pb[0:96, 0:388].rearrange("p (i e) -> p i e", e=97), in1=cev, op=ALU.add),
                        reads=[f"P{5 + half}", "CE"], writes=["CE"])

        def s5(ci):
            tok0, T, nseq, seq0, tk = ctx(ci)
            b4 = ci % 4
            Pn, nn = P[2 + ci % 3], f"P{2 + ci % 3}"
            S.op("pe", lambda e: e.matmul(Pn[0:T, 0:97], lhsT=STm[b4][0:T, 0:T], rhs=VT[b4][0:T, 0:97], start=True, stop=False),
                 reads=[f"STm{b4}", f"VT{b4}"], writes=[nn])
            if nseq == 1:
                S.op("pe", lambda e: e.matmul(Pn[0:T, 0:97], lhsT=B.MQ[:, tk], rhs=CEb[ci % 2][:, 0:97], start=False, stop=True),
                     reads=["MQ", f"CEb{ci % 2}"], writes=[nn])
                S.op("act", lambda e: e.copy(out=CEb[(ci + 1) % 2][:, 0:97], in_=CEp[:, 0:97]), reads=["CEp"], writes=[f"CEb{(ci + 1) % 2}"])
            else:
                S.op("act", lambda e: e.copy(out=diag_view(QZ, 8, 8, 64), in_=B.MQ[:, tk].rearrange("p (i j) -> p i j", j=8)),
                     reads=["MQ"], writes=["QZ"])
                for i in range(8):
                    S.op("pe", lambda e, i=i: e.matmul(Pn[0:64, 0:97], lhsT=QZ[:, i, :], rhs=CEsb[:, seq0 + i, :],
                                                       start=False, stop=(i == 7)),
                         reads=["QZ", "CEsb"], writes=[nn])

        def s6(ci):
            tok0, T, nseq, seq0, tk = ctx(ci)
            Pn, nn = P[2 + ci % 3], f"P{2 + ci % 3}"
            sm_ = SMALL[ci % 6]
            S.op("act", lambda e: e.activation(out=sm_[0:T, 0:1], in_=Pn[0:T, 96:97], func=AF.Abs), reads=[nn], writes=[f"SM{ci % 6}"])

        def s7(ci):
            tok0, T, nseq, seq0, tk = ctx(ci)
            epp = GT[0:T, ci, 8 + h:9 + h]
            Pn, nn = P[2 + ci % 3], f"P{2 + ci % 3}"
            sm_, sn, b4 = SMALL[ci % 6], f"SM{ci % 6}", ci % 4
            S.op("dve", lambda e: e.tensor_tensor(out=sm_[0:T, 0:1], in0=sm_[0:T, 0:1], in1=epp, op=ALU.max), reads=[sn, "GT"], writes=[sn])
            S.op("dve", lambda e: e.reciprocal(out=sm_[0:T, 0:1], in_=sm_[0:T, 0:1]), reads=[sn], writes=[sn])
            S.op("dve", lambda e: e.tensor_scalar_mul(out=Hh[b4][0:T, :], in0=Pn[0:T, 0:96], scalar1=sm_[0:T, 0:1]),
                 reads=[nn, sn], writes=[f"H{b4}"])

        def s8(ci):
            tok0, T, nseq, seq0, tk = ctx(ci)
            sm_, sn, b4 = SMALL[ci % 6], f"SM{ci % 6}", ci % 4
            S.op("act", lambda e: e.activation(out=JK[0:T, :], in_=Hh[b4][0:T, :], func=AF.Square, accum_out=sm_[0:T, 2:3]),
                 reads=[f"H{b4}", sn], writes=["JK", sn])
            S.op("act", lambda e: e.activation(out=sm_[0:T, 3:4], in_=sm_[0:T, 2:3], func=AF.Sqrt, bias=EPS, scale=1.0 / 96),
                 reads=[sn], writes=[sn])

        def s9(ci):
            tok0, T, nseq, seq0, tk = ctx(ci)
            sm_, sn = SMALL[ci % 6], f"SM{ci % 6}"
            S.op("dve", lambda e: e.reciprocal(out=sm_[0:T, 3:4], in_=sm_[0:T, 3:4]), reads=[sn], writes=[sn])

        def s10(ci):
            tok0, T, nseq, seq0, tk = ctx(ci)
            sm_, sn, b4 = SMALL[ci % 6], f"SM{ci % 6}", ci % 4
            S.op("act", lambda e: e.activation(out=HN[ci % 2][0:T, :], in_=Hh[b4][0:T, :], func=AF.Copy, scale=sm_[0:T, 3:4]),
                 reads=[f"H{b4}", sn], writes=[f"HN{ci % 2}"])

        def s11(ci):
            tok0, T, nseq, seq0, tk = ctx(ci)
            S.op("pe", lambda e: e.transpose(out=P[7][0:96, 0:T], in_=HN[ci % 2][0:T, :], identity=c.ident[0:T, 0:T]),
                 reads=[f"HN{ci % 2}", "ident"], writes=["P7"])

        def s12(ci):
            tok0, T, nseq, seq0, tk = ctx(ci)
            S.op("act", lambda e: e.copy(out=HNT[:, tk], in_=P[7][0:96, 0:T]), reads=["P7"], writes=["HNT"])

        stages = [s1, s2, s3, s4, s5, s6, s7, s8, s9, s10, s11, s12]
        order = [11, 10, 9, 8, 7, 6, 5, 4, 3, 2, 1, 0]
        for it in range(NCH + len(stages) - 1):
            for k in order:
                ci = it - k
                if 0 <= ci < NCH:
                    stages[k](ci)
        drain(c, 99)
        S.op("act", lambda e: e.activation(out=HNT, in_=HNT, func=AF.Copy, scale=NG[:, h:h + 1]), reads=["HNT", "mlc"], writes=["HNT"])
        S.op("dve", lambda e: e.scalar_tensor_tensor(out=HNT, in0=CMb_h, scalar=SK[:, h:h + 1], in1=HNT, op0=ALU.mult, op1=ALU.add),
             reads=["HNT", f"CMb{h}", "mlc"], writes=["HNT"])
        S.op("dve", lambda e: e.tensor_tensor(out=YMh[:, 0, :], in0=HNT, in1=ZS, op=ALU.mult), reads=["HNT", "ZS"], writes=["YMh"])
        apply_wout(c, l, YMh, "YMh", 256 + 96 * h, 96, 1, defer=True)
        S.dma("sp", O["p_mlstm_C"][l, h], CEp[:, 0:96], reads=["CEp"])
        S.dma("sp", O["p_mlstm_n"][l, h].rearrange("(d o) -> d o", o=1), CEp[:, 96:97], reads=["CEp"])
        S.dma("sp", O["s_mlstm_C"][l][:, h].rearrange("i d e -> d i e"), CE[:, :, 0:96], reads=["CE"])
        S.dma("sp", O["s_mlstm_n"][l][:, h, :].rearrange("i d -> d i"), CE[:, :, 96], reads=["CE"], allow_slow_non_contiguous=True)
    for h in range(4):
        p2_head(h)
    drain(c, 99)
    S.barrier()
    A.release()
    A.release()


QSCALE_G = 48.0 ** -0.5


def gla_group(c, l):
    S, A, I, O = c.S, c.A, c.I, c.O
    P = c.P
    A.mark()
    w_in_l = I["w_in"][l].rearrange("(k p) f -> p k f", p=128)
    WA_ = A.bf16(8 * 16).rearrange("p (k f) -> p k f", k=8)
    ALR = A.f32(NT)[0:16]
    WUP = A.f32(192)[0:16]
    NBUP, GNG = A.f32(8)[0:48], A.f32(8)[0:96]
    MCH = A.f32(NT)[0:48]
    GAMg = A.f32(NGAM + 7)[0:48]
    S.dma("pool", WA_, w_in_l[:, :, 2432:2448], writes=["WA_"])
    S.dma("sp", WUP, I["gla_w_up"][l], writes=["glac"])
    colvec(c, NBUP[:, 0:4], I["gla_b_up"][l], "(h k) -> k h", "glac", k=48)
    colvec(c, GNG[:, 0:4], I["gla_norm_g"][l], "(h p) -> p h", "glac", p=96)
    S.op("dve", lambda e: e.tensor_scalar_mul(out=NBUP[:, 0:4], in0=NBUP[:, 0:4], scalar1=-1.0), reads=["glac"], writes=["glac"])
    S.op("pool", lambda e: e.memset(MCH, 1.0), writes=["MCH"])
    S.op("pool", lambda e: e.memset(MCH[:, 0:1], 0.0), reads=["MCH"], writes=["MCH"])
    S.op("pool", lambda e: e.memset(MCH[:, 16:NPR].rearrange("p (c t) -> p c t", t=64)[:, :, 0:1], 0.0), reads=["MCH"], writes=["MCH"])
    S.op("pool", lambda e: e.memset(MCH[:, NPR:NT].rearrange("p (i j) -> p i j", j=8)[:, :, 0:1], 0.0), reads=["MCH"], writes=["MCH"])

    def ev_a(ti, t0, tn, ps, pres):
        S.op("act", lambda e: e.copy(out=ALR[:, t0:t0 + tn], in_=ps), reads=[pres], writes=["ALR"])
    proj_fm(c, WA_, "WA_", 0, 16, ev_a)

    WQg = A.bf16(8 * 48).rearrange("p (k f) -> p k f", k=8)
    WKg = A.bf16(8 * 48).rearrange("p (k f) -> p k f", k=8)
    WVg = A.bf16(8 * 96).rearrange("p (k f) -> p k f", k=8)
    WGg = A.bf16(8 * 96).rearrange("p (k f) -> p k f", k=8)
    BC, EO, QG, KG = A.f32(NT)[0:48], A.f32(NT)[0:96], A.f32(NT)[0:48], A.f32(NT)[0:48]
    VF = A.f32(NT)[0:96]
    GGs = A.bf16(NT)[0:96]
    YGh = A.bf16(NT).rearrange("p (o t) -> p o t", o=1)[0:96]
    Sp = [A.f32(96)[0:48] for _ in range(2)]
    Ss = A.f32(NSEQ * 96).rearrange("p (i e) -> p i e", e=96)[0:48]
    QZ = A.f32(8 * 64).rearrange("p (i t) -> p i t", i=8)[0:48]
    KTz = A.bf16(8 * 48).rearrange("p (i d) -> p i d", i=8)[0:64]
    VT = [A.bf16(96)[0:64] for _ in range(4)]
    KT = [A.bf16(48)[0:64] for _ in range(4)]
    STm = [A.bf16(64)[0:64] for _ in range(4)]
    ON = [A.f32(96)[0:64] for _ in range(2)]
    JK = A.f32(96)[0:64]
    SMALL = [A.f32(8)[0:64] for _ in range(3)]
    S.op("pool", lambda e: e.memset(QZ, 0.0), writes=["QZg"])
    st_v = lambda X_: X_[:, NPR:NT].rearrange("p (i j) -> p i j", j=8)

    def g_head(h):
        S.dma("pool", WQg, w_in_l[:, :, 1280 + 48 * h:1280 + 48 * (h + 1)], writes=["WQg"])
        S.dma("pool", WKg, w_in_l[:, :, 1472 + 48 * h:1472 + 48 * (h + 1)], writes=["WKg"])
        S.dma("pool", WVg, w_in_l[:, :, 1664 + 96 * h:1664 + 96 * (h + 1)], writes=["WVg"])
        S.dma("pool", WGg, w_in_l[:, :, 2048 + 96 * h:2048 + 96 * (h + 1)], writes=["WGg"])

        def ev_q(ti, t0, tn, ps, pres):
            S.op("act", lambda e: e.activation(out=QG[:, t0:t0 + tn], in_=ps, func=AF.Copy, scale=QSCALE_G), reads=[pres], writes=["QG"])
        proj_fm(c, WQg, "WQg", 0, 48, ev_q)

        def ev_k(ti, t0, tn, ps, pres):
            S.op("act", lambda e: e.copy(out=KG[:, t0:t0 + tn], in_=ps), reads=[pres], writes=["KG"])
        proj_fm(c, WKg, "WKg", 0, 48, ev_k)

        def ev_g(ti, t0, tn, ps, pres):
            S.op("act", lambda e: e.activation(out=GGs[:, t0:t0 + tn], in_=ps, func=AF.Silu), reads=[pres], writes=["GGs"])
        proj_fm(c, WGg, "WGg", 0, 96, ev_g)

        def ev_v(ti, t0, tn, ps, pres):
            if ti % 2 == 0:
                S.op("dve", lambda e: e.tensor_copy(out=VF[:, t0:t0 + tn], in_=ps), reads=[pres], writes=["VF"])
            else:
                S.op("act", lambda e: e.copy(out=VF[:, t0:t0 + tn], in_=ps), reads=[pres], writes=["VF"])
        proj_fm(c, WVg, "WVg", 0, 96, ev_v)

        def ev_l(ti, t0, tn, ps, pres):
            S.op("act", lambda e: e.activation(out=BC[:, t0:t0 + tn], in_=ps, func=AF.Exp, scale=-1.0, bias=NBUP[:, h:h + 1]),
                 reads=[pres, "glac"], writes=["BC"])
        small_mm_tiles(c, WUP[:, 48 * h:48 * (h + 1)], "glac", ALR, "ALR", 48, ev_l)
        S.op("act", lambda e: e.activation(out=BC, in_=BC, func=AF.Ln, bias=1.0), reads=["BC"], writes=["BC"])
        S.op("dve", lambda e: e.tensor_scalar_mul(out=EO[0:48, :], in0=BC, scalar1=-1.0 / 16.0), reads=["BC"], writes=["EO"])
        S.op("dve", lambda e: e.tensor_tensor_scan(out=BC, data0=MCH, data1=EO[0:48, :], initial=0.0, op0=ALU.mult, op1=ALU.add),
             reads=["EO", "MCH"], writes=["BC"])
        S.op("act", lambda e: e.activation(out=EO[0:48, :], in_=BC, func=AF.Exp), reads=["BC"], writes=["EO"])
        S.op("dve", lambda e: e.tensor_tensor(out=QG, in0=QG, in1=EO[0:48, :], op=ALU.mult), reads=["QG", "EO"], writes=["QG"])
        S.op("dve", lambda e: e.tensor_copy(out=GAMg[:, 0:1], in_=EO[0:48, 15:16]), reads=["EO"], writes=["GAMg"])
        S.op("dve", lambda e: e.tensor_copy(out=GAMg[:, 1:33], in_=EO[0:48, 16:NPR].rearrange("p (c t) -> p c t", t=64)[:, :, 63]),
             reads=["EO"], writes=["GAMg"])
        S.op("dve", lambda e: e.tensor_copy(out=GAMg[:, 33:49], in_=st_v(EO[0:48, :])[:, :, 7]), reads=["EO"], writes=["GAMg"])
        S.op("act", lambda e: e.activation(out=BC, in_=BC, func=AF.Exp, scale=-1.0), reads=["BC"], writes=["BC"])
        S.op("dve", lambda e: e.tensor_tensor(out=KG, in0=KG, in1=BC, op=ALU.mult), reads=["KG", "BC"], writes=["KG"])
        S.op("pool", lambda e: e.memset(Sp[1], 0.0), writes=["Sp1"])
        S.dma("sp", Ss, I["state_gla_S"][l][:, h].rearrange("i k v -> k i v"), writes=["Ss"])
        def ctx(ci):
            tok0, T, nseq, seq0 = CHUNKS[ci]
            return tok0, T, nseq, seq0, slice(tok0, tok0 + T)

        def g1(ci):
            tok0, T, nseq, seq0, tk = ctx(ci)
            Pa, na = P[ci % 2], f"P{ci % 2}"
            S.op("pe", lambda e: e.transpose(out=Pa[0:T, 0:48], in_=KG[:, tk], identity=c.ident[0:48, 0:48]),
                 reads=["KG", "ident"], writes=[na])
            S.op("pe", lambda e: e.matmul(Pa[0:T, 64:64 + T], lhsT=KG[:, tk], rhs=QG[:, tk], start=True, stop=True),
                 reads=["KG", "QG"], writes=[na])
            S.op("pe", lambda e: e.transpose(out=Pa[0:T, 128:224], in_=VF[:, tk], identity=c.ident[0:96, 0:96]),
                 reads=["VF", "ident"], writes=[na])

        def g2(ci):
            tok0, T, nseq, seq0, tk = ctx(ci)
            b4 = ci % 4
            mask = (c.MASKC if nseq == 1 else c.MASKB)[0:T, 0:T]
            Pa, na = P[ci % 2], f"P{ci % 2}"
            S.op("dve", lambda e: e.tensor_copy(out=VT[b4][0:T, :], in_=Pa[0:T, 128:224]), reads=[na], writes=[f"gVT{b4}"])
            S.op("dve", lambda e: e.tensor_copy(out=KT[b4][0:T, :], in_=Pa[0:T, 0:48]), reads=[na], writes=[f"gKT{b4}"])
            S.op("dve", lambda e: e.tensor_tensor(out=STm[b4][0:T, 0:T], in0=Pa[0:T, 64:64 + T], in1=mask, op=ALU.mult),
                 reads=[na, "consts"], writes=[f"gST{b4}"])

        def g3(ci):
            tok0, T, nseq, seq0, tk = ctx(ci)
            b4 = ci % 4
            if nseq == 1:
                Pd, nd = P[5 + ci % 2], f"P{5 + ci % 2}"
                S.op("pe", lambda e: e.matmul(Pd[0:48, 0:96], lhsT=KT[b4][0:T, :], rhs=VT[b4][0:T, :], start=True, stop=True),
                     reads=[f"gKT{b4}", f"gVT{b4}"], writes=[nd])

        def g4(ci):
            tok0, T, nseq, seq0, tk = ctx(ci)
            if nseq == 1:
                Pd, nd = P[5 + ci % 2], f"P{5 + ci % 2}"
                S.op("dve", lambda e: e.tensor_tensor(out=Sp[ci % 2], in0=Pd[0:48, 0:96], in1=Sp[(ci + 1) % 2], op=ALU.add),
                     reads=[nd, f"Sp{(ci + 1) % 2}"], writes=[f"Sp{ci % 2}"])

        def g5(ci):
            tok0, T, nseq, seq0, tk = ctx(ci)
            b4 = ci % 4
            Pn, nn = P[2 + ci % 3], f"P{2 + ci % 3}"
            S.op("pe", lambda e: e.matmul(Pn[0:T, 0:96], lhsT=STm[b4][0:T, 0:T], rhs=VT[b4][0:T, :], start=True, stop=False),
                 reads=[f"gST{b4}", f"gVT{b4}"], writes=[nn])
            if nseq == 1:
                S.op("pe", lambda e: e.matmul(Pn[0:T, 0:96], lhsT=QG[:, tk], rhs=Sp[(ci + 1) % 2], start=False, stop=True),
                     reads=["QG", f"Sp{(ci + 1) % 2}"], writes=[nn])
                S.op("act", lambda e: e.activation(out=Sp[ci % 2], in_=Sp[ci % 2], func=AF.Copy, scale=GAMg[:, ci:ci + 1]),
                     reads=[f"Sp{ci % 2}", "GAMg"], writes=[f"Sp{ci % 2}"])
            else:
                S.op("act", lambda e: e.copy(out=diag_view(QZ, 8, 8, 64), in_=QG[:, tk].rearrange("p (i j) -> p i j", j=8)),
                     reads=["QG"], writes=["QZg"])
                for i in range(8):
                    S.op("pe", lambda e, i=i: e.matmul(Pn[0:64, 0:96], lhsT=QZ[:, i, :], rhs=Ss[:, seq0 + i, :], start=False, stop=(i == 7)),
                         reads=["QZg", "Ss"], writes=[nn])
                S.op("dve", lambda e: e.tensor_tensor(
                    out=KTz, in0=KT[b4][0:64, :].unsqueeze(1).to_broadcast([64, 8, 48]),
                    in1=c.PM[0:64, :].unsqueeze(2).to_broadcast([64, 8, 48]), op=ALU.mult),
                    reads=[f"gKT{b4}", "consts"], writes=["gKTz"])
                for half in range(2):
                    pb = P[5 + half]
                    for j in range(4):
                        S.op("pe", lambda e, pb=pb, j=j, half=half: e.matmul(
                            pb[0:48, j * 96:(j + 1) * 96], lhsT=KTz[:, 4 * half + j, :], rhs=VT[b4][0:64, :], start=True, stop=True),
                            reads=["gKTz", f"gVT{b4}"], writes=[f"P{5 + half}"])
                for half in range(2):
                    pb = P[5 + half]
                    s0 = seq0 + 4 * half
                    sv = Ss[:, s0:s0 + 4, :]
                    S.op("dve", lambda e, pb=pb, sv=sv: e.tensor_tensor(
                        out=sv, in0=pb[0:48, 0:384].rearrange("p (i e) -> p i e", e=96), in1=sv, op=ALU.add),
                        reads=[f"P{5 + half}", "Ss"], writes=["Ss"])
                    S.op("dve", lambda e, sv=sv, s0=s0: e.tensor_tensor(
                        out=sv, in0=sv, in1=GAMg[:, 33 + s0:33 + s0 + 4].unsqueeze(2).to_broadcast([48, 4, 96]), op=ALU.mult),
                        reads=["Ss", "GAMg"], writes=["Ss"])

        def g6(ci):
            tok0, T, nseq, seq0, tk = ctx(ci)
            Pn, nn = P[2 + ci % 3], f"P{2 + ci % 3}"
            sm_, sn = SMALL[ci % 3], f"gSM{ci % 3}"
            S.op("act", lambda e: e.activation(out=JK[0:T, :], in_=Pn[0:T, 0:96], func=AF.Square, accum_out=sm_[0:T, 0:1]),
                 reads=[nn], writes=["gJK", sn])
            S.op("act", lambda e: e.activation(out=sm_[0:T, 1:2], in_=sm_[0:T, 0:1], func=AF.Sqrt, bias=EPS, scale=1.0 / 96),
                 reads=[sn], writes=[sn])

        def g7(ci):
            tok0, T, nseq, seq0, tk = ctx(ci)
            Pn, nn = P[2 + ci % 3], f"P{2 + ci % 3}"
            sm_, sn = SMALL[ci % 3], f"gSM{ci % 3}"
            S.op("dve", lambda e: e.reciprocal(out=sm_[0:T, 1:2], in_=sm_[0:T, 1:2]), reads=[sn], writes=[sn])
            S.op("dve", lambda e: e.tensor_scalar_mul(out=ON[ci % 2][0:T, :], in0=Pn[0:T, 0:96], scalar1=sm_[0:T, 1:2]),
                 reads=[nn, sn], writes=[f"gON{ci % 2}"])

        def g8(ci):
            tok0, T, nseq, seq0, tk = ctx(ci)
            S.op("pe", lambda e: e.transpose(out=P[7][0:96, 0:T], in_=ON[ci % 2][0:T, :], identity=c.ident[0:T, 0:T]),
                 reads=[f"gON{ci % 2}", "ident"], writes=["P7"])

        def g9(ci):
            tok0, T, nseq, seq0, tk = ctx(ci)
            S.op("act", lambda e: e.copy(out=EO[:, tk], in_=P[7][0:96, 0:T]), reads=["P7"], writes=["EO"])

        plan = [(g9, 8), (g8, 7), (g7, 6), (g6, 5), (g5, 4), (g4, 3), (g3, 2), (g2, 1), (g1, 0)]
        for it in range(NCH + 8):
            for fn, k in plan:
                ci = it - k
                if 0 <= ci < NCH:
                    fn(ci)
        drain(c, 99)
        S.op("dve", lambda e: e.scalar_tensor_tensor(out=YGh[:, 0, :], in0=EO, scalar=GNG[:, h:h + 1], in1=GGs, op0=ALU.mult, op1=ALU.mult),
             reads=["EO", "GGs", "glac"], writes=["YGh"])
        apply_wout(c, l, YGh, "YGh", 640 + 96 * h, 96, 1, defer=True)
        S.dma("sp", O["p_gla_S"][l, h], Sp[32 % 2], reads=[f"Sp{32 % 2}"])
        S.dma("sp", O["s_gla_S"][l][:, h].rearrange("i k v -> k i v"), Ss, reads=["Ss"])

    for h in range(4):
        g_head(h)
    drain(c, 99)
    S.barrier()
    A.release()


_NC_CACHE = {}


def kernel(**inputs):
    import os
    stage = inputs.pop("_stage", int(os.environ.get("KSTAGE", "99")))
    raw = inputs.pop("_raw", False)
    if stage not in _NC_CACHE:
        _NC_CACHE[stage] = build(stage)
    nc = _NC_CACHE[stage]
    f = lambda a: np.ascontiguousarray(np.asarray(a, dtype=np.float32))
    shared = {}
    for nm in ("ffn1_norm_g", "mix_norm_g", "ffn2_norm_g", "final_norm_g", "ffn1_w1", "ffn1_w3", "ffn1_w2",
               "ffn2_w1", "ffn2_w3", "ffn2_w2", "w_in", "w_out", "lru_conv_w", "lru_conv_b", "lru_wa", "lru_ba",
               "lru_wx", "lru_bx", "lru_lambda", "ml_conv_w", "ml_conv_b", "ml_wq", "ml_wk", "ml_wv", "ml_w_if",
               "ml_b_if", "ml_norm_g", "ml_skip", "gla_w_up", "gla_b_up", "gla_norm_g"):
        shared[nm] = f(inputs[nm])
    shared["meta"] = f(inputs["meta_tokens"])
    xp = f(inputs["x_prompt"])
    xs = f(inputs["x_sample"])
    st_names = ("state_lru_h", "state_lru_conv", "state_mlstm_C", "state_mlstm_n", "state_mlstm_m",
                "state_mlstm_conv", "state_gla_S")
    states = {nm: f(inputs[nm]) for nm in st_names}
    in_maps = []
    for ci in range(8):
        m = dict(shared)
        m["xp"] = xp[ci]
        m["xs"] = xs[ci * NSEQ:(ci + 1) * NSEQ].reshape(NSM, D)
        for nm in st_names:
            m[nm] = np.ascontiguousarray(states[nm][:, ci * NSEQ:(ci + 1) * NSEQ])
        in_maps.append(m)
    res = run_bass_kernel_spmd(nc, in_maps, core_ids=list(range(8)))
    R = res.results
    if raw:
        return R
    yp = np.stack([R[ci]["yp"] for ci in range(8)], 0)
    ys = np.concatenate([R[ci]["ys"].reshape(NSEQ, TS, D) for ci in range(8)], 0)
    outs = [yp, ys]
    onames = ("lru_h", "lru_conv", "mlstm_C", "mlstm_n", "mlstm_m", "mlstm_conv", "gla_S")
    for nm in onames:
        outs.append(np.stack([R[ci]["p_" + nm] for ci in range(8)], 1))
    for nm in onames:
        outs.append(np.concatenate([R[ci]["s_" + nm] for ci in range(8)], 1))
    return tuple(outs)
```

```python
import numpy as np
from contextlib import ExitStack
import concourse.bass as bass
import concourse.mybir as mybir
from concourse.bass_utils import run_bass_kernel_spmd

F32 = mybir.dt.float32
BF16 = mybir.dt.bfloat16
AF = mybir.ActivationFunctionType
ALU = mybir.AluOpType
AX = mybir.AxisListType

ENGS = ("pe", "act", "dve", "pool", "sp")

D = 1024
DEPTH = 2
NPR = 2064
NSM = 128
NT = NPR + NSM
NSEQ = 16
TS = 8
DFF = 2816
NF = DFF // 128
EPS = 1e-6
TILES = [(0, 512), (512, 512), (1024, 512), (1536, 512), (2048, 144)]
FGROUPS = [(0, 4), (4, 4), (8, 4), (12, 4), (16, 4), (20, 2)]
DIN = 2448


class Sched:
    def __init__(self, nc, n_dma_sems=10):
        self.nc = nc
        self.prog = {e: [] for e in ENGS}
        self.cnt = {}
        self.sems = {}
        self.seen = {e: {} for e in ENGS}
        self.last_w = {}
        self.readers = {}
        self.n_dma_sems = n_dma_sems
        self.dma_rr = {e: 0 for e in ENGS}
        self.n_ops = 0

    def open(self, stack):
        for e in ENGS:
            self.sems[e] = stack.enter_context(self.nc.semaphore(f"s_{e}"))
            self.cnt[e] = 0
        for q in ("sp", "pool", "act"):
            for i in range(self.n_dma_sems):
                k = f"d_{q}{i}"
                self.sems[k] = stack.enter_context(self.nc.semaphore(k))
                self.cnt[k] = 0

    CHILD = {"P0": ("P0k", "P0s"), "P1": ("P1k", "P1s"), "P6": ("P6v", "P6t"), "P7": ("P7v", "P7t")}

    def _expand(self, names):
        out = []
        for n in names:
            out.append(n)
            out.extend(self.CHILD.get(n, ()))
        return out

    def _deps(self, eng, reads, writes, skip_same_pe=False):
        deps = {}

        def add(ev):
            if ev is None:
                return
            k, v = ev
            if skip_same_pe and k == "pe":
                return
            if deps.get(k, 0) < v:
                deps[k] = v
        for r in reads:
            add(self.last_w.get(r))
            if r[0] == "P" and r[1:2].isdigit():
                for ev in self.readers.get(r, ()):
                    if ev[0] != eng:
                        add(ev)
        for w in writes:
            add(self.last_w.get(w))
            for ev in self.readers.get(w, ()):
                add(ev)
        out = []
        for k, v in deps.items():
            if self.seen[eng].get(k, 0) < v:
                self.seen[eng][k] = v
                out.append((k, v))
        return out

    def _commit(self, ev, reads, writes):
        for w in writes:
            self.last_w[w] = ev
            self.readers[w] = []
        for r in reads:
            if r in writes:
                continue
            self.readers.setdefault(r, []).append(ev)

    def op(self, eng, fn, reads=(), writes=()):
        reads, writes = self._expand(reads), self._expand(writes)
        waits = self._deps(eng, reads, writes, skip_same_pe=(eng == "pe"))
        self.cnt[eng] += 1
        ev = (eng, self.cnt[eng])
        sems = self.sems

        def emit(e, waits=waits, fn=fn, sem=sems[eng]):
            for k, v in waits:
                e.wait_ge(sems[k], v)
            fn(e).then_inc(sem, 1)
        self.prog[eng].append(emit)
        self._commit(ev, reads, writes)
        self.n_ops += 1
        return ev

    def dma(self, q, out, in_, reads=(), writes=(), **kw):
        i = self.dma_rr[q]
        self.dma_rr[q] = (i + 1) % self.n_dma_sems
        k = f"d_{q}{i}"
        reads, writes = self._expand(reads), self._expand(writes)
        waits = self._deps(q, reads, writes)
        prev = self.cnt[k]
        if prev and self.seen[q].get(k, 0) < prev:
            self.seen[q][k] = prev
            waits.append((k, prev))
        self.cnt[k] = prev + 16
        ev = (k, prev + 16)
        sems = self.sems

        def emit(e, waits=waits, sem=sems[k], out=out, in_=in_, kw=kw):
            for kk, v in waits:
                e.wait_ge(sems[kk], v)
            e.dma_start(out=out, in_=in_, **kw).then_inc(sem, 16)
        self.prog[q].append(emit)
        self._commit(ev, reads, writes)
        self.n_ops += 1
        return ev

    def barrier(self):
        final = {k: v for k, v in self.cnt.items() if v > 0}
        sems = self.sems
        for eng in ENGS:
            waits = [(k, v) for k, v in final.items() if self.seen[eng].get(k, 0) < v]
            for k, v in waits:
                self.seen[eng][k] = v

            def emit(e, waits=waits):
                for k, v in waits:
                    e.wait_ge(sems[k], v)
            self.prog[eng].append(emit)
        self.last_w = {}
        self.readers = {}

    def run(self, block):
        prog = self.prog

        @block.sync
        def _(e):
            for f in prog["sp"]:
                f(e)

        @block.tensor
        def _(e):
            for f in prog["pe"]:
                f(e)

        @block.scalar
        def _(e):
            for f in prog["act"]:
                f(e)

        @block.vector
        def _(e):
            for f in prog["dve"]:
                f(e)

        @block.gpsimd
        def _(e):
            for f in prog["pool"]:
                f(e)


class Arena:
    def __init__(self, ap, nwords):
        self.ap = ap
        self.n = nwords
        self.top = 0
        self.marks = []

    def f32(self, nwords, parts=128):
        req = nwords
        nwords = (nwords + 7) // 8 * 8
        assert self.top + nwords <= self.n, f"arena overflow {self.top}+{nwords}>{self.n}"
        v = self.ap[0:parts, self.top:self.top + req]
        self.top += nwords
        return v

    def bf16(self, nelem, parts=128):
        nw = (nelem + 1) // 2
        nw = (nw + 7) // 8 * 8
        assert self.top + nw <= self.n, f"arena overflow {self.top}+{nw}>{self.n}"
        v = self.ap[0:parts, self.top:self.top + nw].bitcast(BF16)
        self.top += nw
        return v[:, 0:nelem]

    def mark(self):
        self.marks.append(self.top)

    def release(self):
        self.top = self.marks.pop()


class Ctx:
    pass


def build(stage=99):
    nc = bass.Bass("TRN2", target_bir_lowering=False)
    c = Ctx()
    c.nc = nc
    import os
    c.debug = bool(os.environ.get("KDEBUG"))
    di = lambda name, shape: nc.dram_tensor(name, list(shape), F32, kind="ExternalInput").ap()
    do = lambda name, shape: nc.dram_tensor(name, list(shape), F32, kind="ExternalOutput").ap()
    I = {}
    I["xp"] = di("xp", [2048, D])
    I["xs"] = di("xs", [NSM, D])
    I["meta"] = di("meta", [16, D])
    for nm in ("ffn1_norm_g", "mix_norm_g", "ffn2_norm_g"):
        I[nm] = di(nm, [DEPTH, D])
    I["final_norm_g"] = di("final_norm_g", [D])
    for nm in ("ffn1_w1", "ffn1_w3", "ffn2_w1", "ffn2_w3"):
        I[nm] = di(nm, [DEPTH, D, DFF])
    for nm in ("ffn1_w2", "ffn2_w2"):
        I[nm] = di(nm, [DEPTH, DFF, D])
    I["w_in"] = di("w_in", [DEPTH, D, DIN])
    I["w_out"] = di("w_out", [DEPTH, D, D])
    for nm, shp in (("lru_conv_w", [4, 256]), ("lru_conv_b", [256]), ("lru_wa", [4, 64, 64]), ("lru_ba", [256]),
                    ("lru_wx", [4, 64, 64]), ("lru_bx", [256]), ("lru_lambda", [256]),
                    ("ml_conv_w", [4, 384]), ("ml_conv_b", [384]), ("ml_wq", [4, 96, 96]), ("ml_wk", [4, 96, 96]),
                    ("ml_wv", [4, 96, 96]), ("ml_w_if", [1152, 8]), ("ml_b_if", [8]), ("ml_norm_g", [384]),
                    ("ml_skip", [384]), ("gla_w_up", [16, 192]), ("gla_b_up", [192]), ("gla_norm_g", [384])):
        I[nm] = di(nm, [DEPTH] + shp)
    for nm, shp in (("state_lru_h", [256]), ("state_lru_conv", [3, 256]), ("state_mlstm_C", [4, 96, 96]),
                    ("state_mlstm_n", [4, 96]), ("state_mlstm_m", [4]), ("state_mlstm_conv", [3, 384]),
                    ("state_gla_S", [4, 48, 96])):
        I[nm] = di(nm, [DEPTH, NSEQ] + shp)
    O = {}
    for nm, shp in (("lru_h", [256]), ("lru_conv", [3, 256]), ("mlstm_C", [4, 96, 96]), ("mlstm_n", [4, 96]),
                    ("mlstm_m", [4]), ("mlstm_conv", [3, 384]), ("gla_S", [4, 48, 96])):
        O["p_" + nm] = do("p_" + nm, [DEPTH] + shp)
        O["s_" + nm] = do("s_" + nm, [DEPTH, NSEQ] + shp)
    O["yp"] = do("yp", [2048, D])
    O["ys"] = do("ys", [NSM, D])
    if stage < 0:
        O["dbg"] = do("dbg", [128, 6, 512])
    c.I, c.O = I, O

    with ExitStack() as st:
        S = Sched(nc)
        S.open(st)
        c.S = S
        NW = 212000 // 4
        arena_t = st.enter_context(nc.sbuf_tensor("arena", [128, NW], F32))
        A = Arena(arena_t[:], NW)
        c.A = A
        c.P = [st.enter_context(nc.psum_tensor(f"P{i}", [128, 512], F32))[:] for i in range(8)]
        block = st.enter_context(nc.Block())

        c.X = A.f32(8 * NT).rearrange("p (k t) -> p k t", k=8)
        c.XN = A.bf16(8 * NT).rearrange("p (k t) -> p k t", k=8)
        c.ident = A.f32(128)
        c.ones_bf = A.bf16(128)
        c.ident_bf = A.bf16(128)
        c.gains = A.f32(7 * 8).rearrange("p (n k) -> p n k", n=7)
        c.MASKC = A.f32(64)
        c.MASKB = A.f32(64)
        c.PM = A.f32(8)
        c.SEL = A.f32(4 * 96)
        c.SMASK = A.f32(128)
        c.SNEG = A.f32(128)
        c.sq = [A.bf16(512), A.bf16(512)]
        c.rstd = [A.f32(512), A.f32(512)]
        setup_consts(c)
        load_x(c)
        if stage < 0:
            c.dbg = A.f32(6 * 512).rearrange("p (n t) -> p n t", n=6)
            S.op("pool", lambda e: e.memset(c.dbg, 0.0), writes=["dbg"])
            ffn(c, 0, "ffn1", 0, dbg=True)
            S.barrier()
            S.dma("sp", O["dbg"], c.dbg, reads=["dbg"])
        for l in range(DEPTH):
            if stage >= 1 and stage not in (31, 32):
                ffn(c, l, "ffn1", 3 * l + 0)
            if stage >= 2:
                mixer(c, l, {31: 11, 32: 12}.get(stage, stage))
            if stage >= 3 and (stage < 10 or stage >= 40):
                ffn(c, l, "ffn2", 3 * l + 2)
            if 10 <= stage < 40:
                break
        store_y(c)
        S.barrier()
        S.run(block)
    return nc


def setup_consts(c):
    S, nc = c.S, c.nc
    S.op("pool", lambda e: e.memset(c.ident, 1.0), writes=["ident"])
    S.op("pool", lambda e: e.affine_select(out=c.ident, in_=c.ident, pattern=[[-1, 128]],
                                           compare_op=ALU.is_equal, fill=0.0, base=0,
                                           channel_multiplier=1), reads=["ident"], writes=["ident"])
    S.op("pool", lambda e: e.memset(c.ones_bf, 1.0), writes=["ones_bf"])
    S.op("dve", lambda e: e.tensor_copy(out=c.ident_bf, in_=c.ident), reads=["ident"], writes=["ident"])
    for t_ in (c.MASKC, c.MASKB, c.PM, c.SEL, c.SMASK):
        S.op("pool", lambda e, t_=t_: e.memset(t_, 1.0), writes=["consts"])
    S.op("pool", lambda e: e.memset(c.SNEG, 0.0), writes=["consts"])
    mc, mb_ = c.MASKC[0:64, :], c.MASKB[0:64, :].rearrange("p (i j) -> p i j", j=8)
    S.op("pool", lambda e: e.affine_select(out=mc, in_=mc, pattern=[[1, 64]], compare_op=ALU.is_ge, fill=0.0,
                                           base=0, channel_multiplier=-1), reads=["consts"], writes=["consts"])
    S.op("pool", lambda e: e.affine_select(out=mb_, in_=mb_, pattern=[[8, 8], [1, 8]], compare_op=ALU.is_ge,
                                           fill=0.0, base=0, channel_multiplier=-1), reads=["consts"], writes=["consts"])
    S.op("pool", lambda e: e.affine_select(out=mb_, in_=mb_, pattern=[[-8, 8], [0, 8]], compare_op=ALU.is_ge,
                                           fill=0.0, base=0, channel_multiplier=1), reads=["consts"], writes=["consts"])
    pm = c.PM[0:64, :]
    S.op("pool", lambda e: e.affine_select(out=pm, in_=pm, pattern=[[-8, 8]], compare_op=ALU.is_ge, fill=0.0,
                                           base=0, channel_multiplier=1), reads=["consts"], writes=["consts"])
    S.op("pool", lambda e: e.affine_select(out=pm, in_=pm, pattern=[[8, 8]], compare_op=ALU.is_ge, fill=0.0,
                                           base=7, channel_multiplier=-1), reads=["consts"], writes=["consts"])
    sel = c.SEL[0:4, :].rearrange("p (h m) -> p h m", h=4)
    S.op("pool", lambda e: e.affine_select(out=sel, in_=sel, pattern=[[1, 4], [0, 96]], compare_op=ALU.is_equal,
                                           fill=0.0, base=0, channel_multiplier=-1), reads=["consts"], writes=["consts"])
    S.op("pool", lambda e: e.memset(c.SMASK.rearrange("p (i j) -> p i j", j=8)[:, :, 0:1], 0.0),
         reads=["consts"], writes=["consts"])
    S.op("pool", lambda e: e.memset(c.SNEG.rearrange("p (i j) -> p i j", j=8)[:, :, 0:1], -1e30),
         reads=["consts"], writes=["consts"])
    names = ["ffn1_norm_g", "mix_norm_g", "ffn2_norm_g"]
    for l in range(DEPTH):
        for j, nm in enumerate(names):
            S.dma("sp", c.gains[:, 3 * l + j, :], c.I[nm][l].rearrange("(k p) -> p k", p=128),
                  writes=["gains"], allow_slow_non_contiguous=True)
    S.dma("sp", c.gains[:, 6, :], c.I["final_norm_g"].rearrange("(k p) -> p k", p=128),
          writes=["gains"], allow_slow_non_contiguous=True)


def load_x(c):
    S, A = c.S, c.A
    A.mark()
    stg = [A.f32(D), A.f32(D)]
    blocks = [("meta", 0, 16, 0)] + [("xp", i * 128, 128, 16 + i * 128) for i in range(16)] + [("xs", 0, 128, NPR)]
    for bi, (src, r0, n, t0) in enumerate(blocks):
        sg = stg[bi % 2]
        S.dma("sp", sg[0:n, :], c.I[src][r0:r0 + n, :], writes=[f"stg{bi % 2}"])
        for half in range(2):
            pb = c.P[6 + half]
            for kk in range(4):
                k = half * 4 + kk
                S.op("pe", lambda e, pb=pb, kk=kk, k=k, sg=sg, n=n: e.transpose(
                    out=pb[:, kk * 128:kk * 128 + n], in_=sg[0:n, k * 128:(k + 1) * 128], identity=c.ident[0:n, 0:n]),
                    reads=[f"stg{bi % 2}", "ident"], writes=[f"P{6 + half}"])
            eng = "dve" if half == 0 else "act"
            src_ap = pb.rearrange("p (k t) -> p k t", k=4)[:, :, 0:n]
            dst_ap = c.X[:, half * 4:half * 4 + 4, t0:t0 + n]
            if eng == "dve":
                S.op("dve", lambda e, s=src_ap, d=dst_ap: e.tensor_copy(out=d, in_=s),
                     reads=[f"P{6 + half}"], writes=[f"Xb{bi}"])
            else:
                S.op("act", lambda e, s=src_ap, d=dst_ap: e.copy(out=d, in_=s),
                     reads=[f"P{6 + half}"], writes=[f"Xb{bi}"])
    S.barrier()
    A.release()


def xres(m, ti):
    return f"X_{m}_{ti}"


def rmsnorm_to_xn(c, gi, tiles=None):
    S, A = c.S, c.A
    sq, rstd = c.sq, c.rstd
    n = 0
    for ti, (t0, tn) in enumerate(TILES):
        pb = c.P[6 + ti % 2]
        pbn = f"P{6 + ti % 2}"
        for k in range(8):
            b = n % 2
            n += 1
            S.op("act", lambda e, b=b, k=k, t0=t0, tn=tn: e.activation(
                out=sq[b][:, 0:tn], in_=c.X[:, k, t0:t0 + tn], func=AF.Square),
                reads=[xres(k, ti)], writes=[f"sq{b}"])
            S.op("pe", lambda e, b=b, k=k, tn=tn, pb=pb: e.matmul(
                pb[:, 0:tn], lhsT=c.ones_bf, rhs=sq[b][:, 0:tn], start=(k == 0), stop=(k == 7)),
                reads=[f"sq{b}", "ones_bf"], writes=[pbn])
        rb = rstd[ti % 2]
        rbn = f"rstd{ti % 2}"
        S.op("act", lambda e, rb=rb, pb=pb, tn=tn: e.activation(
            out=rb[:, 0:tn], in_=pb[:, 0:tn], func=AF.Sqrt, bias=EPS, scale=1.0 / D),
            reads=[pbn], writes=[rbn])
        S.op("dve", lambda e, rb=rb, tn=tn: e.reciprocal(out=rb[:, 0:tn], in_=rb[:, 0:tn]),
             reads=[rbn], writes=[rbn])
        for k in range(8):
            eng = "dve"
            S.op(eng, lambda e, rb=rb, k=k, t0=t0, tn=tn: e.scalar_tensor_tensor(
                out=c.XN[:, k, t0:t0 + tn], in0=c.X[:, k, t0:t0 + tn], scalar=c.gains[:, gi, k:k + 1],
                in1=rb[:, 0:tn], op0=ALU.mult, op1=ALU.mult),
                reads=[xres(k, ti), rbn, "gains"], writes=[f"XN_{ti}"])


def ffn(c, l, which, gi, dbg=False):
    S, A = c.S, c.A
    rmsnorm_to_xn(c, gi)
    A.mark()
    w1d = c.I[f"{which}_w1"][l].rearrange("(k p) f -> p k f", p=128)
    w3d = c.I[f"{which}_w3"][l].rearrange("(k p) f -> p k f", p=128)
    w2d = c.I[f"{which}_w2"][l].rearrange("(f p) m -> p f m", p=128)
    W1 = [A.bf16(8 * 512).rearrange("p (k f) -> p k f", k=8) for _ in range(2)]
    W3 = [A.bf16(8 * 512).rearrange("p (k f) -> p k f", k=8) for _ in range(2)]
    W2 = [A.bf16(4 * 1024).rearrange("p (f m) -> p f m", f=4) for _ in range(2)]
    G = [A.bf16(4 * 512).rearrange("p (f t) -> p f t", f=4) for _ in range(2)]
    SL = [A.f32(512) for _ in range(2)]

    def load(gidx):
        f0, F = FGROUPS[gidx]
        b = gidx % 2
        S.dma("pool", W1[b][:, :, 0:F * 128], w1d[:, :, f0 * 128:(f0 + F) * 128], writes=[f"W1_{b}"])
        S.dma("pool", W3[b][:, :, 0:F * 128], w3d[:, :, f0 * 128:(f0 + F) * 128], writes=[f"W3_{b}"])
        S.dma("pool", W2[b][:, 0:F, :], w2d[:, f0:f0 + F, :], writes=[f"W2_{b}"])

    load(0)
    if dbg:
        S.op("dve", lambda e: e.tensor_copy(out=c.dbg[:, 0, :], in_=c.XN[:, 0, 0:512]), reads=["XN_0"], writes=["dbg"])
        S.op("dve", lambda e: e.tensor_copy(out=c.dbg[:, 1, :], in_=W1[0][:, 0, :]), reads=["W1_0"], writes=["dbg"])
        S.op("dve", lambda e: e.tensor_copy(out=c.dbg[:, 2, :], in_=W2[0][:, 0, 0:512]), reads=["W2_0"], writes=["dbg"])
    it = 0
    pendingB = None
    nsl = 0
    for gidx, (f0, F) in enumerate(FGROUPS):
        b = gidx % 2
        for ti, (t0, tn) in enumerate(TILES):
            gb = it % 2
            for fi in range(F):
                hb = (it * 4 + fi) % 2
                p1, p3 = c.P[hb], c.P[2 + hb]
                for k in range(8):
                    S.op("pe", lambda e, p1=p1, k=k, fi=fi, b=b, t0=t0, tn=tn: e.matmul(
                        p1[:, 0:tn], lhsT=W1[b][:, k, fi * 128:(fi + 1) * 128], rhs=c.XN[:, k, t0:t0 + tn],
                        start=(k == 0), stop=(k == 7)),
                        reads=[f"W1_{b}", f"XN_{ti}"], writes=[f"P{hb}"])
                for k in range(8):
                    S.op("pe", lambda e, p3=p3, k=k, fi=fi, b=b, t0=t0, tn=tn: e.matmul(
                        p3[:, 0:tn], lhsT=W3[b][:, k, fi * 128:(fi + 1) * 128], rhs=c.XN[:, k, t0:t0 + tn],
                        start=(k == 0), stop=(k == 7)),
                        reads=[f"W3_{b}", f"XN_{ti}"], writes=[f"P{2 + hb}"])
                sb_ = nsl % 2
                nsl += 1
                S.op("act", lambda e, p1=p1, sb_=sb_, tn=tn: e.activation(
                    out=SL[sb_][:, 0:tn], in_=p1[:, 0:tn], func=AF.Silu),
                    reads=[f"P{hb}"], writes=[f"SL{sb_}"])
                S.op("dve", lambda e, p3=p3, sb_=sb_, gb=gb, fi=fi, tn=tn: e.tensor_tensor(
                    out=G[gb][:, fi, 0:tn], in0=p3[:, 0:tn], in1=SL[sb_][:, 0:tn], op=ALU.mult),
                    reads=[f"P{2 + hb}", f"SL{sb_}"], writes=[f"G{gb}_{fi}"])
                if dbg and it == 0 and fi == 0:
                    S.op("dve", lambda e, sb_=sb_: e.tensor_copy(out=c.dbg[:, 3, :], in_=SL[sb_]), reads=[f"SL{sb_}"], writes=["dbg"])
                    S.op("dve", lambda e, gb=gb: e.tensor_copy(out=c.dbg[:, 4, :], in_=G[gb][:, 0, :]), reads=[f"G{gb}_0"], writes=["dbg"])
                    S.op("dve", lambda e, p3=p3: e.tensor_copy(out=c.dbg[:, 5, :], in_=p3), reads=[f"P{2 + hb}"], writes=["dbg"])
            if pendingB is not None:
                pendingB()
            if ti == 0 and gidx + 1 < len(FGROUPS):
                load(gidx + 1)

            def phaseB(gb=gb, b=b, F=F, ti=ti, t0=t0, tn=tn, it=it):
                for m in range(8):
                    yb = m % 2
                    py = c.P[4 + yb]
                    for fi in range(F):
                        S.op("pe", lambda e, py=py, fi=fi, m=m: e.matmul(
                            py[:, 0:tn], lhsT=W2[b][:, fi, m * 128:(m + 1) * 128], rhs=G[gb][:, fi, 0:tn],
                            start=(fi == 0), stop=(fi == F - 1)),
                            reads=[f"W2_{b}", f"G{gb}_{fi}"], writes=[f"P{4 + yb}"])
                    S.op("dve", lambda e, py=py, m=m: e.scalar_tensor_tensor(
                        out=c.X[:, m, t0:t0 + tn], in0=py[:, 0:tn], scalar=0.5, in1=c.X[:, m, t0:t0 + tn],
                        op0=ALU.mult, op1=ALU.add),
                        reads=[f"P{4 + yb}", xres(m, ti)], writes=[xres(m, ti)])
            pendingB = phaseB
            it += 1
    pendingB()
    S.barrier()
    A.release()


def store_y(c):
    S, A = c.S, c.A
    A.mark()
    sq, rstd = c.sq, c.rstd
    YF = A.f32(8 * 512).rearrange("p (k t) -> p k t", k=8)
    ostg = [A.f32(D), A.f32(D)]
    nsq = 0
    nob = 0
    for ti, (t0, tn) in enumerate(TILES):
        pb = c.P[6 + ti % 2]
        pbn = f"P{6 + ti % 2}"
        for k in range(8):
            b = nsq % 2
            nsq += 1
            S.op("act", lambda e, b=b, k=k, t0=t0, tn=tn: e.activation(
                out=sq[b][:, 0:tn], in_=c.X[:, k, t0:t0 + tn], func=AF.Square),
                reads=[xres(k, ti)], writes=[f"sq{b}"])
            S.op("pe", lambda e, b=b, k=k, tn=tn, pb=pb: e.matmul(
                pb[:, 0:tn], lhsT=c.ones_bf, rhs=sq[b][:, 0:tn], start=(k == 0), stop=(k == 7)),
                reads=[f"sq{b}", "ones_bf"], writes=[pbn])
        rb = rstd[ti % 2]
        rbn = f"rstd{ti % 2}"
        S.op("act", lambda e, rb=rb, pb=pb, tn=tn: e.activation(
            out=rb[:, 0:tn], in_=pb[:, 0:tn], func=AF.Sqrt, bias=EPS, scale=1.0 / D),
            reads=[pbn], writes=[rbn])
        S.op("dve", lambda e, rb=rb, tn=tn: e.reciprocal(out=rb[:, 0:tn], in_=rb[:, 0:tn]),
             reads=[rbn], writes=[rbn])
        for k in range(8):
            eng = "dve"
            S.op(eng, lambda e, rb=rb, k=k, t0=t0, tn=tn: e.scalar_tensor_tensor(
                out=YF[:, k, 0:tn], in0=c.X[:, k, t0:t0 + tn], scalar=c.gains[:, 6, k:k + 1],
                in1=rb[:, 0:tn], op0=ALU.mult, op1=ALU.mult),
                reads=[xres(k, ti), rbn, "gains"], writes=["YF"])
        blks = []
        if ti < 4:
            for j in range(4):
                tok = t0 + j * 128
                lo = max(tok, 16)
                blks.append((lo - t0, tok + 128 - lo, c.O["yp"], lo - 16))
        else:
            blks.append((0, 16, c.O["yp"], 2032))
            blks.append((16, 128, c.O["ys"], 0))
        for (o0, n, dst, r0) in blks:
            ob = nob % 2
            nob += 1
            for half in range(2):
                pb2 = c.P[half]
                for kk in range(4):
                    k = half * 4 + kk
                    S.op("pe", lambda e, pb2=pb2, kk=kk, k=k, o0=o0, n=n: e.transpose(
                        out=pb2[0:n, kk * 128:(kk + 1) * 128], in_=YF[:, k, o0:o0 + n], identity=c.ident),
                        reads=["YF", "ident"], writes=[f"P{half}"])
                if half == 0:
                    S.op("dve", lambda e, pb2=pb2, ob=ob, n=n: e.tensor_copy(
                        out=ostg[ob][0:n, 0:512], in_=pb2[0:n, :]), reads=["P0"], writes=[f"ostg{ob}"])
                else:
                    S.op("act", lambda e, pb2=pb2, ob=ob, n=n: e.copy(
                        out=ostg[ob][0:n, 512:1024], in_=pb2[0:n, :]), reads=["P1"], writes=[f"ostg{ob}"])
            S.dma("sp", dst[r0:r0 + n, :], ostg[ob][0:n, :], reads=[f"ostg{ob}"])
    A.release()


NEX = 2246
NE = 2243
SMP0 = 2067
ETILES = [(0, 512), (512, 512), (1024, 512), (1536, 512), (2048, 195)]


def ext_in_views(U):
    return U[:, 3:3 + NPR], U[:, SMP0 + 3:SMP0 + 3 + 176].rearrange("p (i e) -> p i e", e=11)[:, :, 0:8]


def ext_out_views(V):
    return V[:, 0:NPR], V[:, SMP0:SMP0 + 176].rearrange("p (i e) -> p i e", e=11)[:, :, 0:8]


def dst_views(arr, layout, t0, tn):
    if layout == "norm":
        return [(0, tn, arr[:, t0:t0 + tn], False)]
    pr, sm = ext_in_views(arr) if layout == "ext_in" else ext_out_views(arr)
    if t0 + tn <= NPR:
        return [(0, tn, pr[:, t0:t0 + tn], False)]
    return [(0, NPR - t0, pr[:, t0:NPR], False), (NPR - t0, tn, sm, True)]


def pview(ps, c0, c1, strided):
    v = ps[:, c0:c1]
    return v.rearrange("p (i j) -> p i j", j=8) if strided else v


def proj_fm(c, W, wres, col0, M, evac):
    S = c.S
    for ti, (t0, tn) in enumerate(TILES):
        pi = c.pp % 2
        c.pp += 1
        ps = c.P[pi]
        for k in range(8):
            S.op("pe", lambda e, ps=ps, k=k, t0=t0, tn=tn: e.matmul(
                ps[0:M, 0:tn], lhsT=W[:, k, col0:col0 + M], rhs=c.XN[:, k, t0:t0 + tn],
                start=(k == 0), stop=(k == 7)), reads=[wres, f"XN_{ti}"], writes=[f"P{pi}"])
        evac(ti, t0, tn, ps[0:M, 0:tn], f"P{pi}")
        drain(c, 1)


def load_T(c, dram_ap, n, F, blocks):
    S = c.S
    S.dma("sp", c.stgT[0:n, 0:F], dram_ap, writes=["stgT"])
    for (col0, nb, dst, dres, j) in blocks:
        S.op("pe", lambda e, col0=col0, nb=nb: e.transpose(
            out=c.P[7][0:nb, 0:n], in_=c.stgT[0:n, col0:col0 + nb], identity=c.ident[0:n, 0:n]),
            reads=["stgT", "ident"], writes=["P7"])
        src = c.P[7][0:nb, 0:n]
        if j:
            src = src.rearrange("p (i j) -> p i j", j=j)
        S.op("act", lambda e, dst=dst, src=src: e.copy(out=dst, in_=src), reads=["P7"], writes=[dres])


def store_T(c, src_ap, sres, P_, n, dram_ap):
    S = c.S
    S.op("pe", lambda e: e.transpose(out=c.P[7][0:n, 0:P_], in_=src_ap, identity=c.ident[0:P_, 0:P_]),
         reads=[sres, "ident"], writes=["P7"])
    S.op("act", lambda e: e.copy(out=c.stgO[0:n, 0:P_], in_=c.P[7][0:n, 0:P_]), reads=["P7"], writes=["stgO"])
    S.dma("sp", dram_ap, c.stgO[0:n, 0:P_], reads=["stgO"])


def colvec(c, dst, dram_1d, pattern, res, **kw):
    c.S.dma("sp", dst, dram_1d.rearrange(pattern, **kw), writes=[res], allow_slow_non_contiguous=True)


def dbg(c, name, ap, reads):
    if not getattr(c, "debug", False):
        return
    d = c.nc.dram_tensor("dbg_" + name, list(ap.shape), F32, kind="ExternalOutput").ap()
    c.S.dma("sp", d, ap, reads=reads)


class CutHere(Exception):
    pass


def cut(c, n):
    import os
    if int(os.environ.get("KCUT", "0")) == n:
        raise CutHere()


def mixer(c, l, stage=99):
    S, A = c.S, c.A
    rmsnorm_to_xn(c, 3 * l + 1)
    A.mark()
    saved = (A.top, list(A.marks))
    c.pp = 0
    c.pending = []
    c.stgT = A.f32(512)
    c.stgO = A.f32(128)
    c.WO = A.bf16(2 * 1024).rearrange("p (k m) -> p k m", k=2)
    try:
        lru_group(c, l)
        if stage >= 11:
            mlstm_group(c, l)
        if stage >= 12:
            gla_group(c, l)
    except CutHere:
        A.top, A.marks = saved[0], saved[1]
    drain(c, 99)
    S.barrier()
    A.release()


def drain(c, n=1):
    while n > 0 and c.pending:
        c.pending.pop(0)()
        n -= 1


def apply_wout(c, l, Y, yres, row0, kp, nk, defer=False):
    S = c.S
    wod = c.I["w_out"][l]
    drain(c, 99)
    if nk == 1:
        c.wo_slot = (getattr(c, "wo_slot", 0) + 1) % 2
        slots = [c.wo_slot]
    else:
        slots = [0, 1]
    for j in range(nk):
        S.dma("pool", c.WO[0:kp, slots[j], :], wod[row0 + kp * j:row0 + kp * (j + 1), :], writes=[f"WO{slots[j]}"])

    def piece(ti, t0, tn):
        for m in range(8):
            yb = m % 2
            py = c.P[4 + yb]
            pn = f"P{4 + yb}"
            for j in range(nk):
                S.op("pe", lambda e, py=py, j=j, m=m: e.matmul(
                    py[:, 0:tn], lhsT=c.WO[0:kp, slots[j], m * 128:(m + 1) * 128], rhs=Y[:, j, t0:t0 + tn],
                    start=(j == 0), stop=(j == nk - 1)), reads=[f"WO{slots[j]}", yres], writes=[pn])
            S.op("dve", lambda e, py=py, m=m: e.tensor_tensor(
                out=c.X[:, m, t0:t0 + tn], in0=py[:, 0:tn], in1=c.X[:, m, t0:t0 + tn], op=ALU.add),
                reads=[pn, xres(m, ti)], writes=[xres(m, ti)])
    for ti, (t0, tn) in enumerate(TILES):
        if defer:
            c.pending.append(lambda ti=ti, t0=t0, tn=tn: piece(ti, t0, tn))
        else:
            piece(ti, t0, tn)


def lru_group(c, l):
    S, A, I, O = c.S, c.A, c.I, c.O
    A.mark()
    WIN = A.bf16(8 * 512).rearrange("p (k f) -> p k f", k=8)
    S.dma("pool", WIN, I["w_in"][l].rearrange("(k p) f -> p k f", p=128)[:, :, 0:512], writes=["WINlru"])
    YR = A.bf16(2 * NT).rearrange("p (k t) -> p k t", k=2)
    CW = A.f32(8).rearrange("p (c j) -> p c j", c=2)
    CB, BA, BX, LAM, CNEG = A.f32(8), A.f32(8), A.f32(8), A.f32(8), A.f32(8)
    WA = A.f32(256).rearrange("p (c m) -> p c m", c=2)
    WX = A.f32(256).rearrange("p (c m) -> p c m", c=2)
    H0 = A.f32(16)
    HL = A.f32(16)
    CS = A.f32(48)
    UE, GR, XR, AC, IG, T1 = A.f32(NEX), A.f32(NT), A.f32(NE), A.f32(NE), A.f32(NE), A.f32(NE)
    for cc in range(2):
        colvec(c, CW[:, cc, :], I["lru_conv_w"][l][:, cc * 128:(cc + 1) * 128], "j p -> p j", "lruc")
    colvec(c, CB[:, 0:2], I["lru_conv_b"][l], "(c p) -> p c", "lruc", p=128)
    colvec(c, BA[:, 0:2], I["lru_ba"][l], "(c p) -> p c", "lruc", p=128)
    colvec(c, BX[:, 0:2], I["lru_bx"][l], "(c p) -> p c", "lruc", p=128)
    colvec(c, LAM[:, 0:2], I["lru_lambda"][l], "(c p) -> p c", "lruc", p=128)
    S.op("pool", lambda e: e.memset(WA, 0.0), writes=["lruW"])
    S.op("pool", lambda e: e.memset(WX, 0.0), writes=["lruW"])
    for cc in range(2):
        for bb in range(2):
            n = 2 * cc + bb
            S.dma("sp", WA[64 * bb:64 * bb + 64, cc, 64 * bb:64 * bb + 64], I["lru_wa"][l, n], writes=["lruW"])
            S.dma("sp", WX[64 * bb:64 * bb + 64, cc, 64 * bb:64 * bb + 64], I["lru_wx"][l, n], writes=["lruW"])
    S.op("act", lambda e: e.activation(out=CNEG[:, 0:2], in_=LAM[:, 0:2], func=AF.Exp, scale=-1.0),
         reads=["lruc"], writes=["cneg"])
    S.op("act", lambda e: e.activation(out=CNEG[:, 0:2], in_=CNEG[:, 0:2], func=AF.Ln, bias=1.0),
         reads=["cneg"], writes=["cneg"])
    S.op("dve", lambda e: e.tensor_scalar_mul(out=CNEG[:, 0:2], in0=CNEG[:, 0:2], scalar1=-8.0),
         reads=["cneg"], writes=["cneg"])
    hist = UE[:, SMP0:SMP0 + 176].rearrange("p (i e) -> p i e", e=11)[:, :, 0:3]
    for cc in range(2):
        S.op("pool", lambda e: e.memset(UE, 0.0), writes=["UE"])
        load_T(c, I["state_lru_conv"][l].rearrange("i j c -> (i j) c"), 48, 256,
               [(cc * 128, 128, hist, "UE", 3)])
        load_T(c, I["state_lru_h"][l], 16, 256, [(cc * 128, 128, H0[:, 0:16], "H0", 0)])

        def ev_u(ti, t0, tn, ps, pres):
            for (c0, c1, dst, st_) in dst_views(UE, "ext_in", t0, tn):
                S.op("act", lambda e, s=pview(ps, c0, c1, st_), d=dst: e.copy(out=d, in_=s),
                     reads=[pres], writes=["UE"])
        proj_fm(c, WIN, "WINlru", cc * 128, 128, ev_u)

        def ev_g(ti, t0, tn, ps, pres):
            S.op("act", lambda e, ps=ps, t0=t0, tn=tn: e.activation(out=GR[:, t0:t0 + tn], in_=ps,
                                                                     func=AF.Gelu_apprx_tanh),
                 reads=[pres], writes=["GR"])
        proj_fm(c, WIN, "WINlru", 256 + cc * 128, 128, ev_g)
        S.op("dve", lambda e, cc=cc: e.tensor_scalar(out=XR, in0=UE[:, 0:NE], scalar1=CW[:, cc, 0:1],
                                                     scalar2=CB[:, cc:cc + 1], op0=ALU.mult, op1=ALU.add),
             reads=["UE", "lruc"], writes=["XR"])
        for j in range(1, 4):
            S.op("dve", lambda e, cc=cc, j=j: e.scalar_tensor_tensor(
                out=XR, in0=UE[:, j:NE + j], scalar=CW[:, cc, j:j + 1], in1=XR, op0=ALU.mult, op1=ALU.add),
                reads=["UE", "XR", "lruc"], writes=["XR"])
        for (t0, tn) in ETILES:
            for (Wm, bias, dst, dres) in ((WA, BA, AC, "AC"), (WX, BX, IG, "IG")):
                pi = c.pp % 2
                c.pp += 1
                ps = c.P[pi]
                S.op("pe", lambda e, ps=ps, Wm=Wm, cc=cc, t0=t0, tn=tn: e.matmul(
                    ps[:, 0:tn], lhsT=Wm[:, cc, :], rhs=XR[:, t0:t0 + tn], start=True, stop=True),
                    reads=["lruW", "XR"], writes=[f"P{pi}"])
                S.op("act", lambda e, ps=ps, bias=bias, dst=dst, cc=cc, t0=t0, tn=tn: e.activation(
                    out=dst[:, t0:t0 + tn], in_=ps[:, 0:tn], func=AF.Sigmoid, bias=bias[:, cc:cc + 1]),
                    reads=[f"P{pi}", "lruc"], writes=[dres])
        S.op("act", lambda e, cc=cc: e.activation(out=AC, in_=AC, func=AF.Exp, scale=CNEG[:, cc:cc + 1]),
             reads=["AC", "cneg"], writes=["AC"])
        S.op("pool", lambda e: e.tensor_tensor(out=T1, in0=IG, in1=XR, op=ALU.mult),
             reads=["IG", "XR"], writes=["T1"])
        S.op("dve", lambda e: e.tensor_tensor(out=IG, in0=AC, in1=AC, op=ALU.mult),
             reads=["AC", "T1"], writes=["IG"])
        S.op("dve", lambda e: e.tensor_scalar(out=IG, in0=IG, scalar1=-1.0, scalar2=1.0, op0=ALU.mult, op1=ALU.add),
             reads=["IG"], writes=["IG"])
        S.op("act", lambda e: e.activation(out=IG, in_=IG, func=AF.Sqrt), reads=["IG"], writes=["IG"])
        S.op("dve", lambda e: e.tensor_tensor(out=IG, in0=IG, in1=T1, op=ALU.mult),
             reads=["IG", "T1"], writes=["IG"])
        fix_a = AC[:, SMP0 - 1:SMP0 - 1 + 176].rearrange("p (i e) -> p i e", e=11)[:, :, 0]
        fix_b = IG[:, SMP0 - 1:SMP0 - 1 + 176].rearrange("p (i e) -> p i e", e=11)[:, :, 0]
        S.op("dve", lambda e, fa=fix_a: e.memset(fa, 0.0), reads=["AC"], writes=["AC"])
        S.op("dve", lambda e, fb=fix_b: e.tensor_copy(out=fb, in_=H0[:, 0:16]), reads=["IG", "H0"], writes=["IG"])
        S.op("dve", lambda e: e.tensor_tensor_scan(out=T1, data0=AC, data1=IG, initial=0.0,
                                                   op0=ALU.mult, op1=ALU.add),
             reads=["AC", "IG"], writes=["T1"])
        hp, hs = ext_out_views(T1)
        S.op("dve", lambda e, cc=cc, hp=hp: e.tensor_tensor(out=YR[:, cc, 0:NPR], in0=GR[:, 0:NPR], in1=hp, op=ALU.mult),
             reads=["GR", "T1"], writes=["YR"])
        S.op("dve", lambda e, cc=cc, hs=hs: e.tensor_tensor(
            out=YR[:, cc, NPR:NT].rearrange("p (i j) -> p i j", j=8),
            in0=GR[:, NPR:NT].rearrange("p (i j) -> p i j", j=8), in1=hs, op=ALU.mult),
            reads=["GR", "T1"], writes=["YR"])
        S.dma("sp", O["p_lru_h"][l, cc * 128:(cc + 1) * 128].rearrange("(p o) -> p o", o=1), T1[:, NPR - 1:NPR],
              reads=["T1"])
        S.op("pool", lambda e, hs=hs: e.tensor_copy(out=HL[:, 0:16], in_=hs[:, :, 7]), reads=["T1"], writes=["HL"])
        store_T(c, HL[:, 0:16], "HL", 128, 16, O["s_lru_h"][l][:, cc * 128:(cc + 1) * 128])
        up, us = ext_in_views(UE)
        S.dma("sp", O["p_lru_conv"][l][:, cc * 128:(cc + 1) * 128].rearrange("j p -> p j"), up[:, NPR - 3:NPR],
              reads=["UE"], allow_slow_non_contiguous=True)
        S.op("pool", lambda e, us=us: e.tensor_copy(out=CS[:, 0:48].rearrange("p (i j) -> p i j", j=3),
                                                    in_=us[:, :, 5:8]), reads=["UE"], writes=["CS"])
        store_T(c, CS[:, 0:48], "CS", 128, 48,
                O["s_lru_conv"][l].rearrange("i j c -> (i j) c")[:, cc * 128:(cc + 1) * 128])
    apply_wout(c, l, YR, "YR", 0, 128, 2)
    S.barrier()
    A.release()


CHUNKS = [(0, 16, 1, 0)] + [(16 + 64 * j, 64, 1, 0) for j in range(32)] + [(NPR, 64, 8, 0), (NPR + 64, 64, 8, 8)]
NCH = len(CHUNKS)
NGAM = 33 + NSEQ
QSCALE_M = 96.0 ** -0.5


def diag_view(ap, nblk, blk, rowlen):
    pstep, pn = ap.ap[0]
    return bass.AP(ap.tensor, ap.offset, [[pstep, pn], [rowlen + blk, nblk], [1, blk]])


def small_mm_tiles(c, Wm, wres, src, sres, M, evac):
    S = c.S
    for ti, (t0, tn) in enumerate(TILES):
        pi = c.pp % 2
        c.pp += 1
        ps = c.P[pi]
        S.op("pe", lambda e, ps=ps, t0=t0, tn=tn: e.matmul(ps[0:M, 0:tn], lhsT=Wm, rhs=src[:, t0:t0 + tn],
                                                          start=True, stop=True),
             reads=[wres, sres], writes=[f"P{pi}"])
        evac(ti, t0, tn, ps[0:M, 0:tn], f"P{pi}")


def mlstm_group(c, l):
    S, A, I, O = c.S, c.A, c.I, c.O
    A.mark()
    P = c.P
    w_in_l = I["w_in"][l].rearrange("(k p) f -> p k f", p=128)
    WQ = A.bf16(384).rearrange("p (h e) -> p h e", h=4)[0:96]
    WK = A.bf16(384).rearrange("p (h e) -> p h e", h=4)[0:96]
    WV = A.bf16(384).rearrange("p (h e) -> p h e", h=4)[0:96]
    WIF = A.bf16(96).rearrange("p (x g) -> p x g", g=8)[0:96]
    MCW = A.f32(16).rearrange("p (h j) -> p h j", h=4)[0:96]
    MCB, NG, SK = A.f32(8)[0:96], A.f32(8)[0:96], A.f32(8)[0:96]
    BI, NBF = A.f32(8)[0:4], A.f32(8)[0:4]
    M0T = A.f32(16)[0:4]
    AE = A.f32(56)[0:4]
    MNEW = A.f32(24)[0:4]
    GT = A.f32(NCH * 16).rearrange("p (c g) -> p c g", g=16)[0:64]
    GAM = A.f32(4 * NGAM).rearrange("p (h g) -> p h g", h=4)[0:96]
    UMbA = [A.bf16(NT)[0:96] for _ in range(4)]
    CMbA = [A.bf16(NT)[0:96] for _ in range(4)]
    CSs = A.f32(48)[0:96]
    A.mark()
    G8 = A.f32(NT)[0:8]
    for W_, nm in ((WQ, "ml_wq"), (WK, "ml_wk"), (WV, "ml_wv")):
        S.dma("pool", W_, I[nm][l].rearrange("h d e -> d h e"), writes=["mlW"])
    S.dma("pool", WIF, I["ml_w_if"][l].rearrange("(x d) g -> d x g", d=96), writes=["mlW"])
    for h in range(4):
        colvec(c, MCW[:, h, :], I["ml_conv_w"][l][:, h * 96:(h + 1) * 96], "j p -> p j", "mlc")
    colvec(c, MCB[:, 0:4], I["ml_conv_b"][l], "(h p) -> p h", "mlc", p=96)
    colvec(c, NG[:, 0:4], I["ml_norm_g"][l], "(h p) -> p h", "mlc", p=96)
    colvec(c, SK[:, 0:4], I["ml_skip"][l], "(h p) -> p h", "mlc", p=96)
    colvec(c, BI[:, 0:1], I["ml_b_if"][l][0:4], "(g o) -> g o", "mlc", o=1)
    colvec(c, NBF[:, 0:1], I["ml_b_if"][l][4:8], "(g o) -> g o", "mlc", o=1)
    colvec(c, M0T[:, 0:16], I["state_mlstm_m"][l], "i h -> h i", "mlc")
    S.op("dve", lambda e: e.tensor_scalar_mul(out=NBF[:, 0:1], in0=NBF[:, 0:1], scalar1=-1.0),
         reads=["mlc"], writes=["mlc"])

    cut(c, 10)

    def head_feats(h, B, need_v, part="both"):
        UMb_h, CMb_h = B.UMb, B.CMb
        if not need_v:
            for (Wm, dst, dres) in ((WQ[:, h, :], B.MQ, "MQ"), (WK[:, h, :], B.MK, "MK")):
                def ev2(ti, t0, tn, ps, pres, dst=dst, dres=dres):
                    if ti % 2 == 0:
                        S.op("act", lambda e: e.copy(out=dst[:, t0:t0 + tn], in_=ps), reads=[pres], writes=[dres])
                    else:
                        S.op("dve", lambda e: e.tensor_copy(out=dst[:, t0:t0 + tn], in_=ps), reads=[pres], writes=[dres])
                small_mm_tiles(c, Wm, "mlW", CMb_h, f"CMb{h}", 96, ev2)
            return
        if part == "ii":
            return head_feats_ii(h, B, UMb_h, CMb_h)
        S.dma("pool", B.WINh, w_in_l[:, :, 512 + 96 * h:512 + 96 * (h + 1)], writes=["WINh"])
        S.op("pool", lambda e: e.memset(B.UMx, 0.0), writes=["UMx"])
        hist = B.UMx[:, SMP0:SMP0 + 176].rearrange("p (i e) -> p i e", e=11)[:, :, 0:3]
        load_T(c, I["state_mlstm_conv"][l].rearrange("i j c -> (i j) c"), 48, 384, [(h * 96, 96, hist, "UMx", 3)])

        cut(c, 14)

        def ev_u(ti, t0, tn, ps, pres):
            for (c0, c1, dst, st_) in dst_views(B.UMx, "ext_in", t0, tn):
                S.op("act", lambda e, s_=pview(ps, c0, c1, st_), d=dst: e.copy(out=d, in_=s_),
                     reads=[pres], writes=["UMx"])
            S.op("act", lambda e, ps=ps, t0=t0, tn=tn: e.copy(out=UMb_h[:, t0:t0 + tn], in_=ps),
                 reads=[pres], writes=[f"UMb{h}"])
        proj_fm(c, B.WINh, "WINh", 0, 96, ev_u)
        cut(c, 11)
        cmp_, cms = B.CM[:, 0:NPR], B.CM[:, NPR:NT].rearrange("p (i j) -> p i j", j=8)
        for j in range(4):
            up = B.UMx[:, j:j + NPR]
            us = B.UMx[:, SMP0 + j:SMP0 + j + 176].rearrange("p (i e) -> p i e", e=11)[:, :, 0:8]
            for eng, src, dst in (("dve", up, cmp_), ("dve", us, cms)):
                if j == 0:
                    S.op(eng, lambda e, src=src, dst=dst: e.tensor_scalar(
                        out=dst, in0=src, scalar1=MCW[:, h, 0:1], scalar2=MCB[:, h:h + 1], op0=ALU.mult, op1=ALU.add),
                        reads=["UMx", "mlc"], writes=["CM"])
                else:
                    S.op("dve", lambda e, src=src, dst=dst, j=j: e.scalar_tensor_tensor(
                        out=dst, in0=src, scalar=MCW[:, h, j:j + 1], in1=dst, op0=ALU.mult, op1=ALU.add),
                        reads=["UMx", "CM", "mlc"], writes=["CM"])
        S.op("act", lambda e: e.activation(out=B.CM, in_=B.CM, func=AF.Silu), reads=["CM"], writes=["CM"])
        S.op("act", lambda e: e.copy(out=CMb_h, in_=B.CM), reads=["CM"], writes=[f"CMb{h}"])
        cut(c, 12)
        if part == "i":
            return
        head_feats_ii(h, B, UMb_h, CMb_h)

    def head_feats_ii(h, B, UMb_h, CMb_h):
        todo = [(WQ[:, h, :], CMb_h, f"CMb{h}", B.MQ, "MQ"), (WK[:, h, :], CMb_h, f"CMb{h}", B.MK, "MK"),
                (WV[:, h, :], UMb_h, f"UMb{h}", B.MV, "MV")]
        for (Wm, src, sres, dst, dres) in todo:
            def ev(ti, t0, tn, ps, pres, dst=dst, dres=dres):
                eng = "act" if ti % 2 == 0 else "dve"
                if eng == "act":
                    S.op("act", lambda e: e.copy(out=dst[:, t0:t0 + tn], in_=ps), reads=[pres], writes=[dres])
                else:
                    S.op("dve", lambda e: e.tensor_copy(out=dst[:, t0:t0 + tn], in_=ps), reads=[pres], writes=[dres])
            small_mm_tiles(c, Wm, "mlW", src, sres, 96, ev)
        cut(c, 13)

    class Bufs:
        pass

    A.mark()
    B = Bufs()
    B.WINh = A.bf16(8 * 96).rearrange("p (k f) -> p k f", k=8)
    B.UMx, B.CM = A.f32(NEX)[0:96], A.f32(NT)[0:96]
    B.MQ, B.MK, B.MV = A.bf16(NT)[0:96], A.bf16(NT)[0:96], A.bf16(NT)[0:96]
    def p1_i(h):
        B.UMb, B.CMb = UMbA[h], CMbA[h]
        head_feats(h, B, True, part="i")
        up, us = ext_in_views(B.UMx)
        S.dma("sp", O["p_mlstm_conv"][l][:, h * 96:(h + 1) * 96].rearrange("j p -> p j"), up[:, NPR - 3:NPR],
              reads=["UMx"], allow_slow_non_contiguous=True)
        S.op("dve", lambda e, us=us: e.tensor_copy(out=CSs[:, 0:48].rearrange("p (i j) -> p i j", j=3), in_=us[:, :, 5:8]),
             reads=["UMx"], writes=["CSs"])
        store_T(c, CSs[:, 0:48], "CSs", 96, 48, O["s_mlstm_conv"][l].rearrange("i j c -> (i j) c")[:, h * 96:(h + 1) * 96])

    def p1_ii(h):
        B.UMb, B.CMb = UMbA[h], CMbA[h]
        head_feats(h, B, True, part="ii")
        for ti, (t0, tn) in enumerate(TILES):
            ps = P[2 + ti % 2]
            for xi, (src, sres) in enumerate(((B.MQ, "MQ"), (B.MK, "MK"), (B.MV, "MV"))):
                S.op("pe", lambda e, ps=ps, xi=xi, src=src, t0=t0, tn=tn: e.matmul(
                    ps[0:8, 0:tn], lhsT=WIF[:, xi * 4 + h, :], rhs=src[:, t0:t0 + tn], start=(xi == 0), stop=(xi == 2)),
                    reads=["mlW", sres], writes=[f"P{2 + ti % 2}"])
            if h == 0:
                S.op("dve", lambda e, ps=ps, t0=t0, tn=tn: e.tensor_copy(out=G8[:, t0:t0 + tn], in_=ps[0:8, 0:tn]),
                     reads=[f"P{2 + ti % 2}"], writes=["G8"])
            else:
                S.op("dve", lambda e, ps=ps, t0=t0, tn=tn: e.tensor_tensor(
                    out=G8[:, t0:t0 + tn], in0=ps[0:8, 0:tn], in1=G8[:, t0:t0 + tn], op=ALU.add),
                    reads=[f"P{2 + ti % 2}", "G8"], writes=["G8"])
    p1_i(0)
    for h in range(4):
        if h + 1 < 4:
            p1_i(h + 1)
        p1_ii(h)
    import os
    KCUT = int(os.environ.get("KCUT", "0"))
    if KCUT == 1:
        S.barrier(); A.release(); A.release(); A.release()
        return
    if l == 0:
        dbg(c, "G8", G8, ["G8"])
        dbg(c, "CM3", B.CM, ["CM"])
        dbg(c, "MQ3", B.MQ, ["MQ"])
        dbg(c, "MV3", B.MV, ["MV"])
        dbg(c, "UMn3", UMbA[3], ["UMb3"])
    S.barrier()
    A.release()

    A.mark()
    RF, RB, RA, RG = [A.f32(NT)[0:4] for _ in range(4)]
    RR = G8[0:4, :]
    S.dma("sp", RF, G8[4:8, :], reads=["G8"], writes=["RF"])
    LI = G8[0:4, :]
    S.op("act", lambda e: e.activation(out=LI, in_=LI, func=AF.Identity, bias=BI[:, 0:1]),
         reads=["G8", "mlc", "RF"], writes=["G8"])
    S.op("act", lambda e: e.activation(out=RF, in_=RF, func=AF.Exp, scale=-1.0, bias=NBF[:, 0:1]),
         reads=["RF", "mlc"], writes=["RF"])
    S.op("act", lambda e: e.activation(out=RF, in_=RF, func=AF.Ln, bias=1.0), reads=["RF"], writes=["RF"])
    sm = lambda X_: X_[:, NPR:NT]
    pr = lambda X_: X_[:, 0:NPR]
    S.op("dve", lambda e: e.tensor_tensor_scan(out=pr(RB), data0=pr(RF), data1=pr(RF), initial=0.0,
                                               op0=ALU.add, op1=ALU.bypass), reads=["RF"], writes=["RB"])
    S.op("dve", lambda e: e.tensor_tensor_scan(out=sm(RB), data0=c.SMASK[0:4, :], data1=sm(RF), initial=0.0,
                                               op0=ALU.mult, op1=ALU.add), reads=["RF", "consts"], writes=["RB"])
    S.op("dve", lambda e: e.tensor_tensor(out=RA, in0=LI, in1=RB, op=ALU.add), reads=["G8", "RB"], writes=["RA"])
    S.op("dve", lambda e: e.tensor_copy(out=RG, in_=RA), reads=["RA"], writes=["RG"])
    S.op("dve", lambda e: e.tensor_scalar_max(out=RG[:, 0:1], in0=RG[:, 0:1], scalar1=0.0), reads=["RG"], writes=["RG"])
    st_v = lambda X_: X_[:, NPR:NT].rearrange("p (i j) -> p i j", j=8)
    S.op("dve", lambda e: e.tensor_tensor(out=st_v(RG)[:, :, 0], in0=st_v(RG)[:, :, 0], in1=M0T[:, 0:16], op=ALU.max),
         reads=["RG", "mlc"], writes=["RG"])
    S.op("dve", lambda e: e.tensor_tensor_scan(out=pr(RF), data0=pr(RG), data1=pr(RG), initial=-1e30,
                                               op0=ALU.max, op1=ALU.max), reads=["RG", "RB", "RA"], writes=["RF"])
    S.op("dve", lambda e: e.tensor_tensor_scan(out=sm(RF), data0=c.SNEG[0:4, :], data1=sm(RG), initial=-1e30,
                                               op0=ALU.add, op1=ALU.max), reads=["RG", "consts"], writes=["RF"])
    S.op("dve", lambda e: e.memset(RG[:, 0:16], 0.0), reads=["RF"], writes=["RG"])
    S.op("dve", lambda e: e.tensor_copy(
        out=RG[:, 16:NPR].rearrange("p (c t) -> p c t", t=64),
        in_=RF[:, 15:15 + 2048].rearrange("p (c t) -> p c t", t=64)[:, :, 0:1].to_broadcast([4, 32, 64])),
        reads=["RF"], writes=["RG"])
    S.op("dve", lambda e: e.tensor_copy(out=st_v(RG), in_=M0T[:, 0:16].unsqueeze(2).to_broadcast([4, 16, 8])),
         reads=["mlc"], writes=["RG"])
    S.op("dve", lambda e: e.tensor_tensor(out=RR, in0=RB, in1=RF, op=ALU.subtract), reads=["RB", "RF"], writes=["G8"])
    S.op("dve", lambda e: e.tensor_scalar_mul(out=MNEW[:, 0:1], in0=RR[:, NPR - 1:NPR], scalar1=-1.0),
         reads=["G8"], writes=["MNEW"])
    S.op("dve", lambda e: e.tensor_scalar_mul(out=MNEW[:, 1:17], in0=st_v(RR)[:, :, 7], scalar1=-1.0),
         reads=["G8"], writes=["MNEW"])
    S.dma("sp", O["p_mlstm_m"][l].rearrange("(h o) -> h o", o=1), MNEW[:, 0:1], reads=["MNEW"])
    S.dma("sp", O["s_mlstm_m"][l].rearrange("i h -> h i"), MNEW[:, 1:17], reads=["MNEW"], allow_slow_non_contiguous=True)
    S.op("act", lambda e: e.activation(out=RR, in_=RR, func=AF.Exp), reads=["G8", "MNEW"], writes=["G8"])
    S.op("dve", lambda e: e.tensor_copy(out=RB[:, 0:16], in_=RF[:, 15:16].to_broadcast([4, 16])), reads=["RF", "G8"], writes=["RB"])
    S.op("dve", lambda e: e.tensor_copy(
        out=RB[:, 16:NPR].rearrange("p (c t) -> p c t", t=64),
        in_=RF[:, 16:NPR].rearrange("p (c t) -> p c t", t=64)[:, :, 63:64].to_broadcast([4, 32, 64])),
        reads=["RF"], writes=["RB"])
    S.op("dve", lambda e: e.tensor_copy(out=st_v(RB), in_=st_v(RF)[:, :, 7:8].to_broadcast([4, 16, 8])), reads=["RF"], writes=["RB"])
    S.op("dve", lambda e: e.tensor_tensor(out=RB, in0=RA, in1=RB, op=ALU.subtract), reads=["RA", "RB"], writes=["RB"])
    S.op("act", lambda e: e.activation(out=RB, in_=RB, func=AF.Exp), reads=["RB"], writes=["RB"])
    S.op("dve", lambda e: e.tensor_tensor(out=RA, in0=RA, in1=RG, op=ALU.subtract), reads=["RA", "RG"], writes=["RA"])
    S.op("act", lambda e: e.activation(out=RA, in_=RA, func=AF.Exp), reads=["RA"], writes=["RA"])
    S.op("dve", lambda e: e.tensor_tensor(out=RG, in0=RG, in1=RF, op=ALU.subtract), reads=["RG", "RF", "RA"], writes=["RG"])
    S.op("act", lambda e: e.activation(out=RG, in_=RG, func=AF.Exp), reads=["RG"], writes=["RG"])
    S.op("dve", lambda e: e.tensor_copy(out=AE[:, 0:1], in_=RG[:, 15:16]), reads=["RG"], writes=["AE"])
    S.op("dve", lambda e: e.tensor_copy(out=AE[:, 1:33], in_=RG[:, 16:NPR].rearrange("p (c t) -> p c t", t=64)[:, :, 63]),
         reads=["RG"], writes=["AE"])
    S.op("dve", lambda e: e.tensor_copy(out=AE[:, 33:49], in_=st_v(RG)[:, :, 7]), reads=["RG"], writes=["AE"])
    S.op("dve", lambda e: e.tensor_scalar_mul(out=RG, in0=RG, scalar1=QSCALE_M), reads=["RG", "AE"], writes=["RG"])
    S.op("dve", lambda e: e.reciprocal(out=RF, in_=RG), reads=["RG", "RF"], writes=["RF"])
    S.op("dve", lambda e: e.tensor_tensor(out=RR, in0=RR, in1=RF, op=ALU.mult), reads=["G8", "RF"], writes=["G8"])
    for ci, (tok0, T, nseq, seq0) in enumerate(CHUNKS):
        for gi_, (Rw, rn) in enumerate(((RG, "RG"), (RA, "RA"), (RR, "G8"), (RB, "RB"))):
            S.op("pe", lambda e, Rw=Rw, gi_=gi_, tok0=tok0, T=T: e.transpose(
                out=P[6][0:T, gi_ * 4:gi_ * 4 + 4], in_=Rw[:, tok0:tok0 + T], identity=c.ident[0:4, 0:4]),
                reads=[rn, "ident"], writes=["P6"])
        S.op("act", lambda e, ci=ci, T=T: e.copy(out=GT[0:T, ci, :], in_=P[6][0:T, 0:16]), reads=["P6"], writes=["GT"])
    if l == 0:
        dbg(c, "alpha", RG, ["RG"])
        dbg(c, "beta", RA, ["RA"])
        dbg(c, "eps", RR, ["G8"])
        dbg(c, "G", RF, ["RF"])
        dbg(c, "negB", RB, ["RB"])
        dbg(c, "GT", GT, ["GT"])
    sel = c.SEL[0:4, :].rearrange("p (h m) -> p h m", h=4)
    for h in range(4):
        S.op("pe", lambda e, h=h: e.matmul(P[7][0:96, 0:NGAM], lhsT=sel[:, h, :], rhs=AE[:, 0:NGAM], start=True, stop=True),
             reads=["consts", "AE"], writes=["P7"])
        S.op("act", lambda e, h=h: e.copy(out=GAM[:, h, :], in_=P[7][0:96, 0:NGAM]), reads=["P7"], writes=["GAM"])
    S.barrier()
    A.release()
    A.release()
    if KCUT == 2:
        A.release()
        return

    A.mark()
    B = Bufs()
    WINz = A.bf16(8 * 96).rearrange("p (k f) -> p k f", k=8)
    B.MQ, B.MK = A.bf16(NT)[0:96], A.bf16(NT)[0:96]
    YMh = A.bf16(NT).rearrange("p (o t) -> p o t", o=1)[0:96]
    ZS = A.bf16(NT)[0:96]
    HNT = A.f32(NT)[0:96]
    CEp = A.f32(104)[0:96]
    CEb = [A.bf16(104)[0:96] for _ in range(2)]
    CE = A.f32(NSEQ * 97).rearrange("p (i e) -> p i e", e=97)[0:96]
    CEsb = A.bf16(NSEQ * 97).rearrange("p (i e) -> p i e", e=97)[0:96]
    QZ = A.bf16(8 * 64).rearrange("p (i t) -> p i t", i=8)[0:96]
    KTz = A.bf16(8 * 96).rearrange("p (i d) -> p i d", i=8)[0:64]
    VT = [A.bf16(104)[0:64] for _ in range(4)]
    KTb = [A.bf16(96)[0:64] for _ in range(4)]
    STm = [A.bf16(64)[0:64] for _ in range(4)]
    Hh = [A.f32(96)[0:64] for _ in range(4)]
    HN = [A.f32(96)[0:64] for _ in range(2)]
    JK = A.f32(96)[0:64]
    SMALL = [A.f32(8)[0:64] for _ in range(6)]
    S.op("pool", lambda e: e.memset(QZ, 0.0), writes=["QZ"])
    for par in range(4):
        S.op("pool", lambda e, par=par: e.memset(VT[par][:, 96:97], 1.0), writes=[f"VT{par}"])

    def p2_head(h):
        B.UMb, B.CMb = UMbA[h], CMbA[h]
        UMb_h, CMb_h = UMbA[h], CMbA[h]
        head_feats(h, B, False)
        S.dma("pool", WINz, w_in_l[:, :, 896 + 96 * h:896 + 96 * (h + 1)], writes=["WINz"])

        def ev_z(ti, t0, tn, ps, pres):
            S.op("act", lambda e: e.activation(out=ZS[:, t0:t0 + tn], in_=ps, func=AF.Sigmoid), reads=[pres], writes=["ZS"])
        proj_fm(c, WINz, "WINz", 0, 96, ev_z)
        S.op("pool", lambda e: e.memset(CEp[:, 0:97], 0.0), writes=["CEp"])
        S.op("pool", lambda e: e.memset(CEb[0][:, 0:97], 0.0), writes=["CEb0"])
        S.dma("sp", CE[:, :, 0:96], I["state_mlstm_C"][l][:, h].rearrange("i d e -> d i e"), writes=["CE"])
        S.dma("sp", CE[:, :, 96], I["state_mlstm_n"][l][:, h, :].rearrange("i d -> d i"), writes=["CE"],
              allow_slow_non_contiguous=True)
        S.op("dve", lambda e: e.tensor_copy(out=CEsb, in_=CE), reads=["CE"], writes=["CEsb"])
        def ctx(ci):
            tok0, T, nseq, seq0 = CHUNKS[ci]
            return tok0, T, nseq, seq0, slice(tok0, tok0 + T)

        def s1(ci):
            tok0, T, nseq, seq0, tk = ctx(ci)
            Pk, nk = P[ci % 2], f"P{ci % 2}"
            S.op("pe", lambda e: e.matmul(Pk[0:T, 0:96], lhsT=UMb_h[:, tk], rhs=WV[:, h, :], start=True, stop=True),
                 reads=[f"UMb{h}", "mlW"], writes=[nk])
            S.op("pe", lambda e: e.matmul(Pk[0:T, 96:192], lhsT=CMb_h[:, tk], rhs=WK[:, h, :], start=True, stop=True),
                 reads=[f"CMb{h}", "mlW"], writes=[nk])
            S.op("pe", lambda e: e.matmul(Pk[0:T, 192:192 + T], lhsT=B.MK[:, tk], rhs=B.MQ[:, tk], start=True, stop=True),
                 reads=["MK", "MQ"], writes=[nk])

        def s2(ci):
            tok0, T, nseq, seq0, tk = ctx(ci)
            b4 = ci % 4
            be, bg = GT[0:T, ci, 4 + h:5 + h], GT[0:T, ci, 12 + h:13 + h]
            mask = (c.MASKC if nseq == 1 else c.MASKB)[0:T, 0:T]
            Pk, nk = P[ci % 2], f"P{ci % 2}"
            S.op("dve", lambda e: e.tensor_copy(out=VT[b4][0:T, 0:96], in_=Pk[0:T, 0:96]), reads=[nk], writes=[f"VT{b4}"])
            S.op("dve", lambda e: e.tensor_scalar_mul(out=KTb[b4][0:T, :], in0=Pk[0:T, 96:192], scalar1=bg),
                 reads=[nk, "GT"], writes=[f"KTb{b4}"])
            S.op("dve", lambda e: e.scalar_tensor_tensor(out=STm[b4][0:T, 0:T], in0=Pk[0:T, 192:192 + T], scalar=be, in1=mask,
                                                         op0=ALU.mult, op1=ALU.mult),
                 reads=[nk, "GT", "consts"], writes=[f"STm{b4}"])

        def s3(ci):
            tok0, T, nseq, seq0, tk = ctx(ci)
            b4 = ci % 4
            if nseq == 1:
                Pd, nd = P[5 + ci % 2], f"P{5 + ci % 2}"
                S.op("pe", lambda e: e.matmul(Pd[0:96, 0:97], lhsT=KTb[b4][0:T, :], rhs=VT[b4][0:T, 0:97], start=True, stop=True),
                     reads=[f"KTb{b4}", f"VT{b4}"], writes=[nd])
            else:
                S.op("dve", lambda e: e.tensor_tensor(
                    out=KTz, in0=KTb[b4][0:64, :].unsqueeze(1).to_broadcast([64, 8, 96]),
                    in1=c.PM[0:64, :].unsqueeze(2).to_broadcast([64, 8, 96]), op=ALU.mult),
                    reads=[f"KTb{b4}", "consts"], writes=["KTz"])
                for half in range(2):
                    pb = P[5 + half]
                    for j in range(4):
                        S.op("pe", lambda e, pb=pb, j=j, half=half: e.matmul(
                            pb[0:96, j * 97:(j + 1) * 97], lhsT=KTz[:, 4 * half + j, :], rhs=VT[b4][0:64, 0:97],
                            start=True, stop=True), reads=["KTz", f"VT{b4}"], writes=[f"P{5 + half}"])

        def s4(ci):
            tok0, T, nseq, seq0, tk = ctx(ci)
            if nseq == 1:
                Pd, nd = P[5 + ci % 2], f"P{5 + ci % 2}"
                S.op("dve", lambda e: e.scalar_tensor_tensor(
                    out=CEp[:, 0:97], in0=CEp[:, 0:97], scalar=GAM[:, h, ci:ci + 1], in1=Pd[0:96, 0:97], op0=ALU.mult, op1=ALU.add),
                    reads=[nd, "CEp", "GAM"], writes=["CEp"])
            else:
                for half in range(2):
                    pb = P[5 + half]
                    s0 = seq0 + 4 * half
                    cev = CE[:, s0:s0 + 4, :]
                    S.op("dve", lambda e, cev=cev, s0=s0: e.tensor_tensor(
                        out=cev, in0=cev, in1=GAM[:, h, 33 + s0:33 + s0 + 4].unsqueeze(2).to_broadcast([96, 4, 97]), op=ALU.mult),
                        reads=["CE", "GAM"], writes=["CE"])
                    S.op("dve", lambda e, pb=pb, cev=cev: e.tensor_tensor(
                        out=cev, in0=pb[0:96, 0:388].rearrange("p (i e) -> p i e", e=97), in1=cev, op=ALU.add),
                        reads=[f"P{5 + half}", "CE"], writes=["CE"])

        def s5(ci):
            tok0, T, nseq, seq0, tk = ctx(ci)
            b4 = ci % 4
            Pn, nn = P[2 + ci % 3], f"P{2 + ci % 3}"
            S.op("pe", lambda e: e.matmul(Pn[0:T, 0:97], lhsT=STm[b4][0:T, 0:T], rhs=VT[b4][0:T, 0:97], start=True, stop=False),
                 reads=[f"STm{b4}", f"VT{b4}"], writes=[nn])
            if nseq == 1:
                S.op("pe", lambda e: e.matmul(Pn[0:T, 0:97], lhsT=B.MQ[:, tk], rhs=CEb[ci % 2][:, 0:97], start=False, stop=True),
                     reads=["MQ", f"CEb{ci % 2}"], writes=[nn])
                S.op("act", lambda e: e.copy(out=CEb[(ci + 1) % 2][:, 0:97], in_=CEp[:, 0:97]), reads=["CEp"], writes=[f"CEb{(ci + 1) % 2}"])
            else:
                S.op("act", lambda e: e.copy(out=diag_view(QZ, 8, 8, 64), in_=B.MQ[:, tk].rearrange("p (i j) -> p i j", j=8)),
                     reads=["MQ"], writes=["QZ"])
                for i in range(8):
                    S.op("pe", lambda e, i=i: e.matmul(Pn[0:64, 0:97], lhsT=QZ[:, i, :], rhs=CEsb[:, seq0 + i, :],
                                                       start=False, stop=(i == 7)),
                         reads=["QZ", "CEsb"], writes=[nn])

        def s6(ci):
            tok0, T, nseq, seq0, tk = ctx(ci)
            Pn, nn = P[2 + ci % 3], f"P{2 + ci % 3}"
            sm_ = SMALL[ci % 6]
            S.op("act", lambda e: e.activation(out=sm_[0:T, 0:1], in_=Pn[0:T, 96:97], func=AF.Abs), reads=[nn], writes=[f"SM{ci % 6}"])

        def s7(ci):
            tok0, T, nseq, seq0, tk = ctx(ci)
            epp = GT[0:T, ci, 8 + h:9 + h]
            Pn, nn = P[2 + ci % 3], f"P{2 + ci % 3}"
            sm_, sn, b4 = SMALL[ci % 6], f"SM{ci % 6}", ci % 4
            S.op("dve", lambda e: e.tensor_tensor(out=sm_[0:T, 0:1], in0=sm_[0:T, 0:1], in1=epp, op=ALU.max), reads=[sn, "GT"], writes=[sn])
            S.op("dve", lambda e: e.reciprocal(out=sm_[0:T, 0:1], in_=sm_[0:T, 0:1]), reads=[sn], writes=[sn])
            S.op("dve", lambda e: e.tensor_scalar_mul(out=Hh[b4][0:T, :], in0=Pn[0:T, 0:96], scalar1=sm_[0:T, 0:1]),
                 reads=[nn, sn], writes=[f"H{b4}"])

        def s8(ci):
            tok0, T, nseq, seq0, tk = ctx(ci)
            sm_, sn, b4 = SMALL[ci % 6], f"SM{ci % 6}", ci % 4
            S.op("act", lambda e: e.activation(out=JK[0:T, :], in_=Hh[b4][0:T, :], func=AF.Square, accum_out=sm_[0:T, 2:3]),
                 reads=[f"H{b4}", sn], writes=["JK", sn])
            S.op("act", lambda e: e.activation(out=sm_[0:T, 3:4], in_=sm_[0:T, 2:3], func=AF.Sqrt, bias=EPS, scale=1.0 / 96),
                 reads=[sn], writes=[sn])

        def s9(ci):
            tok0, T, nseq, seq0, tk = ctx(ci)
            sm_, sn = SMALL[ci % 6], f"SM{ci % 6}"
            S.op("dve", lambda e: e.reciprocal(out=sm_[0:T, 3:4], in_=sm_[0:T, 3:4]), reads=[sn], writes=[sn])

        def s10(ci):
            tok0, T, nseq, seq0, tk = ctx(ci)
            sm_, sn, b4 = SMALL[ci % 6], f"SM{ci % 6}", ci % 4
            S.op("act", lambda e: e.activation(out=HN[ci % 2][0:T, :], in_=Hh[b4][0:T, :], func=AF.Copy, scale=sm_[0:T, 3:4]),
                 reads=[f"H{b4}", sn], writes=[f"HN{ci % 2}"])

        def s11(ci):
            tok0, T, nseq, seq0, tk = ctx(ci)
            S.op("pe", lambda e: e.transpose(out=P[7][0:96, 0:T], in_=HN[ci % 2][0:T, :], identity=c.ident[0:T, 0:T]),
                 reads=[f"HN{ci % 2}", "ident"], writes=["P7"])

        def s12(ci):
            tok0, T, nseq, seq0, tk = ctx(ci)
            S.op("act", lambda e: e.copy(out=HNT[:, tk], in_=P[7][0:96, 0:T]), reads=["P7"], writes=["HNT"])

        stages = [s1, s2, s3, s4, s5, s6, s7, s8, s9, s10, s11, s12]
        order = [11, 10, 9, 8, 7, 6, 5, 4, 3, 2, 1, 0]
        for it in range(NCH + len(stages) - 1):
            for k in order:
                ci = it - k
                if 0 <= ci < NCH:
                    stages[k](ci)
        drain(c, 99)
        S.op("act", lambda e: e.activation(out=HNT, in_=HNT, func=AF.Copy, scale=NG[:, h:h + 1]), reads=["HNT", "mlc"], writes=["HNT"])
        S.op("dve", lambda e: e.scalar_tensor_tensor(out=HNT, in0=CMb_h, scalar=SK[:, h:h + 1], in1=HNT, op0=ALU.mult, op1=ALU.add),
             reads=["HNT", f"CMb{h}", "mlc"], writes=["HNT"])
        S.op("dve", lambda e: e.tensor_tensor(out=YMh[:, 0, :], in0=HNT, in1=ZS, op=ALU.mult), reads=["HNT", "ZS"], writes=["YMh"])
        apply_wout(c, l, YMh, "YMh", 256 + 96 * h, 96, 1, defer=True)
        S.dma("sp", O["p_mlstm_C"][l, h], CEp[:, 0:96], reads=["CEp"])
        S.dma("sp", O["p_mlstm_n"][l, h].rearrange("(d o) -> d o", o=1), CEp[:, 96:97], reads=["CEp"])
        S.dma("sp", O["s_mlstm_C"][l][:, h].rearrange("i d e -> d i e"), CE[:, :, 0:96], reads=["CE"])
        S.dma("sp", O["s_mlstm_n"][l][:, h, :].rearrange("i d -> d i"), CE[:, :, 96], reads=["CE"], allow_slow_non_contiguous=True)
    for h in range(4):
        p2_head(h)
    drain(c, 99)
    S.barrier()
    A.release()
    A.release()


QSCALE_G = 48.0 ** -0.5


def gla_group(c, l):
    S, A, I, O = c.S, c.A, c.I, c.O
    P = c.P
    A.mark()
    w_in_l = I["w_in"][l].rearrange("(k p) f -> p k f", p=128)
    WA_ = A.bf16(8 * 16).rearrange("p (k f) -> p k f", k=8)
    ALR = A.f32(NT)[0:16]
    WUP = A.f32(192)[0:16]
    NBUP, GNG = A.f32(8)[0:48], A.f32(8)[0:96]
    MCH = A.f32(NT)[0:48]
    GAMg = A.f32(NGAM + 7)[0:48]
    S.dma("pool", WA_, w_in_l[:, :, 2432:2448], writes=["WA_"])
    S.dma("sp", WUP, I["gla_w_up"][l], writes=["glac"])
    colvec(c, NBUP[:, 0:4], I["gla_b_up"][l], "(h k) -> k h", "glac", k=48)
    colvec(c, GNG[:, 0:4], I["gla_norm_g"][l], "(h p) -> p h", "glac", p=96)
    S.op("dve", lambda e: e.tensor_scalar_mul(out=NBUP[:, 0:4], in0=NBUP[:, 0:4], scalar1=-1.0), reads=["glac"], writes=["glac"])
    S.op("pool", lambda e: e.memset(MCH, 1.0), writes=["MCH"])
    S.op("pool", lambda e: e.memset(MCH[:, 0:1], 0.0), reads=["MCH"], writes=["MCH"])
    S.op("pool", lambda e: e.memset(MCH[:, 16:NPR].rearrange("p (c t) -> p c t", t=64)[:, :, 0:1], 0.0), reads=["MCH"], writes=["MCH"])
    S.op("pool", lambda e: e.memset(MCH[:, NPR:NT].rearrange("p (i j) -> p i j", j=8)[:, :, 0:1], 0.0), reads=["MCH"], writes=["MCH"])

    def ev_a(ti, t0, tn, ps, pres):
        S.op("act", lambda e: e.copy(out=ALR[:, t0:t0 + tn], in_=ps), reads=[pres], writes=["ALR"])
    proj_fm(c, WA_, "WA_", 0, 16, ev_a)

    WQg = A.bf16(8 * 48).rearrange("p (k f) -> p k f", k=8)
    WKg = A.bf16(8 * 48).rearrange("p (k f) -> p k f", k=8)
    WVg = A.bf16(8 * 96).rearrange("p (k f) -> p k f", k=8)
    WGg = A.bf16(8 * 96).rearrange("p (k f) -> p k f", k=8)
    BC, EO, QG, KG = A.f32(NT)[0:48], A.f32(NT)[0:96], A.f32(NT)[0:48], A.f32(NT)[0:48]
    VF = A.bf16(NT)[0:96]
    GGs = A.bf16(NT)[0:96]
    YGh = A.bf16(NT).rearrange("p (o t) -> p o t", o=1)[0:96]
    Sp = [A.f32(96)[0:48] for _ in range(2)]
    Ss = A.f32(NSEQ * 96).rearrange("p (i e) -> p i e", e=96)[0:48]
    QZ = A.f32(8 * 64).rearrange("p (i t) -> p i t", i=8)[0:48]
    KTz = A.bf16(8 * 48).rearrange("p (i d) -> p i d", i=8)[0:64]
    VT = [A.bf16(96)[0:64] for _ in range(4)]
    KT = [A.bf16(48)[0:64] for _ in range(4)]
    STm = [A.bf16(64)[0:64] for _ in range(4)]
    ON = [A.bf16(96)[0:64] for _ in range(2)]
    JK = A.f32(96)[0:64]
    SMALL = [A.f32(8)[0:64] for _ in range(3)]
    Pbf = [p.bitcast(BF16) for p in P]
    S.op("pool", lambda e: e.memset(QZ, 0.0), writes=["QZg"])
    st_v = lambda X_: X_[:, NPR:NT].rearrange("p (i j) -> p i j", j=8)

    def g_head(h):
        S.dma("pool", WQg, w_in_l[:, :, 1280 + 48 * h:1280 + 48 * (h + 1)], writes=["WQg"])
        S.dma("pool", WKg, w_in_l[:, :, 1472 + 48 * h:1472 + 48 * (h + 1)], writes=["WKg"])
        S.dma("pool", WVg, w_in_l[:, :, 1664 + 96 * h:1664 + 96 * (h + 1)], writes=["WVg"])
        S.dma("pool", WGg, w_in_l[:, :, 2048 + 96 * h:2048 + 96 * (h + 1)], writes=["WGg"])

        def ev_q(ti, t0, tn, ps, pres):
            S.op("act", lambda e: e.activation(out=QG[:, t0:t0 + tn], in_=ps, func=AF.Copy, scale=QSCALE_G), reads=[pres], writes=["QG"])
        proj_fm(c, WQg, "WQg", 0, 48, ev_q)

        def ev_k(ti, t0, tn, ps, pres):
            S.op("act", lambda e: e.copy(out=KG[:, t0:t0 + tn], in_=ps), reads=[pres], writes=["KG"])
        proj_fm(c, WKg, "WKg", 0, 48, ev_k)

        def ev_g(ti, t0, tn, ps, pres):
            S.op("act", lambda e: e.activation(out=GGs[:, t0:t0 + tn], in_=ps, func=AF.Silu), reads=[pres], writes=["GGs"])
        proj_fm(c, WGg, "WGg", 0, 96, ev_g)

        def ev_v(ti, t0, tn, ps, pres):
            if ti % 2 == 0:
                S.op("dve", lambda e: e.tensor_copy(out=VF[:, t0:t0 + tn], in_=ps), reads=[pres], writes=["VF"])
            else:
                S.op("act", lambda e: e.copy(out=VF[:, t0:t0 + tn], in_=ps), reads=[pres], writes=["VF"])
        proj_fm(c, WVg, "WVg", 0, 96, ev_v)

        def ev_l(ti, t0, tn, ps, pres):
            S.op("act", lambda e: e.activation(out=BC[:, t0:t0 + tn], in_=ps, func=AF.Exp, scale=-1.0, bias=NBUP[:, h:h + 1]),
                 reads=[pres, "glac"], writes=["BC"])
        small_mm_tiles(c, WUP[:, 48 * h:48 * (h + 1)], "glac", ALR, "ALR", 48, ev_l)
        S.op("act", lambda e: e.activation(out=BC, in_=BC, func=AF.Ln, bias=1.0), reads=["BC"], writes=["BC"])
        S.op("dve", lambda e: e.tensor_scalar_mul(out=EO[0:48, :], in0=BC, scalar1=-1.0 / 16.0), reads=["BC"], writes=["EO"])
        S.op("dve", lambda e: e.tensor_tensor_scan(out=BC, data0=MCH, data1=EO[0:48, :], initial=0.0, op0=ALU.mult, op1=ALU.add),
             reads=["EO", "MCH"], writes=["BC"])
        S.op("act", lambda e: e.activation(out=EO[0:48, :], in_=BC, func=AF.Exp), reads=["BC"], writes=["EO"])
        S.op("dve", lambda e: e.tensor_tensor(out=QG, in0=QG, in1=EO[0:48, :], op=ALU.mult), reads=["QG", "EO"], writes=["QG"])
        S.op("dve", lambda e: e.tensor_copy(out=GAMg[:, 0:1], in_=EO[0:48, 15:16]), reads=["EO"], writes=["GAMg"])
        S.op("dve", lambda e: e.tensor_copy(out=GAMg[:, 1:33], in_=EO[0:48, 16:NPR].rearrange("p (c t) -> p c t", t=64)[:, :, 63]),
             reads=["EO"], writes=["GAMg"])
        S.op("dve", lambda e: e.tensor_copy(out=GAMg[:, 33:49], in_=st_v(EO[0:48, :])[:, :, 7]), reads=["EO"], writes=["GAMg"])
        S.op("act", lambda e: e.activation(out=BC, in_=BC, func=AF.Exp, scale=-1.0), reads=["BC"], writes=["BC"])
        S.op("dve", lambda e: e.tensor_tensor(out=KG, in0=KG, in1=BC, op=ALU.mult), reads=["KG", "BC"], writes=["KG"])
        S.op("pool", lambda e: e.memset(Sp[1], 0.0), writes=["Sp1"])
        S.dma("sp", Ss, I["state_gla_S"][l][:, h].rearrange("i k v -> k i v"), writes=["Ss"])
        def ctx(ci):
            tok0, T, nseq, seq0 = CHUNKS[ci]
            return tok0, T, nseq, seq0, slice(tok0, tok0 + T)

        def g1(ci):
            tok0, T, nseq, seq0, tk = ctx(ci)
            Pa, na = P[ci % 2], f"P{ci % 2}"
            S.op("pe", lambda e: e.transpose(out=Pa[0:T, 0:48], in_=KG[:, tk], identity=c.ident[0:48, 0:48]),
                 reads=["KG", "ident"], writes=[na])
            S.op("pe", lambda e: e.matmul(Pa[0:T, 64:64 + T], lhsT=KG[:, tk], rhs=QG[:, tk], start=True, stop=True),
                 reads=["KG", "QG"], writes=[na])
            S.op("pe", lambda e: e.transpose(out=Pbf[ci % 2][0:T, 256:352], in_=VF[:, tk], identity=c.ident_bf[0:96, 0:96]),
                 reads=["VF", "ident"], writes=[na])

        def g2(ci):
            tok0, T, nseq, seq0, tk = ctx(ci)
            b4 = ci % 4
            mask = (c.MASKC if nseq == 1 else c.MASKB)[0:T, 0:T]
            Pa, na = P[ci % 2], f"P{ci % 2}"
            S.op("dve", lambda e: e.tensor_copy(out=VT[b4][0:T, :], in_=Pbf[ci % 2][0:T, 256:352]), reads=[na], writes=[f"gVT{b4}"])
            S.op("dve", lambda e: e.tensor_copy(out=KT[b4][0:T, :], in_=Pa[0:T, 0:48]), reads=[na], writes=[f"gKT{b4}"])
            S.op("dve", lambda e: e.tensor_tensor(out=STm[b4][0:T, 0:T], in0=Pa[0:T, 64:64 + T], in1=mask, op=ALU.mult),
                 reads=[na, "consts"], writes=[f"gST{b4}"])

        def g3(ci):
            tok0, T, nseq, seq0, tk = ctx(ci)
            b4 = ci % 4
            if nseq == 1:
                Pd, nd = P[5 + ci % 2], f"P{5 + ci % 2}"
                S.op("pe", lambda e: e.matmul(Pd[0:48, 0:96], lhsT=KT[b4][0:T, :], rhs=VT[b4][0:T, :], start=True, stop=True),
                     reads=[f"gKT{b4}", f"gVT{b4}"], writes=[nd])

        def g4(ci):
            tok0, T, nseq, seq0, tk = ctx(ci)
            if nseq == 1:
                Pd, nd = P[5 + ci % 2], f"P{5 + ci % 2}"
                S.op("dve", lambda e: e.tensor_tensor(out=Sp[ci % 2], in0=Pd[0:48, 0:96], in1=Sp[(ci + 1) % 2], op=ALU.add),
                     reads=[nd, f"Sp{(ci + 1) % 2}"], writes=[f"Sp{ci % 2}"])

        def g5(ci):
            tok0, T, nseq, seq0, tk = ctx(ci)
            b4 = ci % 4
            Pn, nn = P[2 + ci % 3], f"P{2 + ci % 3}"
            S.op("pe", lambda e: e.matmul(Pn[0:T, 0:96], lhsT=STm[b4][0:T, 0:T], rhs=VT[b4][0:T, :], start=True, stop=False),
                 reads=[f"gST{b4}", f"gVT{b4}"], writes=[nn])
            if nseq == 1:
                S.op("pe", lambda e: e.matmul(Pn[0:T, 0:96], lhsT=QG[:, tk], rhs=Sp[(ci + 1) % 2], start=False, stop=True),
                     reads=["QG", f"Sp{(ci + 1) % 2}"], writes=[nn])
                S.op("act", lambda e: e.activation(out=Sp[ci % 2], in_=Sp[ci % 2], func=AF.Copy, scale=GAMg[:, ci:ci + 1]),
                     reads=[f"Sp{ci % 2}", "GAMg"], writes=[f"Sp{ci % 2}"])
            else:
                S.op("act", lambda e: e.copy(out=diag_view(QZ, 8, 8, 64), in_=QG[:, tk].rearrange("p (i j) -> p i j", j=8)),
                     reads=["QG"], writes=["QZg"])
                for i in range(8):
                    S.op("pe", lambda e, i=i: e.matmul(Pn[0:64, 0:96], lhsT=QZ[:, i, :], rhs=Ss[:, seq0 + i, :], start=False, stop=(i == 7)),
                         reads=["QZg", "Ss"], writes=[nn])
                S.op("dve", lambda e: e.tensor_tensor(
                    out=KTz, in0=KT[b4][0:64, :].unsqueeze(1).to_broadcast([64, 8, 48]),
                    in1=c.PM[0:64, :].unsqueeze(2).to_broadcast([64, 8, 48]), op=ALU.mult),
                    reads=[f"gKT{b4}", "consts"], writes=["gKTz"])
                for half in range(2):
                    pb = P[5 + half]
                    for j in range(4):
                        S.op("pe", lambda e, pb=pb, j=j, half=half: e.matmul(
                            pb[0:48, j * 96:(j + 1) * 96], lhsT=KTz[:, 4 * half + j, :], rhs=VT[b4][0:64, :], start=True, stop=True),
                            reads=["gKTz", f"gVT{b4}"], writes=[f"P{5 + half}"])
                for half in range(2):
                    pb = P[5 + half]
                    s0 = seq0 + 4 * half
                    sv = Ss[:, s0:s0 + 4, :]
                    S.op("dve", lambda e, pb=pb, sv=sv: e.tensor_tensor(
                        out=sv, in0=pb[0:48, 0:384].rearrange("p (i e) -> p i e", e=96), in1=sv, op=ALU.add),
                        reads=[f"P{5 + half}", "Ss"], writes=["Ss"])
                    S.op("dve", lambda e, sv=sv, s0=s0: e.tensor_tensor(
                        out=sv, in0=sv, in1=GAMg[:, 33 + s0:33 + s0 + 4].unsqueeze(2).to_broadcast([48, 4, 96]), op=ALU.mult),
                        reads=["Ss", "GAMg"], writes=["Ss"])

        def g6(ci):
            tok0, T, nseq, seq0, tk = ctx(ci)
            Pn, nn = P[2 + ci % 3], f"P{2 + ci % 3}"
            sm_, sn = SMALL[ci % 3], f"gSM{ci % 3}"
            S.op("act", lambda e: e.activation(out=JK[0:T, :], in_=Pn[0:T, 0:96], func=AF.Square, accum_out=sm_[0:T, 0:1]),
                 reads=[nn], writes=["gJK", sn])
            S.op("act", lambda e: e.activation(out=sm_[0:T, 1:2], in_=sm_[0:T, 0:1], func=AF.Sqrt, bias=EPS, scale=1.0 / 96),
                 reads=[sn], writes=[sn])

        def g7(ci):
            tok0, T, nseq, seq0, tk = ctx(ci)
            Pn, nn = P[2 + ci % 3], f"P{2 + ci % 3}"
            sm_, sn = SMALL[ci % 3], f"gSM{ci % 3}"
            S.op("dve", lambda e: e.reciprocal(out=sm_[0:T, 1:2], in_=sm_[0:T, 1:2]), reads=[sn], writes=[sn])
            S.op("dve", lambda e: e.tensor_scalar_mul(out=ON[ci % 2][0:T, :], in0=Pn[0:T, 0:96], scalar1=sm_[0:T, 1:2]),
                 reads=[nn, sn], writes=[f"gON{ci % 2}"])

        def g8(ci):
            tok0, T, nseq, seq0, tk = ctx(ci)
            S.op("pe", lambda e: e.transpose(out=Pbf[7][0:96, 0:T], in_=ON[ci % 2][0:T, :], identity=c.ident_bf[0:T, 0:T]),
                 reads=[f"gON{ci % 2}", "ident"], writes=["P7"])

        def g9(ci):
            tok0, T, nseq, seq0, tk = ctx(ci)
            S.op("act", lambda e: e.copy(out=EO[:, tk], in_=Pbf[7][0:96, 0:T]), reads=["P7"], writes=["EO"])

        plan = [(g9, 8), (g8, 7), (g7, 6), (g6, 5), (g5, 4), (g4, 3), (g3, 2), (g2, 1), (g1, 0)]
        for it in range(NCH + 8):
            for fn, k in plan:
                ci = it - k
                if 0 <= ci < NCH:
                    fn(ci)
        drain(c, 99)
        S.op("dve", lambda e: e.scalar_tensor_tensor(out=YGh[:, 0, :], in0=EO, scalar=GNG[:, h:h + 1], in1=GGs, op0=ALU.mult, op1=ALU.mult),
             reads=["EO", "GGs", "glac"], writes=["YGh"])
        apply_wout(c, l, YGh, "YGh", 640 + 96 * h, 96, 1, defer=True)
        S.dma("sp", O["p_gla_S"][l, h], Sp[32 % 2], reads=[f"Sp{32 % 2}"])
        S.dma("sp", O["s_gla_S"][l][:, h].rearrange("i k v -> k i v"), Ss, reads=["Ss"])

    for h in range(4):
        g_head(h)
    drain(c, 99)
    S.barrier()
    A.release()


_NC_CACHE = {}


def kernel(**inputs):
    import os
    stage = inputs.pop("_stage", int(os.environ.get("KSTAGE", "99")))
    raw = inputs.pop("_raw", False)
    if stage not in _NC_CACHE:
        _NC_CACHE[stage] = build(stage)
    nc = _NC_CACHE[stage]
    f = lambda a: np.ascontiguousarray(np.asarray(a, dtype=np.float32))
    shared = {}
    for nm in ("ffn1_norm_g", "mix_norm_g", "ffn2_norm_g", "final_norm_g", "ffn1_w1", "ffn1_w3", "ffn1_w2",
               "ffn2_w1", "ffn2_w3", "ffn2_w2", "w_in", "w_out", "lru_conv_w", "lru_conv_b", "lru_wa", "lru_ba",
               "lru_wx", "lru_bx", "lru_lambda", "ml_conv_w", "ml_conv_b", "ml_wq", "ml_wk", "ml_wv", "ml_w_if",
               "ml_b_if", "ml_norm_g", "ml_skip", "gla_w_up", "gla_b_up", "gla_norm_g"):
        shared[nm] = f(inputs[nm])
    shared["meta"] = f(inputs["meta_tokens"])
    xp = f(inputs["x_prompt"])
    xs = f(inputs["x_sample"])
    st_names = ("state_lru_h", "state_lru_conv", "state_mlstm_C", "state_mlstm_n", "state_mlstm_m",
                "state_mlstm_conv", "state_gla_S")
    states = {nm: f(inputs[nm]) for nm in st_names}
    in_maps = []
    for ci in range(8):
        m = dict(shared)
        m["xp"] = xp[ci]
        m["xs"] = xs[ci * NSEQ:(ci + 1) * NSEQ].reshape(NSM, D)
        for nm in st_names:
            m[nm] = np.ascontiguousarray(states[nm][:, ci * NSEQ:(ci + 1) * NSEQ])
        in_maps.append(m)
    res = run_bass_kernel_spmd(nc, in_maps, core_ids=list(range(8)))
    R = res.results
    if raw:
        return R
    yp = np.stack([R[ci]["yp"] for ci in range(8)], 0)
    ys = np.concatenate([R[ci]["ys"].reshape(NSEQ, TS, D) for ci in range(8)], 0)
    outs = [yp, ys]
    onames = ("lru_h", "lru_conv", "mlstm_C", "mlstm_n", "mlstm_m", "mlstm_conv", "gla_S")
    for nm in onames:
        outs.append(np.stack([R[ci]["p_" + nm] for ci in range(8)], 1))
    for nm in onames:
        outs.append(np.concatenate([R[ci]["s_" + nm] for ci in range(8)], 1))
    return tuple(outs)
```

```python
import numpy as np
from contextlib import ExitStack
import concourse.bass as bass
import concourse.mybir as mybir
from concourse.bass_utils import run_bass_kernel_spmd

F32 = mybir.dt.float32
BF16 = mybir.dt.bfloat16
AF = mybir.ActivationFunctionType
ALU = mybir.AluOpType
AX = mybir.AxisListType

ENGS = ("pe", "act", "dve", "pool", "sp")

D = 1024
DEPTH = 2
NPR = 2064
NSM = 128
NT = NPR + NSM
NSEQ = 16
TS = 8
DFF = 2816
NF = DFF // 128
EPS = 1e-6
TILES = [(0, 512), (512, 512), (1024, 512), (1536, 512), (2048, 144)]
FGROUPS = [(0, 4), (4, 4), (8, 4), (12, 4), (16, 4), (20, 2)]
DIN = 2448


class Sched:
    def __init__(self, nc, n_dma_sems=10):
        self.nc = nc
        self.prog = {e: [] for e in ENGS}
        self.cnt = {}
        self.sems = {}
        self.seen = {e: {} for e in ENGS}
        self.last_w = {}
        self.readers = {}
        self.n_dma_sems = n_dma_sems
        self.dma_rr = {e: 0 for e in ENGS}
        self.n_ops = 0

    def open(self, stack):
        for e in ENGS:
            self.sems[e] = stack.enter_context(self.nc.semaphore(f"s_{e}"))
            self.cnt[e] = 0
        for q in ("sp", "pool", "act"):
            for i in range(self.n_dma_sems):
                k = f"d_{q}{i}"
                self.sems[k] = stack.enter_context(self.nc.semaphore(k))
                self.cnt[k] = 0

    CHILD = {"P0": ("P0k", "P0s"), "P1": ("P1k", "P1s"), "P6": ("P6v", "P6t"), "P7": ("P7v", "P7t")}

    def _expand(self, names):
        out = []
        for n in names:
            out.append(n)
            out.extend(self.CHILD.get(n, ()))
        return out

    def _deps(self, eng, reads, writes, skip_same_pe=False):
        deps = {}

        def add(ev):
            if ev is None:
                return
            k, v = ev
            if skip_same_pe and k == "pe":
                return
            if deps.get(k, 0) < v:
                deps[k] = v
        for r in reads:
            add(self.last_w.get(r))
            if r[0] == "P" and r[1:2].isdigit():
                for ev in self.readers.get(r, ()):
                    if ev[0] != eng:
                        add(ev)
        for w in writes:
            add(self.last_w.get(w))
            for ev in self.readers.get(w, ()):
                add(ev)
        out = []
        for k, v in deps.items():
            if self.seen[eng].get(k, 0) < v:
                self.seen[eng][k] = v
                out.append((k, v))
        return out

    def _commit(self, ev, reads, writes):
        for w in writes:
            self.last_w[w] = ev
            self.readers[w] = []
        for r in reads:
            if r in writes:
                continue
            self.readers.setdefault(r, []).append(ev)

    def op(self, eng, fn, reads=(), writes=()):
        reads, writes = self._expand(reads), self._expand(writes)
        waits = self._deps(eng, reads, writes, skip_same_pe=(eng == "pe"))
        self.cnt[eng] += 1
        ev = (eng, self.cnt[eng])
        sems = self.sems

        def emit(e, waits=waits, fn=fn, sem=sems[eng]):
            for k, v in waits:
                e.wait_ge(sems[k], v)
            fn(e).then_inc(sem, 1)
        self.prog[eng].append(emit)
        self._commit(ev, reads, writes)
        self.n_ops += 1
        return ev

    def dma(self, q, out, in_, reads=(), writes=(), **kw):
        i = self.dma_rr[q]
        self.dma_rr[q] = (i + 1) % self.n_dma_sems
        k = f"d_{q}{i}"
        reads, writes = self._expand(reads), self._expand(writes)
        waits = self._deps(q, reads, writes)
        prev = self.cnt[k]
        if prev and self.seen[q].get(k, 0) < prev:
            self.seen[q][k] = prev
            waits.append((k, prev))
        self.cnt[k] = prev + 16
        ev = (k, prev + 16)
        sems = self.sems

        def emit(e, waits=waits, sem=sems[k], out=out, in_=in_, kw=kw):
            for kk, v in waits:
                e.wait_ge(sems[kk], v)
            e.dma_start(out=out, in_=in_, **kw).then_inc(sem, 16)
        self.prog[q].append(emit)
        self._commit(ev, reads, writes)
        self.n_ops += 1
        return ev

    def barrier(self):
        final = {k: v for k, v in self.cnt.items() if v > 0}
        sems = self.sems
        for eng in ENGS:
            waits = [(k, v) for k, v in final.items() if self.seen[eng].get(k, 0) < v]
            for k, v in waits:
                self.seen[eng][k] = v

            def emit(e, waits=waits):
                for k, v in waits:
                    e.wait_ge(sems[k], v)
            self.prog[eng].append(emit)
        self.last_w = {}
        self.readers = {}

    def run(self, block):
        prog = self.prog

        @block.sync
        def _(e):
            for f in prog["sp"]:
                f(e)

        @block.tensor
        def _(e):
            for f in prog["pe"]:
                f(e)

        @block.scalar
        def _(e):
            for f in prog["act"]:
                f(e)

        @block.vector
        def _(e):
            for f in prog["dve"]:
                f(e)

        @block.gpsimd
        def _(e):
            for f in prog["pool"]:
                f(e)


class Arena:
    def __init__(self, ap, nwords):
        self.ap = ap
        self.n = nwords
        self.top = 0
        self.marks = []

    def f32(self, nwords, parts=128):
        req = nwords
        nwords = (nwords + 7) // 8 * 8
        assert self.top + nwords <= self.n, f"arena overflow {self.top}+{nwords}>{self.n}"
        v = self.ap[0:parts, self.top:self.top + req]
        self.top += nwords
        return v

    def bf16(self, nelem, parts=128):
        nw = (nelem + 1) // 2
        nw = (nw + 7) // 8 * 8
        assert self.top + nw <= self.n, f"arena overflow {self.top}+{nw}>{self.n}"
        v = self.ap[0:parts, self.top:self.top + nw].bitcast(BF16)
        self.top += nw
        return v[:, 0:nelem]

    def mark(self):
        self.marks.append(self.top)

    def release(self):
        self.top = self.marks.pop()


class Ctx:
    pass


def build(stage=99):
    nc = bass.Bass("TRN2", target_bir_lowering=False)
    c = Ctx()
    c.nc = nc
    import os
    c.debug = bool(os.environ.get("KDEBUG"))
    di = lambda name, shape: nc.dram_tensor(name, list(shape), F32, kind="ExternalInput").ap()
    do = lambda name, shape: nc.dram_tensor(name, list(shape), F32, kind="ExternalOutput").ap()
    I = {}
    I["xp"] = di("xp", [2048, D])
    I["xs"] = di("xs", [NSM, D])
    I["meta"] = di("meta", [16, D])
    for nm in ("ffn1_norm_g", "mix_norm_g", "ffn2_norm_g"):
        I[nm] = di(nm, [DEPTH, D])
    I["final_norm_g"] = di("final_norm_g", [D])
    for nm in ("ffn1_w1", "ffn1_w3", "ffn2_w1", "ffn2_w3"):
        I[nm] = di(nm, [DEPTH, D, DFF])
    for nm in ("ffn1_w2", "ffn2_w2"):
        I[nm] = di(nm, [DEPTH, DFF, D])
    I["w_in"] = di("w_in", [DEPTH, D, DIN])
    I["w_out"] = di("w_out", [DEPTH, D, D])
    for nm, shp in (("lru_conv_w", [4, 256]), ("lru_conv_b", [256]), ("lru_wa", [4, 64, 64]), ("lru_ba", [256]),
                    ("lru_wx", [4, 64, 64]), ("lru_bx", [256]), ("lru_lambda", [256]),
                    ("ml_conv_w", [4, 384]), ("ml_conv_b", [384]), ("ml_wq", [4, 96, 96]), ("ml_wk", [4, 96, 96]),
                    ("ml_wv", [4, 96, 96]), ("ml_w_if", [1152, 8]), ("ml_b_if", [8]), ("ml_norm_g", [384]),
                    ("ml_skip", [384]), ("gla_w_up", [16, 192]), ("gla_b_up", [192]), ("gla_norm_g", [384])):
        I[nm] = di(nm, [DEPTH] + shp)
    for nm, shp in (("state_lru_h", [256]), ("state_lru_conv", [3, 256]), ("state_mlstm_C", [4, 96, 96]),
                    ("state_mlstm_n", [4, 96]), ("state_mlstm_m", [4]), ("state_mlstm_conv", [3, 384]),
                    ("state_gla_S", [4, 48, 96])):
        I[nm] = di(nm, [DEPTH, NSEQ] + shp)
    O = {}
    for nm, shp in (("lru_h", [256]), ("lru_conv", [3, 256]), ("mlstm_C", [4, 96, 96]), ("mlstm_n", [4, 96]),
                    ("mlstm_m", [4]), ("mlstm_conv", [3, 384]), ("gla_S", [4, 48, 96])):
        O["p_" + nm] = do("p_" + nm, [DEPTH] + shp)
        O["s_" + nm] = do("s_" + nm, [DEPTH, NSEQ] + shp)
    O["yp"] = do("yp", [2048, D])
    O["ys"] = do("ys", [NSM, D])
    if stage < 0:
        O["dbg"] = do("dbg", [128, 6, 512])
    c.I, c.O = I, O

    with ExitStack() as st:
        S = Sched(nc)
        S.open(st)
        c.S = S
        NW = 212000 // 4
        arena_t = st.enter_context(nc.sbuf_tensor("arena", [128, NW], F32))
        A = Arena(arena_t[:], NW)
        c.A = A
        c.P = [st.enter_context(nc.psum_tensor(f"P{i}", [128, 512], F32))[:] for i in range(8)]
        block = st.enter_context(nc.Block())

        c.X = A.f32(8 * NT).rearrange("p (k t) -> p k t", k=8)
        c.XN = A.bf16(8 * NT).rearrange("p (k t) -> p k t", k=8)
        c.ident = A.f32(128)
        c.ones_bf = A.bf16(128)
        c.ident_bf = A.bf16(128)
        c.gains = A.f32(7 * 8).rearrange("p (n k) -> p n k", n=7)
        c.MASKC = A.f32(64)
        c.MASKB = A.f32(64)
        c.PM = A.f32(8)
        c.SEL = A.f32(4 * 96)
        c.SMASK = A.f32(128)
        c.SNEG = A.f32(128)
        c.sq = [A.bf16(512), A.bf16(512)]
        c.rstd = [A.f32(512), A.f32(512)]
        setup_consts(c)
        load_x(c)
        if stage < 0:
            c.dbg = A.f32(6 * 512).rearrange("p (n t) -> p n t", n=6)
            S.op("pool", lambda e: e.memset(c.dbg, 0.0), writes=["dbg"])
            ffn(c, 0, "ffn1", 0, dbg=True)
            S.barrier()
            S.dma("sp", O["dbg"], c.dbg, reads=["dbg"])
        for l in range(DEPTH):
            if stage >= 1 and stage not in (31, 32):
                ffn(c, l, "ffn1", 3 * l + 0)
            if stage >= 2:
                mixer(c, l, {31: 11, 32: 12}.get(stage, stage))
            if stage >= 3 and (stage < 10 or stage >= 40):
                ffn(c, l, "ffn2", 3 * l + 2)
            if 10 <= stage < 40:
                break
        store_y(c)
        S.barrier()
        S.run(block)
    return nc


def setup_consts(c):
    S, nc = c.S, c.nc
    S.op("pool", lambda e: e.memset(c.ident, 1.0), writes=["ident"])
    S.op("pool", lambda e: e.affine_select(out=c.ident, in_=c.ident, pattern=[[-1, 128]],
                                           compare_op=ALU.is_equal, fill=0.0, base=0,
                                           channel_multiplier=1), reads=["ident"], writes=["ident"])
    S.op("pool", lambda e: e.memset(c.ones_bf, 1.0), writes=["ones_bf"])
    S.op("dve", lambda e: e.tensor_copy(out=c.ident_bf, in_=c.ident), reads=["ident"], writes=["ident"])
    for t_ in (c.MASKC, c.MASKB, c.PM, c.SEL, c.SMASK):
        S.op("pool", lambda e, t_=t_: e.memset(t_, 1.0), writes=["consts"])
    S.op("pool", lambda e: e.memset(c.SNEG, 0.0), writes=["consts"])
    mc, mb_ = c.MASKC[0:64, :], c.MASKB[0:64, :].rearrange("p (i j) -> p i j", j=8)
    S.op("pool", lambda e: e.affine_select(out=mc, in_=mc, pattern=[[1, 64]], compare_op=ALU.is_ge, fill=0.0,
                                           base=0, channel_multiplier=-1), reads=["consts"], writes=["consts"])
    S.op("pool", lambda e: e.affine_select(out=mb_, in_=mb_, pattern=[[8, 8], [1, 8]], compare_op=ALU.is_ge,
                                           fill=0.0, base=0, channel_multiplier=-1), reads=["consts"], writes=["consts"])
    S.op("pool", lambda e: e.affine_select(out=mb_, in_=mb_, pattern=[[-8, 8], [0, 8]], compare_op=ALU.is_ge,
                                           fill=0.0, base=0, channel_multiplier=1), reads=["consts"], writes=["consts"])
    pm = c.PM[0:64, :]
    S.op("pool", lambda e: e.affine_select(out=pm, in_=pm, pattern=[[-8, 8]], compare_op=ALU.is_ge, fill=0.0,
                                           base=0, channel_multiplier=1), reads=["consts"], writes=["consts"])
    S.op("pool", lambda e: e.affine_select(out=pm, in_=pm, pattern=[[8, 8]], compare_op=ALU.is_ge, fill=0.0,
                                           base=7, channel_multiplier=-1), reads=["consts"], writes=["consts"])
    sel = c.SEL[0:4, :].rearrange("p (h m) -> p h m", h=4)
    S.op("pool", lambda e: e.affine_select(out=sel, in_=sel, pattern=[[1, 4], [0, 96]], compare_op=ALU.is_equal,
                                           fill=0.0, base=0, channel_multiplier=-1), reads=["consts"], writes=["consts"])
    S.op("pool", lambda e: e.memset(c.SMASK.rearrange("p (i j) -> p i j", j=8)[:, :, 0:1], 0.0),
         reads=["consts"], writes=["consts"])
    S.op("pool", lambda e: e.memset(c.SNEG.rearrange("p (i j) -> p i j", j=8)[:, :, 0:1], -1e30),
         reads=["consts"], writes=["consts"])
    names = ["ffn1_norm_g", "mix_norm_g", "ffn2_norm_g"]
    for l in range(DEPTH):
        for j, nm in enumerate(names):
            S.dma("sp", c.gains[:, 3 * l + j, :], c.I[nm][l].rearrange("(k p) -> p k", p=128),
                  writes=["gains"], allow_slow_non_contiguous=True)
    S.dma("sp", c.gains[:, 6, :], c.I["final_norm_g"].rearrange("(k p) -> p k", p=128),
          writes=["gains"], allow_slow_non_contiguous=True)


def load_x(c):
    S, A = c.S, c.A
    A.mark()
    stg = [A.f32(D), A.f32(D)]
    blocks = [("meta", 0, 16, 0)] + [("xp", i * 128, 128, 16 + i * 128) for i in range(16)] + [("xs", 0, 128, NPR)]
    for bi, (src, r0, n, t0) in enumerate(blocks):
        sg = stg[bi % 2]
        S.dma("sp", sg[0:n, :], c.I[src][r0:r0 + n, :], writes=[f"stg{bi % 2}"])
        for half in range(2):
            pb = c.P[6 + half]
            for kk in range(4):
                k = half * 4 + kk
                S.op("pe", lambda e, pb=pb, kk=kk, k=k, sg=sg, n=n: e.transpose(
                    out=pb[:, kk * 128:kk * 128 + n], in_=sg[0:n, k * 128:(k + 1) * 128], identity=c.ident[0:n, 0:n]),
                    reads=[f"stg{bi % 2}", "ident"], writes=[f"P{6 + half}"])
            eng = "dve" if half == 0 else "act"
            src_ap = pb.rearrange("p (k t) -> p k t", k=4)[:, :, 0:n]
            dst_ap = c.X[:, half * 4:half * 4 + 4, t0:t0 + n]
            if eng == "dve":
                S.op("dve", lambda e, s=src_ap, d=dst_ap: e.tensor_copy(out=d, in_=s),
                     reads=[f"P{6 + half}"], writes=[f"Xb{bi}"])
            else:
                S.op("act", lambda e, s=src_ap, d=dst_ap: e.copy(out=d, in_=s),
                     reads=[f"P{6 + half}"], writes=[f"Xb{bi}"])
    S.barrier()
    A.release()


def xres(m, ti):
    return f"X_{m}_{ti}"


def rmsnorm_to_xn(c, gi, tiles=None):
    S, A = c.S, c.A
    sq, rstd = c.sq, c.rstd
    n = 0
    for ti, (t0, tn) in enumerate(TILES):
        pb = c.P[6 + ti % 2]
        pbn = f"P{6 + ti % 2}"
        for k in range(8):
            b = n % 2
            n += 1
            S.op("act", lambda e, b=b, k=k, t0=t0, tn=tn: e.activation(
                out=sq[b][:, 0:tn], in_=c.X[:, k, t0:t0 + tn], func=AF.Square),
                reads=[xres(k, ti)], writes=[f"sq{b}"])
            S.op("pe", lambda e, b=b, k=k, tn=tn, pb=pb: e.matmul(
                pb[:, 0:tn], lhsT=c.ones_bf, rhs=sq[b][:, 0:tn], start=(k == 0), stop=(k == 7)),
                reads=[f"sq{b}", "ones_bf"], writes=[pbn])
        rb = rstd[ti % 2]
        rbn = f"rstd{ti % 2}"
        S.op("act", lambda e, rb=rb, pb=pb, tn=tn: e.activation(
            out=rb[:, 0:tn], in_=pb[:, 0:tn], func=AF.Sqrt, bias=EPS, scale=1.0 / D),
            reads=[pbn], writes=[rbn])
        S.op("dve", lambda e, rb=rb, tn=tn: e.reciprocal(out=rb[:, 0:tn], in_=rb[:, 0:tn]),
             reads=[rbn], writes=[rbn])
        for k in range(8):
            eng = "dve"
            S.op(eng, lambda e, rb=rb, k=k, t0=t0, tn=tn: e.scalar_tensor_tensor(
                out=c.XN[:, k, t0:t0 + tn], in0=c.X[:, k, t0:t0 + tn], scalar=c.gains[:, gi, k:k + 1],
                in1=rb[:, 0:tn], op0=ALU.mult, op1=ALU.mult),
                reads=[xres(k, ti), rbn, "gains"], writes=[f"XN_{ti}"])


def ffn(c, l, which, gi, dbg=False):
    S, A = c.S, c.A
    rmsnorm_to_xn(c, gi)
    A.mark()
    w1d = c.I[f"{which}_w1"][l].rearrange("(k p) f -> p k f", p=128)
    w3d = c.I[f"{which}_w3"][l].rearrange("(k p) f -> p k f", p=128)
    w2d = c.I[f"{which}_w2"][l].rearrange("(f p) m -> p f m", p=128)
    W1 = [A.bf16(8 * 512).rearrange("p (k f) -> p k f", k=8) for _ in range(2)]
    W3 = [A.bf16(8 * 512).rearrange("p (k f) -> p k f", k=8) for _ in range(2)]
    W2 = [A.bf16(4 * 1024).rearrange("p (f m) -> p f m", f=4) for _ in range(2)]
    G = [A.bf16(4 * 512).rearrange("p (f t) -> p f t", f=4) for _ in range(2)]
    SL = [A.f32(512) for _ in range(2)]

    def load(gidx):
        f0, F = FGROUPS[gidx]
        b = gidx % 2
        S.dma("pool", W1[b][:, :, 0:F * 128], w1d[:, :, f0 * 128:(f0 + F) * 128], writes=[f"W1_{b}"])
        S.dma("pool", W3[b][:, :, 0:F * 128], w3d[:, :, f0 * 128:(f0 + F) * 128], writes=[f"W3_{b}"])
        S.dma("pool", W2[b][:, 0:F, :], w2d[:, f0:f0 + F, :], writes=[f"W2_{b}"])

    load(0)
    if dbg:
        S.op("dve", lambda e: e.tensor_copy(out=c.dbg[:, 0, :], in_=c.XN[:, 0, 0:512]), reads=["XN_0"], writes=["dbg"])
        S.op("dve", lambda e: e.tensor_copy(out=c.dbg[:, 1, :], in_=W1[0][:, 0, :]), reads=["W1_0"], writes=["dbg"])
        S.op("dve", lambda e: e.tensor_copy(out=c.dbg[:, 2, :], in_=W2[0][:, 0, 0:512]), reads=["W2_0"], writes=["dbg"])
    it = 0
    pendingB = None
    nsl = 0
    for gidx, (f0, F) in enumerate(FGROUPS):
        b = gidx % 2
        for ti, (t0, tn) in enumerate(TILES):
            gb = it % 2
            for fi in range(F):
                hb = (it * 4 + fi) % 2
                p1, p3 = c.P[hb], c.P[2 + hb]
                for k in range(8):
                    S.op("pe", lambda e, p1=p1, k=k, fi=fi, b=b, t0=t0, tn=tn: e.matmul(
                        p1[:, 0:tn], lhsT=W1[b][:, k, fi * 128:(fi + 1) * 128], rhs=c.XN[:, k, t0:t0 + tn],
                        start=(k == 0), stop=(k == 7)),
                        reads=[f"W1_{b}", f"XN_{ti}"], writes=[f"P{hb}"])
                for k in range(8):
                    S.op("pe", lambda e, p3=p3, k=k, fi=fi, b=b, t0=t0, tn=tn: e.matmul(
                        p3[:, 0:tn], lhsT=W3[b][:, k, fi * 128:(fi + 1) * 128], rhs=c.XN[:, k, t0:t0 + tn],
                        start=(k == 0), stop=(k == 7)),
                        reads=[f"W3_{b}", f"XN_{ti}"], writes=[f"P{2 + hb}"])
                sb_ = nsl % 2
                nsl += 1
                S.op("act", lambda e, p1=p1, sb_=sb_, tn=tn: e.activation(
                    out=SL[sb_][:, 0:tn], in_=p1[:, 0:tn], func=AF.Silu),
                    reads=[f"P{hb}"], writes=[f"SL{sb_}"])
                S.op("dve", lambda e, p3=p3, sb_=sb_, gb=gb, fi=fi, tn=tn: e.tensor_tensor(
                    out=G[gb][:, fi, 0:tn], in0=p3[:, 0:tn], in1=SL[sb_][:, 0:tn], op=ALU.mult),
                    reads=[f"P{2 + hb}", f"SL{sb_}"], writes=[f"G{gb}_{fi}"])
                if dbg and it == 0 and fi == 0:
                    S.op("dve", lambda e, sb_=sb_: e.tensor_copy(out=c.dbg[:, 3, :], in_=SL[sb_]), reads=[f"SL{sb_}"], writes=["dbg"])
                    S.op("dve", lambda e, gb=gb: e.tensor_copy(out=c.dbg[:, 4, :], in_=G[gb][:, 0, :]), reads=[f"G{gb}_0"], writes=["dbg"])
                    S.op("dve", lambda e, p3=p3: e.tensor_copy(out=c.dbg[:, 5, :], in_=p3), reads=[f"P{2 + hb}"], writes=["dbg"])
            if pendingB is not None:
                pendingB()
            if ti == 0 and gidx + 1 < len(FGROUPS):
                load(gidx + 1)

            def phaseB(gb=gb, b=b, F=F, ti=ti, t0=t0, tn=tn, it=it):
                for m in range(8):
                    yb = m % 2
                    py = c.P[4 + yb]
                    for fi in range(F):
                        S.op("pe", lambda e, py=py, fi=fi, m=m: e.matmul(
                            py[:, 0:tn], lhsT=W2[b][:, fi, m * 128:(m + 1) * 128], rhs=G[gb][:, fi, 0:tn],
                            start=(fi == 0), stop=(fi == F - 1)),
                            reads=[f"W2_{b}", f"G{gb}_{fi}"], writes=[f"P{4 + yb}"])
                    S.op("dve", lambda e, py=py, m=m: e.scalar_tensor_tensor(
                        out=c.X[:, m, t0:t0 + tn], in0=py[:, 0:tn], scalar=0.5, in1=c.X[:, m, t0:t0 + tn],
                        op0=ALU.mult, op1=ALU.add),
                        reads=[f"P{4 + yb}", xres(m, ti)], writes=[xres(m, ti)])
            pendingB = phaseB
            it += 1
    pendingB()
    S.barrier()
    A.release()


def store_y(c):
    S, A = c.S, c.A
    A.mark()
    sq, rstd = c.sq, c.rstd
    YF = A.f32(8 * 512).rearrange("p (k t) -> p k t", k=8)
    ostg = [A.f32(D), A.f32(D)]
    nsq = 0
    nob = 0
    for ti, (t0, tn) in enumerate(TILES):
        pb = c.P[6 + ti % 2]
        pbn = f"P{6 + ti % 2}"
        for k in range(8):
            b = nsq % 2
            nsq += 1
            S.op("act", lambda e, b=b, k=k, t0=t0, tn=tn: e.activation(
                out=sq[b][:, 0:tn], in_=c.X[:, k, t0:t0 + tn], func=AF.Square),
                reads=[xres(k, ti)], writes=[f"sq{b}"])
            S.op("pe", lambda e, b=b, k=k, tn=tn, pb=pb: e.matmul(
                pb[:, 0:tn], lhsT=c.ones_bf, rhs=sq[b][:, 0:tn], start=(k == 0), stop=(k == 7)),
                reads=[f"sq{b}", "ones_bf"], writes=[pbn])
        rb = rstd[ti % 2]
        rbn = f"rstd{ti % 2}"
        S.op("act", lambda e, rb=rb, pb=pb, tn=tn: e.activation(
            out=rb[:, 0:tn], in_=pb[:, 0:tn], func=AF.Sqrt, bias=EPS, scale=1.0 / D),
            reads=[pbn], writes=[rbn])
        S.op("dve", lambda e, rb=rb, tn=tn: e.reciprocal(out=rb[:, 0:tn], in_=rb[:, 0:tn]),
             reads=[rbn], writes=[rbn])
        for k in range(8):
            eng = "dve"
            S.op(eng, lambda e, rb=rb, k=k, t0=t0, tn=tn: e.scalar_tensor_tensor(
                out=YF[:, k, 0:tn], in0=c.X[:, k, t0:t0 + tn], scalar=c.gains[:, 6, k:k + 1],
                in1=rb[:, 0:tn], op0=ALU.mult, op1=ALU.mult),
                reads=[xres(k, ti), rbn, "gains"], writes=["YF"])
        blks = []
        if ti < 4:
            for j in range(4):
                tok = t0 + j * 128
                lo = max(tok, 16)
                blks.append((lo - t0, tok + 128 - lo, c.O["yp"], lo - 16))
        else:
            blks.append((0, 16, c.O["yp"], 2032))
            blks.append((16, 128, c.O["ys"], 0))
        for (o0, n, dst, r0) in blks:
            ob = nob % 2
            nob += 1
            for half in range(2):
                pb2 = c.P[half]
                for kk in range(4):
                    k = half * 4 + kk
                    S.op("pe", lambda e, pb2=pb2, kk=kk, k=k, o0=o0, n=n: e.transpose(
                        out=pb2[0:n, kk * 128:(kk + 1) * 128], in_=YF[:, k, o0:o0 + n], identity=c.ident),
                        reads=["YF", "ident"], writes=[f"P{half}"])
                if half == 0:
                    S.op("dve", lambda e, pb2=pb2, ob=ob, n=n: e.tensor_copy(
                        out=ostg[ob][0:n, 0:512], in_=pb2[0:n, :]), reads=["P0"], writes=[f"ostg{ob}"])
                else:
                    S.op("act", lambda e, pb2=pb2, ob=ob, n=n: e.copy(
                        out=ostg[ob][0:n, 512:1024], in_=pb2[0:n, :]), reads=["P1"], writes=[f"ostg{ob}"])
            S.dma("sp", dst[r0:r0 + n, :], ostg[ob][0:n, :], reads=[f"ostg{ob}"])
    A.release()


NEX = 2246
NE = 2243
SMP0 = 2067
ETILES = [(0, 512), (512, 512), (1024, 512), (1536, 512), (2048, 195)]


def ext_in_views(U):
    return U[:, 3:3 + NPR], U[:, SMP0 + 3:SMP0 + 3 + 176].rearrange("p (i e) -> p i e", e=11)[:, :, 0:8]


def ext_out_views(V):
    return V[:, 0:NPR], V[:, SMP0:SMP0 + 176].rearrange("p (i e) -> p i e", e=11)[:, :, 0:8]


def dst_views(arr, layout, t0, tn):
    if layout == "norm":
        return [(0, tn, arr[:, t0:t0 + tn], False)]
    pr, sm = ext_in_views(arr) if layout == "ext_in" else ext_out_views(arr)
    if t0 + tn <= NPR:
        return [(0, tn, pr[:, t0:t0 + tn], False)]
    return [(0, NPR - t0, pr[:, t0:NPR], False), (NPR - t0, tn, sm, True)]


def pview(ps, c0, c1, strided):
    v = ps[:, c0:c1]
    return v.rearrange("p (i j) -> p i j", j=8) if strided else v


def proj_fm(c, W, wres, col0, M, evac):
    S = c.S
    for ti, (t0, tn) in enumerate(TILES):
        pi = c.pp % 2
        c.pp += 1
        ps = c.P[pi]
        for k in range(8):
            S.op("pe", lambda e, ps=ps, k=k, t0=t0, tn=tn: e.matmul(
                ps[0:M, 0:tn], lhsT=W[:, k, col0:col0 + M], rhs=c.XN[:, k, t0:t0 + tn],
                start=(k == 0), stop=(k == 7)), reads=[wres, f"XN_{ti}"], writes=[f"P{pi}"])
        evac(ti, t0, tn, ps[0:M, 0:tn], f"P{pi}")
        drain(c, 1)


def load_T(c, dram_ap, n, F, blocks):
    S = c.S
    S.dma("sp", c.stgT[0:n, 0:F], dram_ap, writes=["stgT"])
    for (col0, nb, dst, dres, j) in blocks:
        S.op("pe", lambda e, col0=col0, nb=nb: e.transpose(
            out=c.P[7][0:nb, 0:n], in_=c.stgT[0:n, col0:col0 + nb], identity=c.ident[0:n, 0:n]),
            reads=["stgT", "ident"], writes=["P7"])
        src = c.P[7][0:nb, 0:n]
        if j:
            src = src.rearrange("p (i j) -> p i j", j=j)
        S.op("act", lambda e, dst=dst, src=src: e.copy(out=dst, in_=src), reads=["P7"], writes=[dres])


def store_T(c, src_ap, sres, P_, n, dram_ap):
    S = c.S
    S.op("pe", lambda e: e.transpose(out=c.P[7][0:n, 0:P_], in_=src_ap, identity=c.ident[0:P_, 0:P_]),
         reads=[sres, "ident"], writes=["P7"])
    S.op("act", lambda e: e.copy(out=c.stgO[0:n, 0:P_], in_=c.P[7][0:n, 0:P_]), reads=["P7"], writes=["stgO"])
    S.dma("sp", dram_ap, c.stgO[0:n, 0:P_], reads=["stgO"])


def colvec(c, dst, dram_1d, pattern, res, **kw):
    c.S.dma("sp", dst, dram_1d.rearrange(pattern, **kw), writes=[res], allow_slow_non_contiguous=True)


def dbg(c, name, ap, reads):
    if not getattr(c, "debug", False):
        return
    d = c.nc.dram_tensor("dbg_" + name, list(ap.shape), F32, kind="ExternalOutput").ap()
    c.S.dma("sp", d, ap, reads=reads)


class CutHere(Exception):
    pass


def cut(c, n):
    import os
    if int(os.environ.get("KCUT", "0")) == n:
        raise CutHere()


def mixer(c, l, stage=99):
    S, A = c.S, c.A
    rmsnorm_to_xn(c, 3 * l + 1)
    A.mark()
    saved = (A.top, list(A.marks))
    c.pp = 0
    c.pending = []
    c.stgT = A.f32(512)
    c.stgO = A.f32(128)
    c.WO = A.bf16(2 * 1024).rearrange("p (k m) -> p k m", k=2)
    try:
        lru_group(c, l)
        if stage >= 11:
            mlstm_group(c, l)
        if stage >= 12:
            gla_group(c, l)
    except CutHere:
        A.top, A.marks = saved[0], saved[1]
    drain(c, 99)
    S.barrier()
    A.release()


def drain(c, n=1):
    while n > 0 and c.pending:
        c.pending.pop(0)()
        n -= 1


def apply_wout(c, l, Y, yres, row0, kp, nk, defer=False):
    S = c.S
    wod = c.I["w_out"][l]
    drain(c, 99)
    if nk == 1:
        c.wo_slot = (getattr(c, "wo_slot", 0) + 1) % 2
        slots = [c.wo_slot]
    else:
        slots = [0, 1]
    for j in range(nk):
        S.dma("pool", c.WO[0:kp, slots[j], :], wod[row0 + kp * j:row0 + kp * (j + 1), :], writes=[f"WO{slots[j]}"])

    def piece(ti, t0, tn):
        for m in range(8):
            yb = m % 2
            py = c.P[4 + yb]
            pn = f"P{4 + yb}"
            for j in range(nk):
                S.op("pe", lambda e, py=py, j=j, m=m: e.matmul(
                    py[:, 0:tn], lhsT=c.WO[0:kp, slots[j], m * 128:(m + 1) * 128], rhs=Y[:, j, t0:t0 + tn],
                    start=(j == 0), stop=(j == nk - 1)), reads=[f"WO{slots[j]}", yres], writes=[pn])
            S.op("dve", lambda e, py=py, m=m: e.tensor_tensor(
                out=c.X[:, m, t0:t0 + tn], in0=py[:, 0:tn], in1=c.X[:, m, t0:t0 + tn], op=ALU.add),
                reads=[pn, xres(m, ti)], writes=[xres(m, ti)])
    for ti, (t0, tn) in enumerate(TILES):
        if defer:
            c.pending.append(lambda ti=ti, t0=t0, tn=tn: piece(ti, t0, tn))
        else:
            piece(ti, t0, tn)


def lru_group(c, l):
    S, A, I, O = c.S, c.A, c.I, c.O
    A.mark()
    WIN = A.bf16(8 * 512).rearrange("p (k f) -> p k f", k=8)
    S.dma("pool", WIN, I["w_in"][l].rearrange("(k p) f -> p k f", p=128)[:, :, 0:512], writes=["WINlru"])
    YR = A.bf16(2 * NT).rearrange("p (k t) -> p k t", k=2)
    CW = A.f32(8).rearrange("p (c j) -> p c j", c=2)
    CB, BA, BX, LAM, CNEG = A.f32(8), A.f32(8), A.f32(8), A.f32(8), A.f32(8)
    WA = A.f32(256).rearrange("p (c m) -> p c m", c=2)
    WX = A.f32(256).rearrange("p (c m) -> p c m", c=2)
    H0 = A.f32(16)
    HL = A.f32(16)
    CS = A.f32(48)
    UE, GR, XR, AC, IG, T1 = A.f32(NEX), A.f32(NT), A.f32(NE), A.f32(NE), A.f32(NE), A.f32(NE)
    for cc in range(2):
        colvec(c, CW[:, cc, :], I["lru_conv_w"][l][:, cc * 128:(cc + 1) * 128], "j p -> p j", "lruc")
    colvec(c, CB[:, 0:2], I["lru_conv_b"][l], "(c p) -> p c", "lruc", p=128)
    colvec(c, BA[:, 0:2], I["lru_ba"][l], "(c p) -> p c", "lruc", p=128)
    colvec(c, BX[:, 0:2], I["lru_bx"][l], "(c p) -> p c", "lruc", p=128)
    colvec(c, LAM[:, 0:2], I["lru_lambda"][l], "(c p) -> p c", "lruc", p=128)
    S.op("pool", lambda e: e.memset(WA, 0.0), writes=["lruW"])
    S.op("pool", lambda e: e.memset(WX, 0.0), writes=["lruW"])
    for cc in range(2):
        for bb in range(2):
            n = 2 * cc + bb
            S.dma("sp", WA[64 * bb:64 * bb + 64, cc, 64 * bb:64 * bb + 64], I["lru_wa"][l, n], writes=["lruW"])
            S.dma("sp", WX[64 * bb:64 * bb + 64, cc, 64 * bb:64 * bb + 64], I["lru_wx"][l, n], writes=["lruW"])
    S.op("act", lambda e: e.activation(out=CNEG[:, 0:2], in_=LAM[:, 0:2], func=AF.Exp, scale=-1.0),
         reads=["lruc"], writes=["cneg"])
    S.op("act", lambda e: e.activation(out=CNEG[:, 0:2], in_=CNEG[:, 0:2], func=AF.Ln, bias=1.0),
         reads=["cneg"], writes=["cneg"])
    S.op("dve", lambda e: e.tensor_scalar_mul(out=CNEG[:, 0:2], in0=CNEG[:, 0:2], scalar1=-8.0),
         reads=["cneg"], writes=["cneg"])
    hist = UE[:, SMP0:SMP0 + 176].rearrange("p (i e) -> p i e", e=11)[:, :, 0:3]
    for cc in range(2):
        S.op("pool", lambda e: e.memset(UE, 0.0), writes=["UE"])
        load_T(c, I["state_lru_conv"][l].rearrange("i j c -> (i j) c"), 48, 256,
               [(cc * 128, 128, hist, "UE", 3)])
        load_T(c, I["state_lru_h"][l], 16, 256, [(cc * 128, 128, H0[:, 0:16], "H0", 0)])

        def ev_u(ti, t0, tn, ps, pres):
            for (c0, c1, dst, st_) in dst_views(UE, "ext_in", t0, tn):
                S.op("act", lambda e, s=pview(ps, c0, c1, st_), d=dst: e.copy(out=d, in_=s),
                     reads=[pres], writes=["UE"])
        proj_fm(c, WIN, "WINlru", cc * 128, 128, ev_u)

        def ev_g(ti, t0, tn, ps, pres):
            S.op("act", lambda e, ps=ps, t0=t0, tn=tn: e.activation(out=GR[:, t0:t0 + tn], in_=ps,
                                                                     func=AF.Gelu_apprx_tanh),
                 reads=[pres], writes=["GR"])
        proj_fm(c, WIN, "WINlru", 256 + cc * 128, 128, ev_g)
        S.op("dve", lambda e, cc=cc: e.tensor_scalar(out=XR, in0=UE[:, 0:NE], scalar1=CW[:, cc, 0:1],
                                                     scalar2=CB[:, cc:cc + 1], op0=ALU.mult, op1=ALU.add),
             reads=["UE", "lruc"], writes=["XR"])
        for j in range(1, 4):
            S.op("dve", lambda e, cc=cc, j=j: e.scalar_tensor_tensor(
                out=XR, in0=UE[:, j:NE + j], scalar=CW[:, cc, j:j + 1], in1=XR, op0=ALU.mult, op1=ALU.add),
                reads=["UE", "XR", "lruc"], writes=["XR"])
        for (t0, tn) in ETILES:
            for (Wm, bias, dst, dres) in ((WA, BA, AC, "AC"), (WX, BX, IG, "IG")):
                pi = c.pp % 2
                c.pp += 1
                ps = c.P[pi]
                S.op("pe", lambda e, ps=ps, Wm=Wm, cc=cc, t0=t0, tn=tn: e.matmul(
                    ps[:, 0:tn], lhsT=Wm[:, cc, :], rhs=XR[:, t0:t0 + tn], start=True, stop=True),
                    reads=["lruW", "XR"], writes=[f"P{pi}"])
                S.op("act", lambda e, ps=ps, bias=bias, dst=dst, cc=cc, t0=t0, tn=tn: e.activation(
                    out=dst[:, t0:t0 + tn], in_=ps[:, 0:tn], func=AF.Sigmoid, bias=bias[:, cc:cc + 1]),
                    reads=[f"P{pi}", "lruc"], writes=[dres])
        S.op("act", lambda e, cc=cc: e.activation(out=AC, in_=AC, func=AF.Exp, scale=CNEG[:, cc:cc + 1]),
             reads=["AC", "cneg"], writes=["AC"])
        S.op("pool", lambda e: e.tensor_tensor(out=T1, in0=IG, in1=XR, op=ALU.mult),
             reads=["IG", "XR"], writes=["T1"])
        S.op("dve", lambda e: e.tensor_tensor(out=IG, in0=AC, in1=AC, op=ALU.mult),
             reads=["AC", "T1"], writes=["IG"])
        S.op("dve", lambda e: e.tensor_scalar(out=IG, in0=IG, scalar1=-1.0, scalar2=1.0, op0=ALU.mult, op1=ALU.add),
             reads=["IG"], writes=["IG"])
        S.op("act", lambda e: e.activation(out=IG, in_=IG, func=AF.Sqrt), reads=["IG"], writes=["IG"])
        S.op("dve", lambda e: e.tensor_tensor(out=IG, in0=IG, in1=T1, op=ALU.mult),
             reads=["IG", "T1"], writes=["IG"])
        fix_a = AC[:, SMP0 - 1:SMP0 - 1 + 176].rearrange("p (i e) -> p i e", e=11)[:, :, 0]
        fix_b = IG[:, SMP0 - 1:SMP0 - 1 + 176].rearrange("p (i e) -> p i e", e=11)[:, :, 0]
        S.op("dve", lambda e, fa=fix_a: e.memset(fa, 0.0), reads=["AC"], writes=["AC"])
        S.op("dve", lambda e, fb=fix_b: e.tensor_copy(out=fb, in_=H0[:, 0:16]), reads=["IG", "H0"], writes=["IG"])
        S.op("dve", lambda e: e.tensor_tensor_scan(out=T1, data0=AC, data1=IG, initial=0.0,
                                                   op0=ALU.mult, op1=ALU.add),
             reads=["AC", "IG"], writes=["T1"])
        hp, hs = ext_out_views(T1)
        S.op("dve", lambda e, cc=cc, hp=hp: e.tensor_tensor(out=YR[:, cc, 0:NPR], in0=GR[:, 0:NPR], in1=hp, op=ALU.mult),
             reads=["GR", "T1"], writes=["YR"])
        S.op("dve", lambda e, cc=cc, hs=hs: e.tensor_tensor(
            out=YR[:, cc, NPR:NT].rearrange("p (i j) -> p i j", j=8),
            in0=GR[:, NPR:NT].rearrange("p (i j) -> p i j", j=8), in1=hs, op=ALU.mult),
            reads=["GR", "T1"], writes=["YR"])
        S.dma("sp", O["p_lru_h"][l, cc * 128:(cc + 1) * 128].rearrange("(p o) -> p o", o=1), T1[:, NPR - 1:NPR],
              reads=["T1"])
        S.op("pool", lambda e, hs=hs: e.tensor_copy(out=HL[:, 0:16], in_=hs[:, :, 7]), reads=["T1"], writes=["HL"])
        store_T(c, HL[:, 0:16], "HL", 128, 16, O["s_lru_h"][l][:, cc * 128:(cc + 1) * 128])
        up, us = ext_in_views(UE)
        S.dma("sp", O["p_lru_conv"][l][:, cc * 128:(cc + 1) * 128].rearrange("j p -> p j"), up[:, NPR - 3:NPR],
              reads=["UE"], allow_slow_non_contiguous=True)
        S.op("pool", lambda e, us=us: e.tensor_copy(out=CS[:, 0:48].rearrange("p (i j) -> p i j", j=3),
                                                    in_=us[:, :, 5:8]), reads=["UE"], writes=["CS"])
        store_T(c, CS[:, 0:48], "CS", 128, 48,
                O["s_lru_conv"][l].rearrange("i j c -> (i j) c")[:, cc * 128:(cc + 1) * 128])
    apply_wout(c, l, YR, "YR", 0, 128, 2)
    S.barrier()
    A.release()


CHUNKS = [(0, 16, 1, 0)] + [(16 + 64 * j, 64, 1, 0) for j in range(32)] + [(NPR, 64, 8, 0), (NPR + 64, 64, 8, 8)]
NCH = len(CHUNKS)
NGAM = 33 + NSEQ
QSCALE_M = 96.0 ** -0.5


def diag_view(ap, nblk, blk, rowlen):
    pstep, pn = ap.ap[0]
    return bass.AP(ap.tensor, ap.offset, [[pstep, pn], [rowlen + blk, nblk], [1, blk]])


def small_mm_tiles(c, Wm, wres, src, sres, M, evac):
    S = c.S
    for ti, (t0, tn) in enumerate(TILES):
        pi = c.pp % 2
        c.pp += 1
        ps = c.P[pi]
        S.op("pe", lambda e, ps=ps, t0=t0, tn=tn: e.matmul(ps[0:M, 0:tn], lhsT=Wm, rhs=src[:, t0:t0 + tn],
                                                          start=True, stop=True),
             reads=[wres, sres], writes=[f"P{pi}"])
        evac(ti, t0, tn, ps[0:M, 0:tn], f"P{pi}")


def mlstm_group(c, l):
    S, A, I, O = c.S, c.A, c.I, c.O
    A.mark()
    P = c.P
    w_in_l = I["w_in"][l].rearrange("(k p) f -> p k f", p=128)
    WQ = A.bf16(384).rearrange("p (h e) -> p h e", h=4)[0:96]
    WK = A.bf16(384).rearrange("p (h e) -> p h e", h=4)[0:96]
    WV = A.bf16(384).rearrange("p (h e) -> p h e", h=4)[0:96]
    WIF = A.bf16(96).rearrange("p (x g) -> p x g", g=8)[0:96]
    MCW = A.f32(16).rearrange("p (h j) -> p h j", h=4)[0:96]
    MCB, NG, SK = A.f32(8)[0:96], A.f32(8)[0:96], A.f32(8)[0:96]
    BI, NBF = A.f32(8)[0:4], A.f32(8)[0:4]
    M0T = A.f32(16)[0:4]
    AE = A.f32(56)[0:4]
    MNEW = A.f32(24)[0:4]
    GT = A.f32(NCH * 16).rearrange("p (c g) -> p c g", g=16)[0:64]
    GAM = A.f32(4 * NGAM).rearrange("p (h g) -> p h g", h=4)[0:96]
    UMbA = [A.bf16(NT)[0:96] for _ in range(4)]
    CMbA = [A.bf16(NT)[0:96] for _ in range(4)]
    CSs = A.f32(48)[0:96]
    A.mark()
    G8 = A.f32(NT)[0:8]
    for W_, nm in ((WQ, "ml_wq"), (WK, "ml_wk"), (WV, "ml_wv")):
        S.dma("pool", W_, I[nm][l].rearrange("h d e -> d h e"), writes=["mlW"])
    S.dma("pool", WIF, I["ml_w_if"][l].rearrange("(x d) g -> d x g", d=96), writes=["mlW"])
    for h in range(4):
        colvec(c, MCW[:, h, :], I["ml_conv_w"][l][:, h * 96:(h + 1) * 96], "j p -> p j", "mlc")
    colvec(c, MCB[:, 0:4], I["ml_conv_b"][l], "(h p) -> p h", "mlc", p=96)
    colvec(c, NG[:, 0:4], I["ml_norm_g"][l], "(h p) -> p h", "mlc", p=96)
    colvec(c, SK[:, 0:4], I["ml_skip"][l], "(h p) -> p h", "mlc", p=96)
    colvec(c, BI[:, 0:1], I["ml_b_if"][l][0:4], "(g o) -> g o", "mlc", o=1)
    colvec(c, NBF[:, 0:1], I["ml_b_if"][l][4:8], "(g o) -> g o", "mlc", o=1)
    colvec(c, M0T[:, 0:16], I["state_mlstm_m"][l], "i h -> h i", "mlc")
    S.op("dve", lambda e: e.tensor_scalar_mul(out=NBF[:, 0:1], in0=NBF[:, 0:1], scalar1=-1.0),
         reads=["mlc"], writes=["mlc"])

    cut(c, 10)

    def head_feats(h, B, need_v, part="both"):
        UMb_h, CMb_h = B.UMb, B.CMb
        if not need_v:
            for (Wm, dst, dres) in ((WQ[:, h, :], B.MQ, "MQ"), (WK[:, h, :], B.MK, "MK")):
                def ev2(ti, t0, tn, ps, pres, dst=dst, dres=dres):
                    if ti % 2 == 0:
                        S.op("act", lambda e: e.copy(out=dst[:, t0:t0 + tn], in_=ps), reads=[pres], writes=[dres])
                    else:
                        S.op("dve", lambda e: e.tensor_copy(out=dst[:, t0:t0 + tn], in_=ps), reads=[pres], writes=[dres])
                small_mm_tiles(c, Wm, "mlW", CMb_h, f"CMb{h}", 96, ev2)
            return
        if part == "ii":
            return head_feats_ii(h, B, UMb_h, CMb_h)
        S.dma("pool", B.WINh, w_in_l[:, :, 512 + 96 * h:512 + 96 * (h + 1)], writes=["WINh"])
        S.op("pool", lambda e: e.memset(B.UMx, 0.0), writes=["UMx"])
        hist = B.UMx[:, SMP0:SMP0 + 176].rearrange("p (i e) -> p i e", e=11)[:, :, 0:3]
        load_T(c, I["state_mlstm_conv"][l].rearrange("i j c -> (i j) c"), 48, 384, [(h * 96, 96, hist, "UMx", 3)])

        cut(c, 14)

        def ev_u(ti, t0, tn, ps, pres):
            for (c0, c1, dst, st_) in dst_views(B.UMx, "ext_in", t0, tn):
                S.op("act", lambda e, s_=pview(ps, c0, c1, st_), d=dst: e.copy(out=d, in_=s_),
                     reads=[pres], writes=["UMx"])
            S.op("act", lambda e, ps=ps, t0=t0, tn=tn: e.copy(out=UMb_h[:, t0:t0 + tn], in_=ps),
                 reads=[pres], writes=[f"UMb{h}"])
        proj_fm(c, B.WINh, "WINh", 0, 96, ev_u)
        cut(c, 11)
        cmp_, cms = B.CM[:, 0:NPR], B.CM[:, NPR:NT].rearrange("p (i j) -> p i j", j=8)
        for j in range(4):
            up = B.UMx[:, j:j + NPR]
            us = B.UMx[:, SMP0 + j:SMP0 + j + 176].rearrange("p (i e) -> p i e", e=11)[:, :, 0:8]
            for eng, src, dst in (("dve", up, cmp_), ("dve", us, cms)):
                if j == 0:
                    S.op(eng, lambda e, src=src, dst=dst: e.tensor_scalar(
                        out=dst, in0=src, scalar1=MCW[:, h, 0:1], scalar2=MCB[:, h:h + 1], op0=ALU.mult, op1=ALU.add),
                        reads=["UMx", "mlc"], writes=["CM"])
                else:
                    S.op("dve", lambda e, src=src, dst=dst, j=j: e.scalar_tensor_tensor(
                        out=dst, in0=src, scalar=MCW[:, h, j:j + 1], in1=dst, op0=ALU.mult, op1=ALU.add),
                        reads=["UMx", "CM", "mlc"], writes=["CM"])
        S.op("act", lambda e: e.activation(out=B.CM, in_=B.CM, func=AF.Silu), reads=["CM"], writes=["CM"])
        S.op("act", lambda e: e.copy(out=CMb_h, in_=B.CM), reads=["CM"], writes=[f"CMb{h}"])
        cut(c, 12)
        if part == "i":
            return
        head_feats_ii(h, B, UMb_h, CMb_h)

    def head_feats_ii(h, B, UMb_h, CMb_h):
        todo = [(WQ[:, h, :], CMb_h, f"CMb{h}", B.MQ, "MQ"), (WK[:, h, :], CMb_h, f"CMb{h}", B.MK, "MK"),
                (WV[:, h, :], UMb_h, f"UMb{h}", B.MV, "MV")]
        for (Wm, src, sres, dst, dres) in todo:
            def ev(ti, t0, tn, ps, pres, dst=dst, dres=dres):
                eng = "act" if ti % 2 == 0 else "dve"
                if eng == "act":
                    S.op("act", lambda e: e.copy(out=dst[:, t0:t0 + tn], in_=ps), reads=[pres], writes=[dres])
                else:
                    S.op("dve", lambda e: e.tensor_copy(out=dst[:, t0:t0 + tn], in_=ps), reads=[pres], writes=[dres])
            small_mm_tiles(c, Wm, "mlW", src, sres, 96, ev)
        cut(c, 13)

    class Bufs:
        pass

    A.mark()
    B = Bufs()
    B.WINh = A.bf16(8 * 96).rearrange("p (k f) -> p k f", k=8)
    B.UMx, B.CM = A.f32(NEX)[0:96], A.f32(NT)[0:96]
    B.MQ, B.MK, B.MV = A.bf16(NT)[0:96], A.bf16(NT)[0:96], A.bf16(NT)[0:96]
    def p1_i(h):
        B.UMb, B.CMb = UMbA[h], CMbA[h]
        head_feats(h, B, True, part="i")
        up, us = ext_in_views(B.UMx)
        S.dma("sp", O["p_mlstm_conv"][l][:, h * 96:(h + 1) * 96].rearrange("j p -> p j"), up[:, NPR - 3:NPR],
              reads=["UMx"], allow_slow_non_contiguous=True)
        S.op("dve", lambda e, us=us: e.tensor_copy(out=CSs[:, 0:48].rearrange("p (i j) -> p i j", j=3), in_=us[:, :, 5:8]),
             reads=["UMx"], writes=["CSs"])
        store_T(c, CSs[:, 0:48], "CSs", 96, 48, O["s_mlstm_conv"][l].rearrange("i j c -> (i j) c")[:, h * 96:(h + 1) * 96])

    def p1_ii(h):
        B.UMb, B.CMb = UMbA[h], CMbA[h]
        head_feats(h, B, True, part="ii")
        for ti, (t0, tn) in enumerate(TILES):
            ps = P[2 + ti % 2]
            for xi, (src, sres) in enumerate(((B.MQ, "MQ"), (B.MK, "MK"), (B.MV, "MV"))):
                S.op("pe", lambda e, ps=ps, xi=xi, src=src, t0=t0, tn=tn: e.matmul(
                    ps[0:8, 0:tn], lhsT=WIF[:, xi * 4 + h, :], rhs=src[:, t0:t0 + tn], start=(xi == 0), stop=(xi == 2)),
                    reads=["mlW", sres], writes=[f"P{2 + ti % 2}"])
            if h == 0:
                S.op("dve", lambda e, ps=ps, t0=t0, tn=tn: e.tensor_copy(out=G8[:, t0:t0 + tn], in_=ps[0:8, 0:tn]),
                     reads=[f"P{2 + ti % 2}"], writes=["G8"])
            else:
                S.op("dve", lambda e, ps=ps, t0=t0, tn=tn: e.tensor_tensor(
                    out=G8[:, t0:t0 + tn], in0=ps[0:8, 0:tn], in1=G8[:, t0:t0 + tn], op=ALU.add),
                    reads=[f"P{2 + ti % 2}", "G8"], writes=["G8"])
    p1_i(0)
    for h in range(4):
        if h + 1 < 4:
            p1_i(h + 1)
        p1_ii(h)
    import os
    KCUT = int(os.environ.get("KCUT", "0"))
    if KCUT == 1:
        S.barrier(); A.release(); A.release(); A.release()
        return
    if l == 0:
        dbg(c, "G8", G8, ["G8"])
        dbg(c, "CM3", B.CM, ["CM"])
        dbg(c, "MQ3", B.MQ, ["MQ"])
        dbg(c, "MV3", B.MV, ["MV"])
        dbg(c, "UMn3", UMbA[3], ["UMb3"])
    S.barrier()
    A.release()

    A.mark()
    RF, RB, RA, RG = [A.f32(NT)[0:4] for _ in range(4)]
    RR = G8[0:4, :]
    S.dma("sp", RF, G8[4:8, :], reads=["G8"], writes=["RF"])
    LI = G8[0:4, :]
    S.op("act", lambda e: e.activation(out=LI, in_=LI, func=AF.Identity, bias=BI[:, 0:1]),
         reads=["G8", "mlc", "RF"], writes=["G8"])
    S.op("act", lambda e: e.activation(out=RF, in_=RF, func=AF.Exp, scale=-1.0, bias=NBF[:, 0:1]),
         reads=["RF", "mlc"], writes=["RF"])
    S.op("act", lambda e: e.activation(out=RF, in_=RF, func=AF.Ln, bias=1.0), reads=["RF"], writes=["RF"])
    sm = lambda X_: X_[:, NPR:NT]
    pr = lambda X_: X_[:, 0:NPR]
    S.op("dve", lambda e: e.tensor_tensor_scan(out=pr(RB), data0=pr(RF), data1=pr(RF), initial=0.0,
                                               op0=ALU.add, op1=ALU.bypass), reads=["RF"], writes=["RB"])
    S.op("dve", lambda e: e.tensor_tensor_scan(out=sm(RB), data0=c.SMASK[0:4, :], data1=sm(RF), initial=0.0,
                                               op0=ALU.mult, op1=ALU.add), reads=["RF", "consts"], writes=["RB"])
    S.op("dve", lambda e: e.tensor_tensor(out=RA, in0=LI, in1=RB, op=ALU.add), reads=["G8", "RB"], writes=["RA"])
    S.op("dve", lambda e: e.tensor_copy(out=RG, in_=RA), reads=["RA"], writes=["RG"])
    S.op("dve", lambda e: e.tensor_scalar_max(out=RG[:, 0:1], in0=RG[:, 0:1], scalar1=0.0), reads=["RG"], writes=["RG"])
    st_v = lambda X_: X_[:, NPR:NT].rearrange("p (i j) -> p i j", j=8)
    S.op("dve", lambda e: e.tensor_tensor(out=st_v(RG)[:, :, 0], in0=st_v(RG)[:, :, 0], in1=M0T[:, 0:16], op=ALU.max),
         reads=["RG", "mlc"], writes=["RG"])
    S.op("dve", lambda e: e.tensor_tensor_scan(out=pr(RF), data0=pr(RG), data1=pr(RG), initial=-1e30,
                                               op0=ALU.max, op1=ALU.max), reads=["RG", "RB", "RA"], writes=["RF"])
    S.op("dve", lambda e: e.tensor_tensor_scan(out=sm(RF), data0=c.SNEG[0:4, :], data1=sm(RG), initial=-1e30,
                                               op0=ALU.add, op1=ALU.max), reads=["RG", "consts"], writes=["RF"])
    S.op("dve", lambda e: e.memset(RG[:, 0:16], 0.0), reads=["RF"], writes=["RG"])
    S.op("dve", lambda e: e.tensor_copy(
        out=RG[:, 16:NPR].rearrange("p (c t) -> p c t", t=64),
        in_=RF[:, 15:15 + 2048].rearrange("p (c t) -> p c t", t=64)[:, :, 0:1].to_broadcast([4, 32, 64])),
        reads=["RF"], writes=["RG"])
    S.op("dve", lambda e: e.tensor_copy(out=st_v(RG), in_=M0T[:, 0:16].unsqueeze(2).to_broadcast([4, 16, 8])),
         reads=["mlc"], writes=["RG"])
    S.op("dve", lambda e: e.tensor_tensor(out=RR, in0=RB, in1=RF, op=ALU.subtract), reads=["RB", "RF"], writes=["G8"])
    S.op("dve", lambda e: e.tensor_scalar_mul(out=MNEW[:, 0:1], in0=RR[:, NPR - 1:NPR], scalar1=-1.0),
         reads=["G8"], writes=["MNEW"])
    S.op("dve", lambda e: e.tensor_scalar_mul(out=MNEW[:, 1:17], in0=st_v(RR)[:, :, 7], scalar1=-1.0),
         reads=["G8"], writes=["MNEW"])
    S.dma("sp", O["p_mlstm_m"][l].rearrange("(h o) -> h o", o=1), MNEW[:, 0:1], reads=["MNEW"])
    S.dma("sp", O["s_mlstm_m"][l].rearrange("i h -> h i"), MNEW[:, 1:17], reads=["MNEW"], allow_slow_non_contiguous=True)
    S.op("act", lambda e: e.activation(out=RR, in_=RR, func=AF.Exp), reads=["G8", "MNEW"], writes=["G8"])
    S.op("dve", lambda e: e.tensor_copy(out=RB[:, 0:16], in_=RF[:, 15:16].to_broadcast([4, 16])), reads=["RF", "G8"], writes=["RB"])
    S.op("dve", lambda e: e.tensor_copy(
        out=RB[:, 16:NPR].rearrange("p (c t) -> p c t", t=64),
        in_=RF[:, 16:NPR].rearrange("p (c t) -> p c t", t=64)[:, :, 63:64].to_broadcast([4, 32, 64])),
        reads=["RF"], writes=["RB"])
    S.op("dve", lambda e: e.tensor_copy(out=st_v(RB), in_=st_v(RF)[:, :, 7:8].to_broadcast([4, 16, 8])), reads=["RF"], writes=["RB"])
    S.op("dve", lambda e: e.tensor_tensor(out=RB, in0=RA, in1=RB, op=ALU.subtract), reads=["RA", "RB"], writes=["RB"])
    S.op("act", lambda e: e.activation(out=RB, in_=RB, func=AF.Exp), reads=["RB"], writes=["RB"])
    S.op("dve", lambda e: e.tensor_tensor(out=RA, in0=RA, in1=RG, op=ALU.subtract), reads=["RA", "RG"], writes=["RA"])
    S.op("act", lambda e: e.activation(out=RA, in_=RA, func=AF.Exp), reads=["RA"], writes=["RA"])
    S.op("dve", lambda e: e.tensor_tensor(out=RG, in0=RG, in1=RF, op=ALU.subtract), reads=["RG", "RF", "RA"], writes=["RG"])
    S.op("act", lambda e: e.activation(out=RG, in_=RG, func=AF.Exp), reads=["RG"], writes=["RG"])
    S.op("dve", lambda e: e.tensor_copy(out=AE[:, 0:1], in_=RG[:, 15:16]), reads=["RG"], writes=["AE"])
    S.op("dve", lambda e: e.tensor_copy(out=AE[:, 1:33], in_=RG[:, 16:NPR].rearrange("p (c t) -> p c t", t=64)[:, :, 63]),
         reads=["RG"], writes=["AE"])
    S.op("dve", lambda e: e.tensor_copy(out=AE[:, 33:49], in_=st_v(RG)[:, :, 7]), reads=["RG"], writes=["AE"])
    S.op("dve", lambda e: e.tensor_scalar_mul(out=RG, in0=RG, scalar1=QSCALE_M), reads=["RG", "AE"], writes=["RG"])
    S.op("dve", lambda e: e.reciprocal(out=RF, in_=RG), reads=["RG", "RF"], writes=["RF"])
    S.op("dve", lambda e: e.tensor_tensor(out=RR, in0=RR, in1=RF, op=ALU.mult), reads=["G8", "RF"], writes=["G8"])
    for ci, (tok0, T, nseq, seq0) in enumerate(CHUNKS):
        for gi_, (Rw, rn) in enumerate(((RG, "RG"), (RA, "RA"), (RR, "G8"), (RB, "RB"))):
            S.op("pe", lambda e, Rw=Rw, gi_=gi_, tok0=tok0, T=T: e.transpose(
                out=P[6][0:T, gi_ * 4:gi_ * 4 + 4], in_=Rw[:, tok0:tok0 + T], identity=c.ident[0:4, 0:4]),
                reads=[rn, "ident"], writes=["P6"])
        S.op("act", lambda e, ci=ci, T=T: e.copy(out=GT[0:T, ci, :], in_=P[6][0:T, 0:16]), reads=["P6"], writes=["GT"])
    if l == 0:
        dbg(c, "alpha", RG, ["RG"])
        dbg(c, "beta", RA, ["RA"])
        dbg(c, "eps", RR, ["G8"])
        dbg(c, "G", RF, ["RF"])
        dbg(c, "negB", RB, ["RB"])
        dbg(c, "GT", GT, ["GT"])
    sel = c.SEL[0:4, :].rearrange("p (h m) -> p h m", h=4)
    for h in range(4):
        S.op("pe", lambda e, h=h: e.matmul(P[7][0:96, 0:NGAM], lhsT=sel[:, h, :], rhs=AE[:, 0:NGAM], start=True, stop=True),
             reads=["consts", "AE"], writes=["P7"])
        S.op("act", lambda e, h=h: e.copy(out=GAM[:, h, :], in_=P[7][0:96, 0:NGAM]), reads=["P7"], writes=["GAM"])
    S.barrier()
    A.release()
    A.release()
    if KCUT == 2:
        A.release()
        return

    A.mark()
    B = Bufs()
    WINz = A.bf16(8 * 96).rearrange("p (k f) -> p k f", k=8)
    B.MQ, B.MK = A.bf16(NT)[0:96], A.bf16(NT)[0:96]
    YMh = A.bf16(NT).rearrange("p (o t) -> p o t", o=1)[0:96]
    ZS = A.bf16(NT)[0:96]
    HNT = A.f32(NT)[0:96]
    CEp = A.f32(104)[0:96]
    CEb = [A.bf16(104)[0:96] for _ in range(2)]
    CE = A.f32(NSEQ * 97).rearrange("p (i e) -> p i e", e=97)[0:96]
    CEsb = A.bf16(NSEQ * 97).rearrange("p (i e) -> p i e", e=97)[0:96]
    QZ = A.bf16(8 * 64).rearrange("p (i t) -> p i t", i=8)[0:96]
    KTz = A.bf16(8 * 96).rearrange("p (i d) -> p i d", i=8)[0:64]
    VT = [A.bf16(104)[0:64] for _ in range(4)]
    KTb = [A.bf16(96)[0:64] for _ in range(4)]
    STm = [A.bf16(64)[0:64] for _ in range(4)]
    Hh = [A.f32(96)[0:64] for _ in range(4)]
    HN = [A.f32(96)[0:64] for _ in range(2)]
    JK = A.f32(96)[0:64]
    SMALL = [A.f32(8)[0:64] for _ in range(6)]
    S.op("pool", lambda e: e.memset(QZ, 0.0), writes=["QZ"])
    for par in range(4):
        S.op("pool", lambda e, par=par: e.memset(VT[par][:, 96:97], 1.0), writes=[f"VT{par}"])

    def p2_head(h):
        B.UMb, B.CMb = UMbA[h], CMbA[h]
        UMb_h, CMb_h = UMbA[h], CMbA[h]
        head_feats(h, B, False)
        S.dma("pool", WINz, w_in_l[:, :, 896 + 96 * h:896 + 96 * (h + 1)], writes=["WINz"])

        def ev_z(ti, t0, tn, ps, pres):
            S.op("act", lambda e: e.activation(out=ZS[:, t0:t0 + tn], in_=ps, func=AF.Sigmoid), reads=[pres], writes=["ZS"])
        proj_fm(c, WINz, "WINz", 0, 96, ev_z)
        S.op("pool", lambda e: e.memset(CEp[:, 0:97], 0.0), writes=["CEp"])
        S.op("pool", lambda e: e.memset(CEb[0][:, 0:97], 0.0), writes=["CEb0"])
        S.dma("sp", CE[:, :, 0:96], I["state_mlstm_C"][l][:, h].rearrange("i d e -> d i e"), writes=["CE"])
        S.dma("sp", CE[:, :, 96], I["state_mlstm_n"][l][:, h, :].rearrange("i d -> d i"), writes=["CE"],
              allow_slow_non_contiguous=True)
        S.op("dve", lambda e: e.tensor_copy(out=CEsb, in_=CE), reads=["CE"], writes=["CEsb"])
        def ctx(ci):
            tok0, T, nseq, seq0 = CHUNKS[ci]
            return tok0, T, nseq, seq0, slice(tok0, tok0 + T)

        def s1(ci):
            tok0, T, nseq, seq0, tk = ctx(ci)
            Pk, nk = P[ci % 2], f"P{ci % 2}"
            S.op("pe", lambda e: e.matmul(Pk[0:T, 0:96], lhsT=UMb_h[:, tk], rhs=WV[:, h, :], start=True, stop=True),
                 reads=[f"UMb{h}", "mlW"], writes=[nk])
            S.op("pe", lambda e: e.matmul(Pk[0:T, 96:192], lhsT=CMb_h[:, tk], rhs=WK[:, h, :], start=True, stop=True),
                 reads=[f"CMb{h}", "mlW"], writes=[nk])
            S.op("pe", lambda e: e.matmul(Pk[0:T, 192:192 + T], lhsT=B.MK[:, tk], rhs=B.MQ[:, tk], start=True, stop=True),
                 reads=["MK", "MQ"], writes=[nk])

        def s2(ci):
            tok0, T, nseq, seq0, tk = ctx(ci)
            b4 = ci % 4
            be, bg = GT[0:T, ci, 4 + h:5 + h], GT[0:T, ci, 12 + h:13 + h]
            mask = (c.MASKC if nseq == 1 else c.MASKB)[0:T, 0:T]
            Pk, nk = P[ci % 2], f"P{ci % 2}"
            S.op("dve", lambda e: e.tensor_copy(out=VT[b4][0:T, 0:96], in_=Pk[0:T, 0:96]), reads=[nk], writes=[f"VT{b4}"])
            S.op("dve", lambda e: e.tensor_scalar_mul(out=KTb[b4][0:T, :], in0=Pk[0:T, 96:192], scalar1=bg),
                 reads=[nk, "GT"], writes=[f"KTb{b4}"])
            S.op("dve", lambda e: e.scalar_tensor_tensor(out=STm[b4][0:T, 0:T], in0=Pk[0:T, 192:192 + T], scalar=be, in1=mask,
                                                         op0=ALU.mult, op1=ALU.mult),
                 reads=[nk, "GT", "consts"], writes=[f"STm{b4}"])

        def s3(ci):
            tok0, T, nseq, seq0, tk = ctx(ci)
            b4 = ci % 4
            if nseq == 1:
                Pd, nd = P[5 + ci % 2], f"P{5 + ci % 2}"
                S.op("pe", lambda e: e.matmul(Pd[0:96, 0:97], lhsT=KTb[b4][0:T, :], rhs=VT[b4][0:T, 0:97], start=True, stop=True),
                     reads=[f"KTb{b4}", f"VT{b4}"], writes=[nd])
            else:
                S.op("dve", lambda e: e.tensor_tensor(
                    out=KTz, in0=KTb[b4][0:64, :].unsqueeze(1).to_broadcast([64, 8, 96]),
                    in1=c.PM[0:64, :].unsqueeze(2).to_broadcast([64, 8, 96]), op=ALU.mult),
                    reads=[f"KTb{b4}", "consts"], writes=["KTz"])
                for half in range(2):
                    pb = P[5 + half]
                    for j in range(4):
                        S.op("pe", lambda e, pb=pb, j=j, half=half: e.matmul(
                            pb[0:96, j * 97:(j + 1) * 97], lhsT=KTz[:, 4 * half + j, :], rhs=VT[b4][0:64, 0:97],
                            start=True, stop=True), reads=["KTz", f"VT{b4}"], writes=[f"P{5 + half}"])

        def s4(ci):
            tok0, T, nseq, seq0, tk = ctx(ci)
            if nseq == 1:
                Pd, nd = P[5 + ci % 2], f"P{5 + ci % 2}"
                S.op("dve", lambda e: e.scalar_tensor_tensor(
                    out=CEp[:, 0:97], in0=CEp[:, 0:97], scalar=GAM[:, h, ci:ci + 1], in1=Pd[0:96, 0:97], op0=ALU.mult, op1=ALU.add),
                    reads=[nd, "CEp", "GAM"], writes=["CEp"])
            else:
                for half in range(2):
                    pb = P[5 + half]
                    s0 = seq0 + 4 * half
                    cev = CE[:, s0:s0 + 4, :]
                    S.op("dve", lambda e, cev=cev, s0=s0: e.tensor_tensor(
                        out=cev, in0=cev, in1=GAM[:, h, 33 + s0:33 + s0 + 4].unsqueeze(2).to_broadcast([96, 4, 97]), op=ALU.mult),
                        reads=["CE", "GAM"], writes=["CE"])
                    S.op("dve", lambda e, pb=pb, cev=cev: e.tensor_tensor(
                        out=cev, in0=pb[0:96, 0:388].rearrange("p (i e) -> p i e", e=97), in1=cev, op=ALU.add),
                        reads=[f"P{5 + half}", "CE"], writes=["CE"])

        def s5(ci):
            tok0, T, nseq, seq0, tk = ctx(ci)
            b4 = ci % 4
            Pn, nn = P[2 + ci % 3], f"P{2 + ci % 3}"
            S.op("pe", lambda e: e.matmul(Pn[0:T, 0:97], lhsT=STm[b4][0:T, 0:T], rhs=VT[b4][0:T, 0:97], start=True, stop=False),
                 reads=[f"STm{b4}", f"VT{b4}"], writes=[nn])
            if nseq == 1:
                S.op("pe", lambda e: e.matmul(Pn[0:T, 0:97], lhsT=B.MQ[:, tk], rhs=CEb[ci % 2][:, 0:97], start=False, stop=True),
                     reads=["MQ", f"CEb{ci % 2}"], writes=[nn])
                S.op("act", lambda e: e.copy(out=CEb[(ci + 1) % 2][:, 0:97], in_=CEp[:, 0:97]), reads=["CEp"], writes=[f"CEb{(ci + 1) % 2}"])
            else:
                S.op("act", lambda e: e.copy(out=diag_view(QZ, 8, 8, 64), in_=B.MQ[:, tk].rearrange("p (i j) -> p i j", j=8)),
                     reads=["MQ"], writes=["QZ"])
                for i in range(8):
                    S.op("pe", lambda e, i=i: e.matmul(Pn[0:64, 0:97], lhsT=QZ[:, i, :], rhs=CEsb[:, seq0 + i, :],
                                                       start=False, stop=(i == 7)),
                         reads=["QZ", "CEsb"], writes=[nn])

        def s6(ci):
            tok0, T, nseq, seq0, tk = ctx(ci)
            Pn, nn = P[2 + ci % 3], f"P{2 + ci % 3}"
            sm_ = SMALL[ci % 6]
            S.op("act", lambda e: e.activation(out=sm_[0:T, 0:1], in_=Pn[0:T, 96:97], func=AF.Abs), reads=[nn], writes=[f"SM{ci % 6}"])

        def s7(ci):
            tok0, T, nseq, seq0, tk = ctx(ci)
            epp = GT[0:T, ci, 8 + h:9 + h]
            Pn, nn = P[2 + ci % 3], f"P{2 + ci % 3}"
            sm_, sn, b4 = SMALL[ci % 6], f"SM{ci % 6}", ci % 4
            S.op("dve", lambda e: e.tensor_tensor(out=sm_[0:T, 0:1], in0=sm_[0:T, 0:1], in1=epp, op=ALU.max), reads=[sn, "GT"], writes=[sn])
            S.op("dve", lambda e: e.reciprocal(out=sm_[0:T, 0:1], in_=sm_[0:T, 0:1]), reads=[sn], writes=[sn])
            S.op("dve", lambda e: e.tensor_scalar_mul(out=Hh[b4][0:T, :], in0=Pn[0:T, 0:96], scalar1=sm_[0:T, 0:1]),
                 reads=[nn, sn], writes=[f"H{b4}"])

        def s8(ci):
            tok0, T, nseq, seq0, tk = ctx(ci)
            sm_, sn, b4 = SMALL[ci % 6], f"SM{ci % 6}", ci % 4
            S.op("act", lambda e: e.activation(out=JK[0:T, :], in_=Hh[b4][0:T, :], func=AF.Square, accum_out=sm_[0:T, 2:3]),
                 reads=[f"H{b4}", sn], writes=["JK", sn])
            S.op("act", lambda e: e.activation(out=sm_[0:T, 3:4], in_=sm_[0:T, 2:3], func=AF.Sqrt, bias=EPS, scale=1.0 / 96),
                 reads=[sn], writes=[sn])

        def s9(ci):
            tok0, T, nseq, seq0, tk = ctx(ci)
            sm_, sn = SMALL[ci % 6], f"SM{ci % 6}"
            S.op("dve", lambda e: e.reciprocal(out=sm_[0:T, 3:4], in_=sm_[0:T, 3:4]), reads=[sn], writes=[sn])

        def s10(ci):
            tok0, T, nseq, seq0, tk = ctx(ci)
            sm_, sn, b4 = SMALL[ci % 6], f"SM{ci % 6}", ci % 4
            S.op("act", lambda e: e.activation(out=HN[ci % 2][0:T, :], in_=Hh[b4][0:T, :], func=AF.Copy, scale=sm_[0:T, 3:4]),
                 reads=[f"H{b4}", sn], writes=[f"HN{ci % 2}"])

        def s11(ci):
            tok0, T, nseq, seq0, tk = ctx(ci)
            S.op("pe", lambda e: e.transpose(out=P[7][0:96, 0:T], in_=HN[ci % 2][0:T, :], identity=c.ident[0:T, 0:T]),
                 reads=[f"HN{ci % 2}", "ident"], writes=["P7"])

        def s12(ci):
            tok0, T, nseq, seq0, tk = ctx(ci)
            S.op("act", lambda e: e.copy(out=HNT[:, tk], in_=P[7][0:96, 0:T]), reads=["P7"], writes=["HNT"])

        stages = [s1, s2, s3, s4, s5, s6, s7, s8, s9, s10, s11, s12]
        order = [11, 10, 9, 8, 7, 6, 5, 4, 3, 2, 1, 0]
        for it in range(NCH + len(stages) - 1):
            for k in order:
                ci = it - k
                if 0 <= ci < NCH:
                    stages[k](ci)
        drain(c, 99)
        S.op("act", lambda e: e.activation(out=HNT, in_=HNT, func=AF.Copy, scale=NG[:, h:h + 1]), reads=["HNT", "mlc"], writes=["HNT"])
        S.op("dve", lambda e: e.scalar_tensor_tensor(out=HNT, in0=CMb_h, scalar=SK[:, h:h + 1], in1=HNT, op0=ALU.mult, op1=ALU.add),
             reads=["HNT", f"CMb{h}", "mlc"], writes=["HNT"])
        S.op("dve", lambda e: e.tensor_tensor(out=YMh[:, 0, :], in0=HNT, in1=ZS, op=ALU.mult), reads=["HNT", "ZS"], writes=["YMh"])
        apply_wout(c, l, YMh, "YMh", 256 + 96 * h, 96, 1, defer=True)
        S.dma("sp", O["p_mlstm_C"][l, h], CEp[:, 0:96], reads=["CEp"])
        S.dma("sp", O["p_mlstm_n"][l, h].rearrange("(d o) -> d o", o=1), CEp[:, 96:97], reads=["CEp"])
        S.dma("sp", O["s_mlstm_C"][l][:, h].rearrange("i d e -> d i e"), CE[:, :, 0:96], reads=["CE"])
        S.dma("sp", O["s_mlstm_n"][l][:, h, :].rearrange("i d -> d i"), CE[:, :, 96], reads=["CE"], allow_slow_non_contiguous=True)
    for h in range(4):
        p2_head(h)
    drain(c, 99)
    S.barrier()
    A.release()
    A.release()


QSCALE_G = 48.0 ** -0.5


def gla_group(c, l):
    S, A, I, O = c.S, c.A, c.I, c.O
    P = c.P
    A.mark()
    w_in_l = I["w_in"][l].rearrange("(k p) f -> p k f", p=128)
    WA_ = A.bf16(8 * 16).rearrange("p (k f) -> p k f", k=8)
    ALR = A.bf16(NT)[0:16]
    WUP = A.bf16(192)[0:16]
    NBUP, GNG = A.f32(8)[0:48], A.f32(8)[0:96]
    MCH = A.f32(NT)[0:48]
    GAMg = A.f32(NGAM + 7)[0:48]
    S.dma("pool", WA_, w_in_l[:, :, 2432:2448], writes=["WA_"])
    S.dma("pool", WUP, I["gla_w_up"][l], writes=["glac"])
    colvec(c, NBUP[:, 0:4], I["gla_b_up"][l], "(h k) -> k h", "glac", k=48)
    colvec(c, GNG[:, 0:4], I["gla_norm_g"][l], "(h p) -> p h", "glac", p=96)
    S.op("dve", lambda e: e.tensor_scalar_mul(out=NBUP[:, 0:4], in0=NBUP[:, 0:4], scalar1=-1.0), reads=["glac"], writes=["glac"])
    S.op("pool", lambda e: e.memset(MCH, 1.0), writes=["MCH"])
    S.op("pool", lambda e: e.memset(MCH[:, 0:1], 0.0), reads=["MCH"], writes=["MCH"])
    S.op("pool", lambda e: e.memset(MCH[:, 16:NPR].rearrange("p (c t) -> p c t", t=64)[:, :, 0:1], 0.0), reads=["MCH"], writes=["MCH"])
    S.op("pool", lambda e: e.memset(MCH[:, NPR:NT].rearrange("p (i j) -> p i j", j=8)[:, :, 0:1], 0.0), reads=["MCH"], writes=["MCH"])

    def ev_a(ti, t0, tn, ps, pres):
        S.op("act", lambda e: e.copy(out=ALR[:, t0:t0 + tn], in_=ps), reads=[pres], writes=["ALR"])
    proj_fm(c, WA_, "WA_", 0, 16, ev_a)

    WQg = A.bf16(8 * 48).rearrange("p (k f) -> p k f", k=8)
    WKg = A.bf16(8 * 48).rearrange("p (k f) -> p k f", k=8)
    WVg = A.bf16(8 * 96).rearrange("p (k f) -> p k f", k=8)
    WGg = A.bf16(8 * 96).rearrange("p (k f) -> p k f", k=8)
    BC, EO, QG, KG = A.f32(NT)[0:48], A.f32(NT)[0:96], A.f32(NT)[0:48], A.f32(NT)[0:48]
    QGb, KGb = A.bf16(NT)[0:48], A.bf16(NT)[0:48]
    VF = A.bf16(NT)[0:96]
    GGs = A.bf16(NT)[0:96]
    YGh = A.bf16(NT).rearrange("p (o t) -> p o t", o=1)[0:96]
    Sp = [A.f32(96)[0:48] for _ in range(2)]
    Ss = A.f32(NSEQ * 96).rearrange("p (i e) -> p i e", e=96)[0:48]
    QZ = A.f32(8 * 64).rearrange("p (i t) -> p i t", i=8)[0:48]
    KTz = A.bf16(8 * 48).rearrange("p (i d) -> p i d", i=8)[0:64]
    VT = [A.bf16(96)[0:64] for _ in range(4)]
    KT = [A.bf16(48)[0:64] for _ in range(4)]
    STm = [A.bf16(64)[0:64] for _ in range(4)]
    ON = [A.bf16(96)[0:64] for _ in range(2)]
    JK = A.f32(96)[0:64]
    SMALL = [A.f32(8)[0:64] for _ in range(3)]
    Pbf = [p.bitcast(BF16) for p in P]
    S.op("pool", lambda e: e.memset(QZ, 0.0), writes=["QZg"])
    st_v = lambda X_: X_[:, NPR:NT].rearrange("p (i j) -> p i j", j=8)

    def g_head(h):
        S.dma("pool", WQg, w_in_l[:, :, 1280 + 48 * h:1280 + 48 * (h + 1)], writes=["WQg"])
        S.dma("pool", WKg, w_in_l[:, :, 1472 + 48 * h:1472 + 48 * (h + 1)], writes=["WKg"])
        S.dma("pool", WVg, w_in_l[:, :, 1664 + 96 * h:1664 + 96 * (h + 1)], writes=["WVg"])
        S.dma("pool", WGg, w_in_l[:, :, 2048 + 96 * h:2048 + 96 * (h + 1)], writes=["WGg"])

        def ev_q(ti, t0, tn, ps, pres):
            S.op("act", lambda e: e.activation(out=QG[:, t0:t0 + tn], in_=ps, func=AF.Copy, scale=QSCALE_G), reads=[pres], writes=["QG"])
        proj_fm(c, WQg, "WQg", 0, 48, ev_q)

        def ev_k(ti, t0, tn, ps, pres):
            S.op("act", lambda e: e.copy(out=KG[:, t0:t0 + tn], in_=ps), reads=[pres], writes=["KG"])
        proj_fm(c, WKg, "WKg", 0, 48, ev_k)

        def ev_g(ti, t0, tn, ps, pres):
            S.op("act", lambda e: e.activation(out=GGs[:, t0:t0 + tn], in_=ps, func=AF.Silu), reads=[pres], writes=["GGs"])
        proj_fm(c, WGg, "WGg", 0, 96, ev_g)

        def ev_v(ti, t0, tn, ps, pres):
            if ti % 2 == 0:
                S.op("dve", lambda e: e.tensor_copy(out=VF[:, t0:t0 + tn], in_=ps), reads=[pres], writes=["VF"])
            else:
                S.op("act", lambda e: e.copy(out=VF[:, t0:t0 + tn], in_=ps), reads=[pres], writes=["VF"])
        proj_fm(c, WVg, "WVg", 0, 96, ev_v)

        def ev_l(ti, t0, tn, ps, pres):
            S.op("act", lambda e: e.activation(out=BC[:, t0:t0 + tn], in_=ps, func=AF.Exp, scale=-1.0, bias=NBUP[:, h:h + 1]),
                 reads=[pres, "glac"], writes=["BC"])
        small_mm_tiles(c, WUP[:, 48 * h:48 * (h + 1)], "glac", ALR, "ALR", 48, ev_l)
        S.op("act", lambda e: e.activation(out=BC, in_=BC, func=AF.Ln, bias=1.0), reads=["BC"], writes=["BC"])
        S.op("dve", lambda e: e.tensor_scalar_mul(out=EO[0:48, :], in0=BC, scalar1=-1.0 / 16.0), reads=["BC"], writes=["EO"])
        S.op("dve", lambda e: e.tensor_tensor_scan(out=BC, data0=MCH, data1=EO[0:48, :], initial=0.0, op0=ALU.mult, op1=ALU.add),
             reads=["EO", "MCH"], writes=["BC"])
        S.op("act", lambda e: e.activation(out=EO[0:48, :], in_=BC, func=AF.Exp), reads=["BC"], writes=["EO"])
        S.op("dve", lambda e: e.tensor_tensor(out=QG, in0=QG, in1=EO[0:48, :], op=ALU.mult), reads=["QG", "EO"], writes=["QG"])
        S.op("act", lambda e: e.copy(out=QGb, in_=QG), reads=["QG"], writes=["QGb"])
        S.op("dve", lambda e: e.tensor_copy(out=GAMg[:, 0:1], in_=EO[0:48, 15:16]), reads=["EO"], writes=["GAMg"])
        S.op("dve", lambda e: e.tensor_copy(out=GAMg[:, 1:33], in_=EO[0:48, 16:NPR].rearrange("p (c t) -> p c t", t=64)[:, :, 63]),
             reads=["EO"], writes=["GAMg"])
        S.op("dve", lambda e: e.tensor_copy(out=GAMg[:, 33:49], in_=st_v(EO[0:48, :])[:, :, 7]), reads=["EO"], writes=["GAMg"])
        S.op("act", lambda e: e.activation(out=BC, in_=BC, func=AF.Exp, scale=-1.0), reads=["BC"], writes=["BC"])
        S.op("dve", lambda e: e.tensor_tensor(out=KGb, in0=KG, in1=BC, op=ALU.mult), reads=["KG", "BC"], writes=["KGb"])
        S.op("pool", lambda e: e.memset(Sp[1], 0.0), writes=["Sp1"])
        S.dma("sp", Ss, I["state_gla_S"][l][:, h].rearrange("i k v -> k i v"), writes=["Ss"])
        def ctx(ci):
            tok0, T, nseq, seq0 = CHUNKS[ci]
            return tok0, T, nseq, seq0, slice(tok0, tok0 + T)

        def g1(ci):
            tok0, T, nseq, seq0, tk = ctx(ci)
            Pa, na = P[ci % 2], f"P{ci % 2}"
            S.op("pe", lambda e: e.transpose(out=Pbf[ci % 2][0:T, 0:48], in_=KGb[:, tk], identity=c.ident_bf[0:48, 0:48]),
                 reads=["KGb", "ident"], writes=[na])
            S.op("pe", lambda e: e.matmul(Pa[0:T, 64:64 + T], lhsT=KGb[:, tk], rhs=QGb[:, tk], start=True, stop=True),
                 reads=["KGb", "QGb"], writes=[na])
            S.op("pe", lambda e: e.transpose(out=Pbf[ci % 2][0:T, 256:352], in_=VF[:, tk], identity=c.ident_bf[0:96, 0:96]),
                 reads=["VF", "ident"], writes=[na])

        def g2(ci):
            tok0, T, nseq, seq0, tk = ctx(ci)
            b4 = ci % 4
            mask = (c.MASKC if nseq == 1 else c.MASKB)[0:T, 0:T]
            Pa, na = P[ci % 2], f"P{ci % 2}"
            S.op("dve", lambda e: e.tensor_copy(out=VT[b4][0:T, :], in_=Pbf[ci % 2][0:T, 256:352]), reads=[na], writes=[f"gVT{b4}"])
            S.op("dve", lambda e: e.tensor_copy(out=KT[b4][0:T, :], in_=Pbf[ci % 2][0:T, 0:48]), reads=[na], writes=[f"gKT{b4}"])
            S.op("dve", lambda e: e.tensor_tensor(out=STm[b4][0:T, 0:T], in0=Pa[0:T, 64:64 + T], in1=mask, op=ALU.mult),
                 reads=[na, "consts"], writes=[f"gST{b4}"])

        def g3(ci):
            tok0, T, nseq, seq0, tk = ctx(ci)
            b4 = ci % 4
            if nseq == 1:
                Pd, nd = P[5 + ci % 2], f"P{5 + ci % 2}"
                S.op("pe", lambda e: e.matmul(Pd[0:48, 0:96], lhsT=KT[b4][0:T, :], rhs=VT[b4][0:T, :], start=True, stop=True),
                     reads=[f"gKT{b4}", f"gVT{b4}"], writes=[nd])

        def g4(ci):
            tok0, T, nseq, seq0, tk = ctx(ci)
            if nseq == 1:
                Pd, nd = P[5 + ci % 2], f"P{5 + ci % 2}"
                S.op("dve", lambda e: e.tensor_tensor(out=Sp[ci % 2], in0=Pd[0:48, 0:96], in1=Sp[(ci + 1) % 2], op=ALU.add),
                     reads=[nd, f"Sp{(ci + 1) % 2}"], writes=[f"Sp{ci % 2}"])

        def g5(ci):
            tok0, T, nseq, seq0, tk = ctx(ci)
            b4 = ci % 4
            Pn, nn = P[2 + ci % 3], f"P{2 + ci % 3}"
            S.op("pe", lambda e: e.matmul(Pn[0:T, 0:96], lhsT=STm[b4][0:T, 0:T], rhs=VT[b4][0:T, :], start=True, stop=False),
                 reads=[f"gST{b4}", f"gVT{b4}"], writes=[nn])
            if nseq == 1:
                S.op("pe", lambda e: e.matmul(Pn[0:T, 0:96], lhsT=QG[:, tk], rhs=Sp[(ci + 1) % 2], start=False, stop=True),
                     reads=["QG", f"Sp{(ci + 1) % 2}"], writes=[nn])
                S.op("act", lambda e: e.activation(out=Sp[ci % 2], in_=Sp[ci % 2], func=AF.Copy, scale=GAMg[:, ci:ci + 1]),
                     reads=[f"Sp{ci % 2}", "GAMg"], writes=[f"Sp{ci % 2}"])
            else:
                S.op("act", lambda e: e.copy(out=diag_view(QZ, 8, 8, 64), in_=QG[:, tk].rearrange("p (i j) -> p i j", j=8)),
                     reads=["QG"], writes=["QZg"])
                for i in range(8):
                    S.op("pe", lambda e, i=i: e.matmul(Pn[0:64, 0:96], lhsT=QZ[:, i, :], rhs=Ss[:, seq0 + i, :], start=False, stop=(i == 7)),
                         reads=["QZg", "Ss"], writes=[nn])
                S.op("dve", lambda e: e.tensor_tensor(
                    out=KTz, in0=KT[b4][0:64, :].unsqueeze(1).to_broadcast([64, 8, 48]),
                    in1=c.PM[0:64, :].unsqueeze(2).to_broadcast([64, 8, 48]), op=ALU.mult),
                    reads=[f"gKT{b4}", "consts"], writes=["gKTz"])
                for half in range(2):
                    pb = P[5 + half]
                    for j in range(4):
                        S.op("pe", lambda e, pb=pb, j=j, half=half: e.matmul(
                            pb[0:48, j * 96:(j + 1) * 96], lhsT=KTz[:, 4 * half + j, :], rhs=VT[b4][0:64, :], start=True, stop=True),
                            reads=["gKTz", f"gVT{b4}"], writes=[f"P{5 + half}"])
                for half in range(2):
                    pb = P[5 + half]
                    s0 = seq0 + 4 * half
                    sv = Ss[:, s0:s0 + 4, :]
                    S.op("dve", lambda e, pb=pb, sv=sv: e.tensor_tensor(
                        out=sv, in0=pb[0:48, 0:384].rearrange("p (i e) -> p i e", e=96), in1=sv, op=ALU.add),
                        reads=[f"P{5 + half}", "Ss"], writes=["Ss"])
                    S.op("dve", lambda e, sv=sv, s0=s0: e.tensor_tensor(
                        out=sv, in0=sv, in1=GAMg[:, 33 + s0:33 + s0 + 4].unsqueeze(2).to_broadcast([48, 4, 96]), op=ALU.mult),
                        reads=["Ss", "GAMg"], writes=["Ss"])

        def g6(ci):
            tok0, T, nseq, seq0, tk = ctx(ci)
            Pn, nn = P[2 + ci % 3], f"P{2 + ci % 3}"
            sm_, sn = SMALL[ci % 3], f"gSM{ci % 3}"
            S.op("act", lambda e: e.activation(out=JK[0:T, :], in_=Pn[0:T, 0:96], func=AF.Square, accum_out=sm_[0:T, 0:1]),
                 reads=[nn], writes=["gJK", sn])
            S.op("act", lambda e: e.activation(out=sm_[0:T, 1:2], in_=sm_[0:T, 0:1], func=AF.Sqrt, bias=EPS, scale=1.0 / 96),
                 reads=[sn], writes=[sn])

        def g7(ci):
            tok0, T, nseq, seq0, tk = ctx(ci)
            Pn, nn = P[2 + ci % 3], f"P{2 + ci % 3}"
            sm_, sn = SMALL[ci % 3], f"gSM{ci % 3}"
            S.op("dve", lambda e: e.reciprocal(out=sm_[0:T, 1:2], in_=sm_[0:T, 1:2]), reads=[sn], writes=[sn])
            S.op("dve", lambda e: e.tensor_scalar_mul(out=ON[ci % 2][0:T, :], in0=Pn[0:T, 0:96], scalar1=sm_[0:T, 1:2]),
                 reads=[nn, sn], writes=[f"gON{ci % 2}"])

        def g8(ci):
            tok0, T, nseq, seq0, tk = ctx(ci)
            S.op("pe", lambda e: e.transpose(out=Pbf[7][0:96, 0:T], in_=ON[ci % 2][0:T, :], identity=c.ident_bf[0:T, 0:T]),
                 reads=[f"gON{ci % 2}", "ident"], writes=["P7"])

        def g9(ci):
            tok0, T, nseq, seq0, tk = ctx(ci)
            S.op("act", lambda e: e.copy(out=EO[:, tk], in_=Pbf[7][0:96, 0:T]), reads=["P7"], writes=["EO"])

        plan = [(g9, 8), (g8, 7), (g7, 6), (g6, 5), (g5, 4), (g4, 3), (g3, 2), (g2, 1), (g1, 0)]
        for it in range(NCH + 8):
            for fn, k in plan:
                ci = it - k
                if 0 <= ci < NCH:
                    fn(ci)
        drain(c, 99)
        S.op("dve", lambda e: e.scalar_tensor_tensor(out=YGh[:, 0, :], in0=EO, scalar=GNG[:, h:h + 1], in1=GGs, op0=ALU.mult, op1=ALU.mult),
             reads=["EO", "GGs", "glac"], writes=["YGh"])
        apply_wout(c, l, YGh, "YGh", 640 + 96 * h, 96, 1, defer=True)
        S.dma("sp", O["p_gla_S"][l, h], Sp[32 % 2], reads=[f"Sp{32 % 2}"])
        S.dma("sp", O["s_gla_S"][l][:, h].rearrange("i k v -> k i v"), Ss, reads=["Ss"])

    for h in range(4):
        g_head(h)
    drain(c, 99)
    S.barrier()
    A.release()


_NC_CACHE = {}


def kernel(**inputs):
    import os
    stage = inputs.pop("_stage", int(os.environ.get("KSTAGE", "99")))
    raw = inputs.pop("_raw", False)
    if stage not in _NC_CACHE:
        _NC_CACHE[stage] = build(stage)
    nc = _NC_CACHE[stage]
    f = lambda a: np.ascontiguousarray(np.asarray(a, dtype=np.float32))
    shared = {}
    for nm in ("ffn1_norm_g", "mix_norm_g", "ffn2_norm_g", "final_norm_g", "ffn1_w1", "ffn1_w3", "ffn1_w2",
               "ffn2_w1", "ffn2_w3", "ffn2_w2", "w_in", "w_out", "lru_conv_w", "lru_conv_b", "lru_wa", "lru_ba",
               "lru_wx", "lru_bx", "lru_lambda", "ml_conv_w", "ml_conv_b", "ml_wq", "ml_wk", "ml_wv", "ml_w_if",
               "ml_b_if", "ml_norm_g", "ml_skip", "gla_w_up", "gla_b_up", "gla_norm_g"):
        shared[nm] = f(inputs[nm])
    shared["meta"] = f(inputs["meta_tokens"])
    xp = f(inputs["x_prompt"])
    xs = f(inputs["x_sample"])
    st_names = ("state_lru_h", "state_lru_conv", "state_mlstm_C", "state_mlstm_n", "state_mlstm_m",
                "state_mlstm_conv", "state_gla_S")
    states = {nm: f(inputs[nm]) for nm in st_names}
    in_maps = []
    for ci in range(8):
        m = dict(shared)
        m["xp"] = xp[ci]
        m["xs"] = xs[ci * NSEQ:(ci + 1) * NSEQ].reshape(NSM, D)
        for nm in st_names:
            m[nm] = np.ascontiguousarray(states[nm][:, ci * NSEQ:(ci + 1) * NSEQ])
        in_maps.append(m)
    res = run_bass_kernel_spmd(nc, in_maps, core_ids=list(range(8)))
    R = res.results
    if raw:
        return R
    yp = np.stack([R[ci]["yp"] for ci in range(8)], 0)
    ys = np.concatenate([R[ci]["ys"].reshape(NSEQ, TS, D) for ci in range(8)], 0)
    outs = [yp, ys]
    onames = ("lru_h", "lru_conv", "mlstm_C", "mlstm_n", "mlstm_m", "mlstm_conv", "gla_S")
    for nm in onames:
        outs.append(np.stack([R[ci]["p_" + nm] for ci in range(8)], 1))
    for nm in onames:
        outs.append(np.concatenate([R[ci]["s_" + nm] for ci in range(8)], 1))
    return tuple(outs)
```
